# Optimizing a Trainium2 kernel written in Bass

```python
import numpy as np
import jax
import jax.numpy as jnp
from jax import lax

D_MODEL = 1024
BATCH = 8
SEQ = 4096
DEPTH = 2

CTX_LEN = 256
GRID_W = 64

HEAD_DIM = 64
N_MIXERS = 4
GROUP_W = D_MODEL // N_MIXERS
MIX_HEADS = GROUP_W // HEAD_DIM
D_MIX = N_MIXERS * GROUP_W
D_FF = 2816
N_MOD = 9
ROPE_THETA = 10000.0
EPS = 1e-6
NEG_INF = -1e30

SWA_KV_HEADS = MIX_HEADS // 2
SWA_WINDOW = 128
SWA_BLOCK = 128
DN_CONV = 3
DN_CHUNK = 64
MLA_Q_RANK = D_MODEL // 4
MLA_KV_RANK = D_MODEL // 8
MLA_NOPE = 64
MLA_ROPE = 32
MLA_V = 64
MLA_BLOCK = 128
NA_KR = 8
NA_KC = 16
NA_QC = 16
NA_KSPAN = 2 * NA_QC

IN_SIZES = (GROUP_W, SWA_KV_HEADS * HEAD_DIM, SWA_KV_HEADS * HEAD_DIM,
            3 * GROUP_W, GROUP_W, 2 * MIX_HEADS, 2 * MIX_HEADS,
            MLA_Q_RANK, MLA_KV_RANK, MLA_ROPE,
            GROUP_W, GROUP_W, GROUP_W)
IN_PROJ = sum(IN_SIZES)

kernel_name = 'hybrid_parallel_heads_dit_block'

F32 = jnp.float32


def rms_norm(x, g):
    x32 = x.astype(F32)
    y = x32 * lax.rsqrt(jnp.mean(x32 * x32, axis=-1, keepdims=True) + EPS)
    return (y * g.astype(F32)).astype(x.dtype)


def l2_normalize(x):
    x32 = x.astype(F32)
    return x32 * lax.rsqrt(jnp.sum(x32 * x32, axis=-1, keepdims=True) + EPS)


def modulation(cvec, w, b):
    m = jax.nn.silu(cvec) @ w + b
    m = m.reshape(m.shape[0], 1, N_MOD, D_MODEL)
    return [m[:, :, j] for j in range(N_MOD)]


def pre_norm(h, g, shift, scale):
    return rms_norm(h, g) * (1.0 + scale) + shift


def swiglu(h, wg, wu, wd):
    return (jax.nn.silu(h @ wg) * (h @ wu)) @ wd


def rope_2d(x, row, col):
    half = x.shape[-1] // 2

    def rot(t, pos):
        n = t.shape[-1]
        inv = ROPE_THETA ** (-jnp.arange(0, n, 2, dtype=F32) / n)
        ang = pos[:, None] * inv[None, :]
        cos = jnp.cos(ang)[:, None, :]
        sin = jnp.sin(ang)[:, None, :]
        t1, t2 = jnp.split(t.astype(F32), 2, axis=-1)
        return jnp.concatenate([t1 * cos - t2 * sin, t2 * cos + t1 * sin], axis=-1)

    return jnp.concatenate([rot(x[..., :half], row), rot(x[..., half:], col)], axis=-1).astype(x.dtype)


def joint_softmax(*logits):
    sizes = [l.shape[-1] for l in logits]
    p = jax.nn.softmax(jnp.concatenate([l.astype(F32) for l in logits], axis=-1), axis=-1)
    return jnp.split(p, np.cumsum(sizes)[:-1].tolist(), axis=-1)


def context_attention(q, k, v):
    s = jnp.einsum('bqhd,bkhd->bhqk', q, k).astype(F32)
    p = jax.nn.softmax(s, axis=-1).astype(v.dtype)
    return jnp.einsum('bhqk,bkhd->bqhd', p, v)


def split_in(u):
    return jnp.split(u, np.cumsum(IN_SIZES)[:-1].tolist(), axis=-1)


def swa_mixer(q, k, v, qc, kc, vc, sink, row, col, need_ctx):
    B, S, _ = q.shape
    L = qc.shape[1]
    H, KH, d = MIX_HEADS, SWA_KV_HEADS, HEAD_DIM
    G = H // KH
    scale = d ** -0.5
    q = (rope_2d(q.reshape(B, S, H, d), row, col) * scale).reshape(B, S, KH, G, d)
    k = rope_2d(k.reshape(B, S, KH, d), row, col)
    v = v.reshape(B, S, KH, d)
    kc = kc.reshape(B, L, KH, d)
    vc = vc.reshape(B, L, KH, d)
    sink = sink.astype(F32).reshape(KH, G)
    nb = S // SWA_BLOCK
    span = SWA_BLOCK + 2 * SWA_WINDOW
    idx = np.arange(nb)[:, None] * SWA_BLOCK + np.arange(span)[None, :]
    qpos = np.arange(nb)[:, None] * SWA_BLOCK + np.arange(SWA_BLOCK)[None, :]
    kpos = idx - SWA_WINDOW
    valid = ((np.abs(qpos[:, :, None] - kpos[:, None, :]) <= SWA_WINDOW)
             & (kpos[:, None, :] >= 0) & (kpos[:, None, :] < S))
    pad = ((0, 0), (SWA_WINDOW, SWA_WINDOW), (0, 0), (0, 0))
    kb = jnp.pad(k, pad)[:, idx]
    vb = jnp.pad(v, pad)[:, idx]
    qb = q.reshape(B, nb, SWA_BLOCK, KH, G, d)
    s_loc = jnp.where(valid, jnp.einsum('bnqkgd,bnjkd->bkgnqj', qb, kb).astype(F32), NEG_INF)
    s_ctx = jnp.einsum('bnqkgd,bjkd->bkgnqj', qb, kc)
    s_sink = jnp.broadcast_to(sink[None, :, :, None, None, None], s_ctx.shape[:-1] + (1,))
    p_loc, p_ctx, _ = joint_softmax(s_loc, s_ctx, s_sink)
    o = (jnp.einsum('bkgnqj,bnjkd->bnqkgd', p_loc.astype(v.dtype), vb)
         + jnp.einsum('bkgnqj,bjkd->bnqkgd', p_ctx.astype(v.dtype), vc)).reshape(B, S, GROUP_W)
    if not need_ctx:
        return o, None
    qc = qc.reshape(B, L, KH, G, d) * scale
    sc = jnp.einsum('bqkgd,bjkd->bkgqj', qc, kc)
    sc_sink = jnp.broadcast_to(sink[None, :, :, None, None], sc.shape[:-1] + (1,))
    pc, _ = joint_softmax(sc, sc_sink)
    oc = jnp.einsum('bkgqj,bjkd->bqkgd', pc.astype(vc.dtype), vc).reshape(B, L, GROUP_W)
    return o, oc


def short_conv(x, w):
    C = x.shape[-1]
    p = DN_CONV // 2
    return lax.conv_general_dilated(x, w[:, None, :].astype(x.dtype), window_strides=(1,),
                                    padding=[(p, p)], dimension_numbers=('NWC', 'WIO', 'NWC'),
                                    feature_group_count=C)


def gated_delta_chunked(q, k, v, g, beta, state, with_out):
    B, T, H, dk = q.shape
    dv = v.shape[-1]
    C = DN_CHUNK
    N = T // C

    def blk(t):
        t = t.astype(F32).reshape((B, N, C, H) + t.shape[3:])
        return jnp.moveaxis(t, (1, 3), (0, 2))

    qb, kb, vb, gb, bb = blk(q), blk(k), blk(v), blk(g), blk(beta)
    gc = jnp.cumsum(gb, axis=-1)
    incl = np.tril(np.ones((C, C), dtype=bool))
    strict = np.tril(np.ones((C, C), dtype=bool), -1)
    diff = gc[..., :, None] - gc[..., None, :]
    decay = jnp.where(incl, jnp.exp(jnp.where(incl, diff, 0.0)), 0.0)
    kbeta = kb * bb[..., None]
    a_mat = jnp.where(strict, jnp.einsum('nbhcd,nbhjd->nbhcj', kbeta, kb) * decay, 0.0) + jnp.eye(C, dtype=F32)
    u = lax.linalg.triangular_solve(a_mat, vb * bb[..., None], left_side=True, lower=True, unit_diagonal=True)
    w = lax.linalg.triangular_solve(a_mat, kbeta * jnp.exp(gc)[..., None], left_side=True, lower=True,
                                    unit_diagonal=True)
    xs = (kb, u, w, gc)
    if with_out:
        intra = jnp.einsum('nbhcd,nbhjd->nbhcj', qb, kb) * decay
        xs = xs + (qb, intra)

    def step(s, xs_n):
        k_n, u_n, w_n, g_n = xs_n[:4]
        v_new = u_n - jnp.einsum('bhcd,bhde->bhce', w_n, s)
        g_last = g_n[..., -1]
        s_next = (s * jnp.exp(g_last)[..., None, None]
                  + jnp.einsum('bhcd,bhce->bhde', k_n * jnp.exp(g_last[..., None] - g_n)[..., None], v_new))
        if with_out:
            q_n, a_n = xs_n[4:]
            o = (jnp.einsum('bhcd,bhde->bhce', q_n * jnp.exp(g_n)[..., None], s)
                 + jnp.einsum('bhcj,bhje->bhce', a_n, v_new))
            return s_next, o
        return s_next, None

    s_fin, o = lax.scan(step, state.astype(F32), xs)
    if with_out:
        o = jnp.moveaxis(o, (0, 2), (1, 3)).reshape(B, T, H, dv)
    return o, s_fin


def deltanet_mixer(qkv, z, a, b, qkv_c, z_c, a_c, b_c, conv_w, a_log, dt_bias, norm_g, need_ctx):
    H, d = MIX_HEADS, HEAD_DIM

    def prep(qkv_t, a_t, b_t):
        B, T, _ = qkv_t.shape
        h = jax.nn.silu(short_conv(qkv_t, conv_w))
        q_t, k_t, v_t = jnp.split(h, 3, axis=-1)
        q_t = l2_normalize(q_t.reshape(B, T, H, d)) * (d ** -0.5)
        k_t = l2_normalize(k_t.reshape(B, T, H, d))
        v_t = v_t.reshape(B, T, H, d).astype(F32)
        g_t = -jnp.exp(a_log.astype(F32)) * jax.nn.softplus(a_t.reshape(B, T, 2, H).astype(F32)
                                                            + dt_bias.astype(F32))
        beta_t = jax.nn.sigmoid(b_t.reshape(B, T, 2, H).astype(F32))
        return q_t, k_t, v_t, g_t, beta_t

    def gated_out(o, z_t):
        B, T = z_t.shape[:2]
        y = rms_norm(o, norm_g) * jax.nn.silu(z_t.reshape(B, T, H, d).astype(F32))
        return y.reshape(B, T, GROUP_W).astype(z_t.dtype)

    def rev(t):
        return jnp.flip(t, axis=1)

    qc, kc, vc, gcx, bcx = prep(qkv_c, a_c, b_c)
    ql, kl, vl, gl, bl = prep(qkv, a, b)
    s0 = jnp.zeros((qkv.shape[0], H, d, d), F32)
    oc_f, s_f = gated_delta_chunked(qc, kc, vc, gcx[:, :, 0], bcx[:, :, 0], s0, need_ctx)
    oc_b, s_b = gated_delta_chunked(rev(qc), rev(kc), rev(vc), rev(gcx[:, :, 1]), rev(bcx[:, :, 1]), s0, need_ctx)
    ol_f, _ = gated_delta_chunked(ql, kl, vl, gl[:, :, 0], bl[:, :, 0], s_f, True)
    ol_b, _ = gated_delta_chunked(rev(ql), rev(kl), rev(vl), rev(gl[:, :, 1]), rev(bl[:, :, 1]), s_b, True)
    y = gated_out(ol_f + rev(ol_b), z)
    if not need_ctx:
        return y, None
    return y, gated_out(oc_f + rev(oc_b), z_c)


def mla_mixer(cq, ckv, kr, cq_c, ckv_c, kr_c, q_norm_g, w_uq, kv_norm_g, w_ukv, row, col, need_ctx):
    H = MIX_HEADS
    dqk = MLA_NOPE + MLA_ROPE
    scale = dqk ** -0.5

    def heads(cq_t, ckv_t, kr_t, rotary):
        B, T, _ = cq_t.shape
        q_t = (rms_norm(cq_t, q_norm_g) @ w_uq).reshape(B, T, H, dqk)
        kv = (rms_norm(ckv_t, kv_norm_g) @ w_ukv).reshape(B, T, H, MLA_NOPE + MLA_V)
        q_nope, q_rope = q_t[..., :MLA_NOPE], q_t[..., MLA_NOPE:]
        k_nope, v_t = kv[..., :MLA_NOPE], kv[..., MLA_NOPE:]
        k_rope = kr_t.reshape(B, T, 1, MLA_ROPE)
        if rotary:
            q_rope = rope_2d(q_rope, row, col)
            k_rope = rope_2d(k_rope, row, col)
        q_t = jnp.concatenate([q_nope, q_rope], axis=-1) * scale
        k_t = jnp.concatenate([k_nope, jnp.broadcast_to(k_rope, (B, T, H, MLA_ROPE))], axis=-1)
        return q_t, k_t, v_t

    q, k, v = heads(cq, ckv, kr, True)
    qc, kc, vc = heads(cq_c, ckv_c, kr_c, False)
    B, S = q.shape[:2]
    k_all = jnp.concatenate([kc, k], axis=1)
    v_all = jnp.concatenate([vc, v], axis=1)
    nb = S // MLA_BLOCK

    def block(qi):
        s = jnp.einsum('bqhd,bkhd->bhqk', qi, k_all).astype(F32)
        p = jax.nn.softmax(s, axis=-1).astype(v_all.dtype)
        return jnp.einsum('bhqk,bkhd->bqhd', p, v_all)

    qb = jnp.moveaxis(q.reshape(B, nb, MLA_BLOCK, H, dqk), 1, 0)
    o = jnp.moveaxis(lax.map(block, qb), 0, 1).reshape(B, S, H * MLA_V)
    if not need_ctx:
        return o, None
    oc = context_attention(qc, kc, vc).reshape(B, qc.shape[1], H * MLA_V)
    return o, oc


def na_mixer(q, k, v, qc, kc, vc, rpb, need_ctx):
    B, S, _ = q.shape
    L = qc.shape[1]
    H, d, W = MIX_HEADS, HEAD_DIM, GRID_W
    rows = S // W
    kr = min(NA_KR, rows)
    scale = d ** -0.5
    qg = q.reshape(B, rows, W, H, d) * scale
    kg = k.reshape(B, rows, W, H, d)
    vg = v.reshape(B, rows, W, H, d)
    kc = kc.reshape(B, L, H, d)
    vc = vc.reshape(B, L, H, d)
    ncb = W // NA_QC
    col_start = np.clip(np.arange(ncb) * NA_QC - NA_KC // 2, 0, W - NA_KSPAN)
    kcol = col_start[:, None] + np.arange(NA_KSPAN)[None, :]
    qcol = np.arange(ncb)[:, None] * NA_QC + np.arange(NA_QC)[None, :]
    cs = np.clip(qcol - NA_KC // 2, 0, W - NA_KC)
    col_ok = (kcol[:, None, :] >= cs[:, :, None]) & (kcol[:, None, :] < cs[:, :, None] + NA_KC)
    dc_idx = np.clip(kcol[:, None, :] - qcol[:, :, None] + NA_KC - 1, 0, 2 * NA_KC - 2)

    def row_block(args):
        r, q_row = args
        rs = jnp.clip(r - kr // 2, 0, rows - kr)
        k_rows = lax.dynamic_slice_in_dim(kg, rs, kr, axis=1)[:, :, kcol]
        v_rows = lax.dynamic_slice_in_dim(vg, rs, kr, axis=1)[:, :, kcol]
        qb = q_row.reshape(B, ncb, NA_QC, H, d)
        dr_idx = rs + jnp.arange(kr) - r + NA_KR - 1
        bias = rpb[:, dr_idx[:, None, None, None], dc_idx[None]]
        s_loc = (jnp.einsum('bnqhd,brnkhd->bhnqrk', qb, k_rows).astype(F32)
                 + jnp.transpose(bias, (0, 2, 3, 1, 4)).astype(F32))
        s_loc = jnp.where(col_ok[:, :, None, :], s_loc, NEG_INF).reshape(B, H, ncb, NA_QC, kr * NA_KSPAN)
        s_ctx = jnp.einsum('bnqhd,bkhd->bhnqk', qb, kc)
        p_loc, p_ctx = joint_softmax(s_loc, s_ctx)
        p_loc = p_loc.reshape(B, H, ncb, NA_QC, kr, NA_KSPAN).astype(v_rows.dtype)
        o = (jnp.einsum('bhnqrk,brnkhd->bnqhd', p_loc, v_rows)
             + jnp.einsum('bhnqk,bkhd->bnqhd', p_ctx.astype(vc.dtype), vc))
        return o.reshape(B, W, H * d)

    o = lax.map(row_block, (jnp.arange(rows), jnp.moveaxis(qg, 1, 0)))
    o = jnp.moveaxis(o, 0, 1).reshape(B, S, GROUP_W)
    if not need_ctx:
        return o, None
    oc = context_attention(qc.reshape(B, L, H, d) * scale, kc, vc).reshape(B, L, GROUP_W)
    return o, oc


def setup_inputs(seed: int = 0) -> dict:
    key = jax.random.key(seed)
    ks = iter(jax.random.split(key, 32))
    L, D = DEPTH, D_MODEL

    def nrm(shape, std):
        return std * jax.random.normal(next(ks), shape, F32)

    def gain(shape):
        return 1.0 + nrm(shape, 0.02)

    dt = jnp.exp(jax.random.uniform(next(ks), (L, 2, MIX_HEADS), F32, float(np.log(1e-3)), float(np.log(1e-1))))
    return {
        'x': nrm((BATCH, SEQ, D), 1.0),
        'c': nrm((BATCH, D), 1.0),
        'ctx': nrm((BATCH, CTX_LEN, D), 1.0),
        'c_ctx': nrm((D,), 1.0),
        'ada_w': nrm((L, D, N_MOD * D), 0.5 * D ** -0.5),
        'ada_b': nrm((L, N_MOD * D), 0.01),
        'norm1_g': gain((L, D)),
        'ffn1_wg': nrm((L, D, D_FF), D ** -0.5),
        'ffn1_wu': nrm((L, D, D_FF), D ** -0.5),
        'ffn1_wd': nrm((L, D_FF, D), D_FF ** -0.5),
        'norm2_g': gain((L, D)),
        'w_in': nrm((L, D, IN_PROJ), D ** -0.5),
        'swa_sink': nrm((L, MIX_HEADS), 0.5),
        'dn_conv_w': nrm((L, DN_CONV, 3 * GROUP_W), DN_CONV ** -0.5),
        'dn_a_log': jnp.log(jax.random.uniform(next(ks), (L, 2, MIX_HEADS), F32, 1.0, 16.0)),
        'dn_dt_bias': dt + jnp.log(-jnp.expm1(-dt)),
        'dn_norm_g': gain((L, HEAD_DIM)),
        'mla_q_norm_g': gain((L, MLA_Q_RANK)),
        'mla_w_uq': nrm((L, MLA_Q_RANK, MIX_HEADS * (MLA_NOPE + MLA_ROPE)), MLA_Q_RANK ** -0.5),
        'mla_kv_norm_g': gain((L, MLA_KV_RANK)),
        'mla_w_ukv': nrm((L, MLA_KV_RANK, MIX_HEADS * (MLA_NOPE + MLA_V)), MLA_KV_RANK ** -0.5),
        'na_rpb': nrm((L, MIX_HEADS, 2 * NA_KR - 1, 2 * NA_KC - 1), 0.1),
        'w_out': nrm((L, D_MIX, D), D_MIX ** -0.5),
        'norm3_g': gain((L, D)),
        'ffn2_wg': nrm((L, D, D_FF), D ** -0.5),
        'ffn2_wu': nrm((L, D, D_FF), D ** -0.5),
        'ffn2_wd': nrm((L, D_FF, D), D_FF ** -0.5),
        'final_norm_g': gain((D,)),
    }


def reference(x, c, ctx, c_ctx, ada_w, ada_b, norm1_g, ffn1_wg, ffn1_wu, ffn1_wd, norm2_g, w_in,
              swa_sink, dn_conv_w, dn_a_log, dn_dt_bias, dn_norm_g, mla_q_norm_g, mla_w_uq,
              mla_kv_norm_g, mla_w_ukv, na_rpb, w_out, norm3_g, ffn2_wg, ffn2_wu, ffn2_wd, final_norm_g):
    S = x.shape[1]
    t = jnp.arange(S)
    row = (t // GRID_W).astype(F32)
    col = (t % GRID_W).astype(F32)
    xc = ctx
    for i in range(DEPTH):
        need_ctx = i < DEPTH - 1
        mx = modulation(c, ada_w[i], ada_b[i])
        mc = modulation(c_ctx[None, :], ada_w[i], ada_b[i])
        x = x + 0.5 * mx[2] * swiglu(pre_norm(x, norm1_g[i], mx[0], mx[1]), ffn1_wg[i], ffn1_wu[i], ffn1_wd[i])
        xc = xc + 0.5 * mc[2] * swiglu(pre_norm(xc, norm1_g[i], mc[0], mc[1]), ffn1_wg[i], ffn1_wu[i], ffn1_wd[i])
        ux = split_in(pre_norm(x, norm2_g[i], mx[3], mx[4]) @ w_in[i])
        uc = split_in(pre_norm(xc, norm2_g[i], mc[3], mc[4]) @ w_in[i])
        ya, yac = swa_mixer(ux[0], ux[1], ux[2], uc[0], uc[1], uc[2], swa_sink[i], row, col, need_ctx)
        yd, ydc = deltanet_mixer(ux[3], ux[4], ux[5], ux[6], uc[3], uc[4], uc[5], uc[6], dn_conv_w[i],
                                 dn_a_log[i], dn_dt_bias[i], dn_norm_g[i], need_ctx)
        ym, ymc = mla_mixer(ux[7], ux[8], ux[9], uc[7], uc[8], uc[9], mla_q_norm_g[i], mla_w_uq[i],
                            mla_kv_norm_g[i], mla_w_ukv[i], row, col, need_ctx)
        yn, ync = na_mixer(ux[10], ux[11], ux[12], uc[10], uc[11], uc[12], na_rpb[i], need_ctx)
        x = x + mx[5] * (jnp.concatenate([ya, yd, ym, yn], axis=-1) @ w_out[i])
        x = x + 0.5 * mx[8] * swiglu(pre_norm(x, norm3_g[i], mx[6], mx[7]), ffn2_wg[i], ffn2_wu[i], ffn2_wd[i])
        if need_ctx:
            xc = xc + mc[5] * (jnp.concatenate([yac, ydc, ymc, ync], axis=-1) @ w_out[i])
            xc = xc + 0.5 * mc[8] * swiglu(pre_norm(xc, norm3_g[i], mc[6], mc[7]), ffn2_wg[i], ffn2_wu[i],
                                           ffn2_wd[i])
    return rms_norm(x, final_norm_g)
```

```python
import types
import numpy as np
from contextlib import ExitStack
import concourse.bass as bass
import concourse.mybir as mybir
from concourse.bass_utils import run_bass_kernel_spmd

F32 = mybir.dt.float32
BF16 = mybir.dt.bfloat16
AF = mybir.ActivationFunctionType
ALU = mybir.AluOpType
AX = mybir.AxisListType

D = 1024
S = 4096
L = 256
T = S + L
DFF = 2816
NFF = DFF // 128
DEPTH = 2
EPS = 1e-6
INP = 2736


class Buf:
    def __init__(self, t, name):
        self.t = t
        self.name = name
        self.w = None
        self.r = {}
        self.dsem = None

    def __getitem__(self, idx):
        return self.t[idx]


class KB:
    def __init__(self, nc, es, n_dma_sems=64):
        self.nc = nc
        self.engs = {"pe": nc.tensor, "act": nc.scalar, "dve": nc.vector,
                     "pool": nc.gpsimd, "sp": nc.sync}
        self.sem = {e: es.enter_context(nc.semaphore("se_" + e)) for e in self.engs}
        self.cnt = {e: 0 for e in self.engs}
        self.seen = {e: {} for e in self.engs}
        self.latest = {}
        self.free = [[es.enter_context(nc.semaphore("sd%d" % i)), 0] for i in range(n_dma_sems)]
        self.uid = 0
        self.rr = 0
        self.defer = None
        self.pending = []

    def sb(self, st, shape, dt, name=None, dma=False):
        self.uid += 1
        name = "%s_%d" % (name or "b", self.uid)
        t = st.enter_context(self.nc.sbuf_tensor(name, list(shape), dt))
        b = Buf(t, name)
        if dma:
            b.dsem = self.free.pop()
            st.callback(lambda b=b: self.free.append(b.dsem))
        return b

    def ps(self, st, shape, dt=F32, name=None):
        self.uid += 1
        name = "%s_%d" % (name or "p", self.uid)
        t = st.enter_context(self.nc.psum_tensor(name, list(shape), dt))
        return Buf(t, name)

    def _wait(self, e, tok):
        sem, val, key = tok
        if self.seen[e].get(key, 0) >= val:
            return
        self.engs[e].wait_ge(sem, val)
        self.seen[e][key] = val

    def _deps(self, e, reads, writes):
        toks = []
        for b in reads:
            if b is not None and b.w is not None:
                toks.append(b.w)
        for b in writes:
            if b is None:
                continue
            if b.w is not None:
                toks.append(b.w)
            toks.extend(b.r.values())
        for tok in toks:
            if e == "pe" and tok[2] == "se_pe":
                continue
            self._wait(e, tok)

    def pump(self, n=1):
        q = self.pending
        d, self.defer = self.defer, None
        while n > 0 and q:
            item = q.pop(0)
            if item[0] == "op":
                self.op(*item[1:])
            else:
                self.dma(*item[1], **item[2])
            n -= 1
        self.defer = d

    def op(self, e, fn, reads=(), writes=()):
        if self.defer is not None:
            if fn.__closure__:
                fn = types.FunctionType(fn.__code__, fn.__globals__, fn.__name__, fn.__defaults__,
                                        tuple(types.CellType(c.cell_contents) for c in fn.__closure__))
            self.defer.append(("op", e, fn, list(reads), list(writes)))
            return None
        self._deps(e, reads, writes)
        ins = fn(self.engs[e])
        self.cnt[e] += 1
        ins.then_inc(self.sem[e], 1)
        key = "se_" + e
        tok = (self.sem[e], self.cnt[e], key)
        self.latest[key] = tok
        for b in reads:
            if b is not None:
                b.r[e] = tok
        for b in writes:
            if b is not None:
                b.w = tok
                b.r = {}
        return tok

    def dma(self, q, out_ap, in_ap, out_buf=None, in_buf=None, n=1, fn=None):
        if self.defer is not None:
            self.defer.append(("dma", (q, out_ap, in_ap), dict(out_buf=out_buf, in_buf=in_buf)))
            return None
        sb = out_buf if out_buf is not None else in_buf
        assert sb is not None and sb.dsem is not None, "dma needs an SBUF buf with dsem"
        self._deps(q, [in_buf] if in_buf is not None else [], [out_buf] if out_buf is not None else [])
        pairs = list(zip(out_ap, in_ap)) if isinstance(out_ap, (list, tuple)) else [(out_ap, in_ap)]
        for o, i in pairs:
            self.engs[q].dma_start(out=o, in_=i).then_inc(sb.dsem[0], 16)
            sb.dsem[1] += 16
        key = "sd_" + str(id(sb.dsem))
        tok = (sb.dsem[0], sb.dsem[1], key)
        self.latest[key] = tok
        if out_buf is not None:
            out_buf.w = tok
            out_buf.r = {}
        else:
            in_buf.r["dma_" + q] = tok
        return tok

    def barrier(self):
        for e in self.engs:
            for tok in list(self.latest.values()):
                self._wait(e, tok)

    def finish(self):
        self.barrier()


def mm(K, out, out_ap, lhs, lhs_ap, rhs, rhs_ap, start=True, stop=True):
    return K.op("pe", lambda e: e.matmul(out_ap, lhs_ap, rhs_ap, start=start, stop=stop),
                reads=[lhs, rhs], writes=[out])


def tr(K, out, out_ap, in_, in_ap, ident, ident_ap):
    return K.op("pe", lambda e: e.transpose(out_ap, in_ap, ident_ap), reads=[in_, ident], writes=[out])


class Prog:
    def __init__(self, debug=False, phases=None):
        self.debug = debug
        self.phases = phases


def build_program(debug=False, stop_after=None, only=None, feed=()):
    nc = bass.Bass("TRN2", target_bir_lowering=False)
    kind_s = "ExternalOutput" if debug else "Internal"

    def din(name, shape, dt=F32):
        return nc.dram_tensor(name, list(shape), dt, kind="ExternalInput").ap()

    def dscr(name, shape, dt=F32):
        if name in feed:
            return nc.dram_tensor(name, list(shape), dt, kind="ExternalInput").ap()
        if debug:
            return nc.dram_tensor(name, list(shape), dt, kind="ExternalOutput").ap()
        return nc.dram_tensor(name, list(shape), dt).ap()

    x_in = din("x", [S, D])
    c_in = din("c", [1, D])
    ctx_in = din("ctx", [L, D])
    cctx_in = din("c_ctx", [1, D])
    ada_w = din("ada_w", [DEPTH, D, 9 * D])
    ada_b = din("ada_b", [DEPTH, 9 * D])
    norm1_g = din("norm1_g", [DEPTH, D])
    ffn1_wg = din("ffn1_wg", [DEPTH, D, DFF])
    ffn1_wu = din("ffn1_wu", [DEPTH, D, DFF])
    ffn1_wd = din("ffn1_wd", [DEPTH, DFF, D])
    norm2_g = din("norm2_g", [DEPTH, D])
    norm3_g = din("norm3_g", [DEPTH, D])
    ffn2_wg = din("ffn2_wg", [DEPTH, D, DFF])
    ffn2_wu = din("ffn2_wu", [DEPTH, D, DFF])
    ffn2_wd = din("ffn2_wd", [DEPTH, DFF, D])
    final_g = din("final_norm_g", [1, D])
    ident_in = din("ident", [128, 128])
    w_in = din("w_in", [DEPTH, D, INP])
    w_out = din("w_out", [DEPTH, D, D])
    swa_sink = din("swa_sink", [DEPTH, 4])
    mla_qg = din("mla_q_norm_g", [DEPTH, 256])
    mla_wuq = din("mla_w_uq", [DEPTH, 256, 384])
    mla_kvg = din("mla_kv_norm_g", [DEPTH, 128])
    mla_wukv = din("mla_w_ukv", [DEPTH, 128, 512])
    tab64 = din("tab64", [2, 128, T])
    tab96 = din("tab96", [2, 96, T])
    tab32 = din("tab32", [2, 32, T])
    mask_pn = din("mask_pn", [2, 128, 128])
    rpbx = din("rpbx", [DEPTH, 64, 4, 15, 64])
    tri_in = din("tri", [4, 128, 128])
    dn_conv_w = din("dn_conv_w", [DEPTH, 3, 768])
    dn_a_log = din("dn_a_log", [DEPTH, 8])
    dn_dt_bias = din("dn_dt_bias", [DEPTH, 8])
    dn_norm_g = din("dn_norm_g", [DEPTH, 64])
    okm_in = din("okm", [128, 64])
    y_out = nc.dram_tensor("y", [S, D], F32, kind="ExternalOutput").ap()

    xT = dscr("xT", [D, T])
    swa_qT = dscr("swa_qT", [256, T], BF16)
    swa_kT = dscr("swa_kT", [128, T], BF16)
    swa_v = dscr("swa_v", [T, 128], BF16)
    dn_qkvT = dscr("dn_qkvT", [768, T], F32)
    dn_zab = dscr("dn_zab", [T, 272], F32)
    mla_qT = dscr("mla_qT", [384, T], BF16)
    mla_kT = dscr("mla_kT", [384, T], BF16)
    mla_v = dscr("mla_v", [T, 256], BF16)
    na_qT = dscr("na_qT", [256, T], BF16)
    na_kT = dscr("na_kT", [256, T], BF16)
    na_v = dscr("na_v", [T, 256], BF16)
    ycat = dscr("ycat", [T, D], BF16)
    dn_hT = dscr("dn_hT", [768, T], F32)
    dn_h = dscr("dn_h", [T, 768], F32)
    dn_o = dscr("dn_o", [2, T, 256], F32)

    with ExitStack() as es:
        K = KB(nc, es)
        gs = ExitStack()
        es.enter_context(gs)
        ident = K.sb(gs, [128, 128], F32, "ident", dma=True)
        identb = K.sb(gs, [128, 128], BF16, "identb")
        ones_b = K.sb(gs, [128, 128], BF16, "ones_b")
        ones_f = K.sb(gs, [1, 2], F32, "ones_f")
        K.dma("sp", ident[:], ident_in[:, :], out_buf=ident)
        K.op("dve", lambda e: e.tensor_copy(identb[:], ident[:]), reads=[ident], writes=[identb])
        K.op("dve", lambda e: e.memset(ones_b[:], 1.0), writes=[ones_b])
        K.op("dve", lambda e: e.memset(ones_f[:], 1.0), writes=[ones_f])
        modA = [K.sb(gs, [128, 8, 2], F32, "modA%d" % j) for j in range(3)]
        modB = [K.sb(gs, [128, 8, 2], F32, "modB%d" % j) for j in range(3)]
        modG = [K.sb(gs, [128, 8, 2], F32, "modG%d" % j) for j in range(3)]
        fin_g = K.sb(gs, [128, 8], F32, "fin_g")

        def phase_init():
            with ExitStack() as st:
                xin = [K.sb(st, [128, D], F32, "xin", dma=True) for _ in range(2)]
                xo = [K.sb(st, [128, 8, 128], F32, "xo", dma=True) for _ in range(2)]
                pt = [K.ps(st, [128, 512], F32, "pt") for _ in range(4)]
                xTv = xT.rearrange("(k p) t -> p k t", p=128)
                def ld_(i):
                    src = ctx_in[i * 128:(i + 1) * 128, :] if i < 2 else x_in[(i - 2) * 128:(i - 1) * 128, :]
                    K.dma("sp", xin[i % 2][:], src, out_buf=xin[i % 2])

                ld_(0)
                for i in range(T // 128):
                    xi = xin[i % 2]
                    o = xo[i % 2]
                    if i + 1 < T // 128:
                        ld_(i + 1)
                    for hh in range(2):
                        p = pt[(2 * i + hh) % 4]
                        for kk in range(4):
                            k = hh * 4 + kk
                            tr(K, p, p[:, kk * 128:(kk + 1) * 128], xi, xi[:, k * 128:(k + 1) * 128], ident, ident[:])
                        if hh == 0:
                            K.op("act", lambda e: e.copy(o[:, 0:4, :], p[:].rearrange("p (k t) -> p k t", k=4)),
                                 reads=[p], writes=[o])
                        else:
                            K.op("dve", lambda e: e.tensor_copy(o[:, 4:8, :], p[:].rearrange("p (k t) -> p k t", k=4)),
                                 reads=[p], writes=[o])
                    K.dma("sp", xTv[:, :, i * 128:(i + 1) * 128], o[:], in_buf=o)
                K.barrier()

        def phase_mod(li):
            with ExitStack() as st:
                cv = K.sb(st, [1, 2, D], F32, "cv", dma=True)
                scv = K.sb(st, [128, 8, 2], F32, "scv")
                wblk = [K.sb(st, [128, 8, D], F32, "wblk", dma=True) for _ in range(2)]
                brow = K.sb(st, [1, 9 * D], F32, "brow", dma=True)
                grow = K.sb(st, [1, 4, D], F32, "grow", dma=True)
                pm = K.ps(st, [128, 8, 2], F32, "pm")
                pg = K.ps(st, [128, 4, 8], F32, "pg")
                modT = K.sb(st, [128, 72, 2], F32, "modT")
                gT = K.sb(st, [128, 4, 8], F32, "gT")
                K.dma("sp", [cv[0:1, 0, :], cv[0:1, 1, :]], [c_in[0:1, :], cctx_in[0:1, :]], out_buf=cv)
                K.dma("sp", brow[:], ada_b[li:li + 1, :], out_buf=brow)
                K.dma("sp", [grow[0:1, 0, :], grow[0:1, 1, :], grow[0:1, 2, :], grow[0:1, 3, :]],
                      [norm1_g[li:li + 1, :], norm2_g[li:li + 1, :], norm3_g[li:li + 1, :], final_g[0:1, :]],
                      out_buf=grow)
                for v in range(2):
                    for k in range(8):
                        mm(K, pm, pm[:, k, v:v + 1], cv, cv[0:1, v, k * 128:(k + 1) * 128], ones_f, ones_f[0:1, 0:1])
                K.op("act", lambda e: e.activation(out=scv[:], in_=pm[:], func=AF.Silu), reads=[pm], writes=[scv])
                for gi in range(4):
                    for k in range(8):
                        mm(K, pg, pg[:, gi, k:k + 1], grow, grow[0:1, gi, k * 128:(k + 1) * 128], ones_f, ones_f[0:1, 0:1])
                K.op("dve", lambda e: e.tensor_copy(gT[:], pg[:]), reads=[pg], writes=[gT])
                K.op("dve", lambda e: e.tensor_copy(fin_g[:], gT[:, 3, :]), reads=[gT], writes=[fin_g])
                awv = ada_w[li].rearrange("(k p) n -> p k n", p=128)
                for j in range(9):
                    wb = wblk[j % 2]
                    K.dma("sp", [wb[:, 0:4, :], wb[:, 4:8, :]],
                          [awv[:, 0:4, j * D:(j + 1) * D], awv[:, 4:8, j * D:(j + 1) * D]], out_buf=wb)
                    for m in range(8):
                        for k in range(8):
                            mm(K, pm, pm[:, m, :], wb, wb[:, k, m * 128:(m + 1) * 128], scv, scv[:, k, :],
                               start=(k == 0), stop=False)
                        mm(K, pm, pm[:, m, :], brow, brow[0:1, j * D + m * 128: j * D + (m + 1) * 128],
                           ones_f, ones_f[0:1, 0:2], start=False, stop=True)
                    K.op("dve", lambda e: e.tensor_copy(modT[:, j * 8:(j + 1) * 8, :], pm[:]), reads=[pm], writes=[modT])
                for s3 in range(3):
                    jsh, jsc, jg = 3 * s3, 3 * s3 + 1, 3 * s3 + 2
                    A, Bm, G = modA[s3], modB[s3], modG[s3]
                    K.op("dve", lambda e: e.tensor_scalar(A[:], modT[:, jsc * 8:(jsc + 1) * 8, :], 1.0, float(np.sqrt(D)),
                                                          op0=ALU.add, op1=ALU.mult), reads=[modT], writes=[A])
                    for v in range(2):
                        K.op("dve", lambda e: e.tensor_tensor(A[:, :, v], A[:, :, v], gT[:, s3, :], op=ALU.mult),
                             reads=[A, gT], writes=[A])
                    K.op("dve", lambda e: e.tensor_copy(Bm[:], modT[:, jsh * 8:(jsh + 1) * 8, :]), reads=[modT], writes=[Bm])
                    gsc = 1.0 if s3 == 1 else 0.5
                    K.op("dve", lambda e: e.tensor_scalar(G[:], modT[:, jg * 8:(jg + 1) * 8, :], gsc, None, op0=ALU.mult),
                         reads=[modT], writes=[G])
                K.barrier()

        def groups_all(with_ctx=True):
            g = []
            if with_ctx:
                g.append((0, L, 1))
            for i in range(S // 512):
                g.append((L + i * 512, 512, 0))
            return g

        def phase_ffn(wg_d, wu_d, wd_d, s3, with_ctx=True):
            A, Bm, G = modA[s3], modB[s3], modG[s3]
            with ExitStack() as st:
                Wg = K.sb(st, [128, 8, DFF], BF16, "Wg")
                Wu = K.sb(st, [128, 8, DFF], BF16, "Wu")
                Wd = K.sb(st, [128, NFF, D], BF16, "Wd")
                with ExitStack() as st2:
                    stg = [K.sb(st2, [128, DFF], F32, "stg", dma=True) for _ in range(2)]
                    ci = 0
                    ceng = ["dve", "act", "pool"]
                    for (wsrc, wdst) in ((wg_d, Wg), (wu_d, Wu)):
                        for k in range(8):
                            sg_ = stg[ci % 2]
                            K.dma("sp", sg_[:], wsrc[k * 128:(k + 1) * 128, :], out_buf=sg_)
                            e_ = ceng[ci % 3]
                            if e_ == "act":
                                K.op("act", lambda e: e.copy(wdst[:, k, :], sg_[:]), reads=[sg_], writes=[wdst])
                            else:
                                K.op(e_, lambda e: e.tensor_copy(wdst[:, k, :], sg_[:]), reads=[sg_], writes=[wdst])
                            ci += 1
                    for m in range(0, NFF, 2):
                        sg_ = stg[ci % 2]
                        K.dma("sp", sg_[:, 0:2 * D].rearrange("p (a n) -> p a n", a=2),
                              wd_d[m * 128:(m + 2) * 128, :].rearrange("(a p) n -> p a n", p=128), out_buf=sg_)
                        e_ = ceng[ci % 3]
                        src_ap = sg_[:, 0:2 * D].rearrange("p (a n) -> p a n", a=2)
                        if e_ == "act":
                            K.op("act", lambda e: e.copy(Wd[:, m:m + 2, :], src_ap), reads=[sg_], writes=[Wd])
                        else:
                            K.op(e_, lambda e: e.tensor_copy(Wd[:, m:m + 2, :], src_ap), reads=[sg_], writes=[Wd])
                        ci += 1
                    K.barrier()
                xg = [K.sb(st, [128, 8, 512], F32, "xg", dma=True) for _ in range(2)]
                xn = K.sb(st, [128, 8, 512], BF16, "xn")
                rstd = K.sb(st, [128, 512], F32, "rstd")
                actT = K.sb(st, [128, NFF, 512], BF16, "actT")
                sg = [K.sb(st, [128, 512], F32, "sg") for _ in range(2)]
                pss = K.ps(st, [128, 512], F32, "pss")
                pg = [K.ps(st, [128, 512], F32, "pg") for _ in range(2)]
                pu = [K.ps(st, [128, 512], F32, "pu") for _ in range(2)]
                po = [K.ps(st, [128, 512], F32, "po") for _ in range(2)]
                xTv = xT.rearrange("(k p) t -> p k t", p=128)
                grps = groups_all(with_ctx)

                def load(gi):
                    t0, n, v = grps[gi]
                    b = xg[gi % 2]
                    K.dma("sp", [b[:, 0:4, 0:n], b[:, 4:8, 0:n]], [xTv[:, 0:4, t0:t0 + n], xTv[:, 4:8, t0:t0 + n]], out_buf=b)

                def pre1(gi):
                    t0, n, v = grps[gi]
                    xb = xg[gi % 2]
                    K.op("act", lambda e: e.activation(out=xn[:, :, 0:n], in_=xb[:, :, 0:n], func=AF.Square),
                         reads=[xb], writes=[xn])

                def pre2(gi):
                    t0, n, v = grps[gi]
                    xb = xg[gi % 2]
                    for k in range(8):
                        mm(K, pss, pss[:, 0:n], ones_b, ones_b[:], xn, xn[:, k, 0:n], start=(k == 0), stop=(k == 7))
                    K.op("act", lambda e: e.activation(out=rstd[:, 0:n], in_=pss[:, 0:n], func=AF.Sqrt, scale=1.0, bias=float(D * EPS)),
                         reads=[pss], writes=[rstd])
                    K.op("dve", lambda e: e.reciprocal(rstd[:, 0:n], rstd[:, 0:n]), reads=[rstd], writes=[rstd])
                    for k in range(8):
                        K.op("dve", lambda e: e.scalar_tensor_tensor(out=xn[:, k, 0:n], in0=xb[:, k, 0:n], scalar=A[:, k, v:v + 1],
                                                                      in1=rstd[:, 0:n], op0=ALU.mult, op1=ALU.mult),
                             reads=[xb, A, rstd], writes=[xn])
                    for k in range(8):
                        K.op("act", lambda e: e.activation(out=xn[:, k, 0:n], in_=xn[:, k, 0:n], func=AF.Identity,
                                                           bias=Bm[:, k, v:v + 1], scale=1.0),
                             reads=[xn, Bm], writes=[xn])

                load(0)
                pre1(0)
                pre2(0)
                for gi, (t0, n, v) in enumerate(grps):
                    more = gi + 1 < len(grps)
                    if more:
                        load(gi + 1)
                    xb = xg[gi % 2]
                    for m in range(NFF):
                        pgm, pum, sgm = pg[m % 2], pu[m % 2], sg[m % 2]
                        for k in range(8):
                            mm(K, pgm, pgm[:, 0:n], Wg, Wg[:, k, m * 128:(m + 1) * 128], xn, xn[:, k, 0:n], start=(k == 0), stop=(k == 7))
                        for k in range(8):
                            mm(K, pum, pum[:, 0:n], Wu, Wu[:, k, m * 128:(m + 1) * 128], xn, xn[:, k, 0:n], start=(k == 0), stop=(k == 7))
                        K.op("act", lambda e: e.activation(out=sgm[:, 0:n], in_=pgm[:, 0:n], func=AF.Silu), reads=[pgm], writes=[sgm])
                        K.op("dve", lambda e: e.tensor_tensor(actT[:, m, 0:n], sgm[:, 0:n], pum[:, 0:n], op=ALU.mult),
                             reads=[sgm, pum], writes=[actT])
                    if more:
                        pre1(gi + 1)
                    for f in range(8):
                        pof = po[f % 2]
                        for m in range(NFF):
                            mm(K, pof, pof[:, 0:n], Wd, Wd[:, m, f * 128:(f + 1) * 128], actT, actT[:, m, 0:n], start=(m == 0), stop=(m == NFF - 1))
                        K.op("dve", lambda e: e.scalar_tensor_tensor(out=xb[:, f, 0:n], in0=pof[:, 0:n], scalar=G[:, f, v:v + 1],
                                                                      in1=xb[:, f, 0:n], op0=ALU.mult, op1=ALU.add),
                             reads=[pof, G, xb], writes=[xb])
                        if f == 1 and more:
                            pre2(gi + 1)
                    K.dma("sp", [xTv[:, 0:4, t0:t0 + n], xTv[:, 4:8, t0:t0 + n]], [xb[:, 0:4, 0:n], xb[:, 4:8, 0:n]], in_buf=xb)
                K.barrier()


        def prenorm(xb, xn, rstd, pss, A, Bm, v, n):
            K.op("act", lambda e: e.activation(out=xn[:, :, 0:n], in_=xb[:, :, 0:n], func=AF.Square),
                 reads=[xb], writes=[xn])
            for k in range(8):
                mm(K, pss, pss[:, 0:n], ones_b, ones_b[:], xn, xn[:, k, 0:n], start=(k == 0), stop=(k == 7))
            K.op("act", lambda e: e.activation(out=rstd[:, 0:n], in_=pss[:, 0:n], func=AF.Sqrt, scale=1.0, bias=float(D * EPS)),
                 reads=[pss], writes=[rstd])
            K.op("dve", lambda e: e.reciprocal(rstd[:, 0:n], rstd[:, 0:n]), reads=[rstd], writes=[rstd])
            for k in range(8):
                K.op("dve", lambda e: e.scalar_tensor_tensor(out=xn[:, k, 0:n], in0=xb[:, k, 0:n], scalar=A[:, k, v:v + 1],
                                                              in1=rstd[:, 0:n], op0=ALU.mult, op1=ALU.mult),
                     reads=[xb, A, rstd], writes=[xn])
            for k in range(8):
                K.op("act", lambda e: e.activation(out=xn[:, k, 0:n], in_=xn[:, k, 0:n], func=AF.Identity,
                                                   bias=Bm[:, k, v:v + 1], scale=1.0),
                     reads=[xn, Bm], writes=[xn])

        class Ring:
            def __init__(self, bufs):
                self.bufs = bufs
                self.i = 0

            def next(self):
                b = self.bufs[self.i % len(self.bufs)]
                self.i += 1
                return b

        def rot_cols(dst, src, k, c0, nblk, qw):
            w4 = 4 * qw
            dv = dst[:, k, c0:c0 + nblk * w4].rearrange("p (b q j) -> p b q j", q=4, j=qw)
            sv = src[:, k, c0:c0 + nblk * w4].rearrange("p (b q j) -> p b q j", q=4, j=qw)
            for (qd, qs, sgn) in ((0, 1, -1.0), (1, 0, 1.0), (2, 3, -1.0), (3, 2, 1.0)):
                K.op("pool", lambda e: e.tensor_scalar(dv[:, :, qd, :], sv[:, :, qs, :], sgn, None, op0=ALU.mult),
                     reads=[src], writes=[dst])

        def phase_inproj(li):
            A, Bm = modA[1], modB[1]
            with ExitStack() as st:
                Win = K.sb(st, [128, 8, INP], BF16, "Win")
                Wrot = K.sb(st, [128, 8, INP], BF16, "Wrot")
                Wuq = K.sb(st, [128, 2, 384], BF16, "Wuq")
                Wuqr = K.sb(st, [128, 2, 384], BF16, "Wuqr")
                Wukv = K.sb(st, [128, 512], BF16, "Wukv")
                gqk = K.sb(st, [128, 4], F32, "gqk")
                with ExitStack() as st2:
                    stg = [K.sb(st2, [128, INP], F32, "stg", dma=True) for _ in range(2)]
                    grow = K.sb(st2, [1, 384], F32, "grow", dma=True)
                    pgq = K.ps(st2, [128, 4], F32, "pgq")
                    for k in range(8):
                        sg_ = stg[k % 2]
                        K.dma("sp", sg_[:], w_in[li, k * 128:(k + 1) * 128, :], out_buf=sg_)
                        if k % 2 == 0:
                            K.op("dve", lambda e: e.tensor_copy(Win[:, k, :], sg_[:]), reads=[sg_], writes=[Win])
                        else:
                            K.op("act", lambda e: e.copy(Win[:, k, :], sg_[:]), reads=[sg_], writes=[Win])
                        rot_cols(Wrot, Win, k, 0, 6, 16)
                        rot_cols(Wrot, Win, k, 1936, 1, 8)
                    for c in range(2):
                        sg_ = stg[c % 2]
                        K.dma("sp", sg_[:, 0:384], mla_wuq[li, c * 128:(c + 1) * 128, :], out_buf=sg_)
                        K.op("dve", lambda e: e.tensor_copy(Wuq[:, c, :], sg_[:, 0:384]), reads=[sg_], writes=[Wuq])
                    K.op("pool", lambda e: e.memset(Wuqr[:], 0.0), writes=[Wuqr])
                    for c in range(2):
                        for h in range(4):
                            rot_cols(Wuqr, Wuq, c, h * 96 + 64, 1, 8)
                    sg_ = stg[0]
                    K.dma("sp", sg_[:, 0:512], mla_wukv[li, :, :], out_buf=sg_)
                    K.op("dve", lambda e: e.tensor_copy(Wukv[:], sg_[:, 0:512]), reads=[sg_], writes=[Wukv])
                    K.dma("sp", [grow[0:1, 0:256], grow[0:1, 256:384]], [mla_qg[li:li + 1, :], mla_kvg[li:li + 1, :]], out_buf=grow)
                    for c in range(3):
                        mm(K, pgq, pgq[:, c:c + 1], grow, grow[0:1, c * 128:(c + 1) * 128], ones_f, ones_f[0:1, 0:1])
                    K.op("dve", lambda e: e.tensor_copy(gqk[:, 0:3], pgq[:, 0:3]), reads=[pgq], writes=[gqk])
                    K.barrier()

                xg = [K.sb(st, [128, 8, 512], F32, "xg", dma=True) for _ in range(2)]
                tb64 = [K.sb(st, [128, 2, 512], F32, "tb64", dma=True) for _ in range(2)]
                tb96 = [K.sb(st, [96, 2, 512], F32, "tb96", dma=True) for _ in range(2)]
                tb32 = [K.sb(st, [32, 2, 512], F32, "tb32", dma=True) for _ in range(2)]
                xn = K.sb(st, [128, 8, 512], BF16, "xn")
                rstd = K.sb(st, [128, 512], F32, "rstd")
                t1 = K.sb(st, [128, 512], F32, "t1")
                t2 = K.sb(st, [128, 512], F32, "t2")
                cqf = K.sb(st, [128, 3, 512], F32, "cqf")
                cqs = K.sb(st, [128, 3, 512], BF16, "cqs")
                cqn = K.sb(st, [128, 3, 512], BF16, "cqn")
                rq = K.sb(st, [128, 512], F32, "rq")
                rkv = K.sb(st, [128, 512], F32, "rkv")
                obf = Ring([K.sb(st, [128, 512], BF16, "obf", dma=True) for _ in range(4)])
                of32 = Ring([K.sb(st, [128, 512], F32, "of32", dma=True) for _ in range(3)])
                pss = K.ps(st, [128, 512], F32, "pss")
                pb = Ring([K.ps(st, [128, 512], F32, "pb") for _ in range(7)])
                xTv = xT.rearrange("(k p) t -> p k t", p=128)
                grps = groups_all(True)
                evi = [0]

                def evac(out_b, out_ap, p, p_ap):
                    evi[0] += 1
                    if evi[0] % 2 == 0:
                        K.op("act", lambda e: e.copy(out_ap, p_ap), reads=[p], writes=[out_b])
                    else:
                        K.op("dve", lambda e: e.tensor_copy(out_ap, p_ap), reads=[p], writes=[out_b])

                def load(gi):
                    t0, n, v = grps[gi]
                    b = xg[gi % 2]
                    K.dma("sp", [b[:, 0:4, 0:n], b[:, 4:8, 0:n]], [xTv[:, 0:4, t0:t0 + n], xTv[:, 4:8, t0:t0 + n]], out_buf=b)
                    K.dma("sp", tb64[gi % 2][:, :, 0:n], tab64[:, :, t0:t0 + n].rearrange("c p t -> p c t"), out_buf=tb64[gi % 2])
                    K.dma("sp", tb96[gi % 2][:, :, 0:n], tab96[:, :, t0:t0 + n].rearrange("c p t -> p c t"), out_buf=tb96[gi % 2])
                    K.dma("sp", tb32[gi % 2][:, :, 0:n], tab32[:, :, t0:t0 + n].rearrange("c p t -> p c t"), out_buf=tb32[gi % 2])

                load(0)
                for gi, (t0, n, v) in enumerate(grps):
                    if gi + 1 < len(grps):
                        load(gi + 1)
                    xb = xg[gi % 2]
                    T64, T96, T32 = tb64[gi % 2], tb96[gi % 2], tb32[gi % 2]
                    prenorm(xb, xn, rstd, pss, A, Bm, v, n)

                    def proj(c0, nc_, W=Win):
                        p = pb.next()
                        for k in range(8):
                            mm(K, p, p[0:nc_, 0:n], W, W[:, k, c0:c0 + nc_], xn, xn[:, k, 0:n], start=(k == 0), stop=(k == 7))
                        return p

                    def rope_store(p, pr, tb, np_, dst_ap):
                        ob = obf.next()
                        K.op("dve", lambda e: e.tensor_tensor(t1[0:np_, 0:n], p[0:np_, 0:n], tb[0:np_, 0, 0:n], op=ALU.mult),
                             reads=[p, tb], writes=[t1])
                        K.op("dve", lambda e: e.tensor_tensor(t2[0:np_, 0:n], pr[0:np_, 0:n], tb[0:np_, 1, 0:n], op=ALU.mult),
                             reads=[pr, tb], writes=[t2])
                        K.op("pool", lambda e: e.tensor_tensor(ob[0:np_, 0:n], t1[0:np_, 0:n], t2[0:np_, 0:n], op=ALU.add),
                             reads=[t1, t2], writes=[ob])
                        if isinstance(dst_ap, list):
                            K.dma("sp", dst_ap, [ob[0:np_, 0:n]] * len(dst_ap), in_buf=ob)
                        else:
                            K.dma("sp", dst_ap, ob[0:np_, 0:n], in_buf=ob)

                    def plain_store(p, np_, dst_ap, dt=BF16):
                        ob = obf.next() if dt == BF16 else of32.next()
                        evac(ob, ob[0:np_, 0:n], p, p[0:np_, 0:n])
                        K.dma("sp", dst_ap, ob[0:np_, 0:n], in_buf=ob)

                    for ch in range(2):
                        p = proj(ch * 128, 128)
                        pr = proj(ch * 128, 128, Wrot)
                        rope_store(p, pr, T64, 128, swa_qT[ch * 128:(ch + 1) * 128, t0:t0 + n])
                    p = proj(256, 128)
                    pr = proj(256, 128, Wrot)
                    rope_store(p, pr, T64, 128, swa_kT[:, t0:t0 + n])
                    for ch in range(6):
                        p = proj(512 + ch * 128, 128)
                        plain_store(p, 128, dn_qkvT[ch * 128:(ch + 1) * 128, t0:t0 + n], F32)
                    for ch in range(2):
                        p = proj(1968 + ch * 128, 128)
                        plain_store(p, 128, na_qT[ch * 128:(ch + 1) * 128, t0:t0 + n])
                    for ch in range(2):
                        p = proj(2224 + ch * 128, 128)
                        plain_store(p, 128, na_kT[ch * 128:(ch + 1) * 128, t0:t0 + n])
                    p = proj(1936, 32)
                    pr = proj(1936, 32, Wrot)
                    rope_store(p, pr, T32, 32, [mla_kT[h * 96 + 64:h * 96 + 96, t0:t0 + n] for h in range(4)])
                    for c in range(3):
                        p = proj(1552 + c * 128, 128)
                        K.op("act", lambda e: e.copy(cqf[:, c, 0:n], p[:, 0:n]), reads=[p], writes=[cqf])
                    K.op("act", lambda e: e.activation(out=cqs[:, :, 0:n], in_=cqf[:, :, 0:n], func=AF.Square), reads=[cqf], writes=[cqs])
                    pq_ = pb.next()
                    for c in range(2):
                        mm(K, pq_, pq_[:, 0:n], ones_b, ones_b[:], cqs, cqs[:, c, 0:n], start=(c == 0), stop=(c == 1))
                    K.op("act", lambda e: e.activation(out=rq[:, 0:n], in_=pq_[:, 0:n], func=AF.Sqrt, scale=1.0 / 256.0, bias=float(EPS)),
                         reads=[pq_], writes=[rq])
                    K.op("dve", lambda e: e.reciprocal(rq[:, 0:n], rq[:, 0:n]), reads=[rq], writes=[rq])
                    pk_ = pb.next()
                    mm(K, pk_, pk_[:, 0:n], ones_b, ones_b[:], cqs, cqs[:, 2, 0:n])
                    K.op("act", lambda e: e.activation(out=rkv[:, 0:n], in_=pk_[:, 0:n], func=AF.Sqrt, scale=1.0 / 128.0, bias=float(EPS)),
                         reads=[pk_], writes=[rkv])
                    K.op("dve", lambda e: e.reciprocal(rkv[:, 0:n], rkv[:, 0:n]), reads=[rkv], writes=[rkv])
                    for c in range(3):
                        rr_ = rq if c < 2 else rkv
                        K.op("dve", lambda e: e.scalar_tensor_tensor(out=cqn[:, c, 0:n], in0=cqf[:, c, 0:n], scalar=gqk[:, c:c + 1],
                                                                      in1=rr_[:, 0:n], op0=ALU.mult, op1=ALU.mult),
                             reads=[cqf, gqk, rr_], writes=[cqn])
                    for h in range(4):
                        p = pb.next()
                        pr = pb.next()
                        for c in range(2):
                            mm(K, p, p[0:96, 0:n], Wuq, Wuq[:, c, h * 96:(h + 1) * 96], cqn, cqn[:, c, 0:n], start=(c == 0), stop=(c == 1))
                        for c in range(2):
                            mm(K, pr, pr[0:96, 0:n], Wuqr, Wuqr[:, c, h * 96:(h + 1) * 96], cqn, cqn[:, c, 0:n], start=(c == 0), stop=(c == 1))
                        rope_store(p, pr, T96, 96, mla_qT[h * 96:(h + 1) * 96, t0:t0 + n])
                    for h in range(4):
                        p = pb.next()
                        mm(K, p, p[0:64, 0:n], Wukv, Wukv[:, h * 128:h * 128 + 64], cqn, cqn[:, 2, 0:n])
                        plain_store(p, 64, mla_kT[h * 96:h * 96 + 64, t0:t0 + n])
                    wv_ap = Wukv[:].rearrange("p (h c) -> p h c", h=4)[:, :, 64:128]
                    for tt in range(n // 128):
                        r0 = t0 + tt * 128
                        p = pb.next()
                        mm(K, p, p[:, 0:256].rearrange("p (h c) -> p h c", h=4), cqn, cqn[:, 2, tt * 128:(tt + 1) * 128], Wukv, wv_ap)
                        ob = obf.next()
                        evac(ob, ob[:, 0:256], p, p[:, 0:256])
                        K.dma("sp", mla_v[r0:r0 + 128, :], ob[:, 0:256], in_buf=ob)
                        p = pb.next()
                        for k in range(8):
                            mm(K, p, p[:, 0:128], xn, xn[:, k, tt * 128:(tt + 1) * 128], Win, Win[:, k, 384:512], start=(k == 0), stop=(k == 7))
                        ob = obf.next()
                        evac(ob, ob[:, 0:128], p, p[:, 0:128])
                        K.dma("sp", swa_v[r0:r0 + 128, :], ob[:, 0:128], in_buf=ob)
                        p = pb.next()
                        for k in range(8):
                            mm(K, p, p[:, 0:272], xn, xn[:, k, tt * 128:(tt + 1) * 128], Win, Win[:, k, 1280:1552], start=(k == 0), stop=(k == 7))
                        ob = of32.next()
                        evac(ob, ob[:, 0:272], p, p[:, 0:272])
                        K.dma("sp", dn_zab[r0:r0 + 128, :], ob[:, 0:272], in_buf=ob)
                        p = pb.next()
                        for k in range(8):
                            mm(K, p, p[:, 0:256], xn, xn[:, k, tt * 128:(tt + 1) * 128], Win, Win[:, k, 2480:2736], start=(k == 0), stop=(k == 7))
                        ob = obf.next()
                        evac(ob, ob[:, 0:256], p, p[:, 0:256])
                        K.dma("sp", na_v[r0:r0 + 128, :], ob[:, 0:256], in_buf=ob)
                K.barrier()


        def phase_swa(li, need_ctx):
            with ExitStack() as st:
                kT = K.sb(st, [64, 2, T], BF16, "kT", dma=True)
                qT = K.sb(st, [64, 4, T], BF16, "qT", dma=True)
                Va = K.sb(st, [128, 34, 2, 65], BF16, "Va", dma=True)
                mpn = K.sb(st, [128, 2, 128], F32, "mpn", dma=True)
                mpb = K.sb(st, [128, 2, 2, 128], BF16, "mpb")
                snk = K.sb(st, [128, 4], F32, "snk", dma=True)
                es_ = K.sb(st, [128, 4], F32, "es")
                K.dma("sp", kT[:], swa_kT.rearrange("(h d) t -> d h t", d=64), out_buf=kT)
                K.dma("sp", [qT[:, 0:2, :], qT[:, 2:4, :]],
                      [swa_qT[0:128, :].rearrange("(h d) t -> d h t", d=64), swa_qT[128:256, :].rearrange("(h d) t -> d h t", d=64)], out_buf=qT)
                vv = swa_v.rearrange("(j p) (h d) -> p j h d", p=128, d=64)
                K.dma("sp", [Va[:, j0:j0 + 17, h, 0:64] for h in range(2) for j0 in (0, 17)],
                      [vv[:, j0:j0 + 17, h, :] for h in range(2) for j0 in (0, 17)], out_buf=Va)
                K.op("pool", lambda e: e.memset(Va[:, :, :, 64:65], 1.0), writes=[Va])
                K.dma("sp", mpn[:], mask_pn.rearrange("w p q -> p w q"), out_buf=mpn)
                for w in range(2):
                    for g in range(2):
                        K.op("dve", lambda e: e.tensor_copy(mpb[:, w, g, :], mpn[:, w, :]), reads=[mpn], writes=[mpb])
                K.dma("sp", snk[:], swa_sink[li:li + 1, :].partition_broadcast(128), out_buf=snk)
                K.op("act", lambda e: e.activation(out=es_[:], in_=snk[:], func=AF.Exp), reads=[snk], writes=[es_])
                psS = Ring([K.ps(st, [128, 512], F32, "psS") for _ in range(3)])
                po = Ring([K.ps(st, [128, 512], F32, "po") for _ in range(4)])
                Pt = Ring([K.sb(st, [128, 2, 128], BF16, "Pt") for _ in range(4)])
                ysb = Ring([K.sb(st, [128, 256], BF16, "ysb", dma=True) for _ in range(3)])
                dn_ = Ring([K.sb(st, [128, 2], F32, "dn") for _ in range(4)])
                mi = [0]

                jobs = []

                def block(q0, tiles, yrow0):
                    for kh in range(2):
                        for ti, tl in enumerate(tiles):
                            jobs.append((q0, kh, ti, len(tiles), tl, yrow0))

                def run_jobs():
                    def issue_S(i):
                        q0, kh, ti, nt, (k0, vj, mk), yrow0 = jobs[i]
                        ps = psS.next()
                        mm(K, ps, ps[:, 0:256].rearrange("p (g q) -> p g q", g=2), kT, kT[:, kh, k0:k0 + 128], qT, qT[:, 2 * kh:2 * kh + 2, q0:q0 + 128])
                        return ps

                    ps_next = issue_S(0)
                    yb = None
                    pog = None
                    for i, (q0, kh, ti, nt, (k0, vj, mk), yrow0) in enumerate(jobs):
                        ps = ps_next
                        if i + 1 < len(jobs):
                            ps_next = issue_S(i + 1)
                        if kh == 0 and ti == 0:
                            yb = ysb.next()
                        if ti == 0:
                            pog = [po.next(), po.next()]
                        P = Pt.next()
                        K.op("act", lambda e: e.activation(out=P[:], in_=ps[:, 0:256].rearrange("p (g q) -> p g q", g=2), func=AF.Exp, scale=0.125),
                             reads=[ps], writes=[P])
                        if mk is not None:
                            mi[0] += 1
                            e_ = "dve" if mi[0] % 2 else "pool"
                            K.op(e_, lambda e: e.tensor_tensor(P[:], P[:], mpb[:, mk, :, :], op=ALU.mult), reads=[P, mpb], writes=[P])
                        for g in range(2):
                            mm(K, pog[g], pog[g][:, 0:65], P, P[:, g, :], Va, Va[:, vj, kh, :], start=(ti == 0), stop=(ti == nt - 1))
                        if ti == nt - 1:
                            for g in range(2):
                                h = 2 * kh + g
                                d_ = dn_.next()
                                K.op("dve", lambda e: e.tensor_tensor(d_[:, 0:1], pog[g][:, 64:65], es_[:, h:h + 1], op=ALU.add), reads=[pog[g], es_], writes=[d_])
                                K.op("dve", lambda e: e.reciprocal(d_[:, 1:2], d_[:, 0:1]), reads=[d_], writes=[d_])
                                K.op("dve", lambda e: e.tensor_scalar(yb[:, h * 64:(h + 1) * 64], pog[g][:, 0:64], d_[:, 1:2], None, op0=ALU.mult),
                                     reads=[pog[g], d_], writes=[yb])
                            if kh == 1:
                                K.dma("sp", ycat[yrow0:yrow0 + 128, 0:256], yb[:], in_buf=yb)

                if need_ctx:
                    for qt in range(2):
                        block(qt * 128, [(0, 0, None), (128, 1, None)], qt * 128)
                for i in range(S // 128):
                    tiles = [(0, 0, None), (128, 1, None)]
                    if i > 0:
                        tiles.append((L + (i - 1) * 128, 2 + i - 1, 0))
                    tiles.append((L + i * 128, 2 + i, None))
                    if i < S // 128 - 1:
                        tiles.append((L + (i + 1) * 128, 2 + i + 1, 1))
                    block(L + i * 128, tiles, L + i * 128)
                run_jobs()
                K.barrier()

        def phase_mla(li, need_ctx):
            sc = float(96 ** -0.5)
            with ExitStack() as st:
                kT = K.sb(st, [96, 4, T], BF16, "kT", dma=True)
                qT = K.sb(st, [96, 4, T], BF16, "qT", dma=True)
                Va = K.sb(st, [128, 34, 4, 65], BF16, "Va", dma=True)
                K.dma("sp", [kT[:, h, :] for h in range(4)], [mla_kT[h * 96:(h + 1) * 96, :] for h in range(4)], out_buf=kT)
                K.dma("sp", [qT[:, h, :] for h in range(4)], [mla_qT[h * 96:(h + 1) * 96, :] for h in range(4)], out_buf=qT)
                vv = mla_v.rearrange("(j p) (h d) -> p j h d", p=128, d=64)
                K.dma("sp", [Va[:, j0:j0 + 17, h, 0:64] for h in range(4) for j0 in (0, 17)],
                      [vv[:, j0:j0 + 17, h, :] for h in range(4) for j0 in (0, 17)], out_buf=Va)
                K.op("pool", lambda e: e.memset(Va[:, :, :, 64:65], 1.0), writes=[Va])
                psS = Ring([K.ps(st, [128, 512], F32, "psS") for _ in range(3)])
                po = [K.ps(st, [128, 512], F32, "po") for _ in range(4)]
                Pt = Ring([K.sb(st, [128, 512], BF16, "Pt") for _ in range(4)])
                ysb = Ring([K.sb(st, [128, 256], BF16, "ysb", dma=True) for _ in range(8)])
                dn_ = Ring([K.sb(st, [128, 2], F32, "dn") for _ in range(4)])

                def group(q0, n, ktiles, yrow0):
                    nq = n // 128
                    ybs = [ysb.next() for _ in range(nq)]
                    seq = [(h, ti, kt) for h in range(4) for ti, kt in enumerate(ktiles)]

                    def issue_S(i):
                        h, ti, kt = seq[i]
                        ps = psS.next()
                        mm(K, ps, ps[:, 0:n], kT, kT[:, h, kt * 128:(kt + 1) * 128], qT, qT[:, h, q0:q0 + n])
                        return ps

                    ps_next = issue_S(0)
                    for i, (h, ti, kt) in enumerate(seq):
                        ps = ps_next
                        if i + 1 < len(seq):
                            ps_next = issue_S(i + 1)
                        P = Pt.next()
                        K.op("act", lambda e: e.activation(out=P[:, 0:n], in_=ps[:, 0:n], func=AF.Exp, scale=sc), reads=[ps], writes=[P])
                        for qt in range(nq):
                            mm(K, po[qt], po[qt][:, 0:65], P, P[:, qt * 128:(qt + 1) * 128], Va, Va[:, kt, h, :],
                               start=(ti == 0), stop=(ti == len(ktiles) - 1))
                        if ti == len(ktiles) - 1:
                            for qt in range(nq):
                                d_ = dn_.next()
                                K.op("dve", lambda e: e.reciprocal(d_[:, 1:2], po[qt][:, 64:65]), reads=[po[qt]], writes=[d_])
                                K.op("dve", lambda e: e.tensor_scalar(ybs[qt][:, h * 64:(h + 1) * 64], po[qt][:, 0:64], d_[:, 1:2], None, op0=ALU.mult),
                                     reads=[po[qt], d_], writes=[ybs[qt]])
                    for qt in range(nq):
                        K.dma("sp", ycat[yrow0 + qt * 128:yrow0 + (qt + 1) * 128, 512:768], ybs[qt][:], in_buf=ybs[qt])

                if need_ctx:
                    group(0, 256, [0, 1], 0)
                for qg in range(S // 512):
                    group(L + qg * 512, 512, list(range(34)), L + qg * 512)
                K.barrier()


        def phase_na(li, need_ctx):
            with ExitStack() as st:
                kT = K.sb(st, [64, 4, T], BF16, "kT", dma=True)
                qT = K.sb(st, [64, 4, T], BF16, "qT", dma=True)
                Va = K.sb(st, [128, 34, 4, 65], BF16, "Va", dma=True)
                Vs = K.sb(st, [128, 31, 4, 65], BF16, "Vs", dma=True)
                TA = K.sb(st, [128, 4, 15, 64], BF16, "TA")
                for (dst, src) in ((kT, na_kT), (qT, na_qT)):
                    K.dma("sp", [dst[:, 0:2, :], dst[:, 2:4, :]],
                          [src[0:128, :].rearrange("(h d) t -> d h t", d=64), src[128:256, :].rearrange("(h d) t -> d h t", d=64)], out_buf=dst)
                vv = na_v.rearrange("(j p) (h d) -> p j h d", p=128, d=64)
                K.dma("sp", [Va[:, j0:j0 + 17, h, 0:64] for h in range(4) for j0 in (0, 17)],
                      [vv[:, j0:j0 + 17, h, :] for h in range(4) for j0 in (0, 17)], out_buf=Va)
                K.op("pool", lambda e: e.memset(Va[:, :, :, 64:65], 1.0), writes=[Va])
                vs = na_v[L + 64:L + 64 + 31 * 128, :].rearrange("(j p) (h d) -> p j h d", p=128, d=64)
                K.dma("sp", [Vs[:, :, h, 0:64] for h in range(4)], [vs[:, :, h, :] for h in range(4)], out_buf=Vs)
                K.op("pool", lambda e: e.memset(Vs[:, :, :, 64:65], 1.0), writes=[Vs])
                with ExitStack() as st2:
                    TAr = K.sb(st2, [128, 4, 15, 64], F32, "TAr", dma=True)
                    okm = K.sb(st2, [128, 64], F32, "okm", dma=True)
                    K.op("dve", lambda e: e.memset(TAr[:], 0.0), writes=[TAr])
                    K.dma("sp", [TAr[0:64, :, :, :].rearrange("p h r q -> p (h r q)"), TAr[64:128, :, 0:14, :].rearrange("p h r q -> p h (r q)")],
                          [rpbx[li].rearrange("p h r q -> p (h r q)"), rpbx[li][:, :, 1:15, :].rearrange("p h r q -> p h (r q)")], out_buf=TAr)
                    K.dma("sp", okm[:], okm_in[:, :], out_buf=okm)
                    K.op("act", lambda e: e.activation(out=TAr[:], in_=TAr[:], func=AF.Exp), reads=[TAr], writes=[TAr])
                    for h in range(4):
                        for r_ in range(15):
                            K.op("pool", lambda e: e.tensor_tensor(TA[:, h, r_, :], TAr[:, h, r_, :], okm[:], op=ALU.mult), reads=[TAr, okm], writes=[TA])
                    K.barrier()
                psS = Ring([K.ps(st, [128, 512], F32, "psS") for _ in range(3)])
                po = Ring([K.ps(st, [128, 512], F32, "po") for _ in range(3)])
                P6r = Ring([K.sb(st, [128, 6, 4, 64], BF16, "P6") for _ in range(3)])
                Pf = Ring([K.sb(st, [128, 4, 64], F32, "Pf") for _ in range(3)])
                yrow = Ring([K.sb(st, [64, 256], BF16, "yrow", dma=True) for _ in range(3)])
                ysb = Ring([K.sb(st, [128, 256], BF16, "ysb", dma=True) for _ in range(2)])
                Pc = Ring([K.sb(st, [128, 128], BF16, "Pc") for _ in range(3)])
                dn_ = Ring([K.sb(st, [128, 2], F32, "dn") for _ in range(4)])
                mi = [0]
                if need_ctx:
                    for qt in range(2):
                        yb = ysb.next()
                        for h in range(4):
                            pq = po.next()
                            for kt in range(2):
                                ps = psS.next()
                                mm(K, ps, ps[:, 0:128], kT, kT[:, h, kt * 128:(kt + 1) * 128], qT, qT[:, h, qt * 128:(qt + 1) * 128])
                                P = Pc.next()
                                K.op("act", lambda e: e.activation(out=P[:], in_=ps[:, 0:128], func=AF.Exp, scale=0.125), reads=[ps], writes=[P])
                                mm(K, pq, pq[:, 0:65], P, P[:], Va, Va[:, kt, h, :], start=(kt == 0), stop=(kt == 1))
                            d_ = dn_.next()
                            K.op("dve", lambda e: e.reciprocal(d_[:, 1:2], pq[:, 64:65]), reads=[pq], writes=[d_])
                            K.op("dve", lambda e: e.tensor_scalar(yb[:, h * 64:(h + 1) * 64], pq[:, 0:64], d_[:, 1:2], None, op0=ALU.mult),
                                 reads=[pq, d_], writes=[yb])
                        K.dma("sp", ycat[qt * 128:(qt + 1) * 128, 768:1024], yb[:], in_buf=yb)
                def s_stage(r):
                    rs = min(max(r - 4, 0), 56)
                    dlt = r - rs
                    q0 = L + r * 64
                    P6 = P6r.next()
                    tiles = []
                    for kt in range(4):
                        k0 = L + rs * 64 + kt * 128
                        vt = (Va, 2 + rs // 2 + kt) if rs % 2 == 0 else (Vs, (rs - 1) // 2 + kt)
                        tiles.append((k0, vt, 2 * kt - dlt + 7))
                    tiles.append((0, (Va, 0), None))
                    tiles.append((128, (Va, 1), None))
                    for ti, (k0, vt, dr0) in enumerate(tiles):
                        ps = psS.next()
                        for h in range(4):
                            mm(K, ps, ps[:, h * 64:(h + 1) * 64], kT, kT[:, h, k0:k0 + 128], qT, qT[:, h, q0:q0 + 64])
                        psv = ps[:, 0:256].rearrange("p (h q) -> p h q", h=4)
                        if dr0 is None:
                            K.op("act", lambda e: e.activation(out=P6[:, ti, :, :], in_=psv, func=AF.Exp, scale=0.125), reads=[ps], writes=[P6])
                        else:
                            pf = Pf.next()
                            K.op("act", lambda e: e.activation(out=pf[:], in_=psv, func=AF.Exp, scale=0.125), reads=[ps], writes=[pf])
                            mi[0] += 1
                            e_ = "dve" if mi[0] % 2 else "pool"
                            K.op(e_, lambda e: e.tensor_tensor(P6[:, ti, :, :], pf[:], TA[:, :, dr0, :], op=ALU.mult), reads=[pf, TA], writes=[P6])
                    return (P6, tiles, q0)

                def pv_stage(P6, tiles, q0):
                    yb = yrow.next()
                    for h in range(4):
                        pq = po.next()
                        for ti, (k0, vt, dr0) in enumerate(tiles):
                            Vb, vj = vt
                            mm(K, pq, pq[0:64, 0:65], P6, P6[:, ti, h, :], Vb, Vb[:, vj, h, :], start=(ti == 0), stop=(ti == 5))
                            K.pump(2)
                        d_ = dn_.next()
                        K.op("dve", lambda e: e.reciprocal(d_[0:64, 1:2], pq[0:64, 64:65]), reads=[pq], writes=[d_])
                        K.op("dve", lambda e: e.tensor_scalar(yb[:, h * 64:(h + 1) * 64], pq[0:64, 0:64], d_[0:64, 1:2], None, op0=ALU.mult),
                             reads=[pq, d_], writes=[yb])
                    K.dma("sp", ycat[q0:q0 + 64, 768:1024], yb[:], in_buf=yb)

                prev = s_stage(0)
                for r in range(1, 64):
                    K.defer = K.pending
                    cur = s_stage(r)
                    K.defer = None
                    pv_stage(*prev)
                    K.pump(10 ** 6)
                    prev = cur
                pv_stage(*prev)
                K.barrier()

        def phase_outproj(li, with_ctx):
            G = modG[1]
            with ExitStack() as st:
                Wo = K.sb(st, [128, 8, D], BF16, "Wo")
                with ExitStack() as st2:
                    stg = [K.sb(st2, [128, D], F32, "stg", dma=True) for _ in range(2)]
                    for k in range(8):
                        sg_ = stg[k % 2]
                        K.dma("sp", sg_[:], w_out[li, k * 128:(k + 1) * 128, :], out_buf=sg_)
                        if k % 2 == 0:
                            K.op("dve", lambda e: e.tensor_copy(Wo[:, k, :], sg_[:]), reads=[sg_], writes=[Wo])
                        else:
                            K.op("act", lambda e: e.copy(Wo[:, k, :], sg_[:]), reads=[sg_], writes=[Wo])
                    K.barrier()
                xg = [K.sb(st, [128, 8, 512], F32, "xg", dma=True) for _ in range(2)]
                yt = Ring([K.sb(st, [128, D], BF16, "yt", dma=True) for _ in range(3)])
                gN = K.sb(st, [128, 64], F32, "gN", dma=True)
                K.dma("sp", gN[:], dn_norm_g[li:li + 1, :].partition_broadcast(128), out_buf=gN)
                of_ = Ring([K.sb(st, [128, 2, 256], F32, "of", dma=True) for _ in range(3)])
                zr = Ring([K.sb(st, [128, 256], F32, "z", dma=True) for _ in range(3)])
                osum = Ring([K.sb(st, [128, 256], F32, "osum") for _ in range(2)])
                sqd = Ring([K.sb(st, [128, 256], F32, "sqd") for _ in range(2)])
                scd = Ring([K.sb(st, [128, 8], F32, "scd") for _ in range(2)])
                yTs = [K.sb(st, [128, 8, 512], BF16, "yT") for _ in range(2)]
                ptb = Ring([K.ps(st, [128, 4, 128], BF16, "ptb") for _ in range(4)])
                po = Ring([K.ps(st, [128, 512], F32, "po") for _ in range(3)])
                xTv = xT.rearrange("(k p) t -> p k t", p=128)
                grps = groups_all(with_ctx)
                ev = [0]
                def stageA(gi):
                    t0, n, v = grps[gi]
                    xb = xg[gi % 2]
                    yT = yTs[gi % 2]
                    K.dma("sp", [xb[:, 0:4, 0:n], xb[:, 4:8, 0:n]], [xTv[:, 0:4, t0:t0 + n], xTv[:, 4:8, t0:t0 + n]], out_buf=xb)
                    for tt in range(n // 128):
                        y_ = yt.next()
                        r0_ = t0 + tt * 128
                        K.dma("sp", [y_[:, 0:256], y_[:, 512:1024]], [ycat[r0_:r0_ + 128, 0:256], ycat[r0_:r0_ + 128, 512:1024]], out_buf=y_)
                        o_, z_, os_, sq_, sc_ = of_.next(), zr.next(), osum.next(), sqd.next(), scd.next()
                        K.dma("sp", [o_[:, 0, :], o_[:, 1, :]], [dn_o[0, r0_:r0_ + 128, :], dn_o[1, r0_:r0_ + 128, :]], out_buf=o_)
                        K.dma("sp", z_[:], dn_zab[r0_:r0_ + 128, 0:256], out_buf=z_)
                        K.op("pool", lambda e: e.tensor_tensor(os_[:], o_[:, 0, :], o_[:, 1, :], op=ALU.add), reads=[o_], writes=[os_])
                        K.op("pool", lambda e: e.tensor_tensor(sq_[:], os_[:], os_[:], op=ALU.mult), reads=[os_], writes=[sq_])
                        K.op("dve", lambda e: e.tensor_reduce(out=sc_[:, 0:4], in_=sq_[:].rearrange("p (a b) -> p a b", b=64), axis=AX.X, op=ALU.add), reads=[sq_], writes=[sc_])
                        K.op("act", lambda e: e.activation(out=sc_[:, 4:8], in_=sc_[:, 0:4], func=AF.Sqrt, scale=1.0 / 64.0, bias=float(EPS)), reads=[sc_], writes=[sc_])
                        K.op("dve", lambda e: e.reciprocal(sc_[:, 4:8], sc_[:, 4:8]), reads=[sc_], writes=[sc_])
                        K.op("act", lambda e: e.activation(out=z_[:], in_=z_[:], func=AF.Silu), reads=[z_], writes=[z_])
                        for h in range(4):
                            K.op("dve", lambda e: e.scalar_tensor_tensor(out=os_[:, h * 64:(h + 1) * 64], in0=os_[:, h * 64:(h + 1) * 64], scalar=sc_[:, 4 + h:5 + h],
                                                                          in1=gN[:], op0=ALU.mult, op1=ALU.mult), reads=[os_, sc_, gN], writes=[os_])
                        K.op("pool", lambda e: e.tensor_tensor(y_[:, 256:512], os_[:], z_[:], op=ALU.mult), reads=[os_, z_], writes=[y_])
                        for hh in range(2):
                            p = ptb.next()
                            for kk in range(4):
                                k = hh * 4 + kk
                                tr(K, p, p[:, kk, :], y_, y_[:, k * 128:(k + 1) * 128], identb, identb[:])
                            ev[0] += 1
                            if ev[0] % 2:
                                K.op("dve", lambda e: e.tensor_copy(yT[:, hh * 4:(hh + 1) * 4, tt * 128:(tt + 1) * 128], p[:]), reads=[p], writes=[yT])
                            else:
                                K.op("act", lambda e: e.copy(yT[:, hh * 4:(hh + 1) * 4, tt * 128:(tt + 1) * 128], p[:]), reads=[p], writes=[yT])

                def stageB(gi):
                    t0, n, v = grps[gi]
                    xb = xg[gi % 2]
                    yT = yTs[gi % 2]
                    for f in range(8):
                        pf_ = po.next()
                        for k in range(8):
                            mm(K, pf_, pf_[:, 0:n], Wo, Wo[:, k, f * 128:(f + 1) * 128], yT, yT[:, k, 0:n], start=(k == 0), stop=(k == 7))
                            K.pump(2)
                        K.op("dve", lambda e: e.scalar_tensor_tensor(out=xb[:, f, 0:n], in0=pf_[:, 0:n], scalar=G[:, f, v:v + 1],
                                                                      in1=xb[:, f, 0:n], op0=ALU.mult, op1=ALU.add),
                             reads=[pf_, G, xb], writes=[xb])
                    K.dma("sp", [xTv[:, 0:4, t0:t0 + n], xTv[:, 4:8, t0:t0 + n]], [xb[:, 0:4, 0:n], xb[:, 4:8, 0:n]], in_buf=xb)

                stageA(0)
                for gi in range(len(grps)):
                    if gi + 1 < len(grps):
                        K.defer = K.pending
                        stageA(gi + 1)
                        K.defer = None
                    stageB(gi)
                    K.pump(10 ** 6)
                K.barrier()


        def phase_dn1(li):
            with ExitStack() as st:
                cwr = K.sb(st, [1, 3, 768], F32, "cwr", dma=True)
                pcw = K.ps(st, [128, 6, 3], F32, "pcw")
                cw = K.sb(st, [128, 6, 3], F32, "cw")
                K.dma("sp", cwr[0:1, :, :], dn_conv_w[li:li + 1, :, :], out_buf=cwr)
                for c in range(6):
                    for j in range(3):
                        mm(K, pcw, pcw[:, c, j:j + 1], cwr, cwr[0:1, j, c * 128:(c + 1) * 128], ones_f, ones_f[0:1, 0:1])
                K.op("dve", lambda e: e.tensor_copy(cw[:], pcw[:]), reads=[pcw], writes=[cw])
                xin = [K.sb(st, [128, T], F32, "xin", dma=True) for _ in range(2)]
                hc = [K.sb(st, [128, T], F32, "hc", dma=True) for _ in range(2)]
                otm = Ring([K.sb(st, [128, 4, 128], F32, "otm", dma=True) for _ in range(3)])
                pt = Ring([K.ps(st, [128, 512], F32, "pt") for _ in range(4)])
                ev = [0]
                for c in range(6):
                    x_ = xin[c % 2]
                    h_ = hc[c % 2]
                    K.dma("sp", x_[:], dn_qkvT[c * 128:(c + 1) * 128, :], out_buf=x_)
                    K.op("dve", lambda e: e.tensor_scalar(h_[:], x_[:], cw[:, c, 1:2], None, op0=ALU.mult), reads=[x_, cw], writes=[h_])
                    for (a, b) in ((0, L), (L, T)):
                        K.op("dve", lambda e: e.scalar_tensor_tensor(out=h_[:, a + 1:b], in0=x_[:, a:b - 1], scalar=cw[:, c, 0:1], in1=h_[:, a + 1:b],
                                                                      op0=ALU.mult, op1=ALU.add), reads=[x_, cw, h_], writes=[h_])
                        K.op("dve", lambda e: e.scalar_tensor_tensor(out=h_[:, a:b - 1], in0=x_[:, a + 1:b], scalar=cw[:, c, 2:3], in1=h_[:, a:b - 1],
                                                                      op0=ALU.mult, op1=ALU.add), reads=[x_, cw, h_], writes=[h_])
                    K.op("act", lambda e: e.activation(out=h_[:], in_=h_[:], func=AF.Silu), reads=[h_], writes=[h_])
                    K.dma("sp", dn_hT[c * 128:(c + 1) * 128, :], h_[:], in_buf=h_)
                    for j0 in range(0, 34, 4):
                        nj = min(4, 34 - j0)
                        p = pt.next()
                        for jj in range(nj):
                            tr(K, p, p[:, jj * 128:(jj + 1) * 128], h_, h_[:, (j0 + jj) * 128:(j0 + jj + 1) * 128], ident, ident[:])
                        o = otm.next()
                        ev[0] += 1
                        pv = p[:, 0:nj * 128].rearrange("p (j c) -> p j c", c=128)
                        if ev[0] % 2:
                            K.op("act", lambda e: e.copy(o[:, 0:nj, :], pv), reads=[p], writes=[o])
                        else:
                            K.op("pool", lambda e: e.tensor_copy(o[:, 0:nj, :], pv), reads=[p], writes=[o]) if False else \
                                K.op("dve", lambda e: e.tensor_copy(o[:, 0:nj, :], pv), reads=[p], writes=[o])
                        K.dma("sp", dn_h[j0 * 128:(j0 + nj) * 128, c * 128:(c + 1) * 128].rearrange("(j p) c -> p j c", p=128), o[:, 0:nj, :], in_buf=o)
                K.barrier()

        def phase_dn2(li):
            with ExitStack() as st:
                tri = K.sb(st, [128, 4, 128], F32, "tri", dma=True)
                K.dma("sp", tri[:], tri_in.rearrange("w p q -> p w q"), out_buf=tri)
                TINC = [tri[:, 0, :], tri[:, 2, :]]
                MST = [tri[:, 1, :], tri[:, 3, :]]
                onesf = K.sb(st, [128, 128], F32, "onesf")
                K.op("pool", lambda e: e.memset(onesf[:], 1.0), writes=[onesf])
                dtb = K.sb(st, [128, 8], F32, "dtb", dma=True)
                nA = K.sb(st, [128, 8], F32, "nA", dma=True)
                K.dma("sp", dtb[:], dn_dt_bias[li:li + 1, :].partition_broadcast(128), out_buf=dtb)
                K.dma("sp", nA[:], dn_a_log[li:li + 1, :].partition_broadcast(128), out_buf=nA)
                K.op("act", lambda e: e.activation(out=nA[:], in_=nA[:], func=AF.Exp), reads=[nA], writes=[nA])
                K.op("dve", lambda e: e.tensor_scalar(nA[:], nA[:], -1.0, None, op0=ALU.mult), reads=[nA], writes=[nA])
                Sst = [[[K.sb(st, [64, 64], F32, "S") for _ in range(2)] for _ in range(4)] for _ in range(2)]
                for d in range(2):
                    for h in range(4):
                        K.op("pool", lambda e: e.memset(Sst[d][h][0][:], 0.0), writes=[Sst[d][h][0]])
                NSLOT = 2
                hq = Ring([K.sb(st, [64, 4, 128], F32, "hq", dma=True) for _ in range(2 * NSLOT)])
                hk = Ring([K.sb(st, [64, 4, 128], F32, "hk", dma=True) for _ in range(2 * NSLOT)])
                htok = Ring([K.sb(st, [128, 768], F32, "htok", dma=True) for _ in range(2 * NSLOT)])
                abr = Ring([K.sb(st, [128, 16], F32, "ab", dma=True) for _ in range(2 * NSLOT)])
                scr = Ring([K.sb(st, [128, 96], F32, "sc") for _ in range(2 * NSLOT)])
                sqb = Ring([K.sb(st, [128, 512], F32, "sq") for _ in range(2)])
                otile = Ring([K.sb(st, [128, 256], F32, "ot", dma=True) for _ in range(2 * NSLOT)])
                names128 = ["gsm", "Dsm", "DTim", "Pa", "Pb", "Pta", "Ptb", "Tt", "QKm"]
                names64 = ["Xu", "Xw", "u", "Ke", "vn", "tmp"]
                W = {}
                for sl in range(NSLOT):
                    for d in range(2):
                        for h in range(4):
                            w_ = {n_: K.sb(st, [128, 128], F32, n_) for n_ in names128}
                            w_.update({n_: K.sb(st, [128, 64], F32, n_) for n_ in names64})
                            w_["wT"] = K.sb(st, [64, 128], F32, "wT")
                            W[(sl, d, h)] = w_
                pp = Ring([K.ps(st, [128, 512], F32, "ppb") for _ in range(7)])
                pg = Ring([K.ps(st, [128, 512], F32, "pgb")])
                order = [list(range(34)), [1, 0] + list(range(33, 1, -1))]
                hTv = dn_hT.rearrange("(g h d) t -> g d h t", g=3, d=64)
                ei = [0]

                def evac(dst, dst_ap, p, p_ap):
                    ei[0] += 1
                    if ei[0] % 3 == 0:
                        K.op("dve", lambda e: e.tensor_copy(dst_ap, p_ap), reads=[p], writes=[dst])
                    else:
                        K.op("act", lambda e: e.copy(dst_ap, p_ap), reads=[p], writes=[dst])

                import os as _os
                NS_ = int(_os.environ.get('DN_STEPS', '34'))

                def prepA(s_):
                    return [prepA1(s_, d) for d in range(2)]

                def prepA1(s_, d):
                    if True:
                        j = order[d][s_]
                        HQ, HK, HT, AB, sc, sq = hq.next(), hk.next(), htok.next(), abr.next(), scr.next(), sqb.next()
                        K.dma("sp", HQ[:], hTv[0, :, :, j * 128:(j + 1) * 128], out_buf=HQ)
                        K.dma("sp", HK[:], hTv[1, :, :, j * 128:(j + 1) * 128], out_buf=HK)
                        K.dma("sp", HT[:], dn_h[j * 128:(j + 1) * 128, :], out_buf=HT)
                        K.dma("sp", AB[:], dn_zab[j * 128:(j + 1) * 128, 256:272], out_buf=AB)
                        K.op("dve", lambda e: e.tensor_tensor(sq[:], HT[:, 0:512], HT[:, 0:512], op=ALU.mult), reads=[HT], writes=[sq])
                        K.op("dve", lambda e: e.tensor_reduce(out=sc[:, 0:8], in_=sq[:].rearrange("p (a b) -> p a b", b=64), axis=AX.X, op=ALU.add),
                             reads=[sq], writes=[sc])
                        K.op("act", lambda e: e.activation(out=sc[:, 8:16], in_=sc[:, 0:8], func=AF.Ln, bias=float(EPS), scale=1.0), reads=[sc], writes=[sc])
                        K.op("act", lambda e: e.activation(out=sc[:, 16:24], in_=sc[:, 8:16], func=AF.Exp, scale=-0.5), reads=[sc], writes=[sc])
                        K.op("act", lambda e: e.activation(out=sc[:, 24:32], in_=sc[:, 8:16], func=AF.Exp, scale=0.5), reads=[sc], writes=[sc])
                        K.op("dve", lambda e: e.tensor_scalar(sc[:, 16:20], sc[:, 16:20], 0.125, None, op0=ALU.mult), reads=[sc], writes=[sc])
                        K.op("dve", lambda e: e.tensor_tensor(sc[:, 32:36], AB[:, d * 4:d * 4 + 4], dtb[:, d * 4:d * 4 + 4], op=ALU.add), reads=[AB, dtb], writes=[sc])
                        K.op("act", lambda e: e.activation(out=sc[:, 32:36], in_=sc[:, 32:36], func=AF.Exp), reads=[sc], writes=[sc])
                        K.op("act", lambda e: e.activation(out=sc[:, 32:36], in_=sc[:, 32:36], func=AF.Ln, bias=1.0, scale=1.0), reads=[sc], writes=[sc])
                        K.op("dve", lambda e: e.tensor_tensor(sc[:, 36:40], sc[:, 32:36], nA[:, d * 4:d * 4 + 4], op=ALU.mult), reads=[sc, nA], writes=[sc])
                        K.op("act", lambda e: e.activation(out=sc[:, 40:44], in_=AB[:, 8 + d * 4:12 + d * 4], func=AF.Exp, scale=-1.0), reads=[AB], writes=[sc])
                        K.op("dve", lambda e: e.tensor_scalar(sc[:, 40:44], sc[:, 40:44], 1.0, None, op0=ALU.add), reads=[sc], writes=[sc])
                        K.op("dve", lambda e: e.reciprocal(sc[:, 40:44], sc[:, 40:44]), reads=[sc], writes=[sc])
                        return dict(d=d, j=j, HQ=HQ, HK=HK, HT=HT, AB=AB, sc=sc)

                def prepB(ctxs, s_):
                    U = []
                    for c_ in ctxs:
                        prepB1(c_, s_, U)
                    return U

                def mulcols(sc, o0, a0, b0):
                    K.op("dve", lambda e: e.tensor_tensor(sc[:, o0:o0 + 4], sc[:, a0:a0 + 4], sc[:, b0:b0 + 4], op=ALU.mult), reads=[sc], writes=[sc])

                def prepB1(c_, s_, U):
                    sl = s_ % NSLOT
                    if True:
                        d, j, HQ, HK, HT, AB, sc = c_['d'], c_['j'], c_['HQ'], c_['HK'], c_['HT'], c_['AB'], c_['sc']
                        pg_ = pg.next()
                        mm(K, pg_, pg_[:, 0:4], tri, TINC[d], sc, sc[:, 36:40])
                        mm(K, pg_, pg_[:, 4:8], onesf, onesf[:], sc, sc[:, 36:40])
                        K.op("dve", lambda e: e.tensor_copy(sc[:, 44:52], pg_[:, 0:8]), reads=[pg_], writes=[sc])
                        K.op("act", lambda e: e.activation(out=sc[:, 52:56], in_=sc[:, 44:48], func=AF.Exp), reads=[sc], writes=[sc])
                        K.op("dve", lambda e: e.tensor_tensor(sc[:, 56:60], sc[:, 48:52], sc[:, 44:48], op=ALU.subtract), reads=[sc], writes=[sc])
                        K.op("act", lambda e: e.activation(out=sc[:, 56:60], in_=sc[:, 56:60], func=AF.Exp), reads=[sc], writes=[sc])
                        K.op("act", lambda e: e.activation(out=sc[:, 60:64], in_=sc[:, 48:52], func=AF.Exp), reads=[sc], writes=[sc])
                        for (o0, a0, b0) in ((68, 20, 40), (64, 68, 20), (72, 64, 52), (76, 20, 56), (80, 52, 16)):
                            mulcols(sc, o0, a0, b0)
                        K.op("dve", lambda e: e.tensor_scalar(sc[:, 84:88], sc[:, 28:32], -1.0, None, op0=ALU.mult), reads=[sc], writes=[sc])
                        OT = otile.next()
                        for h in range(4):
                            U.append(dict(d=d, h=h, j=j, HQ=HQ, HK=HK, HT=HT, sc=sc, W=W[(sl, d, h)], OT=OT,
                                          S0=Sst[d][h][s_ % 2], S1=Sst[d][h][(s_ + 1) % 2]))

                U_next = prepB(prepA(0), 0)
                _pp_next = pp.next

                def _pp_pump():
                    K.pump(1)
                    return _pp_next()
                pp.next = _pp_pump
                for s_ in range(NS_):
                    sl = s_ % NSLOT
                    U = U_next
                    ctxA = None
                    if s_ + 1 < NS_:
                        K.defer = K.pending
                        U_next = prepB(prepA(s_ + 1), s_ + 1)
                        K.defer = None

                    def col(u, c0):
                        return u["sc"][:, c0 + u["h"]:c0 + u["h"] + 1]

                    _stg = int(_os.environ.get('DN_STAGE', '99'))
                    if _stg >= 1:
                        for u in U:
                            w_ = u["W"]
                            K.op("dve", lambda e: e.tensor_scalar(w_["gsm"][:], MST[u["d"]], col(u, 36), None, op0=ALU.mult), reads=[tri, u["sc"]], writes=[w_["gsm"]])
                    if _stg >= 2:
                        for u in U:
                            w_ = u["W"]
                            pb_ = pp.next()
                            p1, p2 = pb_, pb_
                            mm(K, p1, p1[:, 0:128], tri, TINC[u["d"]], w_["gsm"], w_["gsm"][:])
                            mm(K, p2, p2[:, 128:256], w_["gsm"], w_["gsm"][:], tri, TINC[u["d"]])
                            K.op("act", lambda e: e.activation(out=w_["Dsm"][:], in_=p1[:, 0:128], func=AF.Exp), reads=[p1], writes=[w_["Dsm"]])
                            K.op("act", lambda e: e.activation(out=w_["DTim"][:], in_=p2[:, 128:256], func=AF.Exp), reads=[p2], writes=[w_["DTim"]])
                            K.op("pool", lambda e: e.tensor_tensor(w_["Dsm"][:], w_["Dsm"][:], MST[u["d"]], op=ALU.mult), reads=[w_["Dsm"], tri], writes=[w_["Dsm"]])
                            K.op("pool", lambda e: e.tensor_tensor(w_["DTim"][:], w_["DTim"][:], TINC[u["d"]], op=ALU.mult), reads=[w_["DTim"], tri], writes=[w_["DTim"]])
                    if _stg >= 3:
                        for u in U:
                            w_ = u["W"]
                            h = u["h"]
                            pb_ = pp.next()
                            p1, p2 = pb_, pb_
                            mm(K, p1, p1[:, 0:128], u["HK"], u["HK"][:, h, :], u["HK"], u["HK"][:, h, :])
                            mm(K, p2, p2[:, 128:256], u["HK"], u["HK"][:, h, :], u["HQ"], u["HQ"][:, h, :])
                            K.op("dve", lambda e: e.scalar_tensor_tensor(out=w_["Pa"][:], in0=p1[:, 0:128], scalar=col(u, 64), in1=w_["Dsm"][:], op0=ALU.mult, op1=ALU.mult),
                                 reads=[p1, u["sc"], w_["Dsm"]], writes=[w_["Pa"]])
                            K.op("dve", lambda e: e.scalar_tensor_tensor(out=w_["QKm"][:], in0=p2[:, 128:256], scalar=col(u, 20), in1=w_["DTim"][:], op0=ALU.mult, op1=ALU.mult),
                                 reads=[p2, u["sc"], w_["DTim"]], writes=[w_["QKm"]])
                    if _stg >= 4:
                        for u in U:
                            w_ = u["W"]
                            p1 = pp.next()
                            tr(K, p1, p1[:, 0:128], w_["Pa"], w_["Pa"][:], ident, ident[:])
                            evac(w_["Pta"], w_["Pta"][:], p1, p1[:, 0:128])
                            K.op("pool", lambda e: e.tensor_tensor(w_["Tt"][:], ident[:], w_["Pta"][:], op=ALU.subtract), reads=[ident, w_["Pta"]], writes=[w_["Tt"]])
                            u["P"], u["Pt"], u["Pn"], u["Ptn"] = w_["Pa"], w_["Pta"], w_["Pb"], w_["Ptb"]
                    if _stg >= 5:
                        for lev in range(1, 7):
                            for u in U:
                                pb_ = pp.next()
                                mm(K, pb_, pb_[:, 0:128], u["Pt"], u["Pt"][:], u["P"], u["P"][:])
                                if lev < 6:
                                    mm(K, pb_, pb_[:, 128:256], u["P"], u["P"][:], u["Pt"], u["Pt"][:])
                                K.op("act", lambda e: e.copy(u["Pn"][:], pb_[:, 0:128]), reads=[pb_], writes=[u["Pn"]])
                                if lev < 6:
                                    K.op("act", lambda e: e.copy(u["Ptn"][:], pb_[:, 128:256]), reads=[pb_], writes=[u["Ptn"]])
                            for g0 in range(0, len(U), 4):
                                pb_ = pp.next()
                                for gi_, u in enumerate(U[g0:g0 + 4]):
                                    w_ = u["W"]
                                    mm(K, pb_, pb_[:, gi_ * 128:(gi_ + 1) * 128], u["Pn"], u["Pn"][:], w_["Tt"], w_["Tt"][:])
                                for gi_, u in enumerate(U[g0:g0 + 4]):
                                    w_ = u["W"]
                                    K.op("dve", lambda e: e.tensor_tensor(w_["Tt"][:], w_["Tt"][:], pb_[:, gi_ * 128:(gi_ + 1) * 128], op=ALU.add), reads=[w_["Tt"], pb_], writes=[w_["Tt"]])
                                    u["P"], u["Pn"] = u["Pn"], u["P"]
                                    u["Pt"], u["Ptn"] = u["Ptn"], u["Pt"]
                    if _stg >= 6:
                        for u in U:
                            w_ = u["W"]
                            h = u["h"]
                            HT = u["HT"]
                            _m8 = int(_os.environ.get('DN_S8', '31'))
                            if _m8 & 1:
                                K.op("pool", lambda e: e.tensor_scalar(w_["Xu"][:], HT[:, 512 + h * 64:512 + (h + 1) * 64], col(u, 68), None, op0=ALU.mult), reads=[HT, u["sc"]], writes=[w_["Xu"]])
                                K.op("pool", lambda e: e.tensor_scalar(w_["Xw"][:], HT[:, 256 + h * 64:256 + (h + 1) * 64], col(u, 72), None, op0=ALU.mult), reads=[HT, u["sc"]], writes=[w_["Xw"]])
                                K.op("pool", lambda e: e.tensor_scalar(w_["Ke"][:], HT[:, 256 + h * 64:256 + (h + 1) * 64], col(u, 76), None, op0=ALU.mult), reads=[HT, u["sc"]], writes=[w_["Ke"]])
                            pb_ = pp.next()
                            p1, p2 = pb_, pb_
                            if _m8 & 2:
                                mm(K, p1, p1[:, 0:64], w_["Tt"], w_["Tt"][:], w_["Xu"], w_["Xu"][:])
                            if _m8 & 4:
                                mm(K, p2, p2[0:64, 128:256], w_["Xw"], w_["Xw"][:], w_["Tt"], w_["Tt"][:])
                            if _m8 & 8:
                                K.op("act", lambda e: e.activation(out=w_["u"][:], in_=p1[:, 0:64], func=AF.Identity, scale=col(u, 28)), reads=[p1, u["sc"]], writes=[w_["u"]])
                            if _m8 & 16:
                                K.op("act", lambda e: e.copy(w_["wT"][:], p2[0:64, 128:256]), reads=[p2], writes=[w_["wT"]])
                    if _stg >= 7:
                        for u in U:
                            w_ = u["W"]
                            p1 = pp.next()
                            mm(K, p1, p1[:, 0:64], w_["wT"], w_["wT"][:], u["S0"], u["S0"][:])
                            K.op("dve", lambda e: e.scalar_tensor_tensor(out=w_["vn"][:], in0=p1[:, 0:64], scalar=col(u, 84), in1=w_["u"][:], op0=ALU.mult, op1=ALU.add),
                                 reads=[p1, u["sc"], w_["u"]], writes=[w_["vn"]])
                        for u in U:
                            w_ = u["W"]
                            h = u["h"]
                            pb_ = pp.next()
                            p1, p2, p3 = pp.next(), pb_, pb_
                            mm(K, p1, p1[:, 0:64], u["HQ"], u["HQ"][:, h, :], u["S0"], u["S0"][:])
                            mm(K, p2, p2[:, 128:192], w_["QKm"], w_["QKm"][:], w_["vn"], w_["vn"][:])
                            mm(K, p3, p3[0:64, 256:320], w_["Ke"], w_["Ke"][:], w_["vn"], w_["vn"][:])
                            K.op("act", lambda e: e.activation(out=w_["tmp"][:], in_=p1[:, 0:64], func=AF.Identity, scale=col(u, 80)), reads=[p1, u["sc"]], writes=[w_["tmp"]])
                            K.op("dve", lambda e: e.scalar_tensor_tensor(out=u["OT"][:, h * 64:(h + 1) * 64], in0=p2[:, 128:192], scalar=col(u, 16), in1=w_["tmp"][:], op0=ALU.mult, op1=ALU.add),
                                 reads=[p2, u["sc"], w_["tmp"]], writes=[u["OT"]])
                            K.op("dve", lambda e: e.scalar_tensor_tensor(out=u["S1"][:], in0=u["S0"][:], scalar=u["sc"][0:64, 60 + h:61 + h], in1=p3[0:64, 256:320], op0=ALU.mult, op1=ALU.add),
                                 reads=[u["S0"], u["sc"], p3], writes=[u["S1"]])
                    for u in U:
                        if u["h"] == 3:
                            j = u["j"]
                            K.dma("sp", dn_o[u["d"], j * 128:(j + 1) * 128, :], u["OT"][:], in_buf=u["OT"])
                    K.pump(10 ** 6)
                K.barrier()

        def phase_dn3(li, need_ctx):
            with ExitStack() as st:
                gN = K.sb(st, [128, 64], F32, "gN", dma=True)
                K.dma("sp", gN[:], dn_norm_g[li:li + 1, :].partition_broadcast(128), out_buf=gN)
                of_ = Ring([K.sb(st, [128, 2, 256], F32, "of", dma=True) for _ in range(2)])
                zr = Ring([K.sb(st, [128, 256], F32, "z", dma=True) for _ in range(2)])
                osum = K.sb(st, [128, 256], F32, "osum")
                sq = K.sb(st, [128, 256], F32, "sq")
                sc = K.sb(st, [128, 8], F32, "sc")
                yb = Ring([K.sb(st, [128, 256], BF16, "yb", dma=True) for _ in range(2)])
                for j in range(0 if need_ctx else 2, 34):
                    o_, z_, y_ = of_.next(), zr.next(), yb.next()
                    K.dma("sp", [o_[:, 0, :], o_[:, 1, :]], [dn_o[0, j * 128:(j + 1) * 128, :], dn_o[1, j * 128:(j + 1) * 128, :]], out_buf=o_)
                    K.dma("sp", z_[:], dn_zab[j * 128:(j + 1) * 128, 0:256], out_buf=z_)
                    K.op("dve", lambda e: e.tensor_tensor(osum[:], o_[:, 0, :], o_[:, 1, :], op=ALU.add), reads=[o_], writes=[osum])
                    K.op("pool", lambda e: e.tensor_tensor(sq[:], osum[:], osum[:], op=ALU.mult), reads=[osum], writes=[sq])
                    K.op("dve", lambda e: e.tensor_reduce(out=sc[:, 0:4], in_=sq[:].rearrange("p (a b) -> p a b", b=64), axis=AX.X, op=ALU.add), reads=[sq], writes=[sc])
                    K.op("act", lambda e: e.activation(out=sc[:, 4:8], in_=sc[:, 0:4], func=AF.Sqrt, scale=1.0 / 64.0, bias=float(EPS)), reads=[sc], writes=[sc])
                    K.op("dve", lambda e: e.reciprocal(sc[:, 4:8], sc[:, 4:8]), reads=[sc], writes=[sc])
                    K.op("act", lambda e: e.activation(out=z_[:], in_=z_[:], func=AF.Silu), reads=[z_], writes=[z_])
                    for h in range(4):
                        K.op("dve", lambda e: e.scalar_tensor_tensor(out=osum[:, h * 64:(h + 1) * 64], in0=osum[:, h * 64:(h + 1) * 64], scalar=sc[:, 4 + h:5 + h],
                                                                      in1=gN[:], op0=ALU.mult, op1=ALU.mult), reads=[osum, sc, gN], writes=[osum])
                    K.op("dve", lambda e: e.tensor_tensor(y_[:], osum[:], z_[:], op=ALU.mult), reads=[osum, z_], writes=[y_])
                    K.dma("sp", ycat[j * 128:(j + 1) * 128, 256:512], y_[:], in_buf=y_)
                K.barrier()

        def phase_final():
            with ExitStack() as st:
                xg = [K.sb(st, [128, 8, 512], F32, "xg", dma=True) for _ in range(2)]
                xn = K.sb(st, [128, 8, 512], BF16, "xn")
                rstd = K.sb(st, [128, 512], F32, "rstd")
                fg = K.sb(st, [128, 8], F32, "fg")
                yo = [K.sb(st, [128, D], F32, "yo", dma=True) for _ in range(2)]
                pss = K.ps(st, [128, 512], F32, "pss")
                pt = [K.ps(st, [128, 512], F32, "pt") for _ in range(4)]
                xTv = xT.rearrange("(k p) t -> p k t", p=128)
                K.op("dve", lambda e: e.tensor_scalar(fg[:], fin_g[:], float(np.sqrt(D)), None, op0=ALU.mult), reads=[fin_g], writes=[fg])
                grps = groups_all(False)
                cnt = 0
                def ldf(gi):
                    t0_, n_, v_ = grps[gi]
                    xb_ = xg[gi % 2]
                    K.dma("sp", [xb_[:, 0:4, :], xb_[:, 4:8, :]], [xTv[:, 0:4, t0_:t0_ + n_], xTv[:, 4:8, t0_:t0_ + n_]], out_buf=xb_)

                ldf(0)
                for gi, (t0, n, v) in enumerate(grps):
                    xb = xg[gi % 2]
                    if gi + 1 < len(grps):
                        ldf(gi + 1)
                    K.op("act", lambda e: e.activation(out=xn[:], in_=xb[:], func=AF.Square), reads=[xb], writes=[xn])
                    for k in range(8):
                        mm(K, pss, pss[:], ones_b, ones_b[:], xn, xn[:, k, :], start=(k == 0), stop=(k == 7))
                    K.op("act", lambda e: e.activation(out=rstd[:], in_=pss[:], func=AF.Sqrt, scale=1.0, bias=float(D * EPS)), reads=[pss], writes=[rstd])
                    K.op("dve", lambda e: e.reciprocal(rstd[:], rstd[:]), reads=[rstd], writes=[rstd])
                    for k in range(8):
                        e_ = "dve"
                        K.op(e_, lambda e: e.scalar_tensor_tensor(out=xb[:, k, :], in0=xb[:, k, :], scalar=fg[:, k:k + 1],
                                                                   in1=rstd[:], op0=ALU.mult, op1=ALU.mult),
                             reads=[xb, fg, rstd], writes=[xb])
                    for tt in range(4):
                        o = yo[cnt % 2]
                        for hh in range(2):
                            p = pt[(2 * cnt + hh) % 4]
                            for kk in range(4):
                                k = hh * 4 + kk
                                tr(K, p, p[:, kk * 128:(kk + 1) * 128], xb, xb[:, k, tt * 128:(tt + 1) * 128], ident, ident[:])
                            if hh == 0:
                                K.op("act", lambda e: e.copy(o[:, 0:512], p[:]), reads=[p], writes=[o])
                            else:
                                K.op("dve", lambda e: e.tensor_copy(o[:, 512:1024], p[:]), reads=[p], writes=[o])
                        r0 = t0 - L + tt * 128
                        K.dma("sp", y_out[r0:r0 + 128, :], o[:], in_buf=o)
                        cnt += 1
                K.barrier()

        if only is not None:
            name, li, need_ctx = only
            if name == "dn":
                K.dma("sp", ident[:], ident_in[:, :], out_buf=ident)
                import os
                sub = os.environ.get("DN_SUB", "123")
                if "1" in sub:
                    phase_dn1(li)
                if "2" in sub:
                    phase_dn2(li)
                if "3" in sub:
                    phase_dn3(li, need_ctx)
            else:
                {"swa": phase_swa, "mla": phase_mla, "na": phase_na}[name](li, need_ctx)
            K.finish()
            return nc
        phase_init()
        for li in range(DEPTH):
            need_ctx = li < DEPTH - 1
            phase_mod(li)
            phase_ffn(ffn1_wg[li], ffn1_wu[li], ffn1_wd[li], 0, with_ctx=True)
            if stop_after == "ffn1":
                break
            phase_inproj(li)
            if stop_after == "inproj":
                break
            phase_swa(li, need_ctx)
            phase_mla(li, need_ctx)
            phase_na(li, need_ctx)
            phase_dn1(li)
            phase_dn2(li)
            phase_outproj(li, need_ctx)
            phase_ffn(ffn2_wg[li], ffn2_wu[li], ffn2_wd[li], 2, with_ctx=need_ctx)
        phase_final()
        K.finish()
    return nc


INPUT_NAMES = ["ada_w", "ada_b", "norm1_g", "ffn1_wg", "ffn1_wu", "ffn1_wd", "norm2_g", "norm3_g",
               "ffn2_wg", "ffn2_wu", "ffn2_wd", "w_in", "w_out", "swa_sink", "mla_q_norm_g", "mla_w_uq",
               "mla_kv_norm_g", "mla_w_ukv", "dn_conv_w", "dn_norm_g"]


def host_consts():
    theta = 10000.0
    tpos = np.arange(S)
    row = (tpos // 64).astype(np.float64)
    col = (tpos % 64).astype(np.float64)

    def tab(nrows, qw, nfreq_total):
        c = np.ones((nrows, T), np.float64)
        s_ = np.zeros((nrows, T), np.float64)
        for d in range(nrows):
            dd = d % (4 * qw)
            q, j = dd // qw, dd % qw
            inv = theta ** (-(2.0 * j) / (2 * qw))
            pos = row if q < 2 else col
            c[d, L:] = np.cos(pos * inv)
            s_[d, L:] = np.sin(pos * inv)
        return np.stack([c, s_]).astype(np.float32)

    t64 = tab(128, 16, 16)
    t32 = tab(32, 8, 8)
    t96 = np.concatenate([np.stack([np.ones((64, T)), np.zeros((64, T))]).astype(np.float32), t32], axis=1)
    kp = np.arange(128)[:, None]
    qq = np.arange(128)[None, :]
    mask_pn = np.stack([(qq <= kp), (kp <= qq)]).astype(np.float32)
    qc = np.arange(64)[None, :]
    kc = np.arange(64)[:, None]
    cs = np.clip(qc - 8, 0, 48)
    ok = ((kc >= cs) & (kc < cs + 16)).astype(np.float32)
    okm = np.concatenate([ok, ok], 0)
    a_ = np.arange(128)[:, None]
    b_ = np.arange(128)[None, :]
    tri = np.stack([a_ <= b_, a_ > b_, a_ >= b_, a_ < b_]).astype(np.float32)
    return {"tri": tri, "okm": okm, "ident": np.eye(128, dtype=np.float32), "tab64": t64, "tab96": np.ascontiguousarray(t96), "tab32": t32,
            "mask_pn": mask_pn}


def make_in_maps(inputs):
    consts = host_consts()
    idx = np.clip(np.arange(64)[:, None] - np.arange(64)[None, :] + 15, 0, 30)
    rpbx = np.ascontiguousarray(np.transpose(inputs["na_rpb"][:, :, :, idx], (0, 3, 1, 2, 4)))
    maps = []
    for b in range(8):
        m = {
            "x": np.ascontiguousarray(inputs["x"][b]),
            "c": np.ascontiguousarray(inputs["c"][b:b + 1]),
            "ctx": np.ascontiguousarray(inputs["ctx"][b]),
            "c_ctx": np.ascontiguousarray(inputs["c_ctx"][None, :]),
            "final_norm_g": np.ascontiguousarray(inputs["final_norm_g"][None, :]),
        }
        m.update(consts)
        m["rpbx"] = rpbx
        m["dn_a_log"] = np.ascontiguousarray(inputs["dn_a_log"].reshape(DEPTH, 8))
        m["dn_dt_bias"] = np.ascontiguousarray(inputs["dn_dt_bias"].reshape(DEPTH, 8))
        for n in INPUT_NAMES:
            m[n] = np.ascontiguousarray(inputs[n])
        maps.append(m)
    return maps


def kernel(**inputs):
    inputs = {k: np.asarray(v) for k, v in inputs.items()}
    nc = build_program()
    res = run_bass_kernel_spmd(nc, make_in_maps(inputs), core_ids=list(range(8)))
    return np.stack([r["y"] for r in res.results], axis=0).astype(np.float32)
```

```python
import types
import numpy as np
from contextlib import ExitStack
import concourse.bass as bass
import concourse.mybir as mybir
from concourse.bass_utils import run_bass_kernel_spmd

F32 = mybir.dt.float32
BF16 = mybir.dt.bfloat16
AF = mybir.ActivationFunctionType
ALU = mybir.AluOpType
AX = mybir.AxisListType

D = 1024
S = 4096
L = 256
T = S + L
DFF = 2816
NFF = DFF // 128
DEPTH = 2
EPS = 1e-6
INP = 2736


class Buf:
    def __init__(self, t, name):
        self.t = t
        self.name = name
        self.w = None
        self.r = {}
        self.dsem = None

    def __getitem__(self, idx):
        return self.t[idx]


class KB:
    def __init__(self, nc, es, n_dma_sems=64):
        self.nc = nc
        self.engs = {"pe": nc.tensor, "act": nc.scalar, "dve": nc.vector,
                     "pool": nc.gpsimd, "sp": nc.sync}
        self.sem = {e: es.enter_context(nc.semaphore("se_" + e)) for e in self.engs}
        self.cnt = {e: 0 for e in self.engs}
        self.seen = {e: {} for e in self.engs}
        self.latest = {}
        self.free = [[es.enter_context(nc.semaphore("sd%d" % i)), 0] for i in range(n_dma_sems)]
        self.uid = 0
        self.rr = 0
        self.defer = None
        self.pending = []

    def sb(self, st, shape, dt, name=None, dma=False):
        self.uid += 1
        name = "%s_%d" % (name or "b", self.uid)
        t = st.enter_context(self.nc.sbuf_tensor(name, list(shape), dt))
        b = Buf(t, name)
        if dma:
            b.dsem = self.free.pop()
            st.callback(lambda b=b: self.free.append(b.dsem))
        return b

    def ps(self, st, shape, dt=F32, name=None):
        self.uid += 1
        name = "%s_%d" % (name or "p", self.uid)
        t = st.enter_context(self.nc.psum_tensor(name, list(shape), dt))
        return Buf(t, name)

    def _wait(self, e, tok):
        sem, val, key = tok
        if self.seen[e].get(key, 0) >= val:
            return
        self.engs[e].wait_ge(sem, val)
        self.seen[e][key] = val

    def _deps(self, e, reads, writes):
        toks = []
        for b in reads:
            if b is not None and b.w is not None:
                toks.append(b.w)
        for b in writes:
            if b is None:
                continue
            if b.w is not None:
                toks.append(b.w)
            toks.extend(b.r.values())
        for tok in toks:
            if e == "pe" and tok[2] == "se_pe":
                continue
            self._wait(e, tok)

    def pump(self, n=1):
        q = self.pending
        d, self.defer = self.defer, None
        while n > 0 and q:
            item = q.pop(0)
            if item[0] == "op":
                self.op(*item[1:])
            else:
                self.dma(*item[1], **item[2])
            n -= 1
        self.defer = d

    def op(self, e, fn, reads=(), writes=()):
        if self.defer is not None:
            if fn.__closure__:
                fn = types.FunctionType(fn.__code__, fn.__globals__, fn.__name__, fn.__defaults__,
                                        tuple(types.CellType(c.cell_contents) for c in fn.__closure__))
            self.defer.append(("op", e, fn, list(reads), list(writes)))
            return None
        self._deps(e, reads, writes)
        ins = fn(self.engs[e])
        self.cnt[e] += 1
        ins.then_inc(self.sem[e], 1)
        key = "se_" + e
        tok = (self.sem[e], self.cnt[e], key)
        self.latest[key] = tok
        for b in reads:
            if b is not None:
                b.r[e] = tok
        for b in writes:
            if b is not None:
                b.w = tok
                b.r = {}
        return tok

    def dma(self, q, out_ap, in_ap, out_buf=None, in_buf=None, n=1, fn=None):
        if self.defer is not None:
            self.defer.append(("dma", (q, out_ap, in_ap), dict(out_buf=out_buf, in_buf=in_buf)))
            return None
        sb = out_buf if out_buf is not None else in_buf
        assert sb is not None and sb.dsem is not None, "dma needs an SBUF buf with dsem"
        self._deps(q, [in_buf] if in_buf is not None else [], [out_buf] if out_buf is not None else [])
        pairs = list(zip(out_ap, in_ap)) if isinstance(out_ap, (list, tuple)) else [(out_ap, in_ap)]
        for o, i in pairs:
            self.engs[q].dma_start(out=o, in_=i).then_inc(sb.dsem[0], 16)
            sb.dsem[1] += 16
        key = "sd_" + str(id(sb.dsem))
        tok = (sb.dsem[0], sb.dsem[1], key)
        self.latest[key] = tok
        if out_buf is not None:
            out_buf.w = tok
            out_buf.r = {}
        else:
            in_buf.r["dma_" + q] = tok
        return tok

    def barrier(self):
        for e in self.engs:
            for tok in list(self.latest.values()):
                self._wait(e, tok)

    def finish(self):
        self.barrier()


def mm(K, out, out_ap, lhs, lhs_ap, rhs, rhs_ap, start=True, stop=True):
    return K.op("pe", lambda e: e.matmul(out_ap, lhs_ap, rhs_ap, start=start, stop=stop),
                reads=[lhs, rhs], writes=[out])


def tr(K, out, out_ap, in_, in_ap, ident, ident_ap):
    return K.op("pe", lambda e: e.transpose(out_ap, in_ap, ident_ap), reads=[in_, ident], writes=[out])


class Prog:
    def __init__(self, debug=False, phases=None):
        self.debug = debug
        self.phases = phases


def build_program(debug=False, stop_after=None, only=None, feed=()):
    nc = bass.Bass("TRN2", target_bir_lowering=False)
    kind_s = "ExternalOutput" if debug else "Internal"

    def din(name, shape, dt=F32):
        return nc.dram_tensor(name, list(shape), dt, kind="ExternalInput").ap()

    def dscr(name, shape, dt=F32):
        if name in feed:
            return nc.dram_tensor(name, list(shape), dt, kind="ExternalInput").ap()
        if debug:
            return nc.dram_tensor(name, list(shape), dt, kind="ExternalOutput").ap()
        return nc.dram_tensor(name, list(shape), dt).ap()

    x_in = din("x", [S, D])
    c_in = din("c", [1, D])
    ctx_in = din("ctx", [L, D])
    cctx_in = din("c_ctx", [1, D])
    ada_w = din("ada_w", [DEPTH, D, 9 * D])
    ada_b = din("ada_b", [DEPTH, 9 * D])
    norm1_g = din("norm1_g", [DEPTH, D])
    ffn1_wg = din("ffn1_wg", [DEPTH, D, DFF])
    ffn1_wu = din("ffn1_wu", [DEPTH, D, DFF])
    ffn1_wd = din("ffn1_wd", [DEPTH, DFF, D])
    norm2_g = din("norm2_g", [DEPTH, D])
    norm3_g = din("norm3_g", [DEPTH, D])
    ffn2_wg = din("ffn2_wg", [DEPTH, D, DFF])
    ffn2_wu = din("ffn2_wu", [DEPTH, D, DFF])
    ffn2_wd = din("ffn2_wd", [DEPTH, DFF, D])
    final_g = din("final_norm_g", [1, D])
    ident_in = din("ident", [128, 128])
    w_in = din("w_in", [DEPTH, D, INP])
    w_out = din("w_out", [DEPTH, D, D])
    swa_sink = din("swa_sink", [DEPTH, 4])
    mla_qg = din("mla_q_norm_g", [DEPTH, 256])
    mla_wuq = din("mla_w_uq", [DEPTH, 256, 384])
    mla_kvg = din("mla_kv_norm_g", [DEPTH, 128])
    mla_wukv = din("mla_w_ukv", [DEPTH, 128, 512])
    tab64 = din("tab64", [2, 128, T])
    tab96 = din("tab96", [2, 96, T])
    tab32 = din("tab32", [2, 32, T])
    mask_pn = din("mask_pn", [2, 128, 128])
    rpbx = din("rpbx", [DEPTH, 64, 4, 15, 64])
    tri_in = din("tri", [4, 128, 128])
    dn_conv_w = din("dn_conv_w", [DEPTH, 3, 768])
    dn_a_log = din("dn_a_log", [DEPTH, 8])
    dn_dt_bias = din("dn_dt_bias", [DEPTH, 8])
    dn_norm_g = din("dn_norm_g", [DEPTH, 64])
    okm_in = din("okm", [128, 64])
    y_out = nc.dram_tensor("y", [S, D], F32, kind="ExternalOutput").ap()

    xT = dscr("xT", [D, T])
    swa_qT = dscr("swa_qT", [256, T], BF16)
    swa_kT = dscr("swa_kT", [128, T], BF16)
    swa_v = dscr("swa_v", [T, 128], BF16)
    dn_qkvT = dscr("dn_qkvT", [768, T], F32)
    dn_zab = dscr("dn_zab", [T, 272], F32)
    mla_qT = dscr("mla_qT", [384, T], BF16)
    mla_kT = dscr("mla_kT", [384, T], BF16)
    mla_v = dscr("mla_v", [T, 256], BF16)
    na_qT = dscr("na_qT", [256, T], BF16)
    na_kT = dscr("na_kT", [256, T], BF16)
    na_v = dscr("na_v", [T, 256], BF16)
    ycat = dscr("ycat", [T, D], BF16)
    dn_hT = dscr("dn_hT", [768, T], F32)
    dn_h = dscr("dn_h", [T, 768], F32)
    dn_o = dscr("dn_o", [2, T, 256], F32)

    with ExitStack() as es:
        K = KB(nc, es)
        gs = ExitStack()
        es.enter_context(gs)
        ident = K.sb(gs, [128, 128], F32, "ident", dma=True)
        identb = K.sb(gs, [128, 128], BF16, "identb")
        ones_b = K.sb(gs, [128, 128], BF16, "ones_b")
        ones_f = K.sb(gs, [1, 2], F32, "ones_f")
        K.dma("sp", ident[:], ident_in[:, :], out_buf=ident)
        K.op("dve", lambda e: e.tensor_copy(identb[:], ident[:]), reads=[ident], writes=[identb])
        K.op("dve", lambda e: e.memset(ones_b[:], 1.0), writes=[ones_b])
        K.op("dve", lambda e: e.memset(ones_f[:], 1.0), writes=[ones_f])
        modA = [K.sb(gs, [128, 8, 2], F32, "modA%d" % j) for j in range(3)]
        modB = [K.sb(gs, [128, 8, 2], F32, "modB%d" % j) for j in range(3)]
        modG = [K.sb(gs, [128, 8, 2], F32, "modG%d" % j) for j in range(3)]
        fin_g = K.sb(gs, [128, 8], F32, "fin_g")

        def phase_init():
            with ExitStack() as st:
                xin = [K.sb(st, [128, D], F32, "xin", dma=True) for _ in range(2)]
                xo = [K.sb(st, [128, 8, 128], F32, "xo", dma=True) for _ in range(2)]
                pt = [K.ps(st, [128, 512], F32, "pt") for _ in range(4)]
                xTv = xT.rearrange("(k p) t -> p k t", p=128)
                def ld_(i):
                    src = ctx_in[i * 128:(i + 1) * 128, :] if i < 2 else x_in[(i - 2) * 128:(i - 1) * 128, :]
                    K.dma("sp", xin[i % 2][:], src, out_buf=xin[i % 2])

                ld_(0)
                for i in range(T // 128):
                    xi = xin[i % 2]
                    o = xo[i % 2]
                    if i + 1 < T // 128:
                        ld_(i + 1)
                    for hh in range(2):
                        p = pt[(2 * i + hh) % 4]
                        for kk in range(4):
                            k = hh * 4 + kk
                            tr(K, p, p[:, kk * 128:(kk + 1) * 128], xi, xi[:, k * 128:(k + 1) * 128], ident, ident[:])
                        if hh == 0:
                            K.op("act", lambda e: e.copy(o[:, 0:4, :], p[:].rearrange("p (k t) -> p k t", k=4)),
                                 reads=[p], writes=[o])
                        else:
                            K.op("dve", lambda e: e.tensor_copy(o[:, 4:8, :], p[:].rearrange("p (k t) -> p k t", k=4)),
                                 reads=[p], writes=[o])
                    K.dma("sp", xTv[:, :, i * 128:(i + 1) * 128], o[:], in_buf=o)
                K.barrier()

        def phase_mod(li):
            with ExitStack() as st:
                cv = K.sb(st, [1, 2, D], F32, "cv", dma=True)
                scv = K.sb(st, [128, 8, 2], F32, "scv")
                wblk = [K.sb(st, [128, 8, D], F32, "wblk", dma=True) for _ in range(2)]
                brow = K.sb(st, [1, 9 * D], F32, "brow", dma=True)
                grow = K.sb(st, [1, 4, D], F32, "grow", dma=True)
                pm = K.ps(st, [128, 8, 2], F32, "pm")
                pg = K.ps(st, [128, 4, 8], F32, "pg")
                modT = K.sb(st, [128, 72, 2], F32, "modT")
                gT = K.sb(st, [128, 4, 8], F32, "gT")
                K.dma("sp", [cv[0:1, 0, :], cv[0:1, 1, :]], [c_in[0:1, :], cctx_in[0:1, :]], out_buf=cv)
                K.dma("sp", brow[:], ada_b[li:li + 1, :], out_buf=brow)
                K.dma("sp", [grow[0:1, 0, :], grow[0:1, 1, :], grow[0:1, 2, :], grow[0:1, 3, :]],
                      [norm1_g[li:li + 1, :], norm2_g[li:li + 1, :], norm3_g[li:li + 1, :], final_g[0:1, :]],
                      out_buf=grow)
                for v in range(2):
                    for k in range(8):
                        mm(K, pm, pm[:, k, v:v + 1], cv, cv[0:1, v, k * 128:(k + 1) * 128], ones_f, ones_f[0:1, 0:1])
                K.op("act", lambda e: e.activation(out=scv[:], in_=pm[:], func=AF.Silu), reads=[pm], writes=[scv])
                for gi in range(4):
                    for k in range(8):
                        mm(K, pg, pg[:, gi, k:k + 1], grow, grow[0:1, gi, k * 128:(k + 1) * 128], ones_f, ones_f[0:1, 0:1])
                K.op("dve", lambda e: e.tensor_copy(gT[:], pg[:]), reads=[pg], writes=[gT])
                K.op("dve", lambda e: e.tensor_copy(fin_g[:], gT[:, 3, :]), reads=[gT], writes=[fin_g])
                awv = ada_w[li].rearrange("(k p) n -> p k n", p=128)
                for j in range(9):
                    wb = wblk[j % 2]
                    K.dma("sp", [wb[:, 0:4, :], wb[:, 4:8, :]],
                          [awv[:, 0:4, j * D:(j + 1) * D], awv[:, 4:8, j * D:(j + 1) * D]], out_buf=wb)
                    for m in range(8):
                        for k in range(8):
                            mm(K, pm, pm[:, m, :], wb, wb[:, k, m * 128:(m + 1) * 128], scv, scv[:, k, :],
                               start=(k == 0), stop=False)
                        mm(K, pm, pm[:, m, :], brow, brow[0:1, j * D + m * 128: j * D + (m + 1) * 128],
                           ones_f, ones_f[0:1, 0:2], start=False, stop=True)
                    K.op("dve", lambda e: e.tensor_copy(modT[:, j * 8:(j + 1) * 8, :], pm[:]), reads=[pm], writes=[modT])
                for s3 in range(3):
                    jsh, jsc, jg = 3 * s3, 3 * s3 + 1, 3 * s3 + 2
                    A, Bm, G = modA[s3], modB[s3], modG[s3]
                    K.op("dve", lambda e: e.tensor_scalar(A[:], modT[:, jsc * 8:(jsc + 1) * 8, :], 1.0, float(np.sqrt(D)),
                                                          op0=ALU.add, op1=ALU.mult), reads=[modT], writes=[A])
                    for v in range(2):
                        K.op("dve", lambda e: e.tensor_tensor(A[:, :, v], A[:, :, v], gT[:, s3, :], op=ALU.mult),
                             reads=[A, gT], writes=[A])
                    K.op("dve", lambda e: e.tensor_copy(Bm[:], modT[:, jsh * 8:(jsh + 1) * 8, :]), reads=[modT], writes=[Bm])
                    gsc = 1.0 if s3 == 1 else 0.5
                    K.op("dve", lambda e: e.tensor_scalar(G[:], modT[:, jg * 8:(jg + 1) * 8, :], gsc, None, op0=ALU.mult),
                         reads=[modT], writes=[G])
                K.barrier()

        def groups_all(with_ctx=True):
            g = []
            if with_ctx:
                g.append((0, L, 1))
            for i in range(S // 512):
                g.append((L + i * 512, 512, 0))
            return g

        def phase_ffn(wg_d, wu_d, wd_d, s3, with_ctx=True):
            A, Bm, G = modA[s3], modB[s3], modG[s3]
            with ExitStack() as st:
                Wg = K.sb(st, [128, 8, DFF], BF16, "Wg")
                Wu = K.sb(st, [128, 8, DFF], BF16, "Wu")
                Wd = K.sb(st, [128, NFF, D], BF16, "Wd")
                with ExitStack() as st2:
                    stg = [K.sb(st2, [128, DFF], F32, "stg", dma=True) for _ in range(2)]
                    ci = 0
                    ceng = ["dve", "act", "pool"]
                    for (wsrc, wdst) in ((wg_d, Wg), (wu_d, Wu)):
                        for k in range(8):
                            sg_ = stg[ci % 2]
                            K.dma("sp", sg_[:], wsrc[k * 128:(k + 1) * 128, :], out_buf=sg_)
                            e_ = ceng[ci % 3]
                            if e_ == "act":
                                K.op("act", lambda e: e.copy(wdst[:, k, :], sg_[:]), reads=[sg_], writes=[wdst])
                            else:
                                K.op(e_, lambda e: e.tensor_copy(wdst[:, k, :], sg_[:]), reads=[sg_], writes=[wdst])
                            ci += 1
                    for m in range(0, NFF, 2):
                        sg_ = stg[ci % 2]
                        K.dma("sp", sg_[:, 0:2 * D].rearrange("p (a n) -> p a n", a=2),
                              wd_d[m * 128:(m + 2) * 128, :].rearrange("(a p) n -> p a n", p=128), out_buf=sg_)
                        e_ = ceng[ci % 3]
                        src_ap = sg_[:, 0:2 * D].rearrange("p (a n) -> p a n", a=2)
                        if e_ == "act":
                            K.op("act", lambda e: e.copy(Wd[:, m:m + 2, :], src_ap), reads=[sg_], writes=[Wd])
                        else:
                            K.op(e_, lambda e: e.tensor_copy(Wd[:, m:m + 2, :], src_ap), reads=[sg_], writes=[Wd])
                        ci += 1
                    K.barrier()
                xg = [K.sb(st, [128, 8, 512], F32, "xg", dma=True) for _ in range(2)]
                xn = K.sb(st, [128, 8, 512], BF16, "xn")
                rstd = K.sb(st, [128, 512], F32, "rstd")
                actT = K.sb(st, [128, NFF, 512], BF16, "actT")
                sg = [K.sb(st, [128, 512], F32, "sg") for _ in range(2)]
                pss = K.ps(st, [128, 512], F32, "pss")
                pg = [K.ps(st, [128, 512], F32, "pg") for _ in range(2)]
                pu = [K.ps(st, [128, 512], F32, "pu") for _ in range(2)]
                po = [K.ps(st, [128, 512], F32, "po") for _ in range(2)]
                xTv = xT.rearrange("(k p) t -> p k t", p=128)
                grps = groups_all(with_ctx)

                def load(gi):
                    t0, n, v = grps[gi]
                    b = xg[gi % 2]
                    K.dma("sp", [b[:, 0:4, 0:n], b[:, 4:8, 0:n]], [xTv[:, 0:4, t0:t0 + n], xTv[:, 4:8, t0:t0 + n]], out_buf=b)

                def pre1(gi):
                    t0, n, v = grps[gi]
                    xb = xg[gi % 2]
                    K.op("act", lambda e: e.activation(out=xn[:, :, 0:n], in_=xb[:, :, 0:n], func=AF.Square),
                         reads=[xb], writes=[xn])

                def pre2(gi):
                    t0, n, v = grps[gi]
                    xb = xg[gi % 2]
                    for k in range(8):
                        mm(K, pss, pss[:, 0:n], ones_b, ones_b[:], xn, xn[:, k, 0:n], start=(k == 0), stop=(k == 7))
                    K.op("act", lambda e: e.activation(out=rstd[:, 0:n], in_=pss[:, 0:n], func=AF.Sqrt, scale=1.0, bias=float(D * EPS)),
                         reads=[pss], writes=[rstd])
                    K.op("dve", lambda e: e.reciprocal(rstd[:, 0:n], rstd[:, 0:n]), reads=[rstd], writes=[rstd])
                    for k in range(8):
                        K.op("dve", lambda e: e.scalar_tensor_tensor(out=xn[:, k, 0:n], in0=xb[:, k, 0:n], scalar=A[:, k, v:v + 1],
                                                                      in1=rstd[:, 0:n], op0=ALU.mult, op1=ALU.mult),
                             reads=[xb, A, rstd], writes=[xn])
                    for k in range(8):
                        K.op("act", lambda e: e.activation(out=xn[:, k, 0:n], in_=xn[:, k, 0:n], func=AF.Identity,
                                                           bias=Bm[:, k, v:v + 1], scale=1.0),
                             reads=[xn, Bm], writes=[xn])

                load(0)
                pre1(0)
                pre2(0)
                for gi, (t0, n, v) in enumerate(grps):
                    more = gi + 1 < len(grps)
                    if more:
                        load(gi + 1)
                    xb = xg[gi % 2]
                    for m in range(NFF):
                        pgm, pum, sgm = pg[m % 2], pu[m % 2], sg[m % 2]
                        for k in range(8):
                            mm(K, pgm, pgm[:, 0:n], Wg, Wg[:, k, m * 128:(m + 1) * 128], xn, xn[:, k, 0:n], start=(k == 0), stop=(k == 7))
                        for k in range(8):
                            mm(K, pum, pum[:, 0:n], Wu, Wu[:, k, m * 128:(m + 1) * 128], xn, xn[:, k, 0:n], start=(k == 0), stop=(k == 7))
                        K.op("act", lambda e: e.activation(out=sgm[:, 0:n], in_=pgm[:, 0:n], func=AF.Silu), reads=[pgm], writes=[sgm])
                        K.op("dve", lambda e: e.tensor_tensor(actT[:, m, 0:n], sgm[:, 0:n], pum[:, 0:n], op=ALU.mult),
                             reads=[sgm, pum], writes=[actT])
                    if more:
                        pre1(gi + 1)
                    for f in range(8):
                        pof = po[f % 2]
                        for m in range(NFF):
                            mm(K, pof, pof[:, 0:n], Wd, Wd[:, m, f * 128:(f + 1) * 128], actT, actT[:, m, 0:n], start=(m == 0), stop=(m == NFF - 1))
                        K.op("dve", lambda e: e.scalar_tensor_tensor(out=xb[:, f, 0:n], in0=pof[:, 0:n], scalar=G[:, f, v:v + 1],
                                                                      in1=xb[:, f, 0:n], op0=ALU.mult, op1=ALU.add),
                             reads=[pof, G, xb], writes=[xb])
                        if f == 1 and more:
                            pre2(gi + 1)
                    K.dma("sp", [xTv[:, 0:4, t0:t0 + n], xTv[:, 4:8, t0:t0 + n]], [xb[:, 0:4, 0:n], xb[:, 4:8, 0:n]], in_buf=xb)
                K.barrier()


        def prenorm(xb, xn, rstd, pss, A, Bm, v, n):
            K.op("act", lambda e: e.activation(out=xn[:, :, 0:n], in_=xb[:, :, 0:n], func=AF.Square),
                 reads=[xb], writes=[xn])
            for k in range(8):
                mm(K, pss, pss[:, 0:n], ones_b, ones_b[:], xn, xn[:, k, 0:n], start=(k == 0), stop=(k == 7))
            K.op("act", lambda e: e.activation(out=rstd[:, 0:n], in_=pss[:, 0:n], func=AF.Sqrt, scale=1.0, bias=float(D * EPS)),
                 reads=[pss], writes=[rstd])
            K.op("dve", lambda e: e.reciprocal(rstd[:, 0:n], rstd[:, 0:n]), reads=[rstd], writes=[rstd])
            for k in range(8):
                K.op("dve", lambda e: e.scalar_tensor_tensor(out=xn[:, k, 0:n], in0=xb[:, k, 0:n], scalar=A[:, k, v:v + 1],
                                                              in1=rstd[:, 0:n], op0=ALU.mult, op1=ALU.mult),
                     reads=[xb, A, rstd], writes=[xn])
            for k in range(8):
                K.op("act", lambda e: e.activation(out=xn[:, k, 0:n], in_=xn[:, k, 0:n], func=AF.Identity,
                                                   bias=Bm[:, k, v:v + 1], scale=1.0),
                     reads=[xn, Bm], writes=[xn])

        class Ring:
            def __init__(self, bufs):
                self.bufs = bufs
                self.i = 0

            def next(self):
                b = self.bufs[self.i % len(self.bufs)]
                self.i += 1
                return b

        def rot_cols(dst, src, k, c0, nblk, qw):
            w4 = 4 * qw
            dv = dst[:, k, c0:c0 + nblk * w4].rearrange("p (b q j) -> p b q j", q=4, j=qw)
            sv = src[:, k, c0:c0 + nblk * w4].rearrange("p (b q j) -> p b q j", q=4, j=qw)
            for (qd, qs, sgn) in ((0, 1, -1.0), (1, 0, 1.0), (2, 3, -1.0), (3, 2, 1.0)):
                K.op("pool", lambda e: e.tensor_scalar(dv[:, :, qd, :], sv[:, :, qs, :], sgn, None, op0=ALU.mult),
                     reads=[src], writes=[dst])

        def phase_inproj(li):
            A, Bm = modA[1], modB[1]
            with ExitStack() as st:
                Win = K.sb(st, [128, 8, INP], BF16, "Win")
                Wrot = K.sb(st, [128, 8, INP], BF16, "Wrot")
                Wuq = K.sb(st, [128, 2, 384], BF16, "Wuq")
                Wuqr = K.sb(st, [128, 2, 384], BF16, "Wuqr")
                Wukv = K.sb(st, [128, 512], BF16, "Wukv")
                gqk = K.sb(st, [128, 4], F32, "gqk")
                with ExitStack() as st2:
                    stg = [K.sb(st2, [128, INP], F32, "stg", dma=True) for _ in range(2)]
                    grow = K.sb(st2, [1, 384], F32, "grow", dma=True)
                    pgq = K.ps(st2, [128, 4], F32, "pgq")
                    for k in range(8):
                        sg_ = stg[k % 2]
                        K.dma("sp", sg_[:], w_in[li, k * 128:(k + 1) * 128, :], out_buf=sg_)
                        if k % 2 == 0:
                            K.op("dve", lambda e: e.tensor_copy(Win[:, k, :], sg_[:]), reads=[sg_], writes=[Win])
                        else:
                            K.op("act", lambda e: e.copy(Win[:, k, :], sg_[:]), reads=[sg_], writes=[Win])
                        rot_cols(Wrot, Win, k, 0, 6, 16)
                        rot_cols(Wrot, Win, k, 1936, 1, 8)
                    for c in range(2):
                        sg_ = stg[c % 2]
                        K.dma("sp", sg_[:, 0:384], mla_wuq[li, c * 128:(c + 1) * 128, :], out_buf=sg_)
                        K.op("dve", lambda e: e.tensor_copy(Wuq[:, c, :], sg_[:, 0:384]), reads=[sg_], writes=[Wuq])
                    K.op("pool", lambda e: e.memset(Wuqr[:], 0.0), writes=[Wuqr])
                    for c in range(2):
                        for h in range(4):
                            rot_cols(Wuqr, Wuq, c, h * 96 + 64, 1, 8)
                    sg_ = stg[0]
                    K.dma("sp", sg_[:, 0:512], mla_wukv[li, :, :], out_buf=sg_)
                    K.op("dve", lambda e: e.tensor_copy(Wukv[:], sg_[:, 0:512]), reads=[sg_], writes=[Wukv])
                    K.dma("sp", [grow[0:1, 0:256], grow[0:1, 256:384]], [mla_qg[li:li + 1, :], mla_kvg[li:li + 1, :]], out_buf=grow)
                    for c in range(3):
                        mm(K, pgq, pgq[:, c:c + 1], grow, grow[0:1, c * 128:(c + 1) * 128], ones_f, ones_f[0:1, 0:1])
                    K.op("dve", lambda e: e.tensor_copy(gqk[:, 0:3], pgq[:, 0:3]), reads=[pgq], writes=[gqk])
                    K.barrier()

                xg = [K.sb(st, [128, 8, 512], F32, "xg", dma=True) for _ in range(2)]
                tb64 = [K.sb(st, [128, 2, 512], F32, "tb64", dma=True) for _ in range(2)]
                tb96 = [K.sb(st, [96, 2, 512], F32, "tb96", dma=True) for _ in range(2)]
                tb32 = [K.sb(st, [32, 2, 512], F32, "tb32", dma=True) for _ in range(2)]
                xns = [K.sb(st, [128, 8, 512], BF16, "xn") for _ in range(2)]
                rstd = K.sb(st, [128, 512], F32, "rstd")
                t1 = K.sb(st, [128, 512], F32, "t1")
                t2 = K.sb(st, [128, 512], F32, "t2")
                cqf = K.sb(st, [128, 3, 512], F32, "cqf")
                cqs = K.sb(st, [128, 3, 512], BF16, "cqs")
                cqn = K.sb(st, [128, 3, 512], BF16, "cqn")
                rq = K.sb(st, [128, 512], F32, "rq")
                rkv = K.sb(st, [128, 512], F32, "rkv")
                obf = Ring([K.sb(st, [128, 512], BF16, "obf", dma=True) for _ in range(4)])
                of32 = Ring([K.sb(st, [128, 512], F32, "of32", dma=True) for _ in range(3)])
                pss = K.ps(st, [128, 512], F32, "pss")
                pb = Ring([K.ps(st, [128, 512], F32, "pb") for _ in range(7)])
                xTv = xT.rearrange("(k p) t -> p k t", p=128)
                grps = groups_all(True)
                evi = [0]

                def evac(out_b, out_ap, p, p_ap):
                    evi[0] += 1
                    if evi[0] % 2 == 0:
                        K.op("act", lambda e: e.copy(out_ap, p_ap), reads=[p], writes=[out_b])
                    else:
                        K.op("dve", lambda e: e.tensor_copy(out_ap, p_ap), reads=[p], writes=[out_b])

                def load(gi):
                    t0, n, v = grps[gi]
                    b = xg[gi % 2]
                    K.dma("sp", [b[:, 0:4, 0:n], b[:, 4:8, 0:n]], [xTv[:, 0:4, t0:t0 + n], xTv[:, 4:8, t0:t0 + n]], out_buf=b)
                    K.dma("sp", tb64[gi % 2][:, :, 0:n], tab64[:, :, t0:t0 + n].rearrange("c p t -> p c t"), out_buf=tb64[gi % 2])
                    K.dma("sp", tb96[gi % 2][:, :, 0:n], tab96[:, :, t0:t0 + n].rearrange("c p t -> p c t"), out_buf=tb96[gi % 2])
                    K.dma("sp", tb32[gi % 2][:, :, 0:n], tab32[:, :, t0:t0 + n].rearrange("c p t -> p c t"), out_buf=tb32[gi % 2])

                load(0)
                prenorm(xg[0], xns[0], rstd, pss, A, Bm, grps[0][2], grps[0][1])
                for gi, (t0, n, v) in enumerate(grps):
                    if gi + 1 < len(grps):
                        load(gi + 1)
                        K.defer = K.pending
                        prenorm(xg[(gi + 1) % 2], xns[(gi + 1) % 2], rstd, pss, A, Bm, grps[gi + 1][2], grps[gi + 1][1])
                        K.defer = None
                    xb = xg[gi % 2]
                    xn = xns[gi % 2]
                    T64, T96, T32 = tb64[gi % 2], tb96[gi % 2], tb32[gi % 2]

                    def proj(c0, nc_, W=Win):
                        p = pb.next()
                        for k in range(8):
                            mm(K, p, p[0:nc_, 0:n], W, W[:, k, c0:c0 + nc_], xn, xn[:, k, 0:n], start=(k == 0), stop=(k == 7))
                        K.pump(1)
                        return p

                    def rope_store(p, pr, tb, np_, dst_ap):
                        ob = obf.next()
                        K.op("dve", lambda e: e.tensor_tensor(t1[0:np_, 0:n], p[0:np_, 0:n], tb[0:np_, 0, 0:n], op=ALU.mult),
                             reads=[p, tb], writes=[t1])
                        K.op("dve", lambda e: e.tensor_tensor(t2[0:np_, 0:n], pr[0:np_, 0:n], tb[0:np_, 1, 0:n], op=ALU.mult),
                             reads=[pr, tb], writes=[t2])
                        K.op("pool", lambda e: e.tensor_tensor(ob[0:np_, 0:n], t1[0:np_, 0:n], t2[0:np_, 0:n], op=ALU.add),
                             reads=[t1, t2], writes=[ob])
                        if isinstance(dst_ap, list):
                            K.dma("sp", dst_ap, [ob[0:np_, 0:n]] * len(dst_ap), in_buf=ob)
                        else:
                            K.dma("sp", dst_ap, ob[0:np_, 0:n], in_buf=ob)

                    def plain_store(p, np_, dst_ap, dt=BF16):
                        ob = obf.next() if dt == BF16 else of32.next()
                        evac(ob, ob[0:np_, 0:n], p, p[0:np_, 0:n])
                        K.dma("sp", dst_ap, ob[0:np_, 0:n], in_buf=ob)

                    for ch in range(2):
                        p = proj(ch * 128, 128)
                        pr = proj(ch * 128, 128, Wrot)
                        rope_store(p, pr, T64, 128, swa_qT[ch * 128:(ch + 1) * 128, t0:t0 + n])
                    p = proj(256, 128)
                    pr = proj(256, 128, Wrot)
                    rope_store(p, pr, T64, 128, swa_kT[:, t0:t0 + n])
                    for ch in range(6):
                        p = proj(512 + ch * 128, 128)
                        plain_store(p, 128, dn_qkvT[ch * 128:(ch + 1) * 128, t0:t0 + n], F32)
                    for ch in range(2):
                        p = proj(1968 + ch * 128, 128)
                        plain_store(p, 128, na_qT[ch * 128:(ch + 1) * 128, t0:t0 + n])
                    for ch in range(2):
                        p = proj(2224 + ch * 128, 128)
                        plain_store(p, 128, na_kT[ch * 128:(ch + 1) * 128, t0:t0 + n])
                    p = proj(1936, 32)
                    pr = proj(1936, 32, Wrot)
                    rope_store(p, pr, T32, 32, [mla_kT[h * 96 + 64:h * 96 + 96, t0:t0 + n] for h in range(4)])
                    for c in range(3):
                        p = proj(1552 + c * 128, 128)
                        K.op("act", lambda e: e.copy(cqf[:, c, 0:n], p[:, 0:n]), reads=[p], writes=[cqf])
                    K.op("act", lambda e: e.activation(out=cqs[:, :, 0:n], in_=cqf[:, :, 0:n], func=AF.Square), reads=[cqf], writes=[cqs])
                    pq_ = pb.next()
                    for c in range(2):
                        mm(K, pq_, pq_[:, 0:n], ones_b, ones_b[:], cqs, cqs[:, c, 0:n], start=(c == 0), stop=(c == 1))
                    K.op("act", lambda e: e.activation(out=rq[:, 0:n], in_=pq_[:, 0:n], func=AF.Sqrt, scale=1.0 / 256.0, bias=float(EPS)),
                         reads=[pq_], writes=[rq])
                    K.op("dve", lambda e: e.reciprocal(rq[:, 0:n], rq[:, 0:n]), reads=[rq], writes=[rq])
                    pk_ = pb.next()
                    mm(K, pk_, pk_[:, 0:n], ones_b, ones_b[:], cqs, cqs[:, 2, 0:n])
                    K.op("act", lambda e: e.activation(out=rkv[:, 0:n], in_=pk_[:, 0:n], func=AF.Sqrt, scale=1.0 / 128.0, bias=float(EPS)),
                         reads=[pk_], writes=[rkv])
                    K.op("dve", lambda e: e.reciprocal(rkv[:, 0:n], rkv[:, 0:n]), reads=[rkv], writes=[rkv])
                    for c in range(3):
                        rr_ = rq if c < 2 else rkv
                        K.op("dve", lambda e: e.scalar_tensor_tensor(out=cqn[:, c, 0:n], in0=cqf[:, c, 0:n], scalar=gqk[:, c:c + 1],
                                                                      in1=rr_[:, 0:n], op0=ALU.mult, op1=ALU.mult),
                             reads=[cqf, gqk, rr_], writes=[cqn])
                    for h in range(4):
                        p = pb.next()
                        pr = pb.next()
                        for c in range(2):
                            mm(K, p, p[0:96, 0:n], Wuq, Wuq[:, c, h * 96:(h + 1) * 96], cqn, cqn[:, c, 0:n], start=(c == 0), stop=(c == 1))
                        for c in range(2):
                            mm(K, pr, pr[0:96, 0:n], Wuqr, Wuqr[:, c, h * 96:(h + 1) * 96], cqn, cqn[:, c, 0:n], start=(c == 0), stop=(c == 1))
                        rope_store(p, pr, T96, 96, mla_qT[h * 96:(h + 1) * 96, t0:t0 + n])
                    for h in range(4):
                        p = pb.next()
                        mm(K, p, p[0:64, 0:n], Wukv, Wukv[:, h * 128:h * 128 + 64], cqn, cqn[:, 2, 0:n])
                        plain_store(p, 64, mla_kT[h * 96:h * 96 + 64, t0:t0 + n])
                    wv_ap = Wukv[:].rearrange("p (h c) -> p h c", h=4)[:, :, 64:128]
                    for tt in range(n // 128):
                        r0 = t0 + tt * 128
                        p = pb.next()
                        mm(K, p, p[:, 0:256].rearrange("p (h c) -> p h c", h=4), cqn, cqn[:, 2, tt * 128:(tt + 1) * 128], Wukv, wv_ap)
                        ob = obf.next()
                        evac(ob, ob[:, 0:256], p, p[:, 0:256])
                        K.dma("sp", mla_v[r0:r0 + 128, :], ob[:, 0:256], in_buf=ob)
                        p = pb.next()
                        for k in range(8):
                            mm(K, p, p[:, 0:128], xn, xn[:, k, tt * 128:(tt + 1) * 128], Win, Win[:, k, 384:512], start=(k == 0), stop=(k == 7))
                        ob = obf.next()
                        evac(ob, ob[:, 0:128], p, p[:, 0:128])
                        K.dma("sp", swa_v[r0:r0 + 128, :], ob[:, 0:128], in_buf=ob)
                        p = pb.next()
                        for k in range(8):
                            mm(K, p, p[:, 0:272], xn, xn[:, k, tt * 128:(tt + 1) * 128], Win, Win[:, k, 1280:1552], start=(k == 0), stop=(k == 7))
                        ob = of32.next()
                        evac(ob, ob[:, 0:272], p, p[:, 0:272])
                        K.dma("sp", dn_zab[r0:r0 + 128, :], ob[:, 0:272], in_buf=ob)
                        p = pb.next()
                        for k in range(8):
                            mm(K, p, p[:, 0:256], xn, xn[:, k, tt * 128:(tt + 1) * 128], Win, Win[:, k, 2480:2736], start=(k == 0), stop=(k == 7))
                        ob = obf.next()
                        evac(ob, ob[:, 0:256], p, p[:, 0:256])
                        K.dma("sp", na_v[r0:r0 + 128, :], ob[:, 0:256], in_buf=ob)
                    K.pump(10 ** 6)
                K.barrier()


        def phase_swa(li, need_ctx):
            with ExitStack() as st:
                kT = K.sb(st, [64, 2, T], BF16, "kT", dma=True)
                qT = K.sb(st, [64, 4, T], BF16, "qT", dma=True)
                Va = K.sb(st, [128, 34, 2, 65], BF16, "Va", dma=True)
                mpn = K.sb(st, [128, 2, 128], F32, "mpn", dma=True)
                mpb = K.sb(st, [128, 2, 2, 128], BF16, "mpb")
                snk = K.sb(st, [128, 4], F32, "snk", dma=True)
                es_ = K.sb(st, [128, 4], F32, "es")
                K.dma("sp", kT[:], swa_kT.rearrange("(h d) t -> d h t", d=64), out_buf=kT)
                K.dma("sp", [qT[:, 0:2, :], qT[:, 2:4, :]],
                      [swa_qT[0:128, :].rearrange("(h d) t -> d h t", d=64), swa_qT[128:256, :].rearrange("(h d) t -> d h t", d=64)], out_buf=qT)
                vv = swa_v.rearrange("(j p) (h d) -> p j h d", p=128, d=64)
                K.dma("sp", [Va[:, j0:j0 + 17, h, 0:64] for h in range(2) for j0 in (0, 17)],
                      [vv[:, j0:j0 + 17, h, :] for h in range(2) for j0 in (0, 17)], out_buf=Va)
                K.op("pool", lambda e: e.memset(Va[:, :, :, 64:65], 1.0), writes=[Va])
                K.dma("sp", mpn[:], mask_pn.rearrange("w p q -> p w q"), out_buf=mpn)
                for w in range(2):
                    for g in range(2):
                        K.op("dve", lambda e: e.tensor_copy(mpb[:, w, g, :], mpn[:, w, :]), reads=[mpn], writes=[mpb])
                K.dma("sp", snk[:], swa_sink[li:li + 1, :].partition_broadcast(128), out_buf=snk)
                K.op("act", lambda e: e.activation(out=es_[:], in_=snk[:], func=AF.Exp), reads=[snk], writes=[es_])
                psS = Ring([K.ps(st, [128, 512], F32, "psS") for _ in range(3)])
                po = Ring([K.ps(st, [128, 512], F32, "po") for _ in range(4)])
                Pt = Ring([K.sb(st, [128, 2, 128], BF16, "Pt") for _ in range(4)])
                ysb = Ring([K.sb(st, [128, 256], BF16, "ysb", dma=True) for _ in range(3)])
                dn_ = Ring([K.sb(st, [128, 2], F32, "dn") for _ in range(4)])
                mi = [0]

                jobs = []

                def block(q0, tiles, yrow0):
                    for kh in range(2):
                        for ti, tl in enumerate(tiles):
                            jobs.append((q0, kh, ti, len(tiles), tl, yrow0))

                def run_jobs():
                    def issue_S(i):
                        q0, kh, ti, nt, (k0, vj, mk), yrow0 = jobs[i]
                        ps = psS.next()
                        mm(K, ps, ps[:, 0:256].rearrange("p (g q) -> p g q", g=2), kT, kT[:, kh, k0:k0 + 128], qT, qT[:, 2 * kh:2 * kh + 2, q0:q0 + 128])
                        return ps

                    ps_next = issue_S(0)
                    yb = None
                    pog = None
                    for i, (q0, kh, ti, nt, (k0, vj, mk), yrow0) in enumerate(jobs):
                        ps = ps_next
                        if i + 1 < len(jobs):
                            ps_next = issue_S(i + 1)
                        if kh == 0 and ti == 0:
                            yb = ysb.next()
                        if ti == 0:
                            pog = [po.next(), po.next()]
                        P = Pt.next()
                        K.op("act", lambda e: e.activation(out=P[:], in_=ps[:, 0:256].rearrange("p (g q) -> p g q", g=2), func=AF.Exp, scale=0.125),
                             reads=[ps], writes=[P])
                        if mk is not None:
                            mi[0] += 1
                            e_ = "dve" if mi[0] % 2 else "pool"
                            K.op(e_, lambda e: e.tensor_tensor(P[:], P[:], mpb[:, mk, :, :], op=ALU.mult), reads=[P, mpb], writes=[P])
                        for g in range(2):
                            mm(K, pog[g], pog[g][:, 0:65], P, P[:, g, :], Va, Va[:, vj, kh, :], start=(ti == 0), stop=(ti == nt - 1))
                        if ti == nt - 1:
                            for g in range(2):
                                h = 2 * kh + g
                                d_ = dn_.next()
                                K.op("dve", lambda e: e.tensor_tensor(d_[:, 0:1], pog[g][:, 64:65], es_[:, h:h + 1], op=ALU.add), reads=[pog[g], es_], writes=[d_])
                                K.op("dve", lambda e: e.reciprocal(d_[:, 1:2], d_[:, 0:1]), reads=[d_], writes=[d_])
                                K.op("dve", lambda e: e.tensor_scalar(yb[:, h * 64:(h + 1) * 64], pog[g][:, 0:64], d_[:, 1:2], None, op0=ALU.mult),
                                     reads=[pog[g], d_], writes=[yb])
                            if kh == 1:
                                K.dma("sp", ycat[yrow0:yrow0 + 128, 0:256], yb[:], in_buf=yb)

                if need_ctx:
                    for qt in range(2):
                        block(qt * 128, [(0, 0, None), (128, 1, None)], qt * 128)
                for i in range(S // 128):
                    tiles = [(0, 0, None), (128, 1, None)]
                    if i > 0:
                        tiles.append((L + (i - 1) * 128, 2 + i - 1, 0))
                    tiles.append((L + i * 128, 2 + i, None))
                    if i < S // 128 - 1:
                        tiles.append((L + (i + 1) * 128, 2 + i + 1, 1))
                    block(L + i * 128, tiles, L + i * 128)
                run_jobs()
                K.barrier()

        def phase_mla(li, need_ctx):
            sc = float(96 ** -0.5)
            with ExitStack() as st:
                kT = K.sb(st, [96, 4, T], BF16, "kT", dma=True)
                qT = K.sb(st, [96, 4, T], BF16, "qT", dma=True)
                Va = K.sb(st, [128, 34, 4, 65], BF16, "Va", dma=True)
                K.dma("sp", [kT[:, h, :] for h in range(4)], [mla_kT[h * 96:(h + 1) * 96, :] for h in range(4)], out_buf=kT)
                K.dma("sp", [qT[:, h, :] for h in range(4)], [mla_qT[h * 96:(h + 1) * 96, :] for h in range(4)], out_buf=qT)
                vv = mla_v.rearrange("(j p) (h d) -> p j h d", p=128, d=64)
                K.dma("sp", [Va[:, j0:j0 + 17, h, 0:64] for h in range(4) for j0 in (0, 17)],
                      [vv[:, j0:j0 + 17, h, :] for h in range(4) for j0 in (0, 17)], out_buf=Va)
                K.op("pool", lambda e: e.memset(Va[:, :, :, 64:65], 1.0), writes=[Va])
                psS = Ring([K.ps(st, [128, 512], F32, "psS") for _ in range(3)])
                po = [K.ps(st, [128, 512], F32, "po") for _ in range(4)]
                Pt = Ring([K.sb(st, [128, 512], BF16, "Pt") for _ in range(4)])
                ysb = Ring([K.sb(st, [128, 256], BF16, "ysb", dma=True) for _ in range(8)])
                dn_ = Ring([K.sb(st, [128, 2], F32, "dn") for _ in range(4)])

                def group(q0, n, ktiles, yrow0):
                    nq = n // 128
                    ybs = [ysb.next() for _ in range(nq)]
                    seq = [(h, ti, kt) for h in range(4) for ti, kt in enumerate(ktiles)]

                    def issue_S(i):
                        h, ti, kt = seq[i]
                        ps = psS.next()
                        mm(K, ps, ps[:, 0:n], kT, kT[:, h, kt * 128:(kt + 1) * 128], qT, qT[:, h, q0:q0 + n])
                        return ps

                    ps_next = issue_S(0)
                    for i, (h, ti, kt) in enumerate(seq):
                        ps = ps_next
                        if i + 1 < len(seq):
                            ps_next = issue_S(i + 1)
                        P = Pt.next()
                        K.op("act", lambda e: e.activation(out=P[:, 0:n], in_=ps[:, 0:n], func=AF.Exp, scale=sc), reads=[ps], writes=[P])
                        for qt in range(nq):
                            mm(K, po[qt], po[qt][:, 0:65], P, P[:, qt * 128:(qt + 1) * 128], Va, Va[:, kt, h, :],
                               start=(ti == 0), stop=(ti == len(ktiles) - 1))
                        if ti == len(ktiles) - 1:
                            for qt in range(nq):
                                d_ = dn_.next()
                                K.op("dve", lambda e: e.reciprocal(d_[:, 1:2], po[qt][:, 64:65]), reads=[po[qt]], writes=[d_])
                                K.op("dve", lambda e: e.tensor_scalar(ybs[qt][:, h * 64:(h + 1) * 64], po[qt][:, 0:64], d_[:, 1:2], None, op0=ALU.mult),
                                     reads=[po[qt], d_], writes=[ybs[qt]])
                    for qt in range(nq):
                        K.dma("sp", ycat[yrow0 + qt * 128:yrow0 + (qt + 1) * 128, 512:768], ybs[qt][:], in_buf=ybs[qt])

                if need_ctx:
                    group(0, 256, [0, 1], 0)
                for qg in range(S // 512):
                    group(L + qg * 512, 512, list(range(34)), L + qg * 512)
                K.barrier()


        def phase_na(li, need_ctx):
            with ExitStack() as st:
                kT = K.sb(st, [64, 4, T], BF16, "kT", dma=True)
                qT = K.sb(st, [64, 4, T], BF16, "qT", dma=True)
                Va = K.sb(st, [128, 34, 4, 65], BF16, "Va", dma=True)
                Vs = K.sb(st, [128, 31, 4, 65], BF16, "Vs", dma=True)
                TA = K.sb(st, [128, 4, 15, 64], BF16, "TA")
                for (dst, src) in ((kT, na_kT), (qT, na_qT)):
                    K.dma("sp", [dst[:, 0:2, :], dst[:, 2:4, :]],
                          [src[0:128, :].rearrange("(h d) t -> d h t", d=64), src[128:256, :].rearrange("(h d) t -> d h t", d=64)], out_buf=dst)
                vv = na_v.rearrange("(j p) (h d) -> p j h d", p=128, d=64)
                K.dma("sp", [Va[:, j0:j0 + 17, h, 0:64] for h in range(4) for j0 in (0, 17)],
                      [vv[:, j0:j0 + 17, h, :] for h in range(4) for j0 in (0, 17)], out_buf=Va)
                K.op("pool", lambda e: e.memset(Va[:, :, :, 64:65], 1.0), writes=[Va])
                vs = na_v[L + 64:L + 64 + 31 * 128, :].rearrange("(j p) (h d) -> p j h d", p=128, d=64)
                K.dma("sp", [Vs[:, :, h, 0:64] for h in range(4)], [vs[:, :, h, :] for h in range(4)], out_buf=Vs)
                K.op("pool", lambda e: e.memset(Vs[:, :, :, 64:65], 1.0), writes=[Vs])
                with ExitStack() as st2:
                    TAr = K.sb(st2, [128, 4, 15, 64], F32, "TAr", dma=True)
                    okm = K.sb(st2, [128, 64], F32, "okm", dma=True)
                    K.op("dve", lambda e: e.memset(TAr[:], 0.0), writes=[TAr])
                    K.dma("sp", [TAr[0:64, :, :, :].rearrange("p h r q -> p (h r q)"), TAr[64:128, :, 0:14, :].rearrange("p h r q -> p h (r q)")],
                          [rpbx[li].rearrange("p h r q -> p (h r q)"), rpbx[li][:, :, 1:15, :].rearrange("p h r q -> p h (r q)")], out_buf=TAr)
                    K.dma("sp", okm[:], okm_in[:, :], out_buf=okm)
                    K.op("act", lambda e: e.activation(out=TAr[:], in_=TAr[:], func=AF.Exp), reads=[TAr], writes=[TAr])
                    for h in range(4):
                        for r_ in range(15):
                            K.op("pool", lambda e: e.tensor_tensor(TA[:, h, r_, :], TAr[:, h, r_, :], okm[:], op=ALU.mult), reads=[TAr, okm], writes=[TA])
                    K.barrier()
                psS = Ring([K.ps(st, [128, 512], F32, "psS") for _ in range(3)])
                po = Ring([K.ps(st, [128, 512], F32, "po") for _ in range(3)])
                P6r = Ring([K.sb(st, [128, 6, 4, 64], BF16, "P6") for _ in range(3)])
                Pf = Ring([K.sb(st, [128, 4, 64], F32, "Pf") for _ in range(3)])
                yrow = Ring([K.sb(st, [64, 256], BF16, "yrow", dma=True) for _ in range(3)])
                ysb = Ring([K.sb(st, [128, 256], BF16, "ysb", dma=True) for _ in range(2)])
                Pc = Ring([K.sb(st, [128, 128], BF16, "Pc") for _ in range(3)])
                dn_ = Ring([K.sb(st, [128, 2], F32, "dn") for _ in range(4)])
                mi = [0]
                if need_ctx:
                    for qt in range(2):
                        yb = ysb.next()
                        for h in range(4):
                            pq = po.next()
                            for kt in range(2):
                                ps = psS.next()
                                mm(K, ps, ps[:, 0:128], kT, kT[:, h, kt * 128:(kt + 1) * 128], qT, qT[:, h, qt * 128:(qt + 1) * 128])
                                P = Pc.next()
                                K.op("act", lambda e: e.activation(out=P[:], in_=ps[:, 0:128], func=AF.Exp, scale=0.125), reads=[ps], writes=[P])
                                mm(K, pq, pq[:, 0:65], P, P[:], Va, Va[:, kt, h, :], start=(kt == 0), stop=(kt == 1))
                            d_ = dn_.next()
                            K.op("dve", lambda e: e.reciprocal(d_[:, 1:2], pq[:, 64:65]), reads=[pq], writes=[d_])
                            K.op("dve", lambda e: e.tensor_scalar(yb[:, h * 64:(h + 1) * 64], pq[:, 0:64], d_[:, 1:2], None, op0=ALU.mult),
                                 reads=[pq, d_], writes=[yb])
                        K.dma("sp", ycat[qt * 128:(qt + 1) * 128, 768:1024], yb[:], in_buf=yb)
                def s_stage(r):
                    rs = min(max(r - 4, 0), 56)
                    dlt = r - rs
                    q0 = L + r * 64
                    P6 = P6r.next()
                    tiles = []
                    for kt in range(4):
                        k0 = L + rs * 64 + kt * 128
                        vt = (Va, 2 + rs // 2 + kt) if rs % 2 == 0 else (Vs, (rs - 1) // 2 + kt)
                        tiles.append((k0, vt, 2 * kt - dlt + 7))
                    tiles.append((0, (Va, 0), None))
                    tiles.append((128, (Va, 1), None))
                    for ti, (k0, vt, dr0) in enumerate(tiles):
                        ps = psS.next()
                        for h in range(4):
                            mm(K, ps, ps[:, h * 64:(h + 1) * 64], kT, kT[:, h, k0:k0 + 128], qT, qT[:, h, q0:q0 + 64])
                        psv = ps[:, 0:256].rearrange("p (h q) -> p h q", h=4)
                        if dr0 is None:
                            K.op("act", lambda e: e.activation(out=P6[:, ti, :, :], in_=psv, func=AF.Exp, scale=0.125), reads=[ps], writes=[P6])
                        else:
                            pf = Pf.next()
                            K.op("act", lambda e: e.activation(out=pf[:], in_=psv, func=AF.Exp, scale=0.125), reads=[ps], writes=[pf])
                            mi[0] += 1
                            e_ = "dve" if mi[0] % 2 else "pool"
                            K.op(e_, lambda e: e.tensor_tensor(P6[:, ti, :, :], pf[:], TA[:, :, dr0, :], op=ALU.mult), reads=[pf, TA], writes=[P6])
                    return (P6, tiles, q0)

                def pv_stage(P6, tiles, q0):
                    yb = yrow.next()
                    for h in range(4):
                        pq = po.next()
                        for ti, (k0, vt, dr0) in enumerate(tiles):
                            Vb, vj = vt
                            mm(K, pq, pq[0:64, 0:65], P6, P6[:, ti, h, :], Vb, Vb[:, vj, h, :], start=(ti == 0), stop=(ti == 5))
                        d_ = dn_.next()
                        K.op("dve", lambda e: e.reciprocal(d_[0:64, 1:2], pq[0:64, 64:65]), reads=[pq], writes=[d_])
                        K.op("dve", lambda e: e.tensor_scalar(yb[:, h * 64:(h + 1) * 64], pq[0:64, 0:64], d_[0:64, 1:2], None, op0=ALU.mult),
                             reads=[pq, d_], writes=[yb])
                    K.dma("sp", ycat[q0:q0 + 64, 768:1024], yb[:], in_buf=yb)

                prev = None
                for r in range(64):
                    cur = s_stage(r)
                    if prev is not None:
                        pv_stage(*prev)
                    prev = cur
                pv_stage(*prev)
                K.barrier()

        def phase_outproj(li, with_ctx):
            G = modG[1]
            with ExitStack() as st:
                Wo = K.sb(st, [128, 8, D], BF16, "Wo")
                with ExitStack() as st2:
                    stg = [K.sb(st2, [128, D], F32, "stg", dma=True) for _ in range(2)]
                    for k in range(8):
                        sg_ = stg[k % 2]
                        K.dma("sp", sg_[:], w_out[li, k * 128:(k + 1) * 128, :], out_buf=sg_)
                        if k % 2 == 0:
                            K.op("dve", lambda e: e.tensor_copy(Wo[:, k, :], sg_[:]), reads=[sg_], writes=[Wo])
                        else:
                            K.op("act", lambda e: e.copy(Wo[:, k, :], sg_[:]), reads=[sg_], writes=[Wo])
                    K.barrier()
                xg = [K.sb(st, [128, 8, 512], F32, "xg", dma=True) for _ in range(2)]
                yt = Ring([K.sb(st, [128, D], BF16, "yt", dma=True) for _ in range(3)])
                gN = K.sb(st, [128, 64], F32, "gN", dma=True)
                K.dma("sp", gN[:], dn_norm_g[li:li + 1, :].partition_broadcast(128), out_buf=gN)
                of_ = Ring([K.sb(st, [128, 2, 256], F32, "of", dma=True) for _ in range(3)])
                zr = Ring([K.sb(st, [128, 256], F32, "z", dma=True) for _ in range(3)])
                osum = Ring([K.sb(st, [128, 256], F32, "osum") for _ in range(2)])
                sqd = Ring([K.sb(st, [128, 256], F32, "sqd") for _ in range(2)])
                scd = Ring([K.sb(st, [128, 8], F32, "scd") for _ in range(2)])
                yTs = [K.sb(st, [128, 8, 512], BF16, "yT") for _ in range(2)]
                ptb = Ring([K.ps(st, [128, 4, 128], BF16, "ptb") for _ in range(4)])
                po = Ring([K.ps(st, [128, 512], F32, "po") for _ in range(3)])
                xTv = xT.rearrange("(k p) t -> p k t", p=128)
                grps = groups_all(with_ctx)
                ev = [0]
                def stageA(gi):
                    t0, n, v = grps[gi]
                    xb = xg[gi % 2]
                    yT = yTs[gi % 2]
                    K.dma("sp", [xb[:, 0:4, 0:n], xb[:, 4:8, 0:n]], [xTv[:, 0:4, t0:t0 + n], xTv[:, 4:8, t0:t0 + n]], out_buf=xb)
                    for tt in range(n // 128):
                        y_ = yt.next()
                        r0_ = t0 + tt * 128
                        K.dma("sp", [y_[:, 0:256], y_[:, 512:1024]], [ycat[r0_:r0_ + 128, 0:256], ycat[r0_:r0_ + 128, 512:1024]], out_buf=y_)
                        o_, z_, os_, sq_, sc_ = of_.next(), zr.next(), osum.next(), sqd.next(), scd.next()
                        K.dma("sp", [o_[:, 0, :], o_[:, 1, :]], [dn_o[0, r0_:r0_ + 128, :], dn_o[1, r0_:r0_ + 128, :]], out_buf=o_)
                        K.dma("sp", z_[:], dn_zab[r0_:r0_ + 128, 0:256], out_buf=z_)
                        K.op("pool", lambda e: e.tensor_tensor(os_[:], o_[:, 0, :], o_[:, 1, :], op=ALU.add), reads=[o_], writes=[os_])
                        K.op("pool", lambda e: e.tensor_tensor(sq_[:], os_[:], os_[:], op=ALU.mult), reads=[os_], writes=[sq_])
                        K.op("dve", lambda e: e.tensor_reduce(out=sc_[:, 0:4], in_=sq_[:].rearrange("p (a b) -> p a b", b=64), axis=AX.X, op=ALU.add), reads=[sq_], writes=[sc_])
                        K.op("act", lambda e: e.activation(out=sc_[:, 4:8], in_=sc_[:, 0:4], func=AF.Sqrt, scale=1.0 / 64.0, bias=float(EPS)), reads=[sc_], writes=[sc_])
                        K.op("dve", lambda e: e.reciprocal(sc_[:, 4:8], sc_[:, 4:8]), reads=[sc_], writes=[sc_])
                        K.op("act", lambda e: e.activation(out=z_[:], in_=z_[:], func=AF.Silu), reads=[z_], writes=[z_])
                        for h in range(4):
                            K.op("dve", lambda e: e.scalar_tensor_tensor(out=os_[:, h * 64:(h + 1) * 64], in0=os_[:, h * 64:(h + 1) * 64], scalar=sc_[:, 4 + h:5 + h],
                                                                          in1=gN[:], op0=ALU.mult, op1=ALU.mult), reads=[os_, sc_, gN], writes=[os_])
                        K.op("pool", lambda e: e.tensor_tensor(y_[:, 256:512], os_[:], z_[:], op=ALU.mult), reads=[os_, z_], writes=[y_])
                        for hh in range(2):
                            p = ptb.next()
                            for kk in range(4):
                                k = hh * 4 + kk
                                tr(K, p, p[:, kk, :], y_, y_[:, k * 128:(k + 1) * 128], identb, identb[:])
                            ev[0] += 1
                            if ev[0] % 2:
                                K.op("dve", lambda e: e.tensor_copy(yT[:, hh * 4:(hh + 1) * 4, tt * 128:(tt + 1) * 128], p[:]), reads=[p], writes=[yT])
                            else:
                                K.op("act", lambda e: e.copy(yT[:, hh * 4:(hh + 1) * 4, tt * 128:(tt + 1) * 128], p[:]), reads=[p], writes=[yT])

                def stageB(gi):
                    t0, n, v = grps[gi]
                    xb = xg[gi % 2]
                    yT = yTs[gi % 2]
                    for f in range(8):
                        pf_ = po.next()
                        for k in range(8):
                            mm(K, pf_, pf_[:, 0:n], Wo, Wo[:, k, f * 128:(f + 1) * 128], yT, yT[:, k, 0:n], start=(k == 0), stop=(k == 7))
                            K.pump(2)
                        K.op("dve", lambda e: e.scalar_tensor_tensor(out=xb[:, f, 0:n], in0=pf_[:, 0:n], scalar=G[:, f, v:v + 1],
                                                                      in1=xb[:, f, 0:n], op0=ALU.mult, op1=ALU.add),
                             reads=[pf_, G, xb], writes=[xb])
                    K.dma("sp", [xTv[:, 0:4, t0:t0 + n], xTv[:, 4:8, t0:t0 + n]], [xb[:, 0:4, 0:n], xb[:, 4:8, 0:n]], in_buf=xb)

                stageA(0)
                for gi in range(len(grps)):
                    if gi + 1 < len(grps):
                        K.defer = K.pending
                        stageA(gi + 1)
                        K.defer = None
                    stageB(gi)
                    K.pump(10 ** 6)
                K.barrier()


        def phase_dn1(li):
            with ExitStack() as st:
                cwr = K.sb(st, [1, 3, 768], F32, "cwr", dma=True)
                pcw = K.ps(st, [128, 6, 3], F32, "pcw")
                cw = K.sb(st, [128, 6, 3], F32, "cw")
                K.dma("sp", cwr[0:1, :, :], dn_conv_w[li:li + 1, :, :], out_buf=cwr)
                for c in range(6):
                    for j in range(3):
                        mm(K, pcw, pcw[:, c, j:j + 1], cwr, cwr[0:1, j, c * 128:(c + 1) * 128], ones_f, ones_f[0:1, 0:1])
                K.op("dve", lambda e: e.tensor_copy(cw[:], pcw[:]), reads=[pcw], writes=[cw])
                xin = [K.sb(st, [128, T], F32, "xin", dma=True) for _ in range(2)]
                hc = [K.sb(st, [128, T], F32, "hc", dma=True) for _ in range(2)]
                otm = Ring([K.sb(st, [128, 4, 128], F32, "otm", dma=True) for _ in range(3)])
                pt = Ring([K.ps(st, [128, 512], F32, "pt") for _ in range(4)])
                ev = [0]
                for c in range(6):
                    x_ = xin[c % 2]
                    h_ = hc[c % 2]
                    K.dma("sp", x_[:], dn_qkvT[c * 128:(c + 1) * 128, :], out_buf=x_)
                    K.op("dve", lambda e: e.tensor_scalar(h_[:], x_[:], cw[:, c, 1:2], None, op0=ALU.mult), reads=[x_, cw], writes=[h_])
                    for (a, b) in ((0, L), (L, T)):
                        K.op("dve", lambda e: e.scalar_tensor_tensor(out=h_[:, a + 1:b], in0=x_[:, a:b - 1], scalar=cw[:, c, 0:1], in1=h_[:, a + 1:b],
                                                                      op0=ALU.mult, op1=ALU.add), reads=[x_, cw, h_], writes=[h_])
                        K.op("dve", lambda e: e.scalar_tensor_tensor(out=h_[:, a:b - 1], in0=x_[:, a + 1:b], scalar=cw[:, c, 2:3], in1=h_[:, a:b - 1],
                                                                      op0=ALU.mult, op1=ALU.add), reads=[x_, cw, h_], writes=[h_])
                    K.op("act", lambda e: e.activation(out=h_[:], in_=h_[:], func=AF.Silu), reads=[h_], writes=[h_])
                    K.dma("sp", dn_hT[c * 128:(c + 1) * 128, :], h_[:], in_buf=h_)
                    for j0 in range(0, 34, 4):
                        nj = min(4, 34 - j0)
                        p = pt.next()
                        for jj in range(nj):
                            tr(K, p, p[:, jj * 128:(jj + 1) * 128], h_, h_[:, (j0 + jj) * 128:(j0 + jj + 1) * 128], ident, ident[:])
                        o = otm.next()
                        ev[0] += 1
                        pv = p[:, 0:nj * 128].rearrange("p (j c) -> p j c", c=128)
                        if ev[0] % 2:
                            K.op("act", lambda e: e.copy(o[:, 0:nj, :], pv), reads=[p], writes=[o])
                        else:
                            K.op("pool", lambda e: e.tensor_copy(o[:, 0:nj, :], pv), reads=[p], writes=[o]) if False else \
                                K.op("dve", lambda e: e.tensor_copy(o[:, 0:nj, :], pv), reads=[p], writes=[o])
                        K.dma("sp", dn_h[j0 * 128:(j0 + nj) * 128, c * 128:(c + 1) * 128].rearrange("(j p) c -> p j c", p=128), o[:, 0:nj, :], in_buf=o)
                K.barrier()

        def phase_dn2(li):
            with ExitStack() as st:
                tri = K.sb(st, [128, 4, 128], F32, "tri", dma=True)
                K.dma("sp", tri[:], tri_in.rearrange("w p q -> p w q"), out_buf=tri)
                TINC = [tri[:, 0, :], tri[:, 2, :]]
                MST = [tri[:, 1, :], tri[:, 3, :]]
                onesf = K.sb(st, [128, 128], F32, "onesf")
                K.op("pool", lambda e: e.memset(onesf[:], 1.0), writes=[onesf])
                dtb = K.sb(st, [128, 8], F32, "dtb", dma=True)
                nA = K.sb(st, [128, 8], F32, "nA", dma=True)
                K.dma("sp", dtb[:], dn_dt_bias[li:li + 1, :].partition_broadcast(128), out_buf=dtb)
                K.dma("sp", nA[:], dn_a_log[li:li + 1, :].partition_broadcast(128), out_buf=nA)
                K.op("act", lambda e: e.activation(out=nA[:], in_=nA[:], func=AF.Exp), reads=[nA], writes=[nA])
                K.op("dve", lambda e: e.tensor_scalar(nA[:], nA[:], -1.0, None, op0=ALU.mult), reads=[nA], writes=[nA])
                Sst = [[[K.sb(st, [64, 64], F32, "S") for _ in range(2)] for _ in range(4)] for _ in range(2)]
                for d in range(2):
                    for h in range(4):
                        K.op("pool", lambda e: e.memset(Sst[d][h][0][:], 0.0), writes=[Sst[d][h][0]])
                NSLOT = 2
                hq = Ring([K.sb(st, [64, 4, 128], F32, "hq", dma=True) for _ in range(2 * NSLOT)])
                hk = Ring([K.sb(st, [64, 4, 128], F32, "hk", dma=True) for _ in range(2 * NSLOT)])
                htok = Ring([K.sb(st, [128, 768], F32, "htok", dma=True) for _ in range(2 * NSLOT)])
                abr = Ring([K.sb(st, [128, 16], F32, "ab", dma=True) for _ in range(2 * NSLOT)])
                scr = Ring([K.sb(st, [128, 96], F32, "sc") for _ in range(2 * NSLOT)])
                sqb = Ring([K.sb(st, [128, 512], F32, "sq") for _ in range(2)])
                otile = Ring([K.sb(st, [128, 256], F32, "ot", dma=True) for _ in range(2 * NSLOT)])
                names128 = ["gsm", "Dsm", "DTim", "Pa", "Pb", "Pta", "Ptb", "Tt", "QKm"]
                names64 = ["Xu", "Xw", "u", "Ke", "vn", "tmp"]
                W = {}
                for sl in range(NSLOT):
                    for d in range(2):
                        for h in range(4):
                            w_ = {n_: K.sb(st, [128, 128], F32, n_) for n_ in names128}
                            w_.update({n_: K.sb(st, [128, 64], F32, n_) for n_ in names64})
                            w_["wT"] = K.sb(st, [64, 128], F32, "wT")
                            W[(sl, d, h)] = w_
                pp = Ring([K.ps(st, [128, 512], F32, "ppb") for _ in range(7)])
                pg = Ring([K.ps(st, [128, 512], F32, "pgb")])
                order = [list(range(34)), [1, 0] + list(range(33, 1, -1))]
                hTv = dn_hT.rearrange("(g h d) t -> g d h t", g=3, d=64)
                ei = [0]

                def evac(dst, dst_ap, p, p_ap):
                    ei[0] += 1
                    if ei[0] % 3 == 0:
                        K.op("dve", lambda e: e.tensor_copy(dst_ap, p_ap), reads=[p], writes=[dst])
                    else:
                        K.op("act", lambda e: e.copy(dst_ap, p_ap), reads=[p], writes=[dst])

                import os as _os
                NS_ = int(_os.environ.get('DN_STEPS', '34'))

                def prepA(s_):
                    return [prepA1(s_, d) for d in range(2)]

                def prepA1(s_, d):
                    if True:
                        j = order[d][s_]
                        HQ, HK, HT, AB, sc, sq = hq.next(), hk.next(), htok.next(), abr.next(), scr.next(), sqb.next()
                        K.dma("sp", HQ[:], hTv[0, :, :, j * 128:(j + 1) * 128], out_buf=HQ)
                        K.dma("sp", HK[:], hTv[1, :, :, j * 128:(j + 1) * 128], out_buf=HK)
                        K.dma("sp", HT[:], dn_h[j * 128:(j + 1) * 128, :], out_buf=HT)
                        K.dma("sp", AB[:], dn_zab[j * 128:(j + 1) * 128, 256:272], out_buf=AB)
                        K.op("dve", lambda e: e.tensor_tensor(sq[:], HT[:, 0:512], HT[:, 0:512], op=ALU.mult), reads=[HT], writes=[sq])
                        K.op("dve", lambda e: e.tensor_reduce(out=sc[:, 0:8], in_=sq[:].rearrange("p (a b) -> p a b", b=64), axis=AX.X, op=ALU.add),
                             reads=[sq], writes=[sc])
                        K.op("act", lambda e: e.activation(out=sc[:, 8:16], in_=sc[:, 0:8], func=AF.Ln, bias=float(EPS), scale=1.0), reads=[sc], writes=[sc])
                        K.op("act", lambda e: e.activation(out=sc[:, 16:24], in_=sc[:, 8:16], func=AF.Exp, scale=-0.5), reads=[sc], writes=[sc])
                        K.op("act", lambda e: e.activation(out=sc[:, 24:32], in_=sc[:, 8:16], func=AF.Exp, scale=0.5), reads=[sc], writes=[sc])
                        K.op("dve", lambda e: e.tensor_scalar(sc[:, 16:20], sc[:, 16:20], 0.125, None, op0=ALU.mult), reads=[sc], writes=[sc])
                        K.op("dve", lambda e: e.tensor_tensor(sc[:, 32:36], AB[:, d * 4:d * 4 + 4], dtb[:, d * 4:d * 4 + 4], op=ALU.add), reads=[AB, dtb], writes=[sc])
                        K.op("act", lambda e: e.activation(out=sc[:, 32:36], in_=sc[:, 32:36], func=AF.Exp), reads=[sc], writes=[sc])
                        K.op("act", lambda e: e.activation(out=sc[:, 32:36], in_=sc[:, 32:36], func=AF.Ln, bias=1.0, scale=1.0), reads=[sc], writes=[sc])
                        K.op("dve", lambda e: e.tensor_tensor(sc[:, 36:40], sc[:, 32:36], nA[:, d * 4:d * 4 + 4], op=ALU.mult), reads=[sc, nA], writes=[sc])
                        K.op("act", lambda e: e.activation(out=sc[:, 40:44], in_=AB[:, 8 + d * 4:12 + d * 4], func=AF.Exp, scale=-1.0), reads=[AB], writes=[sc])
                        K.op("dve", lambda e: e.tensor_scalar(sc[:, 40:44], sc[:, 40:44], 1.0, None, op0=ALU.add), reads=[sc], writes=[sc])
                        K.op("dve", lambda e: e.reciprocal(sc[:, 40:44], sc[:, 40:44]), reads=[sc], writes=[sc])
                        return dict(d=d, j=j, HQ=HQ, HK=HK, HT=HT, AB=AB, sc=sc)

                def prepB(ctxs, s_):
                    U = []
                    for c_ in ctxs:
                        prepB1(c_, s_, U)
                    return U

                def mulcols(sc, o0, a0, b0):
                    K.op("dve", lambda e: e.tensor_tensor(sc[:, o0:o0 + 4], sc[:, a0:a0 + 4], sc[:, b0:b0 + 4], op=ALU.mult), reads=[sc], writes=[sc])

                def prepB1(c_, s_, U):
                    sl = s_ % NSLOT
                    if True:
                        d, j, HQ, HK, HT, AB, sc = c_['d'], c_['j'], c_['HQ'], c_['HK'], c_['HT'], c_['AB'], c_['sc']
                        pg_ = pg.next()
                        mm(K, pg_, pg_[:, 0:4], tri, TINC[d], sc, sc[:, 36:40])
                        mm(K, pg_, pg_[:, 4:8], onesf, onesf[:], sc, sc[:, 36:40])
                        K.op("dve", lambda e: e.tensor_copy(sc[:, 44:52], pg_[:, 0:8]), reads=[pg_], writes=[sc])
                        K.op("act", lambda e: e.activation(out=sc[:, 52:56], in_=sc[:, 44:48], func=AF.Exp), reads=[sc], writes=[sc])
                        K.op("dve", lambda e: e.tensor_tensor(sc[:, 56:60], sc[:, 48:52], sc[:, 44:48], op=ALU.subtract), reads=[sc], writes=[sc])
                        K.op("act", lambda e: e.activation(out=sc[:, 56:60], in_=sc[:, 56:60], func=AF.Exp), reads=[sc], writes=[sc])
                        K.op("act", lambda e: e.activation(out=sc[:, 60:64], in_=sc[:, 48:52], func=AF.Exp), reads=[sc], writes=[sc])
                        for (o0, a0, b0) in ((68, 20, 40), (64, 68, 20), (72, 64, 52), (76, 20, 56), (80, 52, 16)):
                            mulcols(sc, o0, a0, b0)
                        K.op("dve", lambda e: e.tensor_scalar(sc[:, 84:88], sc[:, 28:32], -1.0, None, op0=ALU.mult), reads=[sc], writes=[sc])
                        OT = otile.next()
                        for h in range(4):
                            U.append(dict(d=d, h=h, j=j, HQ=HQ, HK=HK, HT=HT, sc=sc, W=W[(sl, d, h)], OT=OT,
                                          S0=Sst[d][h][s_ % 2], S1=Sst[d][h][(s_ + 1) % 2]))

                U_next = prepB(prepA(0), 0)
                _pp_next = pp.next

                def _pp_pump():
                    K.pump(1)
                    return _pp_next()
                pp.next = _pp_pump
                for s_ in range(NS_):
                    sl = s_ % NSLOT
                    U = U_next
                    ctxA = None
                    if s_ + 1 < NS_:
                        K.defer = K.pending
                        U_next = prepB(prepA(s_ + 1), s_ + 1)
                        K.defer = None

                    def col(u, c0):
                        return u["sc"][:, c0 + u["h"]:c0 + u["h"] + 1]

                    _stg = int(_os.environ.get('DN_STAGE', '99'))
                    if _stg >= 1:
                        for u in U:
                            w_ = u["W"]
                            K.op("dve", lambda e: e.tensor_scalar(w_["gsm"][:], MST[u["d"]], col(u, 36), None, op0=ALU.mult), reads=[tri, u["sc"]], writes=[w_["gsm"]])
                    if _stg >= 2:
                        for u in U:
                            w_ = u["W"]
                            pb_ = pp.next()
                            p1, p2 = pb_, pb_
                            mm(K, p1, p1[:, 0:128], tri, TINC[u["d"]], w_["gsm"], w_["gsm"][:])
                            mm(K, p2, p2[:, 128:256], w_["gsm"], w_["gsm"][:], tri, TINC[u["d"]])
                            K.op("act", lambda e: e.activation(out=w_["Dsm"][:], in_=p1[:, 0:128], func=AF.Exp), reads=[p1], writes=[w_["Dsm"]])
                            K.op("act", lambda e: e.activation(out=w_["DTim"][:], in_=p2[:, 128:256], func=AF.Exp), reads=[p2], writes=[w_["DTim"]])
                            K.op("pool", lambda e: e.tensor_tensor(w_["Dsm"][:], w_["Dsm"][:], MST[u["d"]], op=ALU.mult), reads=[w_["Dsm"], tri], writes=[w_["Dsm"]])
                            K.op("pool", lambda e: e.tensor_tensor(w_["DTim"][:], w_["DTim"][:], TINC[u["d"]], op=ALU.mult), reads=[w_["DTim"], tri], writes=[w_["DTim"]])
                    if _stg >= 3:
                        for u in U:
                            w_ = u["W"]
                            h = u["h"]
                            pb_ = pp.next()
                            p1, p2 = pb_, pb_
                            mm(K, p1, p1[:, 0:128], u["HK"], u["HK"][:, h, :], u["HK"], u["HK"][:, h, :])
                            mm(K, p2, p2[:, 128:256], u["HK"], u["HK"][:, h, :], u["HQ"], u["HQ"][:, h, :])
                            K.op("dve", lambda e: e.scalar_tensor_tensor(out=w_["Pa"][:], in0=p1[:, 0:128], scalar=col(u, 64), in1=w_["Dsm"][:], op0=ALU.mult, op1=ALU.mult),
                                 reads=[p1, u["sc"], w_["Dsm"]], writes=[w_["Pa"]])
                            K.op("dve", lambda e: e.scalar_tensor_tensor(out=w_["QKm"][:], in0=p2[:, 128:256], scalar=col(u, 20), in1=w_["DTim"][:], op0=ALU.mult, op1=ALU.mult),
                                 reads=[p2, u["sc"], w_["DTim"]], writes=[w_["QKm"]])
                    if _stg >= 4:
                        for u in U:
                            w_ = u["W"]
                            p1 = pp.next()
                            tr(K, p1, p1[:, 0:128], w_["Pa"], w_["Pa"][:], ident, ident[:])
                            evac(w_["Pta"], w_["Pta"][:], p1, p1[:, 0:128])
                            K.op("pool", lambda e: e.tensor_tensor(w_["Tt"][:], ident[:], w_["Pta"][:], op=ALU.subtract), reads=[ident, w_["Pta"]], writes=[w_["Tt"]])
                            u["P"], u["Pt"], u["Pn"], u["Ptn"] = w_["Pa"], w_["Pta"], w_["Pb"], w_["Ptb"]
                    if _stg >= 5:
                        for lev in range(1, 7):
                            for u in U:
                                pb_ = pp.next()
                                mm(K, pb_, pb_[:, 0:128], u["Pt"], u["Pt"][:], u["P"], u["P"][:])
                                if lev < 6:
                                    mm(K, pb_, pb_[:, 128:256], u["P"], u["P"][:], u["Pt"], u["Pt"][:])
                                K.op("act", lambda e: e.copy(u["Pn"][:], pb_[:, 0:128]), reads=[pb_], writes=[u["Pn"]])
                                if lev < 6:
                                    K.op("act", lambda e: e.copy(u["Ptn"][:], pb_[:, 128:256]), reads=[pb_], writes=[u["Ptn"]])
                            for g0 in range(0, len(U), 4):
                                pb_ = pp.next()
                                for gi_, u in enumerate(U[g0:g0 + 4]):
                                    w_ = u["W"]
                                    mm(K, pb_, pb_[:, gi_ * 128:(gi_ + 1) * 128], u["Pn"], u["Pn"][:], w_["Tt"], w_["Tt"][:])
                                for gi_, u in enumerate(U[g0:g0 + 4]):
                                    w_ = u["W"]
                                    K.op("dve", lambda e: e.tensor_tensor(w_["Tt"][:], w_["Tt"][:], pb_[:, gi_ * 128:(gi_ + 1) * 128], op=ALU.add), reads=[w_["Tt"], pb_], writes=[w_["Tt"]])
                                    u["P"], u["Pn"] = u["Pn"], u["P"]
                                    u["Pt"], u["Ptn"] = u["Ptn"], u["Pt"]
                    if _stg >= 6:
                        for u in U:
                            w_ = u["W"]
                            h = u["h"]
                            HT = u["HT"]
                            _m8 = int(_os.environ.get('DN_S8', '31'))
                            if _m8 & 1:
                                K.op("pool", lambda e: e.tensor_scalar(w_["Xu"][:], HT[:, 512 + h * 64:512 + (h + 1) * 64], col(u, 68), None, op0=ALU.mult), reads=[HT, u["sc"]], writes=[w_["Xu"]])
                                K.op("pool", lambda e: e.tensor_scalar(w_["Xw"][:], HT[:, 256 + h * 64:256 + (h + 1) * 64], col(u, 72), None, op0=ALU.mult), reads=[HT, u["sc"]], writes=[w_["Xw"]])
                                K.op("pool", lambda e: e.tensor_scalar(w_["Ke"][:], HT[:, 256 + h * 64:256 + (h + 1) * 64], col(u, 76), None, op0=ALU.mult), reads=[HT, u["sc"]], writes=[w_["Ke"]])
                            pb_ = pp.next()
                            p1, p2 = pb_, pb_
                            if _m8 & 2:
                                mm(K, p1, p1[:, 0:64], w_["Tt"], w_["Tt"][:], w_["Xu"], w_["Xu"][:])
                            if _m8 & 4:
                                mm(K, p2, p2[0:64, 128:256], w_["Xw"], w_["Xw"][:], w_["Tt"], w_["Tt"][:])
                            if _m8 & 8:
                                K.op("act", lambda e: e.activation(out=w_["u"][:], in_=p1[:, 0:64], func=AF.Identity, scale=col(u, 28)), reads=[p1, u["sc"]], writes=[w_["u"]])
                            if _m8 & 16:
                                K.op("act", lambda e: e.copy(w_["wT"][:], p2[0:64, 128:256]), reads=[p2], writes=[w_["wT"]])
                    if _stg >= 7:
                        for u in U:
                            w_ = u["W"]
                            p1 = pp.next()
                            mm(K, p1, p1[:, 0:64], w_["wT"], w_["wT"][:], u["S0"], u["S0"][:])
                            K.op("dve", lambda e: e.scalar_tensor_tensor(out=w_["vn"][:], in0=p1[:, 0:64], scalar=col(u, 84), in1=w_["u"][:], op0=ALU.mult, op1=ALU.add),
                                 reads=[p1, u["sc"], w_["u"]], writes=[w_["vn"]])
                        for u in U:
                            w_ = u["W"]
                            h = u["h"]
                            pb_ = pp.next()
                            p1, p2, p3 = pp.next(), pb_, pb_
                            mm(K, p1, p1[:, 0:64], u["HQ"], u["HQ"][:, h, :], u["S0"], u["S0"][:])
                            mm(K, p2, p2[:, 128:192], w_["QKm"], w_["QKm"][:], w_["vn"], w_["vn"][:])
                            mm(K, p3, p3[0:64, 256:320], w_["Ke"], w_["Ke"][:], w_["vn"], w_["vn"][:])
                            K.op("act", lambda e: e.activation(out=w_["tmp"][:], in_=p1[:, 0:64], func=AF.Identity, scale=col(u, 80)), reads=[p1, u["sc"]], writes=[w_["tmp"]])
                            K.op("dve", lambda e: e.scalar_tensor_tensor(out=u["OT"][:, h * 64:(h + 1) * 64], in0=p2[:, 128:192], scalar=col(u, 16), in1=w_["tmp"][:], op0=ALU.mult, op1=ALU.add),
                                 reads=[p2, u["sc"], w_["tmp"]], writes=[u["OT"]])
                            K.op("dve", lambda e: e.scalar_tensor_tensor(out=u["S1"][:], in0=u["S0"][:], scalar=u["sc"][0:64, 60 + h:61 + h], in1=p3[0:64, 256:320], op0=ALU.mult, op1=ALU.add),
                                 reads=[u["S0"], u["sc"], p3], writes=[u["S1"]])
                    for u in U:
                        if u["h"] == 3:
                            j = u["j"]
                            K.dma("sp", dn_o[u["d"], j * 128:(j + 1) * 128, :], u["OT"][:], in_buf=u["OT"])
                    K.pump(10 ** 6)
                K.barrier()

        def phase_dn3(li, need_ctx):
            with ExitStack() as st:
                gN = K.sb(st, [128, 64], F32, "gN", dma=True)
                K.dma("sp", gN[:], dn_norm_g[li:li + 1, :].partition_broadcast(128), out_buf=gN)
                of_ = Ring([K.sb(st, [128, 2, 256], F32, "of", dma=True) for _ in range(2)])
                zr = Ring([K.sb(st, [128, 256], F32, "z", dma=True) for _ in range(2)])
                osum = K.sb(st, [128, 256], F32, "osum")
                sq = K.sb(st, [128, 256], F32, "sq")
                sc = K.sb(st, [128, 8], F32, "sc")
                yb = Ring([K.sb(st, [128, 256], BF16, "yb", dma=True) for _ in range(2)])
                for j in range(0 if need_ctx else 2, 34):
                    o_, z_, y_ = of_.next(), zr.next(), yb.next()
                    K.dma("sp", [o_[:, 0, :], o_[:, 1, :]], [dn_o[0, j * 128:(j + 1) * 128, :], dn_o[1, j * 128:(j + 1) * 128, :]], out_buf=o_)
                    K.dma("sp", z_[:], dn_zab[j * 128:(j + 1) * 128, 0:256], out_buf=z_)
                    K.op("dve", lambda e: e.tensor_tensor(osum[:], o_[:, 0, :], o_[:, 1, :], op=ALU.add), reads=[o_], writes=[osum])
                    K.op("pool", lambda e: e.tensor_tensor(sq[:], osum[:], osum[:], op=ALU.mult), reads=[osum], writes=[sq])
                    K.op("dve", lambda e: e.tensor_reduce(out=sc[:, 0:4], in_=sq[:].rearrange("p (a b) -> p a b", b=64), axis=AX.X, op=ALU.add), reads=[sq], writes=[sc])
                    K.op("act", lambda e: e.activation(out=sc[:, 4:8], in_=sc[:, 0:4], func=AF.Sqrt, scale=1.0 / 64.0, bias=float(EPS)), reads=[sc], writes=[sc])
                    K.op("dve", lambda e: e.reciprocal(sc[:, 4:8], sc[:, 4:8]), reads=[sc], writes=[sc])
                    K.op("act", lambda e: e.activation(out=z_[:], in_=z_[:], func=AF.Silu), reads=[z_], writes=[z_])
                    for h in range(4):
                        K.op("dve", lambda e: e.scalar_tensor_tensor(out=osum[:, h * 64:(h + 1) * 64], in0=osum[:, h * 64:(h + 1) * 64], scalar=sc[:, 4 + h:5 + h],
                                                                      in1=gN[:], op0=ALU.mult, op1=ALU.mult), reads=[osum, sc, gN], writes=[osum])
                    K.op("dve", lambda e: e.tensor_tensor(y_[:], osum[:], z_[:], op=ALU.mult), reads=[osum, z_], writes=[y_])
                    K.dma("sp", ycat[j * 128:(j + 1) * 128, 256:512], y_[:], in_buf=y_)
                K.barrier()

        def phase_final():
            with ExitStack() as st:
                xg = [K.sb(st, [128, 8, 512], F32, "xg", dma=True) for _ in range(2)]
                xn = K.sb(st, [128, 8, 512], BF16, "xn")
                rstd = K.sb(st, [128, 512], F32, "rstd")
                fg = K.sb(st, [128, 8], F32, "fg")
                yo = [K.sb(st, [128, D], F32, "yo", dma=True) for _ in range(2)]
                pss = K.ps(st, [128, 512], F32, "pss")
                pt = [K.ps(st, [128, 512], F32, "pt") for _ in range(4)]
                xTv = xT.rearrange("(k p) t -> p k t", p=128)
                K.op("dve", lambda e: e.tensor_scalar(fg[:], fin_g[:], float(np.sqrt(D)), None, op0=ALU.mult), reads=[fin_g], writes=[fg])
                grps = groups_all(False)
                cnt = 0
                def ldf(gi):
                    t0_, n_, v_ = grps[gi]
                    xb_ = xg[gi % 2]
                    K.dma("sp", [xb_[:, 0:4, :], xb_[:, 4:8, :]], [xTv[:, 0:4, t0_:t0_ + n_], xTv[:, 4:8, t0_:t0_ + n_]], out_buf=xb_)

                ldf(0)
                for gi, (t0, n, v) in enumerate(grps):
                    xb = xg[gi % 2]
                    if gi + 1 < len(grps):
                        ldf(gi + 1)
                    K.op("act", lambda e: e.activation(out=xn[:], in_=xb[:], func=AF.Square), reads=[xb], writes=[xn])
                    for k in range(8):
                        mm(K, pss, pss[:], ones_b, ones_b[:], xn, xn[:, k, :], start=(k == 0), stop=(k == 7))
                    K.op("act", lambda e: e.activation(out=rstd[:], in_=pss[:], func=AF.Sqrt, scale=1.0, bias=float(D * EPS)), reads=[pss], writes=[rstd])
                    K.op("dve", lambda e: e.reciprocal(rstd[:], rstd[:]), reads=[rstd], writes=[rstd])
                    for k in range(8):
                        e_ = "dve"
                        K.op(e_, lambda e: e.scalar_tensor_tensor(out=xb[:, k, :], in0=xb[:, k, :], scalar=fg[:, k:k + 1],
                                                                   in1=rstd[:], op0=ALU.mult, op1=ALU.mult),
                             reads=[xb, fg, rstd], writes=[xb])
                    for tt in range(4):
                        o = yo[cnt % 2]
                        for hh in range(2):
                            p = pt[(2 * cnt + hh) % 4]
                            for kk in range(4):
                                k = hh * 4 + kk
                                tr(K, p, p[:, kk * 128:(kk + 1) * 128], xb, xb[:, k, tt * 128:(tt + 1) * 128], ident, ident[:])
                            if hh == 0:
                                K.op("act", lambda e: e.copy(o[:, 0:512], p[:]), reads=[p], writes=[o])
                            else:
                                K.op("dve", lambda e: e.tensor_copy(o[:, 512:1024], p[:]), reads=[p], writes=[o])
                        r0 = t0 - L + tt * 128
                        K.dma("sp", y_out[r0:r0 + 128, :], o[:], in_buf=o)
                        cnt += 1
                K.barrier()

        if only is not None:
            name, li, need_ctx = only
            if name == "dn":
                K.dma("sp", ident[:], ident_in[:, :], out_buf=ident)
                import os
                sub = os.environ.get("DN_SUB", "123")
                if "1" in sub:
                    phase_dn1(li)
                if "2" in sub:
                    phase_dn2(li)
                if "3" in sub:
                    phase_dn3(li, need_ctx)
            else:
                {"swa": phase_swa, "mla": phase_mla, "na": phase_na}[name](li, need_ctx)
            K.finish()
            return nc
        phase_init()
        for li in range(DEPTH):
            need_ctx = li < DEPTH - 1
            phase_mod(li)
            phase_ffn(ffn1_wg[li], ffn1_wu[li], ffn1_wd[li], 0, with_ctx=True)
            if stop_after == "ffn1":
                break
            phase_inproj(li)
            if stop_after == "inproj":
                break
            phase_swa(li, need_ctx)
            phase_mla(li, need_ctx)
            phase_na(li, need_ctx)
            phase_dn1(li)
            phase_dn2(li)
            phase_outproj(li, need_ctx)
            phase_ffn(ffn2_wg[li], ffn2_wu[li], ffn2_wd[li], 2, with_ctx=need_ctx)
        phase_final()
        K.finish()
    return nc


INPUT_NAMES = ["ada_w", "ada_b", "norm1_g", "ffn1_wg", "ffn1_wu", "ffn1_wd", "norm2_g", "norm3_g",
               "ffn2_wg", "ffn2_wu", "ffn2_wd", "w_in", "w_out", "swa_sink", "mla_q_norm_g", "mla_w_uq",
               "mla_kv_norm_g", "mla_w_ukv", "dn_conv_w", "dn_norm_g"]


def host_consts():
    theta = 10000.0
    tpos = np.arange(S)
    row = (tpos // 64).astype(np.float64)
    col = (tpos % 64).astype(np.float64)

    def tab(nrows, qw, nfreq_total):
        c = np.ones((nrows, T), np.float64)
        s_ = np.zeros((nrows, T), np.float64)
        for d in range(nrows):
            dd = d % (4 * qw)
            q, j = dd // qw, dd % qw
            inv = theta ** (-(2.0 * j) / (2 * qw))
            pos = row if q < 2 else col
            c[d, L:] = np.cos(pos * inv)
            s_[d, L:] = np.sin(pos * inv)
        return np.stack([c, s_]).astype(np.float32)

    t64 = tab(128, 16, 16)
    t32 = tab(32, 8, 8)
    t96 = np.concatenate([np.stack([np.ones((64, T)), np.zeros((64, T))]).astype(np.float32), t32], axis=1)
    kp = np.arange(128)[:, None]
    qq = np.arange(128)[None, :]
    mask_pn = np.stack([(qq <= kp), (kp <= qq)]).astype(np.float32)
    qc = np.arange(64)[None, :]
    kc = np.arange(64)[:, None]
    cs = np.clip(qc - 8, 0, 48)
    ok = ((kc >= cs) & (kc < cs + 16)).astype(np.float32)
    okm = np.concatenate([ok, ok], 0)
    a_ = np.arange(128)[:, None]
    b_ = np.arange(128)[None, :]
    tri = np.stack([a_ <= b_, a_ > b_, a_ >= b_, a_ < b_]).astype(np.float32)
    return {"tri": tri, "okm": okm, "ident": np.eye(128, dtype=np.float32), "tab64": t64, "tab96": np.ascontiguousarray(t96), "tab32": t32,
            "mask_pn": mask_pn}


def make_in_maps(inputs):
    consts = host_consts()
    idx = np.clip(np.arange(64)[:, None] - np.arange(64)[None, :] + 15, 0, 30)
    rpbx = np.ascontiguousarray(np.transpose(inputs["na_rpb"][:, :, :, idx], (0, 3, 1, 2, 4)))
    maps = []
    for b in range(8):
        m = {
            "x": np.ascontiguousarray(inputs["x"][b]),
            "c": np.ascontiguousarray(inputs["c"][b:b + 1]),
            "ctx": np.ascontiguousarray(inputs["ctx"][b]),
            "c_ctx": np.ascontiguousarray(inputs["c_ctx"][None, :]),
            "final_norm_g": np.ascontiguousarray(inputs["final_norm_g"][None, :]),
        }
        m.update(consts)
        m["rpbx"] = rpbx
        m["dn_a_log"] = np.ascontiguousarray(inputs["dn_a_log"].reshape(DEPTH, 8))
        m["dn_dt_bias"] = np.ascontiguousarray(inputs["dn_dt_bias"].reshape(DEPTH, 8))
        for n in INPUT_NAMES:
            m[n] = np.ascontiguousarray(inputs[n])
        maps.append(m)
    return maps


def kernel(**inputs):
    inputs = {k: np.asarray(v) for k, v in inputs.items()}
    nc = build_program()
    res = run_bass_kernel_spmd(nc, make_in_maps(inputs), core_ids=list(range(8)))
    return np.stack([r["y"] for r in res.results], axis=0).astype(np.float32)
```

```python
import types
import numpy as np
from contextlib import ExitStack
import concourse.bass as bass
import concourse.mybir as mybir
from concourse.bass_utils import run_bass_kernel_spmd

F32 = mybir.dt.float32
BF16 = mybir.dt.bfloat16
AF = mybir.ActivationFunctionType
ALU = mybir.AluOpType
AX = mybir.AxisListType

D = 1024
S = 4096
L = 256
T = S + L
DFF = 2816
NFF = DFF // 128
DEPTH = 2
EPS = 1e-6
INP = 2736


class Buf:
    def __init__(self, t, name):
        self.t = t
        self.name = name
        self.w = None
        self.r = {}
        self.dsem = None

    def __getitem__(self, idx):
        return self.t[idx]


class KB:
    def __init__(self, nc, es, n_dma_sems=64):
        self.nc = nc
        self.engs = {"pe": nc.tensor, "act": nc.scalar, "dve": nc.vector,
                     "pool": nc.gpsimd, "sp": nc.sync}
        self.sem = {e: es.enter_context(nc.semaphore("se_" + e)) for e in self.engs}
        self.cnt = {e: 0 for e in self.engs}
        self.seen = {e: {} for e in self.engs}
        self.latest = {}
        self.free = [[es.enter_context(nc.semaphore("sd%d" % i)), 0] for i in range(n_dma_sems)]
        self.uid = 0
        self.rr = 0
        self.defer = None
        self.pending = []

    def sb(self, st, shape, dt, name=None, dma=False):
        self.uid += 1
        name = "%s_%d" % (name or "b", self.uid)
        t = st.enter_context(self.nc.sbuf_tensor(name, list(shape), dt))
        b = Buf(t, name)
        if dma:
            b.dsem = self.free.pop()
            st.callback(lambda b=b: self.free.append(b.dsem))
        return b

    def ps(self, st, shape, dt=F32, name=None):
        self.uid += 1
        name = "%s_%d" % (name or "p", self.uid)
        t = st.enter_context(self.nc.psum_tensor(name, list(shape), dt))
        return Buf(t, name)

    def _wait(self, e, tok):
        sem, val, key = tok
        if self.seen[e].get(key, 0) >= val:
            return
        self.engs[e].wait_ge(sem, val)
        self.seen[e][key] = val

    def _deps(self, e, reads, writes):
        toks = []
        for b in reads:
            if b is not None and b.w is not None:
                toks.append(b.w)
        for b in writes:
            if b is None:
                continue
            if b.w is not None:
                toks.append(b.w)
            toks.extend(b.r.values())
        for tok in toks:
            if e == "pe" and tok[2] == "se_pe":
                continue
            self._wait(e, tok)

    def pump(self, n=1):
        q = self.pending
        d, self.defer = self.defer, None
        while n > 0 and q:
            item = q.pop(0)
            if item[0] == "op":
                self.op(*item[1:])
            else:
                self.dma(*item[1], **item[2])
            n -= 1
        self.defer = d

    def op(self, e, fn, reads=(), writes=()):
        if self.defer is not None:
            if fn.__closure__:
                fn = types.FunctionType(fn.__code__, fn.__globals__, fn.__name__, fn.__defaults__,
                                        tuple(types.CellType(c.cell_contents) for c in fn.__closure__))
            self.defer.append(("op", e, fn, list(reads), list(writes)))
            return None
        self._deps(e, reads, writes)
        ins = fn(self.engs[e])
        self.cnt[e] += 1
        ins.then_inc(self.sem[e], 1)
        key = "se_" + e
        tok = (self.sem[e], self.cnt[e], key)
        self.latest[key] = tok
        for b in reads:
            if b is not None:
                b.r[e] = tok
        for b in writes:
            if b is not None:
                b.w = tok
                b.r = {}
        return tok

    def dma(self, q, out_ap, in_ap, out_buf=None, in_buf=None, n=1, fn=None):
        if self.defer is not None:
            self.defer.append(("dma", (q, out_ap, in_ap), dict(out_buf=out_buf, in_buf=in_buf)))
            return None
        sb = out_buf if out_buf is not None else in_buf
        assert sb is not None and sb.dsem is not None, "dma needs an SBUF buf with dsem"
        self._deps(q, [in_buf] if in_buf is not None else [], [out_buf] if out_buf is not None else [])
        pairs = list(zip(out_ap, in_ap)) if isinstance(out_ap, (list, tuple)) else [(out_ap, in_ap)]
        for o, i in pairs:
            self.engs[q].dma_start(out=o, in_=i).then_inc(sb.dsem[0], 16)
            sb.dsem[1] += 16
        key = "sd_" + str(id(sb.dsem))
        tok = (sb.dsem[0], sb.dsem[1], key)
        self.latest[key] = tok
        if out_buf is not None:
            out_buf.w = tok
            out_buf.r = {}
        else:
            in_buf.r["dma_" + q] = tok
        return tok

    def barrier(self):
        for e in self.engs:
            for tok in list(self.latest.values()):
                self._wait(e, tok)

    def finish(self):
        self.barrier()


def mm(K, out, out_ap, lhs, lhs_ap, rhs, rhs_ap, start=True, stop=True):
    return K.op("pe", lambda e: e.matmul(out_ap, lhs_ap, rhs_ap, start=start, stop=stop),
                reads=[lhs, rhs], writes=[out])


def tr(K, out, out_ap, in_, in_ap, ident, ident_ap):
    return K.op("pe", lambda e: e.transpose(out_ap, in_ap, ident_ap), reads=[in_, ident], writes=[out])


class Prog:
    def __init__(self, debug=False, phases=None):
        self.debug = debug
        self.phases = phases


def build_program(debug=False, stop_after=None, only=None, feed=()):
    nc = bass.Bass("TRN2", target_bir_lowering=False)
    kind_s = "ExternalOutput" if debug else "Internal"

    def din(name, shape, dt=F32):
        return nc.dram_tensor(name, list(shape), dt, kind="ExternalInput").ap()

    def dscr(name, shape, dt=F32):
        if name in feed:
            return nc.dram_tensor(name, list(shape), dt, kind="ExternalInput").ap()
        if debug:
            return nc.dram_tensor(name, list(shape), dt, kind="ExternalOutput").ap()
        return nc.dram_tensor(name, list(shape), dt).ap()

    x_in = din("x", [S, D])
    c_in = din("c", [1, D])
    ctx_in = din("ctx", [L, D])
    cctx_in = din("c_ctx", [1, D])
    ada_w = din("ada_w", [DEPTH, D, 9 * D])
    ada_b = din("ada_b", [DEPTH, 9 * D])
    norm1_g = din("norm1_g", [DEPTH, D])
    ffn1_wg = din("ffn1_wg", [DEPTH, D, DFF])
    ffn1_wu = din("ffn1_wu", [DEPTH, D, DFF])
    ffn1_wd = din("ffn1_wd", [DEPTH, DFF, D])
    norm2_g = din("norm2_g", [DEPTH, D])
    norm3_g = din("norm3_g", [DEPTH, D])
    ffn2_wg = din("ffn2_wg", [DEPTH, D, DFF])
    ffn2_wu = din("ffn2_wu", [DEPTH, D, DFF])
    ffn2_wd = din("ffn2_wd", [DEPTH, DFF, D])
    final_g = din("final_norm_g", [1, D])
    ident_in = din("ident", [128, 128])
    w_in = din("w_in", [DEPTH, D, INP])
    w_out = din("w_out", [DEPTH, D, D])
    swa_sink = din("swa_sink", [DEPTH, 4])
    mla_qg = din("mla_q_norm_g", [DEPTH, 256])
    mla_wuq = din("mla_w_uq", [DEPTH, 256, 384])
    mla_kvg = din("mla_kv_norm_g", [DEPTH, 128])
    mla_wukv = din("mla_w_ukv", [DEPTH, 128, 512])
    tab64 = din("tab64", [2, 128, T])
    tab96 = din("tab96", [2, 96, T])
    tab32 = din("tab32", [2, 32, T])
    mask_pn = din("mask_pn", [2, 128, 128])
    rpbx = din("rpbx", [DEPTH, 64, 4, 15, 64])
    tri_in = din("tri", [4, 128, 128])
    dn_conv_w = din("dn_conv_w", [DEPTH, 3, 768])
    dn_a_log = din("dn_a_log", [DEPTH, 8])
    dn_dt_bias = din("dn_dt_bias", [DEPTH, 8])
    dn_norm_g = din("dn_norm_g", [DEPTH, 64])
    okm_in = din("okm", [128, 64])
    y_out = nc.dram_tensor("y", [S, D], F32, kind="ExternalOutput").ap()

    xT = dscr("xT", [D, T])
    swa_qT = dscr("swa_qT", [256, T], BF16)
    swa_kT = dscr("swa_kT", [128, T], BF16)
    swa_v = dscr("swa_v", [T, 128], BF16)
    dn_qkvT = dscr("dn_qkvT", [768, T], F32)
    dn_zab = dscr("dn_zab", [T, 272], F32)
    mla_qT = dscr("mla_qT", [384, T], BF16)
    mla_kT = dscr("mla_kT", [384, T], BF16)
    mla_v = dscr("mla_v", [T, 256], BF16)
    na_qT = dscr("na_qT", [256, T], BF16)
    na_kT = dscr("na_kT", [256, T], BF16)
    na_v = dscr("na_v", [T, 256], BF16)
    ycat = dscr("ycat", [T, D], BF16)
    dn_hT = dscr("dn_hT", [768, T], F32)
    dn_h = dscr("dn_h", [T, 768], F32)
    dn_o = dscr("dn_o", [2, T, 256], F32)

    with ExitStack() as es:
        K = KB(nc, es)
        gs = ExitStack()
        es.enter_context(gs)
        ident = K.sb(gs, [128, 128], F32, "ident", dma=True)
        identb = K.sb(gs, [128, 128], BF16, "identb")
        ones_b = K.sb(gs, [128, 128], BF16, "ones_b")
        ones_f = K.sb(gs, [1, 2], F32, "ones_f")
        K.dma("sp", ident[:], ident_in[:, :], out_buf=ident)
        K.op("dve", lambda e: e.tensor_copy(identb[:], ident[:]), reads=[ident], writes=[identb])
        K.op("dve", lambda e: e.memset(ones_b[:], 1.0), writes=[ones_b])
        K.op("dve", lambda e: e.memset(ones_f[:], 1.0), writes=[ones_f])
        modA = [K.sb(gs, [128, 8, 2], F32, "modA%d" % j) for j in range(3)]
        modB = [K.sb(gs, [128, 8, 2], F32, "modB%d" % j) for j in range(3)]
        modG = [K.sb(gs, [128, 8, 2], F32, "modG%d" % j) for j in range(3)]
        fin_g = K.sb(gs, [128, 8], F32, "fin_g")

        def phase_init():
            with ExitStack() as st:
                xin = [K.sb(st, [128, D], F32, "xin", dma=True) for _ in range(2)]
                xo = [K.sb(st, [128, 8, 128], F32, "xo", dma=True) for _ in range(2)]
                pt = [K.ps(st, [128, 512], F32, "pt") for _ in range(4)]
                xTv = xT.rearrange("(k p) t -> p k t", p=128)
                def ld_(i):
                    src = ctx_in[i * 128:(i + 1) * 128, :] if i < 2 else x_in[(i - 2) * 128:(i - 1) * 128, :]
                    K.dma("sp", xin[i % 2][:], src, out_buf=xin[i % 2])

                ld_(0)
                for i in range(T // 128):
                    xi = xin[i % 2]
                    o = xo[i % 2]
                    if i + 1 < T // 128:
                        ld_(i + 1)
                    for hh in range(2):
                        p = pt[(2 * i + hh) % 4]
                        for kk in range(4):
                            k = hh * 4 + kk
                            tr(K, p, p[:, kk * 128:(kk + 1) * 128], xi, xi[:, k * 128:(k + 1) * 128], ident, ident[:])
                        if hh == 0:
                            K.op("act", lambda e: e.copy(o[:, 0:4, :], p[:].rearrange("p (k t) -> p k t", k=4)),
                                 reads=[p], writes=[o])
                        else:
                            K.op("dve", lambda e: e.tensor_copy(o[:, 4:8, :], p[:].rearrange("p (k t) -> p k t", k=4)),
                                 reads=[p], writes=[o])
                    K.dma("sp", xTv[:, :, i * 128:(i + 1) * 128], o[:], in_buf=o)
                K.barrier()

        def phase_mod(li):
            with ExitStack() as st:
                cv = K.sb(st, [1, 2, D], F32, "cv", dma=True)
                scv = K.sb(st, [128, 8, 2], F32, "scv")
                wblk = [K.sb(st, [128, 8, D], F32, "wblk", dma=True) for _ in range(3)]
                brow = K.sb(st, [1, 9 * D], F32, "brow", dma=True)
                grow = K.sb(st, [1, 4, D], F32, "grow", dma=True)
                pm = K.ps(st, [128, 8, 2], F32, "pm")
                pg = K.ps(st, [128, 4, 8], F32, "pg")
                modT = K.sb(st, [128, 72, 2], F32, "modT")
                gT = K.sb(st, [128, 4, 8], F32, "gT")
                K.dma("sp", [cv[0:1, 0, :], cv[0:1, 1, :]], [c_in[0:1, :], cctx_in[0:1, :]], out_buf=cv)
                K.dma("sp", brow[:], ada_b[li:li + 1, :], out_buf=brow)
                K.dma("sp", [grow[0:1, 0, :], grow[0:1, 1, :], grow[0:1, 2, :], grow[0:1, 3, :]],
                      [norm1_g[li:li + 1, :], norm2_g[li:li + 1, :], norm3_g[li:li + 1, :], final_g[0:1, :]],
                      out_buf=grow)
                for v in range(2):
                    for k in range(8):
                        mm(K, pm, pm[:, k, v:v + 1], cv, cv[0:1, v, k * 128:(k + 1) * 128], ones_f, ones_f[0:1, 0:1])
                K.op("act", lambda e: e.activation(out=scv[:], in_=pm[:], func=AF.Silu), reads=[pm], writes=[scv])
                for gi in range(4):
                    for k in range(8):
                        mm(K, pg, pg[:, gi, k:k + 1], grow, grow[0:1, gi, k * 128:(k + 1) * 128], ones_f, ones_f[0:1, 0:1])
                K.op("dve", lambda e: e.tensor_copy(gT[:], pg[:]), reads=[pg], writes=[gT])
                K.op("dve", lambda e: e.tensor_copy(fin_g[:], gT[:, 3, :]), reads=[gT], writes=[fin_g])
                awv = ada_w[li].rearrange("(k p) n -> p k n", p=128)
                for j in range(9):
                    wb = wblk[j % 3]
                    K.dma("sp", [wb[:, 0:4, :], wb[:, 4:8, :]],
                          [awv[:, 0:4, j * D:(j + 1) * D], awv[:, 4:8, j * D:(j + 1) * D]], out_buf=wb)
                    for m in range(8):
                        for k in range(8):
                            mm(K, pm, pm[:, m, :], wb, wb[:, k, m * 128:(m + 1) * 128], scv, scv[:, k, :],
                               start=(k == 0), stop=False)
                        mm(K, pm, pm[:, m, :], brow, brow[0:1, j * D + m * 128: j * D + (m + 1) * 128],
                           ones_f, ones_f[0:1, 0:2], start=False, stop=True)
                    K.op("dve", lambda e: e.tensor_copy(modT[:, j * 8:(j + 1) * 8, :], pm[:]), reads=[pm], writes=[modT])
                for s3 in range(3):
                    jsh, jsc, jg = 3 * s3, 3 * s3 + 1, 3 * s3 + 2
                    A, Bm, G = modA[s3], modB[s3], modG[s3]
                    K.op("dve", lambda e: e.tensor_scalar(A[:], modT[:, jsc * 8:(jsc + 1) * 8, :], 1.0, float(np.sqrt(D)),
                                                          op0=ALU.add, op1=ALU.mult), reads=[modT], writes=[A])
                    for v in range(2):
                        K.op("dve", lambda e: e.tensor_tensor(A[:, :, v], A[:, :, v], gT[:, s3, :], op=ALU.mult),
                             reads=[A, gT], writes=[A])
                    K.op("dve", lambda e: e.tensor_copy(Bm[:], modT[:, jsh * 8:(jsh + 1) * 8, :]), reads=[modT], writes=[Bm])
                    gsc = 1.0 if s3 == 1 else 0.5
                    K.op("dve", lambda e: e.tensor_scalar(G[:], modT[:, jg * 8:(jg + 1) * 8, :], gsc, None, op0=ALU.mult),
                         reads=[modT], writes=[G])
                K.barrier()

        def groups_all(with_ctx=True):
            g = []
            if with_ctx:
                g.append((0, L, 1))
            for i in range(S // 512):
                g.append((L + i * 512, 512, 0))
            return g

        def phase_ffn(wg_d, wu_d, wd_d, s3, with_ctx=True):
            A, Bm, G = modA[s3], modB[s3], modG[s3]
            with ExitStack() as st:
                Wg = K.sb(st, [128, 8, DFF], BF16, "Wg")
                Wu = K.sb(st, [128, 8, DFF], BF16, "Wu")
                Wd = K.sb(st, [128, NFF, D], BF16, "Wd")
                with ExitStack() as st2:
                    stg = [K.sb(st2, [128, DFF], F32, "stg", dma=True) for _ in range(4)]
                    ci = 0
                    ceng = ["dve", "act", "pool"]
                    for (wsrc, wdst) in ((wg_d, Wg), (wu_d, Wu)):
                        for k in range(8):
                            sg_ = stg[ci % 4]
                            K.dma("sp", sg_[:], wsrc[k * 128:(k + 1) * 128, :], out_buf=sg_)
                            e_ = ceng[ci % 3]
                            if e_ == "act":
                                K.op("act", lambda e: e.copy(wdst[:, k, :], sg_[:]), reads=[sg_], writes=[wdst])
                            else:
                                K.op(e_, lambda e: e.tensor_copy(wdst[:, k, :], sg_[:]), reads=[sg_], writes=[wdst])
                            ci += 1
                    for m in range(0, NFF, 2):
                        sg_ = stg[ci % 4]
                        K.dma("sp", sg_[:, 0:2 * D].rearrange("p (a n) -> p a n", a=2),
                              wd_d[m * 128:(m + 2) * 128, :].rearrange("(a p) n -> p a n", p=128), out_buf=sg_)
                        e_ = ceng[ci % 3]
                        src_ap = sg_[:, 0:2 * D].rearrange("p (a n) -> p a n", a=2)
                        if e_ == "act":
                            K.op("act", lambda e: e.copy(Wd[:, m:m + 2, :], src_ap), reads=[sg_], writes=[Wd])
                        else:
                            K.op(e_, lambda e: e.tensor_copy(Wd[:, m:m + 2, :], src_ap), reads=[sg_], writes=[Wd])
                        ci += 1
                    K.barrier()
                xg = [K.sb(st, [128, 8, 512], F32, "xg", dma=True) for _ in range(2)]
                xn = K.sb(st, [128, 8, 512], BF16, "xn")
                rstd = K.sb(st, [128, 512], F32, "rstd")
                actT = K.sb(st, [128, NFF, 512], BF16, "actT")
                sg = [K.sb(st, [128, 512], F32, "sg") for _ in range(2)]
                pss = K.ps(st, [128, 512], F32, "pss")
                pg = [K.ps(st, [128, 512], F32, "pg") for _ in range(2)]
                pu = [K.ps(st, [128, 512], F32, "pu") for _ in range(2)]
                po = [K.ps(st, [128, 512], F32, "po") for _ in range(2)]
                xTv = xT.rearrange("(k p) t -> p k t", p=128)
                grps = groups_all(with_ctx)

                def load(gi):
                    t0, n, v = grps[gi]
                    b = xg[gi % 2]
                    K.dma("sp", [b[:, 0:4, 0:n], b[:, 4:8, 0:n]], [xTv[:, 0:4, t0:t0 + n], xTv[:, 4:8, t0:t0 + n]], out_buf=b)

                def pre1(gi):
                    t0, n, v = grps[gi]
                    xb = xg[gi % 2]
                    K.op("act", lambda e: e.activation(out=xn[:, :, 0:n], in_=xb[:, :, 0:n], func=AF.Square),
                         reads=[xb], writes=[xn])

                def pre2(gi):
                    t0, n, v = grps[gi]
                    xb = xg[gi % 2]
                    for k in range(8):
                        mm(K, pss, pss[:, 0:n], ones_b, ones_b[:], xn, xn[:, k, 0:n], start=(k == 0), stop=(k == 7))
                    K.op("act", lambda e: e.activation(out=rstd[:, 0:n], in_=pss[:, 0:n], func=AF.Sqrt, scale=1.0, bias=float(D * EPS)),
                         reads=[pss], writes=[rstd])
                    K.op("dve", lambda e: e.reciprocal(rstd[:, 0:n], rstd[:, 0:n]), reads=[rstd], writes=[rstd])
                    for k in range(8):
                        K.op("dve", lambda e: e.scalar_tensor_tensor(out=xn[:, k, 0:n], in0=xb[:, k, 0:n], scalar=A[:, k, v:v + 1],
                                                                      in1=rstd[:, 0:n], op0=ALU.mult, op1=ALU.mult),
                             reads=[xb, A, rstd], writes=[xn])
                    for k in range(8):
                        K.op("act", lambda e: e.activation(out=xn[:, k, 0:n], in_=xn[:, k, 0:n], func=AF.Identity,
                                                           bias=Bm[:, k, v:v + 1], scale=1.0),
                             reads=[xn, Bm], writes=[xn])

                load(0)
                pre1(0)
                pre2(0)
                for gi, (t0, n, v) in enumerate(grps):
                    more = gi + 1 < len(grps)
                    if more:
                        load(gi + 1)
                    xb = xg[gi % 2]
                    for m in range(NFF):
                        pgm, pum, sgm = pg[m % 2], pu[m % 2], sg[m % 2]
                        for k in range(8):
                            mm(K, pgm, pgm[:, 0:n], Wg, Wg[:, k, m * 128:(m + 1) * 128], xn, xn[:, k, 0:n], start=(k == 0), stop=(k == 7))
                        for k in range(8):
                            mm(K, pum, pum[:, 0:n], Wu, Wu[:, k, m * 128:(m + 1) * 128], xn, xn[:, k, 0:n], start=(k == 0), stop=(k == 7))
                        K.op("act", lambda e: e.activation(out=sgm[:, 0:n], in_=pgm[:, 0:n], func=AF.Silu), reads=[pgm], writes=[sgm])
                        K.op("dve", lambda e: e.tensor_tensor(actT[:, m, 0:n], sgm[:, 0:n], pum[:, 0:n], op=ALU.mult),
                             reads=[sgm, pum], writes=[actT])
                    if more:
                        pre1(gi + 1)
                    for f in range(8):
                        pof = po[f % 2]
                        for m in range(NFF):
                            mm(K, pof, pof[:, 0:n], Wd, Wd[:, m, f * 128:(f + 1) * 128], actT, actT[:, m, 0:n], start=(m == 0), stop=(m == NFF - 1))
                        K.op("dve", lambda e: e.scalar_tensor_tensor(out=xb[:, f, 0:n], in0=pof[:, 0:n], scalar=G[:, f, v:v + 1],
                                                                      in1=xb[:, f, 0:n], op0=ALU.mult, op1=ALU.add),
                             reads=[pof, G, xb], writes=[xb])
                        if f == 1 and more:
                            pre2(gi + 1)
                    K.dma("sp", [xTv[:, 0:4, t0:t0 + n], xTv[:, 4:8, t0:t0 + n]], [xb[:, 0:4, 0:n], xb[:, 4:8, 0:n]], in_buf=xb)
                K.barrier()


        def prenorm(xb, xn, rstd, pss, A, Bm, v, n):
            K.op("act", lambda e: e.activation(out=xn[:, :, 0:n], in_=xb[:, :, 0:n], func=AF.Square),
                 reads=[xb], writes=[xn])
            for k in range(8):
                mm(K, pss, pss[:, 0:n], ones_b, ones_b[:], xn, xn[:, k, 0:n], start=(k == 0), stop=(k == 7))
            K.op("act", lambda e: e.activation(out=rstd[:, 0:n], in_=pss[:, 0:n], func=AF.Sqrt, scale=1.0, bias=float(D * EPS)),
                 reads=[pss], writes=[rstd])
            K.op("dve", lambda e: e.reciprocal(rstd[:, 0:n], rstd[:, 0:n]), reads=[rstd], writes=[rstd])
            for k in range(8):
                K.op("dve", lambda e: e.scalar_tensor_tensor(out=xn[:, k, 0:n], in0=xb[:, k, 0:n], scalar=A[:, k, v:v + 1],
                                                              in1=rstd[:, 0:n], op0=ALU.mult, op1=ALU.mult),
                     reads=[xb, A, rstd], writes=[xn])
            for k in range(8):
                K.op("act", lambda e: e.activation(out=xn[:, k, 0:n], in_=xn[:, k, 0:n], func=AF.Identity,
                                                   bias=Bm[:, k, v:v + 1], scale=1.0),
                     reads=[xn, Bm], writes=[xn])

        class Ring:
            def __init__(self, bufs):
                self.bufs = bufs
                self.i = 0

            def next(self):
                b = self.bufs[self.i % len(self.bufs)]
                self.i += 1
                return b

        def rot_cols(dst, src, k, c0, nblk, qw):
            w4 = 4 * qw
            dv = dst[:, k, c0:c0 + nblk * w4].rearrange("p (b q j) -> p b q j", q=4, j=qw)
            sv = src[:, k, c0:c0 + nblk * w4].rearrange("p (b q j) -> p b q j", q=4, j=qw)
            for (qd, qs, sgn) in ((0, 1, -1.0), (1, 0, 1.0), (2, 3, -1.0), (3, 2, 1.0)):
                K.op("pool", lambda e: e.tensor_scalar(dv[:, :, qd, :], sv[:, :, qs, :], sgn, None, op0=ALU.mult),
                     reads=[src], writes=[dst])

        def phase_inproj(li):
            A, Bm = modA[1], modB[1]
            with ExitStack() as st:
                Win = K.sb(st, [128, 8, INP], BF16, "Win")
                Wrot = K.sb(st, [128, 8, INP], BF16, "Wrot")
                Wuq = K.sb(st, [128, 2, 384], BF16, "Wuq")
                Wuqr = K.sb(st, [128, 2, 384], BF16, "Wuqr")
                Wukv = K.sb(st, [128, 512], BF16, "Wukv")
                gqk = K.sb(st, [128, 4], F32, "gqk")
                with ExitStack() as st2:
                    stg = [K.sb(st2, [128, INP], F32, "stg", dma=True) for _ in range(3)]
                    grow = K.sb(st2, [1, 384], F32, "grow", dma=True)
                    pgq = K.ps(st2, [128, 4], F32, "pgq")
                    for k in range(8):
                        sg_ = stg[k % 3]
                        K.dma("sp", sg_[:], w_in[li, k * 128:(k + 1) * 128, :], out_buf=sg_)
                        if k % 2 == 0:
                            K.op("dve", lambda e: e.tensor_copy(Win[:, k, :], sg_[:]), reads=[sg_], writes=[Win])
                        else:
                            K.op("act", lambda e: e.copy(Win[:, k, :], sg_[:]), reads=[sg_], writes=[Win])
                        rot_cols(Wrot, Win, k, 0, 6, 16)
                        rot_cols(Wrot, Win, k, 1936, 1, 8)
                    for c in range(2):
                        sg_ = stg[c % 2]
                        K.dma("sp", sg_[:, 0:384], mla_wuq[li, c * 128:(c + 1) * 128, :], out_buf=sg_)
                        K.op("dve", lambda e: e.tensor_copy(Wuq[:, c, :], sg_[:, 0:384]), reads=[sg_], writes=[Wuq])
                    K.op("pool", lambda e: e.memset(Wuqr[:], 0.0), writes=[Wuqr])
                    for c in range(2):
                        for h in range(4):
                            rot_cols(Wuqr, Wuq, c, h * 96 + 64, 1, 8)
                    sg_ = stg[0]
                    K.dma("sp", sg_[:, 0:512], mla_wukv[li, :, :], out_buf=sg_)
                    K.op("dve", lambda e: e.tensor_copy(Wukv[:], sg_[:, 0:512]), reads=[sg_], writes=[Wukv])
                    K.dma("sp", [grow[0:1, 0:256], grow[0:1, 256:384]], [mla_qg[li:li + 1, :], mla_kvg[li:li + 1, :]], out_buf=grow)
                    for c in range(3):
                        mm(K, pgq, pgq[:, c:c + 1], grow, grow[0:1, c * 128:(c + 1) * 128], ones_f, ones_f[0:1, 0:1])
                    K.op("dve", lambda e: e.tensor_copy(gqk[:, 0:3], pgq[:, 0:3]), reads=[pgq], writes=[gqk])
                    K.barrier()

                xg = [K.sb(st, [128, 8, 512], F32, "xg", dma=True) for _ in range(2)]
                tb64 = [K.sb(st, [128, 2, 512], F32, "tb64", dma=True) for _ in range(2)]
                tb96 = [K.sb(st, [96, 2, 512], F32, "tb96", dma=True) for _ in range(2)]
                tb32 = [K.sb(st, [32, 2, 512], F32, "tb32", dma=True) for _ in range(2)]
                xn = K.sb(st, [128, 8, 512], BF16, "xn")
                rstd = K.sb(st, [128, 512], F32, "rstd")
                t1 = K.sb(st, [128, 512], F32, "t1")
                t2 = K.sb(st, [128, 512], F32, "t2")
                cqf = K.sb(st, [128, 3, 512], F32, "cqf")
                cqs = K.sb(st, [128, 3, 512], BF16, "cqs")
                cqn = K.sb(st, [128, 3, 512], BF16, "cqn")
                rq = K.sb(st, [128, 512], F32, "rq")
                rkv = K.sb(st, [128, 512], F32, "rkv")
                obf = Ring([K.sb(st, [128, 512], BF16, "obf", dma=True) for _ in range(4)])
                of32 = Ring([K.sb(st, [128, 512], F32, "of32", dma=True) for _ in range(3)])
                pss = K.ps(st, [128, 512], F32, "pss")
                pb = Ring([K.ps(st, [128, 512], F32, "pb") for _ in range(7)])
                xTv = xT.rearrange("(k p) t -> p k t", p=128)
                grps = groups_all(True)
                evi = [0]

                def evac(out_b, out_ap, p, p_ap):
                    evi[0] += 1
                    if evi[0] % 2 == 0:
                        K.op("act", lambda e: e.copy(out_ap, p_ap), reads=[p], writes=[out_b])
                    else:
                        K.op("dve", lambda e: e.tensor_copy(out_ap, p_ap), reads=[p], writes=[out_b])

                def load(gi):
                    t0, n, v = grps[gi]
                    b = xg[gi % 2]
                    K.dma("sp", [b[:, 0:4, 0:n], b[:, 4:8, 0:n]], [xTv[:, 0:4, t0:t0 + n], xTv[:, 4:8, t0:t0 + n]], out_buf=b)
                    K.dma("sp", tb64[gi % 2][:, :, 0:n], tab64[:, :, t0:t0 + n].rearrange("c p t -> p c t"), out_buf=tb64[gi % 2])
                    K.dma("sp", tb96[gi % 2][:, :, 0:n], tab96[:, :, t0:t0 + n].rearrange("c p t -> p c t"), out_buf=tb96[gi % 2])
                    K.dma("sp", tb32[gi % 2][:, :, 0:n], tab32[:, :, t0:t0 + n].rearrange("c p t -> p c t"), out_buf=tb32[gi % 2])

                load(0)
                for gi, (t0, n, v) in enumerate(grps):
                    if gi + 1 < len(grps):
                        load(gi + 1)
                    xb = xg[gi % 2]
                    T64, T96, T32 = tb64[gi % 2], tb96[gi % 2], tb32[gi % 2]
                    prenorm(xb, xn, rstd, pss, A, Bm, v, n)

                    def proj(c0, nc_, W=Win):
                        p = pb.next()
                        for k in range(8):
                            mm(K, p, p[0:nc_, 0:n], W, W[:, k, c0:c0 + nc_], xn, xn[:, k, 0:n], start=(k == 0), stop=(k == 7))
                        return p

                    def rope_store(p, pr, tb, np_, dst_ap):
                        ob = obf.next()
                        K.op("dve", lambda e: e.tensor_tensor(t1[0:np_, 0:n], p[0:np_, 0:n], tb[0:np_, 0, 0:n], op=ALU.mult),
                             reads=[p, tb], writes=[t1])
                        K.op("dve", lambda e: e.tensor_tensor(t2[0:np_, 0:n], pr[0:np_, 0:n], tb[0:np_, 1, 0:n], op=ALU.mult),
                             reads=[pr, tb], writes=[t2])
                        K.op("pool", lambda e: e.tensor_tensor(ob[0:np_, 0:n], t1[0:np_, 0:n], t2[0:np_, 0:n], op=ALU.add),
                             reads=[t1, t2], writes=[ob])
                        if isinstance(dst_ap, list):
                            K.dma("sp", dst_ap, [ob[0:np_, 0:n]] * len(dst_ap), in_buf=ob)
                        else:
                            K.dma("sp", dst_ap, ob[0:np_, 0:n], in_buf=ob)

                    def plain_store(p, np_, dst_ap, dt=BF16):
                        ob = obf.next() if dt == BF16 else of32.next()
                        evac(ob, ob[0:np_, 0:n], p, p[0:np_, 0:n])
                        K.dma("sp", dst_ap, ob[0:np_, 0:n], in_buf=ob)

                    for ch in range(2):
                        p = proj(ch * 128, 128)
                        pr = proj(ch * 128, 128, Wrot)
                        rope_store(p, pr, T64, 128, swa_qT[ch * 128:(ch + 1) * 128, t0:t0 + n])
                    p = proj(256, 128)
                    pr = proj(256, 128, Wrot)
                    rope_store(p, pr, T64, 128, swa_kT[:, t0:t0 + n])
                    for ch in range(6):
                        p = proj(512 + ch * 128, 128)
                        plain_store(p, 128, dn_qkvT[ch * 128:(ch + 1) * 128, t0:t0 + n], F32)
                    for ch in range(2):
                        p = proj(1968 + ch * 128, 128)
                        plain_store(p, 128, na_qT[ch * 128:(ch + 1) * 128, t0:t0 + n])
                    for ch in range(2):
                        p = proj(2224 + ch * 128, 128)
                        plain_store(p, 128, na_kT[ch * 128:(ch + 1) * 128, t0:t0 + n])
                    p = proj(1936, 32)
                    pr = proj(1936, 32, Wrot)
                    rope_store(p, pr, T32, 32, [mla_kT[h * 96 + 64:h * 96 + 96, t0:t0 + n] for h in range(4)])
                    for c in range(3):
                        p = proj(1552 + c * 128, 128)
                        K.op("act", lambda e: e.copy(cqf[:, c, 0:n], p[:, 0:n]), reads=[p], writes=[cqf])
                    K.op("act", lambda e: e.activation(out=cqs[:, :, 0:n], in_=cqf[:, :, 0:n], func=AF.Square), reads=[cqf], writes=[cqs])
                    pq_ = pb.next()
                    for c in range(2):
                        mm(K, pq_, pq_[:, 0:n], ones_b, ones_b[:], cqs, cqs[:, c, 0:n], start=(c == 0), stop=(c == 1))
                    K.op("act", lambda e: e.activation(out=rq[:, 0:n], in_=pq_[:, 0:n], func=AF.Sqrt, scale=1.0 / 256.0, bias=float(EPS)),
                         reads=[pq_], writes=[rq])
                    K.op("dve", lambda e: e.reciprocal(rq[:, 0:n], rq[:, 0:n]), reads=[rq], writes=[rq])
                    pk_ = pb.next()
                    mm(K, pk_, pk_[:, 0:n], ones_b, ones_b[:], cqs, cqs[:, 2, 0:n])
                    K.op("act", lambda e: e.activation(out=rkv[:, 0:n], in_=pk_[:, 0:n], func=AF.Sqrt, scale=1.0 / 128.0, bias=float(EPS)),
                         reads=[pk_], writes=[rkv])
                    K.op("dve", lambda e: e.reciprocal(rkv[:, 0:n], rkv[:, 0:n]), reads=[rkv], writes=[rkv])
                    for c in range(3):
                        rr_ = rq if c < 2 else rkv
                        K.op("dve", lambda e: e.scalar_tensor_tensor(out=cqn[:, c, 0:n], in0=cqf[:, c, 0:n], scalar=gqk[:, c:c + 1],
                                                                      in1=rr_[:, 0:n], op0=ALU.mult, op1=ALU.mult),
                             reads=[cqf, gqk, rr_], writes=[cqn])
                    for h in range(4):
                        p = pb.next()
                        pr = pb.next()
                        for c in range(2):
                            mm(K, p, p[0:96, 0:n], Wuq, Wuq[:, c, h * 96:(h + 1) * 96], cqn, cqn[:, c, 0:n], start=(c == 0), stop=(c == 1))
                        for c in range(2):
                            mm(K, pr, pr[0:96, 0:n], Wuqr, Wuqr[:, c, h * 96:(h + 1) * 96], cqn, cqn[:, c, 0:n], start=(c == 0), stop=(c == 1))
                        rope_store(p, pr, T96, 96, mla_qT[h * 96:(h + 1) * 96, t0:t0 + n])
                    for h in range(4):
                        p = pb.next()
                        mm(K, p, p[0:64, 0:n], Wukv, Wukv[:, h * 128:h * 128 + 64], cqn, cqn[:, 2, 0:n])
                        plain_store(p, 64, mla_kT[h * 96:h * 96 + 64, t0:t0 + n])
                    wv_ap = Wukv[:].rearrange("p (h c) -> p h c", h=4)[:, :, 64:128]
                    for tt in range(n // 128):
                        r0 = t0 + tt * 128
                        p = pb.next()
                        mm(K, p, p[:, 0:256].rearrange("p (h c) -> p h c", h=4), cqn, cqn[:, 2, tt * 128:(tt + 1) * 128], Wukv, wv_ap)
                        ob = obf.next()
                        evac(ob, ob[:, 0:256], p, p[:, 0:256])
                        K.dma("sp", mla_v[r0:r0 + 128, :], ob[:, 0:256], in_buf=ob)
                        p = pb.next()
                        for k in range(8):
                            mm(K, p, p[:, 0:128], xn, xn[:, k, tt * 128:(tt + 1) * 128], Win, Win[:, k, 384:512], start=(k == 0), stop=(k == 7))
                        ob = obf.next()
                        evac(ob, ob[:, 0:128], p, p[:, 0:128])
                        K.dma("sp", swa_v[r0:r0 + 128, :], ob[:, 0:128], in_buf=ob)
                        p = pb.next()
                        for k in range(8):
                            mm(K, p, p[:, 0:272], xn, xn[:, k, tt * 128:(tt + 1) * 128], Win, Win[:, k, 1280:1552], start=(k == 0), stop=(k == 7))
                        ob = of32.next()
                        evac(ob, ob[:, 0:272], p, p[:, 0:272])
                        K.dma("sp", dn_zab[r0:r0 + 128, :], ob[:, 0:272], in_buf=ob)
                        p = pb.next()
                        for k in range(8):
                            mm(K, p, p[:, 0:256], xn, xn[:, k, tt * 128:(tt + 1) * 128], Win, Win[:, k, 2480:2736], start=(k == 0), stop=(k == 7))
                        ob = obf.next()
                        evac(ob, ob[:, 0:256], p, p[:, 0:256])
                        K.dma("sp", na_v[r0:r0 + 128, :], ob[:, 0:256], in_buf=ob)
                K.barrier()


        def phase_swa(li, need_ctx):
            with ExitStack() as st:
                kT = K.sb(st, [64, 2, T], BF16, "kT", dma=True)
                qT = K.sb(st, [64, 4, T], BF16, "qT", dma=True)
                Va = K.sb(st, [128, 34, 2, 65], BF16, "Va", dma=True)
                mpn = K.sb(st, [128, 2, 128], F32, "mpn", dma=True)
                mpb = K.sb(st, [128, 2, 2, 128], BF16, "mpb")
                snk = K.sb(st, [128, 4], F32, "snk", dma=True)
                es_ = K.sb(st, [128, 4], F32, "es")
                K.dma("sp", kT[:], swa_kT.rearrange("(h d) t -> d h t", d=64), out_buf=kT)
                K.dma("sp", [qT[:, 0:2, :], qT[:, 2:4, :]],
                      [swa_qT[0:128, :].rearrange("(h d) t -> d h t", d=64), swa_qT[128:256, :].rearrange("(h d) t -> d h t", d=64)], out_buf=qT)
                vv = swa_v.rearrange("(j p) (h d) -> p j h d", p=128, d=64)
                K.dma("sp", [Va[:, j0:j0 + 17, h, 0:64] for h in range(2) for j0 in (0, 17)],
                      [vv[:, j0:j0 + 17, h, :] for h in range(2) for j0 in (0, 17)], out_buf=Va)
                K.op("pool", lambda e: e.memset(Va[:, :, :, 64:65], 1.0), writes=[Va])
                K.dma("sp", mpn[:], mask_pn.rearrange("w p q -> p w q"), out_buf=mpn)
                for w in range(2):
                    for g in range(2):
                        K.op("dve", lambda e: e.tensor_copy(mpb[:, w, g, :], mpn[:, w, :]), reads=[mpn], writes=[mpb])
                K.dma("sp", snk[:], swa_sink[li:li + 1, :].partition_broadcast(128), out_buf=snk)
                K.op("act", lambda e: e.activation(out=es_[:], in_=snk[:], func=AF.Exp), reads=[snk], writes=[es_])
                psS = Ring([K.ps(st, [128, 512], F32, "psS") for _ in range(3)])
                po = Ring([K.ps(st, [128, 512], F32, "po") for _ in range(4)])
                Pt = Ring([K.sb(st, [128, 2, 128], BF16, "Pt") for _ in range(4)])
                ysb = Ring([K.sb(st, [128, 256], BF16, "ysb", dma=True) for _ in range(3)])
                dn_ = Ring([K.sb(st, [128, 2], F32, "dn") for _ in range(4)])
                mi = [0]

                jobs = []

                def block(q0, tiles, yrow0):
                    for kh in range(2):
                        for ti, tl in enumerate(tiles):
                            jobs.append((q0, kh, ti, len(tiles), tl, yrow0))

                def run_jobs():
                    def issue_S(i):
                        q0, kh, ti, nt, (k0, vj, mk), yrow0 = jobs[i]
                        ps = psS.next()
                        mm(K, ps, ps[:, 0:256].rearrange("p (g q) -> p g q", g=2), kT, kT[:, kh, k0:k0 + 128], qT, qT[:, 2 * kh:2 * kh + 2, q0:q0 + 128])
                        return ps

                    ps_next = issue_S(0)
                    yb = None
                    pog = None
                    for i, (q0, kh, ti, nt, (k0, vj, mk), yrow0) in enumerate(jobs):
                        ps = ps_next
                        if i + 1 < len(jobs):
                            ps_next = issue_S(i + 1)
                        if kh == 0 and ti == 0:
                            yb = ysb.next()
                        if ti == 0:
                            pog = [po.next(), po.next()]
                        P = Pt.next()
                        K.op("act", lambda e: e.activation(out=P[:], in_=ps[:, 0:256].rearrange("p (g q) -> p g q", g=2), func=AF.Exp, scale=0.125),
                             reads=[ps], writes=[P])
                        if mk is not None:
                            mi[0] += 1
                            e_ = "dve" if mi[0] % 2 else "pool"
                            K.op(e_, lambda e: e.tensor_tensor(P[:], P[:], mpb[:, mk, :, :], op=ALU.mult), reads=[P, mpb], writes=[P])
                        for g in range(2):
                            mm(K, pog[g], pog[g][:, 0:65], P, P[:, g, :], Va, Va[:, vj, kh, :], start=(ti == 0), stop=(ti == nt - 1))
                        if ti == nt - 1:
                            for g in range(2):
                                h = 2 * kh + g
                                d_ = dn_.next()
                                K.op("dve", lambda e: e.tensor_tensor(d_[:, 0:1], pog[g][:, 64:65], es_[:, h:h + 1], op=ALU.add), reads=[pog[g], es_], writes=[d_])
                                K.op("dve", lambda e: e.reciprocal(d_[:, 1:2], d_[:, 0:1]), reads=[d_], writes=[d_])
                                K.op("dve", lambda e: e.tensor_scalar(yb[:, h * 64:(h + 1) * 64], pog[g][:, 0:64], d_[:, 1:2], None, op0=ALU.mult),
                                     reads=[pog[g], d_], writes=[yb])
                            if kh == 1:
                                K.dma("sp", ycat[yrow0:yrow0 + 128, 0:256], yb[:], in_buf=yb)

                if need_ctx:
                    for qt in range(2):
                        block(qt * 128, [(0, 0, None), (128, 1, None)], qt * 128)
                for i in range(S // 128):
                    tiles = [(0, 0, None), (128, 1, None)]
                    if i > 0:
                        tiles.append((L + (i - 1) * 128, 2 + i - 1, 0))
                    tiles.append((L + i * 128, 2 + i, None))
                    if i < S // 128 - 1:
                        tiles.append((L + (i + 1) * 128, 2 + i + 1, 1))
                    block(L + i * 128, tiles, L + i * 128)
                run_jobs()
                K.barrier()

        def phase_mla(li, need_ctx):
            sc = float(96 ** -0.5)
            with ExitStack() as st:
                kT = K.sb(st, [96, 4, T], BF16, "kT", dma=True)
                qT = K.sb(st, [96, 4, T], BF16, "qT", dma=True)
                Va = K.sb(st, [128, 34, 4, 65], BF16, "Va", dma=True)
                K.dma("sp", [kT[:, h, :] for h in range(4)], [mla_kT[h * 96:(h + 1) * 96, :] for h in range(4)], out_buf=kT)
                K.dma("sp", [qT[:, h, :] for h in range(4)], [mla_qT[h * 96:(h + 1) * 96, :] for h in range(4)], out_buf=qT)
                vv = mla_v.rearrange("(j p) (h d) -> p j h d", p=128, d=64)
                K.dma("sp", [Va[:, j0:j0 + 17, h, 0:64] for h in range(4) for j0 in (0, 17)],
                      [vv[:, j0:j0 + 17, h, :] for h in range(4) for j0 in (0, 17)], out_buf=Va)
                K.op("pool", lambda e: e.memset(Va[:, :, :, 64:65], 1.0), writes=[Va])
                psS = Ring([K.ps(st, [128, 512], F32, "psS") for _ in range(3)])
                po = [K.ps(st, [128, 512], F32, "po") for _ in range(4)]
                Pt = Ring([K.sb(st, [128, 512], BF16, "Pt") for _ in range(4)])
                ysb = Ring([K.sb(st, [128, 256], BF16, "ysb", dma=True) for _ in range(8)])
                dn_ = Ring([K.sb(st, [128, 2], F32, "dn") for _ in range(4)])

                def group(q0, n, ktiles, yrow0):
                    nq = n // 128
                    ybs = [ysb.next() for _ in range(nq)]
                    seq = [(h, ti, kt) for h in range(4) for ti, kt in enumerate(ktiles)]

                    def issue_S(i):
                        h, ti, kt = seq[i]
                        ps = psS.next()
                        mm(K, ps, ps[:, 0:n], kT, kT[:, h, kt * 128:(kt + 1) * 128], qT, qT[:, h, q0:q0 + n])
                        return ps

                    ps_next = issue_S(0)
                    for i, (h, ti, kt) in enumerate(seq):
                        ps = ps_next
                        if i + 1 < len(seq):
                            ps_next = issue_S(i + 1)
                        P = Pt.next()
                        K.op("act", lambda e: e.activation(out=P[:, 0:n], in_=ps[:, 0:n], func=AF.Exp, scale=sc), reads=[ps], writes=[P])
                        for qt in range(nq):
                            mm(K, po[qt], po[qt][:, 0:65], P, P[:, qt * 128:(qt + 1) * 128], Va, Va[:, kt, h, :],
                               start=(ti == 0), stop=(ti == len(ktiles) - 1))
                        if ti == len(ktiles) - 1:
                            for qt in range(nq):
                                d_ = dn_.next()
                                K.op("dve", lambda e: e.reciprocal(d_[:, 1:2], po[qt][:, 64:65]), reads=[po[qt]], writes=[d_])
                                K.op("dve", lambda e: e.tensor_scalar(ybs[qt][:, h * 64:(h + 1) * 64], po[qt][:, 0:64], d_[:, 1:2], None, op0=ALU.mult),
                                     reads=[po[qt], d_], writes=[ybs[qt]])
                    for qt in range(nq):
                        K.dma("sp", ycat[yrow0 + qt * 128:yrow0 + (qt + 1) * 128, 512:768], ybs[qt][:], in_buf=ybs[qt])

                if need_ctx:
                    group(0, 256, [0, 1], 0)
                for qg in range(S // 512):
                    group(L + qg * 512, 512, list(range(34)), L + qg * 512)
                K.barrier()


        def phase_na(li, need_ctx):
            with ExitStack() as st:
                kT = K.sb(st, [64, 4, T], BF16, "kT", dma=True)
                qT = K.sb(st, [64, 4, T], BF16, "qT", dma=True)
                Va = K.sb(st, [128, 34, 4, 65], BF16, "Va", dma=True)
                Vs = K.sb(st, [128, 31, 4, 65], BF16, "Vs", dma=True)
                TA = K.sb(st, [128, 4, 15, 64], BF16, "TA")
                for (dst, src) in ((kT, na_kT), (qT, na_qT)):
                    K.dma("sp", [dst[:, 0:2, :], dst[:, 2:4, :]],
                          [src[0:128, :].rearrange("(h d) t -> d h t", d=64), src[128:256, :].rearrange("(h d) t -> d h t", d=64)], out_buf=dst)
                vv = na_v.rearrange("(j p) (h d) -> p j h d", p=128, d=64)
                K.dma("sp", [Va[:, j0:j0 + 17, h, 0:64] for h in range(4) for j0 in (0, 17)],
                      [vv[:, j0:j0 + 17, h, :] for h in range(4) for j0 in (0, 17)], out_buf=Va)
                K.op("pool", lambda e: e.memset(Va[:, :, :, 64:65], 1.0), writes=[Va])
                vs = na_v[L + 64:L + 64 + 31 * 128, :].rearrange("(j p) (h d) -> p j h d", p=128, d=64)
                K.dma("sp", [Vs[:, :, h, 0:64] for h in range(4)], [vs[:, :, h, :] for h in range(4)], out_buf=Vs)
                K.op("pool", lambda e: e.memset(Vs[:, :, :, 64:65], 1.0), writes=[Vs])
                with ExitStack() as st2:
                    TAr = K.sb(st2, [128, 4, 15, 64], F32, "TAr", dma=True)
                    okm = K.sb(st2, [128, 64], F32, "okm", dma=True)
                    K.op("dve", lambda e: e.memset(TAr[:], 0.0), writes=[TAr])
                    K.dma("sp", [TAr[0:64, :, :, :].rearrange("p h r q -> p (h r q)"), TAr[64:128, :, 0:14, :].rearrange("p h r q -> p h (r q)")],
                          [rpbx[li].rearrange("p h r q -> p (h r q)"), rpbx[li][:, :, 1:15, :].rearrange("p h r q -> p h (r q)")], out_buf=TAr)
                    K.dma("sp", okm[:], okm_in[:, :], out_buf=okm)
                    K.op("act", lambda e: e.activation(out=TAr[:], in_=TAr[:], func=AF.Exp), reads=[TAr], writes=[TAr])
                    for h in range(4):
                        for r_ in range(15):
                            K.op("pool", lambda e: e.tensor_tensor(TA[:, h, r_, :], TAr[:, h, r_, :], okm[:], op=ALU.mult), reads=[TAr, okm], writes=[TA])
                    K.barrier()
                psS = Ring([K.ps(st, [128, 512], F32, "psS") for _ in range(3)])
                po = Ring([K.ps(st, [128, 512], F32, "po") for _ in range(3)])
                P6r = Ring([K.sb(st, [128, 6, 4, 64], BF16, "P6") for _ in range(3)])
                Pf = Ring([K.sb(st, [128, 4, 64], F32, "Pf") for _ in range(3)])
                yrow = Ring([K.sb(st, [64, 256], BF16, "yrow", dma=True) for _ in range(3)])
                ysb = Ring([K.sb(st, [128, 256], BF16, "ysb", dma=True) for _ in range(2)])
                Pc = Ring([K.sb(st, [128, 128], BF16, "Pc") for _ in range(3)])
                dn_ = Ring([K.sb(st, [128, 2], F32, "dn") for _ in range(4)])
                mi = [0]
                if need_ctx:
                    for qt in range(2):
                        yb = ysb.next()
                        for h in range(4):
                            pq = po.next()
                            for kt in range(2):
                                ps = psS.next()
                                mm(K, ps, ps[:, 0:128], kT, kT[:, h, kt * 128:(kt + 1) * 128], qT, qT[:, h, qt * 128:(qt + 1) * 128])
                                P = Pc.next()
                                K.op("act", lambda e: e.activation(out=P[:], in_=ps[:, 0:128], func=AF.Exp, scale=0.125), reads=[ps], writes=[P])
                                mm(K, pq, pq[:, 0:65], P, P[:], Va, Va[:, kt, h, :], start=(kt == 0), stop=(kt == 1))
                            d_ = dn_.next()
                            K.op("dve", lambda e: e.reciprocal(d_[:, 1:2], pq[:, 64:65]), reads=[pq], writes=[d_])
                            K.op("dve", lambda e: e.tensor_scalar(yb[:, h * 64:(h + 1) * 64], pq[:, 0:64], d_[:, 1:2], None, op0=ALU.mult),
                                 reads=[pq, d_], writes=[yb])
                        K.dma("sp", ycat[qt * 128:(qt + 1) * 128, 768:1024], yb[:], in_buf=yb)
                def s_stage(r):
                    rs = min(max(r - 4, 0), 56)
                    dlt = r - rs
                    q0 = L + r * 64
                    P6 = P6r.next()
                    tiles = []
                    for kt in range(4):
                        k0 = L + rs * 64 + kt * 128
                        vt = (Va, 2 + rs // 2 + kt) if rs % 2 == 0 else (Vs, (rs - 1) // 2 + kt)
                        tiles.append((k0, vt, 2 * kt - dlt + 7))
                    tiles.append((0, (Va, 0), None))
                    tiles.append((128, (Va, 1), None))
                    for ti, (k0, vt, dr0) in enumerate(tiles):
                        ps = psS.next()
                        for h in range(4):
                            mm(K, ps, ps[:, h * 64:(h + 1) * 64], kT, kT[:, h, k0:k0 + 128], qT, qT[:, h, q0:q0 + 64])
                        psv = ps[:, 0:256].rearrange("p (h q) -> p h q", h=4)
                        if dr0 is None:
                            K.op("act", lambda e: e.activation(out=P6[:, ti, :, :], in_=psv, func=AF.Exp, scale=0.125), reads=[ps], writes=[P6])
                        else:
                            pf = Pf.next()
                            K.op("act", lambda e: e.activation(out=pf[:], in_=psv, func=AF.Exp, scale=0.125), reads=[ps], writes=[pf])
                            mi[0] += 1
                            e_ = "dve" if mi[0] % 2 else "pool"
                            K.op(e_, lambda e: e.tensor_tensor(P6[:, ti, :, :], pf[:], TA[:, :, dr0, :], op=ALU.mult), reads=[pf, TA], writes=[P6])
                    return (P6, tiles, q0)

                def pv_stage(P6, tiles, q0):
                    yb = yrow.next()
                    for h in range(4):
                        pq = po.next()
                        for ti, (k0, vt, dr0) in enumerate(tiles):
                            Vb, vj = vt
                            mm(K, pq, pq[0:64, 0:65], P6, P6[:, ti, h, :], Vb, Vb[:, vj, h, :], start=(ti == 0), stop=(ti == 5))
                        d_ = dn_.next()
                        K.op("dve", lambda e: e.reciprocal(d_[0:64, 1:2], pq[0:64, 64:65]), reads=[pq], writes=[d_])
                        K.op("dve", lambda e: e.tensor_scalar(yb[:, h * 64:(h + 1) * 64], pq[0:64, 0:64], d_[0:64, 1:2], None, op0=ALU.mult),
                             reads=[pq, d_], writes=[yb])
                    K.dma("sp", ycat[q0:q0 + 64, 768:1024], yb[:], in_buf=yb)

                prev = None
                for r in range(64):
                    cur = s_stage(r)
                    if prev is not None:
                        pv_stage(*prev)
                    prev = cur
                pv_stage(*prev)
                K.barrier()

        def phase_outproj(li, with_ctx):
            G = modG[1]
            with ExitStack() as st:
                Wo = K.sb(st, [128, 8, D], BF16, "Wo")
                with ExitStack() as st2:
                    stg = [K.sb(st2, [128, D], F32, "stg", dma=True) for _ in range(3)]
                    for k in range(8):
                        sg_ = stg[k % 3]
                        K.dma("sp", sg_[:], w_out[li, k * 128:(k + 1) * 128, :], out_buf=sg_)
                        if k % 2 == 0:
                            K.op("dve", lambda e: e.tensor_copy(Wo[:, k, :], sg_[:]), reads=[sg_], writes=[Wo])
                        else:
                            K.op("act", lambda e: e.copy(Wo[:, k, :], sg_[:]), reads=[sg_], writes=[Wo])
                    K.barrier()
                xg = [K.sb(st, [128, 8, 512], F32, "xg", dma=True) for _ in range(2)]
                yt = Ring([K.sb(st, [128, D], BF16, "yt", dma=True) for _ in range(3)])
                gN = K.sb(st, [128, 64], F32, "gN", dma=True)
                K.dma("sp", gN[:], dn_norm_g[li:li + 1, :].partition_broadcast(128), out_buf=gN)
                of_ = Ring([K.sb(st, [128, 2, 256], F32, "of", dma=True) for _ in range(3)])
                zr = Ring([K.sb(st, [128, 256], F32, "z", dma=True) for _ in range(3)])
                osum = Ring([K.sb(st, [128, 256], F32, "osum") for _ in range(2)])
                sqd = Ring([K.sb(st, [128, 256], F32, "sqd") for _ in range(2)])
                scd = Ring([K.sb(st, [128, 8], F32, "scd") for _ in range(2)])
                yTs = [K.sb(st, [128, 8, 512], BF16, "yT") for _ in range(2)]
                ptb = Ring([K.ps(st, [128, 4, 128], BF16, "ptb") for _ in range(4)])
                po = Ring([K.ps(st, [128, 512], F32, "po") for _ in range(3)])
                xTv = xT.rearrange("(k p) t -> p k t", p=128)
                grps = groups_all(with_ctx)
                ev = [0]
                def stageA(gi):
                    t0, n, v = grps[gi]
                    xb = xg[gi % 2]
                    yT = yTs[gi % 2]
                    K.dma("sp", [xb[:, 0:4, 0:n], xb[:, 4:8, 0:n]], [xTv[:, 0:4, t0:t0 + n], xTv[:, 4:8, t0:t0 + n]], out_buf=xb)
                    for tt in range(n // 128):
                        y_ = yt.next()
                        r0_ = t0 + tt * 128
                        K.dma("sp", [y_[:, 0:256], y_[:, 512:1024]], [ycat[r0_:r0_ + 128, 0:256], ycat[r0_:r0_ + 128, 512:1024]], out_buf=y_)
                        o_, z_, os_, sq_, sc_ = of_.next(), zr.next(), osum.next(), sqd.next(), scd.next()
                        K.dma("sp", [o_[:, 0, :], o_[:, 1, :]], [dn_o[0, r0_:r0_ + 128, :], dn_o[1, r0_:r0_ + 128, :]], out_buf=o_)
                        K.dma("sp", z_[:], dn_zab[r0_:r0_ + 128, 0:256], out_buf=z_)
                        K.op("pool", lambda e: e.tensor_tensor(os_[:], o_[:, 0, :], o_[:, 1, :], op=ALU.add), reads=[o_], writes=[os_])
                        K.op("pool", lambda e: e.tensor_tensor(sq_[:], os_[:], os_[:], op=ALU.mult), reads=[os_], writes=[sq_])
                        K.op("dve", lambda e: e.tensor_reduce(out=sc_[:, 0:4], in_=sq_[:].rearrange("p (a b) -> p a b", b=64), axis=AX.X, op=ALU.add), reads=[sq_], writes=[sc_])
                        K.op("act", lambda e: e.activation(out=sc_[:, 4:8], in_=sc_[:, 0:4], func=AF.Sqrt, scale=1.0 / 64.0, bias=float(EPS)), reads=[sc_], writes=[sc_])
                        K.op("dve", lambda e: e.reciprocal(sc_[:, 4:8], sc_[:, 4:8]), reads=[sc_], writes=[sc_])
                        K.op("act", lambda e: e.activation(out=z_[:], in_=z_[:], func=AF.Silu), reads=[z_], writes=[z_])
                        for h in range(4):
                            K.op("dve", lambda e: e.scalar_tensor_tensor(out=os_[:, h * 64:(h + 1) * 64], in0=os_[:, h * 64:(h + 1) * 64], scalar=sc_[:, 4 + h:5 + h],
                                                                          in1=gN[:], op0=ALU.mult, op1=ALU.mult), reads=[os_, sc_, gN], writes=[os_])
                        K.op("pool", lambda e: e.tensor_tensor(y_[:, 256:512], os_[:], z_[:], op=ALU.mult), reads=[os_, z_], writes=[y_])
                        for hh in range(2):
                            p = ptb.next()
                            for kk in range(4):
                                k = hh * 4 + kk
                                tr(K, p, p[:, kk, :], y_, y_[:, k * 128:(k + 1) * 128], identb, identb[:])
                            ev[0] += 1
                            if ev[0] % 2:
                                K.op("dve", lambda e: e.tensor_copy(yT[:, hh * 4:(hh + 1) * 4, tt * 128:(tt + 1) * 128], p[:]), reads=[p], writes=[yT])
                            else:
                                K.op("act", lambda e: e.copy(yT[:, hh * 4:(hh + 1) * 4, tt * 128:(tt + 1) * 128], p[:]), reads=[p], writes=[yT])

                def stageB(gi):
                    t0, n, v = grps[gi]
                    xb = xg[gi % 2]
                    yT = yTs[gi % 2]
                    for f in range(8):
                        pf_ = po.next()
                        for k in range(8):
                            mm(K, pf_, pf_[:, 0:n], Wo, Wo[:, k, f * 128:(f + 1) * 128], yT, yT[:, k, 0:n], start=(k == 0), stop=(k == 7))
                            K.pump(2)
                        K.op("dve", lambda e: e.scalar_tensor_tensor(out=xb[:, f, 0:n], in0=pf_[:, 0:n], scalar=G[:, f, v:v + 1],
                                                                      in1=xb[:, f, 0:n], op0=ALU.mult, op1=ALU.add),
                             reads=[pf_, G, xb], writes=[xb])
                    K.dma("sp", [xTv[:, 0:4, t0:t0 + n], xTv[:, 4:8, t0:t0 + n]], [xb[:, 0:4, 0:n], xb[:, 4:8, 0:n]], in_buf=xb)

                stageA(0)
                for gi in range(len(grps)):
                    if gi + 1 < len(grps):
                        K.defer = K.pending
                        stageA(gi + 1)
                        K.defer = None
                    stageB(gi)
                    K.pump(10 ** 6)
                K.barrier()


        def phase_dn1(li):
            with ExitStack() as st:
                cwr = K.sb(st, [1, 3, 768], F32, "cwr", dma=True)
                pcw = K.ps(st, [128, 6, 3], F32, "pcw")
                cw = K.sb(st, [128, 6, 3], F32, "cw")
                K.dma("sp", cwr[0:1, :, :], dn_conv_w[li:li + 1, :, :], out_buf=cwr)
                for c in range(6):
                    for j in range(3):
                        mm(K, pcw, pcw[:, c, j:j + 1], cwr, cwr[0:1, j, c * 128:(c + 1) * 128], ones_f, ones_f[0:1, 0:1])
                K.op("dve", lambda e: e.tensor_copy(cw[:], pcw[:]), reads=[pcw], writes=[cw])
                xin = [K.sb(st, [128, T], F32, "xin", dma=True) for _ in range(2)]
                hc = [K.sb(st, [128, T], F32, "hc", dma=True) for _ in range(2)]
                otm = Ring([K.sb(st, [128, 4, 128], F32, "otm", dma=True) for _ in range(3)])
                pt = Ring([K.ps(st, [128, 512], F32, "pt") for _ in range(4)])
                ev = [0]
                for c in range(6):
                    x_ = xin[c % 2]
                    h_ = hc[c % 2]
                    K.dma("sp", x_[:], dn_qkvT[c * 128:(c + 1) * 128, :], out_buf=x_)
                    K.op("dve", lambda e: e.tensor_scalar(h_[:], x_[:], cw[:, c, 1:2], None, op0=ALU.mult), reads=[x_, cw], writes=[h_])
                    for (a, b) in ((0, L), (L, T)):
                        K.op("dve", lambda e: e.scalar_tensor_tensor(out=h_[:, a + 1:b], in0=x_[:, a:b - 1], scalar=cw[:, c, 0:1], in1=h_[:, a + 1:b],
                                                                      op0=ALU.mult, op1=ALU.add), reads=[x_, cw, h_], writes=[h_])
                        K.op("dve", lambda e: e.scalar_tensor_tensor(out=h_[:, a:b - 1], in0=x_[:, a + 1:b], scalar=cw[:, c, 2:3], in1=h_[:, a:b - 1],
                                                                      op0=ALU.mult, op1=ALU.add), reads=[x_, cw, h_], writes=[h_])
                    K.op("act", lambda e: e.activation(out=h_[:], in_=h_[:], func=AF.Silu), reads=[h_], writes=[h_])
                    K.dma("sp", dn_hT[c * 128:(c + 1) * 128, :], h_[:], in_buf=h_)
                    for j0 in range(0, 34, 4):
                        nj = min(4, 34 - j0)
                        p = pt.next()
                        for jj in range(nj):
                            tr(K, p, p[:, jj * 128:(jj + 1) * 128], h_, h_[:, (j0 + jj) * 128:(j0 + jj + 1) * 128], ident, ident[:])
                        o = otm.next()
                        ev[0] += 1
                        pv = p[:, 0:nj * 128].rearrange("p (j c) -> p j c", c=128)
                        if ev[0] % 2:
                            K.op("act", lambda e: e.copy(o[:, 0:nj, :], pv), reads=[p], writes=[o])
                        else:
                            K.op("pool", lambda e: e.tensor_copy(o[:, 0:nj, :], pv), reads=[p], writes=[o]) if False else \
                                K.op("dve", lambda e: e.tensor_copy(o[:, 0:nj, :], pv), reads=[p], writes=[o])
                        K.dma("sp", dn_h[j0 * 128:(j0 + nj) * 128, c * 128:(c + 1) * 128].rearrange("(j p) c -> p j c", p=128), o[:, 0:nj, :], in_buf=o)
                K.barrier()

        def phase_dn2(li):
            with ExitStack() as st:
                tri = K.sb(st, [128, 4, 128], F32, "tri", dma=True)
                K.dma("sp", tri[:], tri_in.rearrange("w p q -> p w q"), out_buf=tri)
                TINC = [tri[:, 0, :], tri[:, 2, :]]
                MST = [tri[:, 1, :], tri[:, 3, :]]
                onesf = K.sb(st, [128, 128], F32, "onesf")
                K.op("pool", lambda e: e.memset(onesf[:], 1.0), writes=[onesf])
                dtb = K.sb(st, [128, 8], F32, "dtb", dma=True)
                nA = K.sb(st, [128, 8], F32, "nA", dma=True)
                K.dma("sp", dtb[:], dn_dt_bias[li:li + 1, :].partition_broadcast(128), out_buf=dtb)
                K.dma("sp", nA[:], dn_a_log[li:li + 1, :].partition_broadcast(128), out_buf=nA)
                K.op("act", lambda e: e.activation(out=nA[:], in_=nA[:], func=AF.Exp), reads=[nA], writes=[nA])
                K.op("dve", lambda e: e.tensor_scalar(nA[:], nA[:], -1.0, None, op0=ALU.mult), reads=[nA], writes=[nA])
                Sst = [[[K.sb(st, [64, 64], F32, "S") for _ in range(2)] for _ in range(4)] for _ in range(2)]
                for d in range(2):
                    for h in range(4):
                        K.op("pool", lambda e: e.memset(Sst[d][h][0][:], 0.0), writes=[Sst[d][h][0]])
                NSLOT = 2
                hq = Ring([K.sb(st, [64, 4, 128], F32, "hq", dma=True) for _ in range(2 * NSLOT)])
                hk = Ring([K.sb(st, [64, 4, 128], F32, "hk", dma=True) for _ in range(2 * NSLOT)])
                htok = Ring([K.sb(st, [128, 768], F32, "htok", dma=True) for _ in range(2 * NSLOT)])
                abr = Ring([K.sb(st, [128, 16], F32, "ab", dma=True) for _ in range(2 * NSLOT)])
                scr = Ring([K.sb(st, [128, 96], F32, "sc") for _ in range(2 * NSLOT)])
                sqb = Ring([K.sb(st, [128, 512], F32, "sq") for _ in range(2)])
                otile = Ring([K.sb(st, [128, 256], F32, "ot", dma=True) for _ in range(2 * NSLOT)])
                names128 = ["gsm", "Dsm", "DTim", "Pa", "Pb", "Pta", "Ptb", "Tt", "QKm"]
                names64 = ["Xu", "Xw", "u", "Ke", "vn", "tmp"]
                W = {}
                for sl in range(NSLOT):
                    for d in range(2):
                        for h in range(4):
                            w_ = {n_: K.sb(st, [128, 128], F32, n_) for n_ in names128}
                            w_.update({n_: K.sb(st, [128, 64], F32, n_) for n_ in names64})
                            w_["wT"] = K.sb(st, [64, 128], F32, "wT")
                            W[(sl, d, h)] = w_
                pp = Ring([K.ps(st, [128, 512], F32, "ppb") for _ in range(7)])
                pg = Ring([K.ps(st, [128, 512], F32, "pgb")])
                order = [list(range(34)), [1, 0] + list(range(33, 1, -1))]
                hTv = dn_hT.rearrange("(g h d) t -> g d h t", g=3, d=64)
                ei = [0]

                def evac(dst, dst_ap, p, p_ap):
                    ei[0] += 1
                    if ei[0] % 3 == 0:
                        K.op("dve", lambda e: e.tensor_copy(dst_ap, p_ap), reads=[p], writes=[dst])
                    else:
                        K.op("act", lambda e: e.copy(dst_ap, p_ap), reads=[p], writes=[dst])

                import os as _os
                NS_ = int(_os.environ.get('DN_STEPS', '34'))

                def prepA(s_):
                    return [prepA1(s_, d) for d in range(2)]

                def prepA1(s_, d):
                    if True:
                        j = order[d][s_]
                        HQ, HK, HT, AB, sc, sq = hq.next(), hk.next(), htok.next(), abr.next(), scr.next(), sqb.next()
                        K.dma("sp", HQ[:], hTv[0, :, :, j * 128:(j + 1) * 128], out_buf=HQ)
                        K.dma("sp", HK[:], hTv[1, :, :, j * 128:(j + 1) * 128], out_buf=HK)
                        K.dma("sp", HT[:], dn_h[j * 128:(j + 1) * 128, :], out_buf=HT)
                        K.dma("sp", AB[:], dn_zab[j * 128:(j + 1) * 128, 256:272], out_buf=AB)
                        K.op("dve", lambda e: e.tensor_tensor(sq[:], HT[:, 0:512], HT[:, 0:512], op=ALU.mult), reads=[HT], writes=[sq])
                        K.op("dve", lambda e: e.tensor_reduce(out=sc[:, 0:8], in_=sq[:].rearrange("p (a b) -> p a b", b=64), axis=AX.X, op=ALU.add),
                             reads=[sq], writes=[sc])
                        K.op("act", lambda e: e.activation(out=sc[:, 8:16], in_=sc[:, 0:8], func=AF.Ln, bias=float(EPS), scale=1.0), reads=[sc], writes=[sc])
                        K.op("act", lambda e: e.activation(out=sc[:, 16:24], in_=sc[:, 8:16], func=AF.Exp, scale=-0.5), reads=[sc], writes=[sc])
                        K.op("act", lambda e: e.activation(out=sc[:, 24:32], in_=sc[:, 8:16], func=AF.Exp, scale=0.5), reads=[sc], writes=[sc])
                        K.op("dve", lambda e: e.tensor_scalar(sc[:, 16:20], sc[:, 16:20], 0.125, None, op0=ALU.mult), reads=[sc], writes=[sc])
                        K.op("dve", lambda e: e.tensor_tensor(sc[:, 32:36], AB[:, d * 4:d * 4 + 4], dtb[:, d * 4:d * 4 + 4], op=ALU.add), reads=[AB, dtb], writes=[sc])
                        K.op("act", lambda e: e.activation(out=sc[:, 32:36], in_=sc[:, 32:36], func=AF.Exp), reads=[sc], writes=[sc])
                        K.op("act", lambda e: e.activation(out=sc[:, 32:36], in_=sc[:, 32:36], func=AF.Ln, bias=1.0, scale=1.0), reads=[sc], writes=[sc])
                        K.op("dve", lambda e: e.tensor_tensor(sc[:, 36:40], sc[:, 32:36], nA[:, d * 4:d * 4 + 4], op=ALU.mult), reads=[sc, nA], writes=[sc])
                        K.op("act", lambda e: e.activation(out=sc[:, 40:44], in_=AB[:, 8 + d * 4:12 + d * 4], func=AF.Exp, scale=-1.0), reads=[AB], writes=[sc])
                        K.op("dve", lambda e: e.tensor_scalar(sc[:, 40:44], sc[:, 40:44], 1.0, None, op0=ALU.add), reads=[sc], writes=[sc])
                        K.op("dve", lambda e: e.reciprocal(sc[:, 40:44], sc[:, 40:44]), reads=[sc], writes=[sc])
                        return dict(d=d, j=j, HQ=HQ, HK=HK, HT=HT, AB=AB, sc=sc)

                def prepB(ctxs, s_):
                    U = []
                    for c_ in ctxs:
                        prepB1(c_, s_, U)
                    return U

                def mulcols(sc, o0, a0, b0):
                    K.op("dve", lambda e: e.tensor_tensor(sc[:, o0:o0 + 4], sc[:, a0:a0 + 4], sc[:, b0:b0 + 4], op=ALU.mult), reads=[sc], writes=[sc])

                def prepB1(c_, s_, U):
                    sl = s_ % NSLOT
                    if True:
                        d, j, HQ, HK, HT, AB, sc = c_['d'], c_['j'], c_['HQ'], c_['HK'], c_['HT'], c_['AB'], c_['sc']
                        pg_ = pg.next()
                        mm(K, pg_, pg_[:, 0:4], tri, TINC[d], sc, sc[:, 36:40])
                        mm(K, pg_, pg_[:, 4:8], onesf, onesf[:], sc, sc[:, 36:40])
                        K.op("dve", lambda e: e.tensor_copy(sc[:, 44:52], pg_[:, 0:8]), reads=[pg_], writes=[sc])
                        K.op("act", lambda e: e.activation(out=sc[:, 52:56], in_=sc[:, 44:48], func=AF.Exp), reads=[sc], writes=[sc])
                        K.op("dve", lambda e: e.tensor_tensor(sc[:, 56:60], sc[:, 48:52], sc[:, 44:48], op=ALU.subtract), reads=[sc], writes=[sc])
                        K.op("act", lambda e: e.activation(out=sc[:, 56:60], in_=sc[:, 56:60], func=AF.Exp), reads=[sc], writes=[sc])
                        K.op("act", lambda e: e.activation(out=sc[:, 60:64], in_=sc[:, 48:52], func=AF.Exp), reads=[sc], writes=[sc])
                        for (o0, a0, b0) in ((68, 20, 40), (64, 68, 20), (72, 64, 52), (76, 20, 56), (80, 52, 16)):
                            mulcols(sc, o0, a0, b0)
                        K.op("dve", lambda e: e.tensor_scalar(sc[:, 84:88], sc[:, 28:32], -1.0, None, op0=ALU.mult), reads=[sc], writes=[sc])
                        OT = otile.next()
                        for h in range(4):
                            U.append(dict(d=d, h=h, j=j, HQ=HQ, HK=HK, HT=HT, sc=sc, W=W[(sl, d, h)], OT=OT,
                                          S0=Sst[d][h][s_ % 2], S1=Sst[d][h][(s_ + 1) % 2]))

                U_next = prepB(prepA(0), 0)
                _pp_next = pp.next

                def _pp_pump():
                    K.pump(1)
                    return _pp_next()
                pp.next = _pp_pump
                for s_ in range(NS_):
                    sl = s_ % NSLOT
                    U = U_next
                    ctxA = None
                    if s_ + 1 < NS_:
                        K.defer = K.pending
                        U_next = prepB(prepA(s_ + 1), s_ + 1)
                        K.defer = None

                    def col(u, c0):
                        return u["sc"][:, c0 + u["h"]:c0 + u["h"] + 1]

                    _stg = int(_os.environ.get('DN_STAGE', '99'))
                    if _stg >= 1:
                        for u in U:
                            w_ = u["W"]
                            K.op("dve", lambda e: e.tensor_scalar(w_["gsm"][:], MST[u["d"]], col(u, 36), None, op0=ALU.mult), reads=[tri, u["sc"]], writes=[w_["gsm"]])
                    if _stg >= 2:
                        for u in U:
                            w_ = u["W"]
                            pb_ = pp.next()
                            p1, p2 = pb_, pb_
                            mm(K, p1, p1[:, 0:128], tri, TINC[u["d"]], w_["gsm"], w_["gsm"][:])
                            mm(K, p2, p2[:, 128:256], w_["gsm"], w_["gsm"][:], tri, TINC[u["d"]])
                            K.op("act", lambda e: e.activation(out=w_["Dsm"][:], in_=p1[:, 0:128], func=AF.Exp), reads=[p1], writes=[w_["Dsm"]])
                            K.op("act", lambda e: e.activation(out=w_["DTim"][:], in_=p2[:, 128:256], func=AF.Exp), reads=[p2], writes=[w_["DTim"]])
                            K.op("pool", lambda e: e.tensor_tensor(w_["Dsm"][:], w_["Dsm"][:], MST[u["d"]], op=ALU.mult), reads=[w_["Dsm"], tri], writes=[w_["Dsm"]])
                            K.op("pool", lambda e: e.tensor_tensor(w_["DTim"][:], w_["DTim"][:], TINC[u["d"]], op=ALU.mult), reads=[w_["DTim"], tri], writes=[w_["DTim"]])
                    if _stg >= 3:
                        for u in U:
                            w_ = u["W"]
                            h = u["h"]
                            pb_ = pp.next()
                            p1, p2 = pb_, pb_
                            mm(K, p1, p1[:, 0:128], u["HK"], u["HK"][:, h, :], u["HK"], u["HK"][:, h, :])
                            mm(K, p2, p2[:, 128:256], u["HK"], u["HK"][:, h, :], u["HQ"], u["HQ"][:, h, :])
                            K.op("dve", lambda e: e.scalar_tensor_tensor(out=w_["Pa"][:], in0=p1[:, 0:128], scalar=col(u, 64), in1=w_["Dsm"][:], op0=ALU.mult, op1=ALU.mult),
                                 reads=[p1, u["sc"], w_["Dsm"]], writes=[w_["Pa"]])
                            K.op("dve", lambda e: e.scalar_tensor_tensor(out=w_["QKm"][:], in0=p2[:, 128:256], scalar=col(u, 20), in1=w_["DTim"][:], op0=ALU.mult, op1=ALU.mult),
                                 reads=[p2, u["sc"], w_["DTim"]], writes=[w_["QKm"]])
                    if _stg >= 4:
                        for u in U:
                            w_ = u["W"]
                            p1 = pp.next()
                            tr(K, p1, p1[:, 0:128], w_["Pa"], w_["Pa"][:], ident, ident[:])
                            evac(w_["Pta"], w_["Pta"][:], p1, p1[:, 0:128])
                            K.op("pool", lambda e: e.tensor_tensor(w_["Tt"][:], ident[:], w_["Pta"][:], op=ALU.subtract), reads=[ident, w_["Pta"]], writes=[w_["Tt"]])
                            u["P"], u["Pt"], u["Pn"], u["Ptn"] = w_["Pa"], w_["Pta"], w_["Pb"], w_["Ptb"]
                    if _stg >= 5:
                        for lev in range(1, 7):
                            for u in U:
                                pb_ = pp.next()
                                mm(K, pb_, pb_[:, 0:128], u["Pt"], u["Pt"][:], u["P"], u["P"][:])
                                if lev < 6:
                                    mm(K, pb_, pb_[:, 128:256], u["P"], u["P"][:], u["Pt"], u["Pt"][:])
                                K.op("act", lambda e: e.copy(u["Pn"][:], pb_[:, 0:128]), reads=[pb_], writes=[u["Pn"]])
                                if lev < 6:
                                    K.op("act", lambda e: e.copy(u["Ptn"][:], pb_[:, 128:256]), reads=[pb_], writes=[u["Ptn"]])
                            for g0 in range(0, len(U), 4):
                                pb_ = pp.next()
                                for gi_, u in enumerate(U[g0:g0 + 4]):
                                    w_ = u["W"]
                                    mm(K, pb_, pb_[:, gi_ * 128:(gi_ + 1) * 128], u["Pn"], u["Pn"][:], w_["Tt"], w_["Tt"][:])
                                for gi_, u in enumerate(U[g0:g0 + 4]):
                                    w_ = u["W"]
                                    K.op("dve", lambda e: e.tensor_tensor(w_["Tt"][:], w_["Tt"][:], pb_[:, gi_ * 128:(gi_ + 1) * 128], op=ALU.add), reads=[w_["Tt"], pb_], writes=[w_["Tt"]])
                                    u["P"], u["Pn"] = u["Pn"], u["P"]
                                    u["Pt"], u["Ptn"] = u["Ptn"], u["Pt"]
                    if _stg >= 6:
                        for u in U:
                            w_ = u["W"]
                            h = u["h"]
                            HT = u["HT"]
                            _m8 = int(_os.environ.get('DN_S8', '31'))
                            if _m8 & 1:
                                K.op("pool", lambda e: e.tensor_scalar(w_["Xu"][:], HT[:, 512 + h * 64:512 + (h + 1) * 64], col(u, 68), None, op0=ALU.mult), reads=[HT, u["sc"]], writes=[w_["Xu"]])
                                K.op("pool", lambda e: e.tensor_scalar(w_["Xw"][:], HT[:, 256 + h * 64:256 + (h + 1) * 64], col(u, 72), None, op0=ALU.mult), reads=[HT, u["sc"]], writes=[w_["Xw"]])
                                K.op("pool", lambda e: e.tensor_scalar(w_["Ke"][:], HT[:, 256 + h * 64:256 + (h + 1) * 64], col(u, 76), None, op0=ALU.mult), reads=[HT, u["sc"]], writes=[w_["Ke"]])
                            pb_ = pp.next()
                            p1, p2 = pb_, pb_
                            if _m8 & 2:
                                mm(K, p1, p1[:, 0:64], w_["Tt"], w_["Tt"][:], w_["Xu"], w_["Xu"][:])
                            if _m8 & 4:
                                mm(K, p2, p2[0:64, 128:256], w_["Xw"], w_["Xw"][:], w_["Tt"], w_["Tt"][:])
                            if _m8 & 8:
                                K.op("act", lambda e: e.activation(out=w_["u"][:], in_=p1[:, 0:64], func=AF.Identity, scale=col(u, 28)), reads=[p1, u["sc"]], writes=[w_["u"]])
                            if _m8 & 16:
                                K.op("act", lambda e: e.copy(w_["wT"][:], p2[0:64, 128:256]), reads=[p2], writes=[w_["wT"]])
                    if _stg >= 7:
                        for u in U:
                            w_ = u["W"]
                            p1 = pp.next()
                            mm(K, p1, p1[:, 0:64], w_["wT"], w_["wT"][:], u["S0"], u["S0"][:])
                            K.op("dve", lambda e: e.scalar_tensor_tensor(out=w_["vn"][:], in0=p1[:, 0:64], scalar=col(u, 84), in1=w_["u"][:], op0=ALU.mult, op1=ALU.add),
                                 reads=[p1, u["sc"], w_["u"]], writes=[w_["vn"]])
                        for u in U:
                            w_ = u["W"]
                            h = u["h"]
                            pb_ = pp.next()
                            p1, p2, p3 = pp.next(), pb_, pb_
                            mm(K, p1, p1[:, 0:64], u["HQ"], u["HQ"][:, h, :], u["S0"], u["S0"][:])
                            mm(K, p2, p2[:, 128:192], w_["QKm"], w_["QKm"][:], w_["vn"], w_["vn"][:])
                            mm(K, p3, p3[0:64, 256:320], w_["Ke"], w_["Ke"][:], w_["vn"], w_["vn"][:])
                            K.op("act", lambda e: e.activation(out=w_["tmp"][:], in_=p1[:, 0:64], func=AF.Identity, scale=col(u, 80)), reads=[p1, u["sc"]], writes=[w_["tmp"]])
                            K.op("dve", lambda e: e.scalar_tensor_tensor(out=u["OT"][:, h * 64:(h + 1) * 64], in0=p2[:, 128:192], scalar=col(u, 16), in1=w_["tmp"][:], op0=ALU.mult, op1=ALU.add),
                                 reads=[p2, u["sc"], w_["tmp"]], writes=[u["OT"]])
                            K.op("dve", lambda e: e.scalar_tensor_tensor(out=u["S1"][:], in0=u["S0"][:], scalar=u["sc"][0:64, 60 + h:61 + h], in1=p3[0:64, 256:320], op0=ALU.mult, op1=ALU.add),
                                 reads=[u["S0"], u["sc"], p3], writes=[u["S1"]])
                    for u in U:
                        if u["h"] == 3:
                            j = u["j"]
                            K.dma("sp", dn_o[u["d"], j * 128:(j + 1) * 128, :], u["OT"][:], in_buf=u["OT"])
                    K.pump(10 ** 6)
                K.barrier()

        def phase_dn3(li, need_ctx):
            with ExitStack() as st:
                gN = K.sb(st, [128, 64], F32, "gN", dma=True)
                K.dma("sp", gN[:], dn_norm_g[li:li + 1, :].partition_broadcast(128), out_buf=gN)
                of_ = Ring([K.sb(st, [128, 2, 256], F32, "of", dma=True) for _ in range(2)])
                zr = Ring([K.sb(st, [128, 256], F32, "z", dma=True) for _ in range(2)])
                osum = K.sb(st, [128, 256], F32, "osum")
                sq = K.sb(st, [128, 256], F32, "sq")
                sc = K.sb(st, [128, 8], F32, "sc")
                yb = Ring([K.sb(st, [128, 256], BF16, "yb", dma=True) for _ in range(2)])
                for j in range(0 if need_ctx else 2, 34):
                    o_, z_, y_ = of_.next(), zr.next(), yb.next()
                    K.dma("sp", [o_[:, 0, :], o_[:, 1, :]], [dn_o[0, j * 128:(j + 1) * 128, :], dn_o[1, j * 128:(j + 1) * 128, :]], out_buf=o_)
                    K.dma("sp", z_[:], dn_zab[j * 128:(j + 1) * 128, 0:256], out_buf=z_)
                    K.op("dve", lambda e: e.tensor_tensor(osum[:], o_[:, 0, :], o_[:, 1, :], op=ALU.add), reads=[o_], writes=[osum])
                    K.op("pool", lambda e: e.tensor_tensor(sq[:], osum[:], osum[:], op=ALU.mult), reads=[osum], writes=[sq])
                    K.op("dve", lambda e: e.tensor_reduce(out=sc[:, 0:4], in_=sq[:].rearrange("p (a b) -> p a b", b=64), axis=AX.X, op=ALU.add), reads=[sq], writes=[sc])
                    K.op("act", lambda e: e.activation(out=sc[:, 4:8], in_=sc[:, 0:4], func=AF.Sqrt, scale=1.0 / 64.0, bias=float(EPS)), reads=[sc], writes=[sc])
                    K.op("dve", lambda e: e.reciprocal(sc[:, 4:8], sc[:, 4:8]), reads=[sc], writes=[sc])
                    K.op("act", lambda e: e.activation(out=z_[:], in_=z_[:], func=AF.Silu), reads=[z_], writes=[z_])
                    for h in range(4):
                        K.op("dve", lambda e: e.scalar_tensor_tensor(out=osum[:, h * 64:(h + 1) * 64], in0=osum[:, h * 64:(h + 1) * 64], scalar=sc[:, 4 + h:5 + h],
                                                                      in1=gN[:], op0=ALU.mult, op1=ALU.mult), reads=[osum, sc, gN], writes=[osum])
                    K.op("dve", lambda e: e.tensor_tensor(y_[:], osum[:], z_[:], op=ALU.mult), reads=[osum, z_], writes=[y_])
                    K.dma("sp", ycat[j * 128:(j + 1) * 128, 256:512], y_[:], in_buf=y_)
                K.barrier()

        def phase_final():
            with ExitStack() as st:
                xg = [K.sb(st, [128, 8, 512], F32, "xg", dma=True) for _ in range(2)]
                xn = K.sb(st, [128, 8, 512], BF16, "xn")
                rstd = K.sb(st, [128, 512], F32, "rstd")
                fg = K.sb(st, [128, 8], F32, "fg")
                yo = [K.sb(st, [128, D], F32, "yo", dma=True) for _ in range(2)]
                pss = K.ps(st, [128, 512], F32, "pss")
                pt = [K.ps(st, [128, 512], F32, "pt") for _ in range(4)]
                xTv = xT.rearrange("(k p) t -> p k t", p=128)
                K.op("dve", lambda e: e.tensor_scalar(fg[:], fin_g[:], float(np.sqrt(D)), None, op0=ALU.mult), reads=[fin_g], writes=[fg])
                grps = groups_all(False)
                cnt = 0
                def ldf(gi):
                    t0_, n_, v_ = grps[gi]
                    xb_ = xg[gi % 2]
                    K.dma("sp", [xb_[:, 0:4, :], xb_[:, 4:8, :]], [xTv[:, 0:4, t0_:t0_ + n_], xTv[:, 4:8, t0_:t0_ + n_]], out_buf=xb_)

                ldf(0)
                for gi, (t0, n, v) in enumerate(grps):
                    xb = xg[gi % 2]
                    if gi + 1 < len(grps):
                        ldf(gi + 1)
                    K.op("act", lambda e: e.activation(out=xn[:], in_=xb[:], func=AF.Square), reads=[xb], writes=[xn])
                    for k in range(8):
                        mm(K, pss, pss[:], ones_b, ones_b[:], xn, xn[:, k, :], start=(k == 0), stop=(k == 7))
                    K.op("act", lambda e: e.activation(out=rstd[:], in_=pss[:], func=AF.Sqrt, scale=1.0, bias=float(D * EPS)), reads=[pss], writes=[rstd])
                    K.op("dve", lambda e: e.reciprocal(rstd[:], rstd[:]), reads=[rstd], writes=[rstd])
                    for k in range(8):
                        e_ = "dve"
                        K.op(e_, lambda e: e.scalar_tensor_tensor(out=xb[:, k, :], in0=xb[:, k, :], scalar=fg[:, k:k + 1],
                                                                   in1=rstd[:], op0=ALU.mult, op1=ALU.mult),
                             reads=[xb, fg, rstd], writes=[xb])
                    for tt in range(4):
                        o = yo[cnt % 2]
                        for hh in range(2):
                            p = pt[(2 * cnt + hh) % 4]
                            for kk in range(4):
                                k = hh * 4 + kk
                                tr(K, p, p[:, kk * 128:(kk + 1) * 128], xb, xb[:, k, tt * 128:(tt + 1) * 128], ident, ident[:])
                            if hh == 0:
                                K.op("act", lambda e: e.copy(o[:, 0:512], p[:]), reads=[p], writes=[o])
                            else:
                                K.op("dve", lambda e: e.tensor_copy(o[:, 512:1024], p[:]), reads=[p], writes=[o])
                        r0 = t0 - L + tt * 128
                        K.dma("sp", y_out[r0:r0 + 128, :], o[:], in_buf=o)
                        cnt += 1
                K.barrier()

        if only is not None:
            name, li, need_ctx = only
            if name == "dn":
                K.dma("sp", ident[:], ident_in[:, :], out_buf=ident)
                import os
                sub = os.environ.get("DN_SUB", "123")
                if "1" in sub:
                    phase_dn1(li)
                if "2" in sub:
                    phase_dn2(li)
                if "3" in sub:
                    phase_dn3(li, need_ctx)
            else:
                {"swa": phase_swa, "mla": phase_mla, "na": phase_na}[name](li, need_ctx)
            K.finish()
            return nc
        phase_init()
        for li in range(DEPTH):
            need_ctx = li < DEPTH - 1
            phase_mod(li)
            phase_ffn(ffn1_wg[li], ffn1_wu[li], ffn1_wd[li], 0, with_ctx=True)
            if stop_after == "ffn1":
                break
            phase_inproj(li)
            if stop_after == "inproj":
                break
            phase_swa(li, need_ctx)
            phase_mla(li, need_ctx)
            phase_na(li, need_ctx)
            phase_dn1(li)
            phase_dn2(li)
            phase_outproj(li, need_ctx)
            phase_ffn(ffn2_wg[li], ffn2_wu[li], ffn2_wd[li], 2, with_ctx=need_ctx)
        phase_final()
        K.finish()
    return nc


INPUT_NAMES = ["ada_w", "ada_b", "norm1_g", "ffn1_wg", "ffn1_wu", "ffn1_wd", "norm2_g", "norm3_g",
               "ffn2_wg", "ffn2_wu", "ffn2_wd", "w_in", "w_out", "swa_sink", "mla_q_norm_g", "mla_w_uq",
               "mla_kv_norm_g", "mla_w_ukv", "dn_conv_w", "dn_norm_g"]


def host_consts():
    theta = 10000.0
    tpos = np.arange(S)
    row = (tpos // 64).astype(np.float64)
    col = (tpos % 64).astype(np.float64)

    def tab(nrows, qw, nfreq_total):
        c = np.ones((nrows, T), np.float64)
        s_ = np.zeros((nrows, T), np.float64)
        for d in range(nrows):
            dd = d % (4 * qw)
            q, j = dd // qw, dd % qw
            inv = theta ** (-(2.0 * j) / (2 * qw))
            pos = row if q < 2 else col
            c[d, L:] = np.cos(pos * inv)
            s_[d, L:] = np.sin(pos * inv)
        return np.stack([c, s_]).astype(np.float32)

    t64 = tab(128, 16, 16)
    t32 = tab(32, 8, 8)
    t96 = np.concatenate([np.stack([np.ones((64, T)), np.zeros((64, T))]).astype(np.float32), t32], axis=1)
    kp = np.arange(128)[:, None]
    qq = np.arange(128)[None, :]
    mask_pn = np.stack([(qq <= kp), (kp <= qq)]).astype(np.float32)
    qc = np.arange(64)[None, :]
    kc = np.arange(64)[:, None]
    cs = np.clip(qc - 8, 0, 48)
    ok = ((kc >= cs) & (kc < cs + 16)).astype(np.float32)
    okm = np.concatenate([ok, ok], 0)
    a_ = np.arange(128)[:, None]
    b_ = np.arange(128)[None, :]
    tri = np.stack([a_ <= b_, a_ > b_, a_ >= b_, a_ < b_]).astype(np.float32)
    return {"tri": tri, "okm": okm, "ident": np.eye(128, dtype=np.float32), "tab64": t64, "tab96": np.ascontiguousarray(t96), "tab32": t32,
            "mask_pn": mask_pn}


def make_in_maps(inputs):
    consts = host_consts()
    idx = np.clip(np.arange(64)[:, None] - np.arange(64)[None, :] + 15, 0, 30)
    rpbx = np.ascontiguousarray(np.transpose(inputs["na_rpb"][:, :, :, idx], (0, 3, 1, 2, 4)))
    maps = []
    for b in range(8):
        m = {
            "x": np.ascontiguousarray(inputs["x"][b]),
            "c": np.ascontiguousarray(inputs["c"][b:b + 1]),
            "ctx": np.ascontiguousarray(inputs["ctx"][b]),
            "c_ctx": np.ascontiguousarray(inputs["c_ctx"][None, :]),
            "final_norm_g": np.ascontiguousarray(inputs["final_norm_g"][None, :]),
        }
        m.update(consts)
        m["rpbx"] = rpbx
        m["dn_a_log"] = np.ascontiguousarray(inputs["dn_a_log"].reshape(DEPTH, 8))
        m["dn_dt_bias"] = np.ascontiguousarray(inputs["dn_dt_bias"].reshape(DEPTH, 8))
        for n in INPUT_NAMES:
            m[n] = np.ascontiguousarray(inputs[n])
        maps.append(m)
    return maps


def kernel(**inputs):
    inputs = {k: np.asarray(v) for k, v in inputs.items()}
    nc = build_program()
    res = run_bass_kernel_spmd(nc, make_in_maps(inputs), core_ids=list(range(8)))
    return np.stack([r["y"] for r in res.results], axis=0).astype(np.float32)
```

```python
import types
import numpy as np
from contextlib import ExitStack
import concourse.bass as bass
import concourse.mybir as mybir
from concourse.bass_utils import run_bass_kernel_spmd

F32 = mybir.dt.float32
BF16 = mybir.dt.bfloat16
AF = mybir.ActivationFunctionType
ALU = mybir.AluOpType
AX = mybir.AxisListType

D = 1024
S = 4096
L = 256
T = S + L
DFF = 2816
NFF = DFF // 128
DEPTH = 2
EPS = 1e-6
INP = 2736


class Buf:
    def __init__(self, t, name):
        self.t = t
        self.name = name
        self.w = None
        self.r = {}
        self.dsem = None

    def __getitem__(self, idx):
        return self.t[idx]


class KB:
    def __init__(self, nc, es, n_dma_sems=64):
        self.nc = nc
        self.engs = {"pe": nc.tensor, "act": nc.scalar, "dve": nc.vector,
                     "pool": nc.gpsimd, "sp": nc.sync}
        self.sem = {e: es.enter_context(nc.semaphore("se_" + e)) for e in self.engs}
        self.cnt = {e: 0 for e in self.engs}
        self.seen = {e: {} for e in self.engs}
        self.latest = {}
        self.free = [[es.enter_context(nc.semaphore("sd%d" % i)), 0] for i in range(n_dma_sems)]
        self.uid = 0
        self.rr = 0
        self.defer = None
        self.pending = []

    def sb(self, st, shape, dt, name=None, dma=False):
        self.uid += 1
        name = "%s_%d" % (name or "b", self.uid)
        t = st.enter_context(self.nc.sbuf_tensor(name, list(shape), dt))
        b = Buf(t, name)
        if dma:
            b.dsem = self.free.pop()
            st.callback(lambda b=b: self.free.append(b.dsem))
        return b

    def ps(self, st, shape, dt=F32, name=None):
        self.uid += 1
        name = "%s_%d" % (name or "p", self.uid)
        t = st.enter_context(self.nc.psum_tensor(name, list(shape), dt))
        return Buf(t, name)

    def _wait(self, e, tok):
        sem, val, key = tok
        if self.seen[e].get(key, 0) >= val:
            return
        self.engs[e].wait_ge(sem, val)
        self.seen[e][key] = val

    def _deps(self, e, reads, writes):
        toks = []
        for b in reads:
            if b is not None and b.w is not None:
                toks.append(b.w)
        for b in writes:
            if b is None:
                continue
            if b.w is not None:
                toks.append(b.w)
            toks.extend(b.r.values())
        for tok in toks:
            if e == "pe" and tok[2] == "se_pe":
                continue
            self._wait(e, tok)

    def pump(self, n=1):
        q = self.pending
        d, self.defer = self.defer, None
        while n > 0 and q:
            item = q.pop(0)
            if item[0] == "op":
                self.op(*item[1:])
            else:
                self.dma(*item[1], **item[2])
            n -= 1
        self.defer = d

    def op(self, e, fn, reads=(), writes=()):
        if self.defer is not None:
            if fn.__closure__:
                fn = types.FunctionType(fn.__code__, fn.__globals__, fn.__name__, fn.__defaults__,
                                        tuple(types.CellType(c.cell_contents) for c in fn.__closure__))
            self.defer.append(("op", e, fn, list(reads), list(writes)))
            return None
        self._deps(e, reads, writes)
        ins = fn(self.engs[e])
        self.cnt[e] += 1
        ins.then_inc(self.sem[e], 1)
        key = "se_" + e
        tok = (self.sem[e], self.cnt[e], key)
        self.latest[key] = tok
        for b in reads:
            if b is not None:
                b.r[e] = tok
        for b in writes:
            if b is not None:
                b.w = tok
                b.r = {}
        return tok

    def dma(self, q, out_ap, in_ap, out_buf=None, in_buf=None, n=1, fn=None):
        if self.defer is not None:
            self.defer.append(("dma", (q, out_ap, in_ap), dict(out_buf=out_buf, in_buf=in_buf)))
            return None
        sb = out_buf if out_buf is not None else in_buf
        assert sb is not None and sb.dsem is not None, "dma needs an SBUF buf with dsem"
        self._deps(q, [in_buf] if in_buf is not None else [], [out_buf] if out_buf is not None else [])
        pairs = list(zip(out_ap, in_ap)) if isinstance(out_ap, (list, tuple)) else [(out_ap, in_ap)]
        for o, i in pairs:
            self.engs[q].dma_start(out=o, in_=i).then_inc(sb.dsem[0], 16)
            sb.dsem[1] += 16
        key = "sd_" + str(id(sb.dsem))
        tok = (sb.dsem[0], sb.dsem[1], key)
        self.latest[key] = tok
        if out_buf is not None:
            out_buf.w = tok
            out_buf.r = {}
        else:
            in_buf.r["dma_" + q] = tok
        return tok

    def barrier(self):
        for e in self.engs:
            for tok in list(self.latest.values()):
                self._wait(e, tok)

    def finish(self):
        self.barrier()


def mm(K, out, out_ap, lhs, lhs_ap, rhs, rhs_ap, start=True, stop=True):
    return K.op("pe", lambda e: e.matmul(out_ap, lhs_ap, rhs_ap, start=start, stop=stop),
                reads=[lhs, rhs], writes=[out])


def tr(K, out, out_ap, in_, in_ap, ident, ident_ap):
    return K.op("pe", lambda e: e.transpose(out_ap, in_ap, ident_ap), reads=[in_, ident], writes=[out])


class Prog:
    def __init__(self, debug=False, phases=None):
        self.debug = debug
        self.phases = phases


def build_program(debug=False, stop_after=None, only=None, feed=()):
    nc = bass.Bass("TRN2", target_bir_lowering=False)
    kind_s = "ExternalOutput" if debug else "Internal"

    def din(name, shape, dt=F32):
        return nc.dram_tensor(name, list(shape), dt, kind="ExternalInput").ap()

    def dscr(name, shape, dt=F32):
        if name in feed:
            return nc.dram_tensor(name, list(shape), dt, kind="ExternalInput").ap()
        if debug:
            return nc.dram_tensor(name, list(shape), dt, kind="ExternalOutput").ap()
        return nc.dram_tensor(name, list(shape), dt).ap()

    x_in = din("x", [S, D])
    c_in = din("c", [1, D])
    ctx_in = din("ctx", [L, D])
    cctx_in = din("c_ctx", [1, D])
    ada_w = din("ada_w", [DEPTH, D, 9 * D])
    ada_b = din("ada_b", [DEPTH, 9 * D])
    norm1_g = din("norm1_g", [DEPTH, D])
    ffn1_wg = din("ffn1_wg", [DEPTH, D, DFF])
    ffn1_wu = din("ffn1_wu", [DEPTH, D, DFF])
    ffn1_wd = din("ffn1_wd", [DEPTH, DFF, D])
    norm2_g = din("norm2_g", [DEPTH, D])
    norm3_g = din("norm3_g", [DEPTH, D])
    ffn2_wg = din("ffn2_wg", [DEPTH, D, DFF])
    ffn2_wu = din("ffn2_wu", [DEPTH, D, DFF])
    ffn2_wd = din("ffn2_wd", [DEPTH, DFF, D])
    final_g = din("final_norm_g", [1, D])
    ident_in = din("ident", [128, 128])
    w_in = din("w_in", [DEPTH, D, INP])
    w_out = din("w_out", [DEPTH, D, D])
    swa_sink = din("swa_sink", [DEPTH, 4])
    mla_qg = din("mla_q_norm_g", [DEPTH, 256])
    mla_wuq = din("mla_w_uq", [DEPTH, 256, 384])
    mla_kvg = din("mla_kv_norm_g", [DEPTH, 128])
    mla_wukv = din("mla_w_ukv", [DEPTH, 128, 512])
    tab64 = din("tab64", [2, 128, T])
    tab96 = din("tab96", [2, 96, T])
    tab32 = din("tab32", [2, 32, T])
    mask_pn = din("mask_pn", [2, 128, 128])
    rpbx = din("rpbx", [DEPTH, 64, 4, 15, 64])
    tri_in = din("tri", [4, 128, 128])
    dn_conv_w = din("dn_conv_w", [DEPTH, 3, 768])
    dn_a_log = din("dn_a_log", [DEPTH, 8])
    dn_dt_bias = din("dn_dt_bias", [DEPTH, 8])
    dn_norm_g = din("dn_norm_g", [DEPTH, 64])
    okm_in = din("okm", [128, 64])
    y_out = nc.dram_tensor("y", [S, D], F32, kind="ExternalOutput").ap()

    xT = dscr("xT", [D, T])
    swa_qT = dscr("swa_qT", [256, T], BF16)
    swa_kT = dscr("swa_kT", [128, T], BF16)
    swa_v = dscr("swa_v", [T, 128], BF16)
    dn_qkvT = dscr("dn_qkvT", [768, T], F32)
    dn_zab = dscr("dn_zab", [T, 272], F32)
    mla_qT = dscr("mla_qT", [384, T], BF16)
    mla_kT = dscr("mla_kT", [384, T], BF16)
    mla_v = dscr("mla_v", [T, 256], BF16)
    na_qT = dscr("na_qT", [256, T], BF16)
    na_kT = dscr("na_kT", [256, T], BF16)
    na_v = dscr("na_v", [T, 256], BF16)
    ycat = dscr("ycat", [T, D], BF16)
    dn_hT = dscr("dn_hT", [768, T], F32)
    dn_h = dscr("dn_h", [T, 768], F32)
    dn_o = dscr("dn_o", [2, T, 256], F32)

    with ExitStack() as es:
        K = KB(nc, es)
        gs = ExitStack()
        es.enter_context(gs)
        ident = K.sb(gs, [128, 128], F32, "ident", dma=True)
        identb = K.sb(gs, [128, 128], BF16, "identb")
        ones_b = K.sb(gs, [128, 128], BF16, "ones_b")
        ones_f = K.sb(gs, [1, 2], F32, "ones_f")
        K.dma("sp", ident[:], ident_in[:, :], out_buf=ident)
        K.op("dve", lambda e: e.tensor_copy(identb[:], ident[:]), reads=[ident], writes=[identb])
        K.op("dve", lambda e: e.memset(ones_b[:], 1.0), writes=[ones_b])
        K.op("dve", lambda e: e.memset(ones_f[:], 1.0), writes=[ones_f])
        modA = [K.sb(gs, [128, 8, 2], F32, "modA%d" % j) for j in range(3)]
        modB = [K.sb(gs, [128, 8, 2], F32, "modB%d" % j) for j in range(3)]
        modG = [K.sb(gs, [128, 8, 2], F32, "modG%d" % j) for j in range(3)]
        fin_g = K.sb(gs, [128, 8], F32, "fin_g")

        def phase_init():
            with ExitStack() as st:
                xin = [K.sb(st, [128, D], F32, "xin", dma=True) for _ in range(2)]
                xo = [K.sb(st, [128, 8, 128], F32, "xo", dma=True) for _ in range(2)]
                pt = [K.ps(st, [128, 512], F32, "pt") for _ in range(4)]
                xTv = xT.rearrange("(k p) t -> p k t", p=128)
                def ld_(i):
                    src = ctx_in[i * 128:(i + 1) * 128, :] if i < 2 else x_in[(i - 2) * 128:(i - 1) * 128, :]
                    K.dma("sp", xin[i % 2][:], src, out_buf=xin[i % 2])

                ld_(0)
                for i in range(T // 128):
                    xi = xin[i % 2]
                    o = xo[i % 2]
                    if i + 1 < T // 128:
                        ld_(i + 1)
                    for hh in range(2):
                        p = pt[(2 * i + hh) % 4]
                        for kk in range(4):
                            k = hh * 4 + kk
                            tr(K, p, p[:, kk * 128:(kk + 1) * 128], xi, xi[:, k * 128:(k + 1) * 128], ident, ident[:])
                        if hh == 0:
                            K.op("act", lambda e: e.copy(o[:, 0:4, :], p[:].rearrange("p (k t) -> p k t", k=4)),
                                 reads=[p], writes=[o])
                        else:
                            K.op("dve", lambda e: e.tensor_copy(o[:, 4:8, :], p[:].rearrange("p (k t) -> p k t", k=4)),
                                 reads=[p], writes=[o])
                    K.dma("sp", xTv[:, :, i * 128:(i + 1) * 128], o[:], in_buf=o)
                K.barrier()

        def phase_mod(li):
            with ExitStack() as st:
                cv = K.sb(st, [1, 2, D], F32, "cv", dma=True)
                scv = K.sb(st, [128, 8, 2], F32, "scv")
                wblk = [K.sb(st, [128, 8, D], F32, "wblk", dma=True) for _ in range(4)]
                brow = K.sb(st, [1, 9 * D], F32, "brow", dma=True)
                grow = K.sb(st, [1, 4, D], F32, "grow", dma=True)
                pm = K.ps(st, [128, 8, 2], F32, "pm")
                pg = K.ps(st, [128, 4, 8], F32, "pg")
                modT = K.sb(st, [128, 72, 2], F32, "modT")
                gT = K.sb(st, [128, 4, 8], F32, "gT")
                K.dma("sp", [cv[0:1, 0, :], cv[0:1, 1, :]], [c_in[0:1, :], cctx_in[0:1, :]], out_buf=cv)
                K.dma("sp", brow[:], ada_b[li:li + 1, :], out_buf=brow)
                K.dma("sp", [grow[0:1, 0, :], grow[0:1, 1, :], grow[0:1, 2, :], grow[0:1, 3, :]],
                      [norm1_g[li:li + 1, :], norm2_g[li:li + 1, :], norm3_g[li:li + 1, :], final_g[0:1, :]],
                      out_buf=grow)
                for v in range(2):
                    for k in range(8):
                        mm(K, pm, pm[:, k, v:v + 1], cv, cv[0:1, v, k * 128:(k + 1) * 128], ones_f, ones_f[0:1, 0:1])
                K.op("act", lambda e: e.activation(out=scv[:], in_=pm[:], func=AF.Silu), reads=[pm], writes=[scv])
                for gi in range(4):
                    for k in range(8):
                        mm(K, pg, pg[:, gi, k:k + 1], grow, grow[0:1, gi, k * 128:(k + 1) * 128], ones_f, ones_f[0:1, 0:1])
                K.op("dve", lambda e: e.tensor_copy(gT[:], pg[:]), reads=[pg], writes=[gT])
                K.op("dve", lambda e: e.tensor_copy(fin_g[:], gT[:, 3, :]), reads=[gT], writes=[fin_g])
                awv = ada_w[li].rearrange("(k p) n -> p k n", p=128)
                for j in range(9):
                    wb = wblk[j % 4]
                    K.dma("sp", [wb[:, 0:4, :], wb[:, 4:8, :]],
                          [awv[:, 0:4, j * D:(j + 1) * D], awv[:, 4:8, j * D:(j + 1) * D]], out_buf=wb)
                    for m in range(8):
                        for k in range(8):
                            mm(K, pm, pm[:, m, :], wb, wb[:, k, m * 128:(m + 1) * 128], scv, scv[:, k, :],
                               start=(k == 0), stop=False)
                        mm(K, pm, pm[:, m, :], brow, brow[0:1, j * D + m * 128: j * D + (m + 1) * 128],
                           ones_f, ones_f[0:1, 0:2], start=False, stop=True)
                    K.op("dve", lambda e: e.tensor_copy(modT[:, j * 8:(j + 1) * 8, :], pm[:]), reads=[pm], writes=[modT])
                for s3 in range(3):
                    jsh, jsc, jg = 3 * s3, 3 * s3 + 1, 3 * s3 + 2
                    A, Bm, G = modA[s3], modB[s3], modG[s3]
                    K.op("dve", lambda e: e.tensor_scalar(A[:], modT[:, jsc * 8:(jsc + 1) * 8, :], 1.0, float(np.sqrt(D)),
                                                          op0=ALU.add, op1=ALU.mult), reads=[modT], writes=[A])
                    for v in range(2):
                        K.op("dve", lambda e: e.tensor_tensor(A[:, :, v], A[:, :, v], gT[:, s3, :], op=ALU.mult),
                             reads=[A, gT], writes=[A])
                    K.op("dve", lambda e: e.tensor_copy(Bm[:], modT[:, jsh * 8:(jsh + 1) * 8, :]), reads=[modT], writes=[Bm])
                    gsc = 1.0 if s3 == 1 else 0.5
                    K.op("dve", lambda e: e.tensor_scalar(G[:], modT[:, jg * 8:(jg + 1) * 8, :], gsc, None, op0=ALU.mult),
                         reads=[modT], writes=[G])
                K.barrier()

        def groups_all(with_ctx=True):
            g = []
            if with_ctx:
                g.append((0, L, 1))
            for i in range(S // 512):
                g.append((L + i * 512, 512, 0))
            return g

        def phase_ffn(wg_d, wu_d, wd_d, s3, with_ctx=True):
            A, Bm, G = modA[s3], modB[s3], modG[s3]
            with ExitStack() as st:
                Wg = K.sb(st, [128, 8, DFF], BF16, "Wg")
                Wu = K.sb(st, [128, 8, DFF], BF16, "Wu")
                Wd = K.sb(st, [128, NFF, D], BF16, "Wd")
                with ExitStack() as st2:
                    stg = [K.sb(st2, [128, DFF], F32, "stg", dma=True) for _ in range(6)]
                    ci = 0
                    ceng = ["dve", "act", "pool"]
                    for (wsrc, wdst) in ((wg_d, Wg), (wu_d, Wu)):
                        for k in range(8):
                            sg_ = stg[ci % 6]
                            K.dma("sp", sg_[:], wsrc[k * 128:(k + 1) * 128, :], out_buf=sg_)
                            e_ = ceng[ci % 3]
                            if e_ == "act":
                                K.op("act", lambda e: e.copy(wdst[:, k, :], sg_[:]), reads=[sg_], writes=[wdst])
                            else:
                                K.op(e_, lambda e: e.tensor_copy(wdst[:, k, :], sg_[:]), reads=[sg_], writes=[wdst])
                            ci += 1
                    for m in range(0, NFF, 2):
                        sg_ = stg[ci % 6]
                        K.dma("sp", sg_[:, 0:2 * D].rearrange("p (a n) -> p a n", a=2),
                              wd_d[m * 128:(m + 2) * 128, :].rearrange("(a p) n -> p a n", p=128), out_buf=sg_)
                        e_ = ceng[ci % 3]
                        src_ap = sg_[:, 0:2 * D].rearrange("p (a n) -> p a n", a=2)
                        if e_ == "act":
                            K.op("act", lambda e: e.copy(Wd[:, m:m + 2, :], src_ap), reads=[sg_], writes=[Wd])
                        else:
                            K.op(e_, lambda e: e.tensor_copy(Wd[:, m:m + 2, :], src_ap), reads=[sg_], writes=[Wd])
                        ci += 1
                    K.barrier()
                xg = [K.sb(st, [128, 8, 512], F32, "xg", dma=True) for _ in range(2)]
                xn = K.sb(st, [128, 8, 512], BF16, "xn")
                rstd = K.sb(st, [128, 512], F32, "rstd")
                actT = K.sb(st, [128, NFF, 512], BF16, "actT")
                sg = [K.sb(st, [128, 512], F32, "sg") for _ in range(2)]
                pss = K.ps(st, [128, 512], F32, "pss")
                pg = [K.ps(st, [128, 512], F32, "pg") for _ in range(2)]
                pu = [K.ps(st, [128, 512], F32, "pu") for _ in range(2)]
                po = [K.ps(st, [128, 512], F32, "po") for _ in range(2)]
                xTv = xT.rearrange("(k p) t -> p k t", p=128)
                grps = groups_all(with_ctx)

                def load(gi):
                    t0, n, v = grps[gi]
                    b = xg[gi % 2]
                    K.dma("sp", [b[:, 0:4, 0:n], b[:, 4:8, 0:n]], [xTv[:, 0:4, t0:t0 + n], xTv[:, 4:8, t0:t0 + n]], out_buf=b)

                def pre1(gi):
                    t0, n, v = grps[gi]
                    xb = xg[gi % 2]
                    K.op("act", lambda e: e.activation(out=xn[:, :, 0:n], in_=xb[:, :, 0:n], func=AF.Square),
                         reads=[xb], writes=[xn])

                def pre2(gi):
                    t0, n, v = grps[gi]
                    xb = xg[gi % 2]
                    for k in range(8):
                        mm(K, pss, pss[:, 0:n], ones_b, ones_b[:], xn, xn[:, k, 0:n], start=(k == 0), stop=(k == 7))
                    K.op("act", lambda e: e.activation(out=rstd[:, 0:n], in_=pss[:, 0:n], func=AF.Sqrt, scale=1.0, bias=float(D * EPS)),
                         reads=[pss], writes=[rstd])
                    K.op("dve", lambda e: e.reciprocal(rstd[:, 0:n], rstd[:, 0:n]), reads=[rstd], writes=[rstd])
                    for k in range(8):
                        K.op("dve", lambda e: e.scalar_tensor_tensor(out=xn[:, k, 0:n], in0=xb[:, k, 0:n], scalar=A[:, k, v:v + 1],
                                                                      in1=rstd[:, 0:n], op0=ALU.mult, op1=ALU.mult),
                             reads=[xb, A, rstd], writes=[xn])
                    for k in range(8):
                        K.op("act", lambda e: e.activation(out=xn[:, k, 0:n], in_=xn[:, k, 0:n], func=AF.Identity,
                                                           bias=Bm[:, k, v:v + 1], scale=1.0),
                             reads=[xn, Bm], writes=[xn])

                load(0)
                pre1(0)
                pre2(0)
                for gi, (t0, n, v) in enumerate(grps):
                    more = gi + 1 < len(grps)
                    if more:
                        load(gi + 1)
                    xb = xg[gi % 2]
                    for m in range(NFF):
                        pgm, pum, sgm = pg[m % 2], pu[m % 2], sg[m % 2]
                        for k in range(8):
                            mm(K, pgm, pgm[:, 0:n], Wg, Wg[:, k, m * 128:(m + 1) * 128], xn, xn[:, k, 0:n], start=(k == 0), stop=(k == 7))
                        for k in range(8):
                            mm(K, pum, pum[:, 0:n], Wu, Wu[:, k, m * 128:(m + 1) * 128], xn, xn[:, k, 0:n], start=(k == 0), stop=(k == 7))
                        K.op("act", lambda e: e.activation(out=sgm[:, 0:n], in_=pgm[:, 0:n], func=AF.Silu), reads=[pgm], writes=[sgm])
                        K.op("dve", lambda e: e.tensor_tensor(actT[:, m, 0:n], sgm[:, 0:n], pum[:, 0:n], op=ALU.mult),
                             reads=[sgm, pum], writes=[actT])
                    if more:
                        pre1(gi + 1)
                    for f in range(8):
                        pof = po[f % 2]
                        for m in range(NFF):
                            mm(K, pof, pof[:, 0:n], Wd, Wd[:, m, f * 128:(f + 1) * 128], actT, actT[:, m, 0:n], start=(m == 0), stop=(m == NFF - 1))
                        K.op("dve", lambda e: e.scalar_tensor_tensor(out=xb[:, f, 0:n], in0=pof[:, 0:n], scalar=G[:, f, v:v + 1],
                                                                      in1=xb[:, f, 0:n], op0=ALU.mult, op1=ALU.add),
                             reads=[pof, G, xb], writes=[xb])
                        if f == 1 and more:
                            pre2(gi + 1)
                    K.dma("sp", [xTv[:, 0:4, t0:t0 + n], xTv[:, 4:8, t0:t0 + n]], [xb[:, 0:4, 0:n], xb[:, 4:8, 0:n]], in_buf=xb)
                K.barrier()


        def prenorm(xb, xn, rstd, pss, A, Bm, v, n):
            K.op("act", lambda e: e.activation(out=xn[:, :, 0:n], in_=xb[:, :, 0:n], func=AF.Square),
                 reads=[xb], writes=[xn])
            for k in range(8):
                mm(K, pss, pss[:, 0:n], ones_b, ones_b[:], xn, xn[:, k, 0:n], start=(k == 0), stop=(k == 7))
            K.op("act", lambda e: e.activation(out=rstd[:, 0:n], in_=pss[:, 0:n], func=AF.Sqrt, scale=1.0, bias=float(D * EPS)),
                 reads=[pss], writes=[rstd])
            K.op("dve", lambda e: e.reciprocal(rstd[:, 0:n], rstd[:, 0:n]), reads=[rstd], writes=[rstd])
            for k in range(8):
                K.op("dve", lambda e: e.scalar_tensor_tensor(out=xn[:, k, 0:n], in0=xb[:, k, 0:n], scalar=A[:, k, v:v + 1],
                                                              in1=rstd[:, 0:n], op0=ALU.mult, op1=ALU.mult),
                     reads=[xb, A, rstd], writes=[xn])
            for k in range(8):
                K.op("act", lambda e: e.activation(out=xn[:, k, 0:n], in_=xn[:, k, 0:n], func=AF.Identity,
                                                   bias=Bm[:, k, v:v + 1], scale=1.0),
                     reads=[xn, Bm], writes=[xn])

        class Ring:
            def __init__(self, bufs):
                self.bufs = bufs
                self.i = 0

            def next(self):
                b = self.bufs[self.i % len(self.bufs)]
                self.i += 1
                return b

        def rot_cols(dst, src, k, c0, nblk, qw):
            w4 = 4 * qw
            dv = dst[:, k, c0:c0 + nblk * w4].rearrange("p (b q j) -> p b q j", q=4, j=qw)
            sv = src[:, k, c0:c0 + nblk * w4].rearrange("p (b q j) -> p b q j", q=4, j=qw)
            for (qd, qs, sgn) in ((0, 1, -1.0), (1, 0, 1.0), (2, 3, -1.0), (3, 2, 1.0)):
                K.op("pool", lambda e: e.tensor_scalar(dv[:, :, qd, :], sv[:, :, qs, :], sgn, None, op0=ALU.mult),
                     reads=[src], writes=[dst])

        def phase_inproj(li):
            A, Bm = modA[1], modB[1]
            with ExitStack() as st:
                Win = K.sb(st, [128, 8, INP], BF16, "Win")
                Wrot = K.sb(st, [128, 8, INP], BF16, "Wrot")
                Wuq = K.sb(st, [128, 2, 384], BF16, "Wuq")
                Wuqr = K.sb(st, [128, 2, 384], BF16, "Wuqr")
                Wukv = K.sb(st, [128, 512], BF16, "Wukv")
                gqk = K.sb(st, [128, 4], F32, "gqk")
                with ExitStack() as st2:
                    stg = [K.sb(st2, [128, INP], F32, "stg", dma=True) for _ in range(5)]
                    grow = K.sb(st2, [1, 384], F32, "grow", dma=True)
                    pgq = K.ps(st2, [128, 4], F32, "pgq")
                    for k in range(8):
                        sg_ = stg[k % 5]
                        K.dma("sp", sg_[:], w_in[li, k * 128:(k + 1) * 128, :], out_buf=sg_)
                        if k % 2 == 0:
                            K.op("dve", lambda e: e.tensor_copy(Win[:, k, :], sg_[:]), reads=[sg_], writes=[Win])
                        else:
                            K.op("act", lambda e: e.copy(Win[:, k, :], sg_[:]), reads=[sg_], writes=[Win])
                        rot_cols(Wrot, Win, k, 0, 6, 16)
                        rot_cols(Wrot, Win, k, 1936, 1, 8)
                    for c in range(2):
                        sg_ = stg[c % 2]
                        K.dma("sp", sg_[:, 0:384], mla_wuq[li, c * 128:(c + 1) * 128, :], out_buf=sg_)
                        K.op("dve", lambda e: e.tensor_copy(Wuq[:, c, :], sg_[:, 0:384]), reads=[sg_], writes=[Wuq])
                    K.op("pool", lambda e: e.memset(Wuqr[:], 0.0), writes=[Wuqr])
                    for c in range(2):
                        for h in range(4):
                            rot_cols(Wuqr, Wuq, c, h * 96 + 64, 1, 8)
                    sg_ = stg[0]
                    K.dma("sp", sg_[:, 0:512], mla_wukv[li, :, :], out_buf=sg_)
                    K.op("dve", lambda e: e.tensor_copy(Wukv[:], sg_[:, 0:512]), reads=[sg_], writes=[Wukv])
                    K.dma("sp", [grow[0:1, 0:256], grow[0:1, 256:384]], [mla_qg[li:li + 1, :], mla_kvg[li:li + 1, :]], out_buf=grow)
                    for c in range(3):
                        mm(K, pgq, pgq[:, c:c + 1], grow, grow[0:1, c * 128:(c + 1) * 128], ones_f, ones_f[0:1, 0:1])
                    K.op("dve", lambda e: e.tensor_copy(gqk[:, 0:3], pgq[:, 0:3]), reads=[pgq], writes=[gqk])
                    K.barrier()

                xg = [K.sb(st, [128, 8, 512], F32, "xg", dma=True) for _ in range(2)]
                tb64 = [K.sb(st, [128, 2, 512], F32, "tb64", dma=True) for _ in range(2)]
                tb96 = [K.sb(st, [96, 2, 512], F32, "tb96", dma=True) for _ in range(2)]
                tb32 = [K.sb(st, [32, 2, 512], F32, "tb32", dma=True) for _ in range(2)]
                xn = K.sb(st, [128, 8, 512], BF16, "xn")
                rstd = K.sb(st, [128, 512], F32, "rstd")
                t1 = K.sb(st, [128, 512], F32, "t1")
                t2 = K.sb(st, [128, 512], F32, "t2")
                cqf = K.sb(st, [128, 3, 512], F32, "cqf")
                cqs = K.sb(st, [128, 3, 512], BF16, "cqs")
                cqn = K.sb(st, [128, 3, 512], BF16, "cqn")
                rq = K.sb(st, [128, 512], F32, "rq")
                rkv = K.sb(st, [128, 512], F32, "rkv")
                obf = Ring([K.sb(st, [128, 512], BF16, "obf", dma=True) for _ in range(4)])
                of32 = Ring([K.sb(st, [128, 512], F32, "of32", dma=True) for _ in range(3)])
                pss = K.ps(st, [128, 512], F32, "pss")
                pb = Ring([K.ps(st, [128, 512], F32, "pb") for _ in range(7)])
                xTv = xT.rearrange("(k p) t -> p k t", p=128)
                grps = groups_all(True)
                evi = [0]

                def evac(out_b, out_ap, p, p_ap):
                    evi[0] += 1
                    if evi[0] % 2 == 0:
                        K.op("act", lambda e: e.copy(out_ap, p_ap), reads=[p], writes=[out_b])
                    else:
                        K.op("dve", lambda e: e.tensor_copy(out_ap, p_ap), reads=[p], writes=[out_b])

                def load(gi):
                    t0, n, v = grps[gi]
                    b = xg[gi % 2]
                    K.dma("sp", [b[:, 0:4, 0:n], b[:, 4:8, 0:n]], [xTv[:, 0:4, t0:t0 + n], xTv[:, 4:8, t0:t0 + n]], out_buf=b)
                    K.dma("sp", tb64[gi % 2][:, :, 0:n], tab64[:, :, t0:t0 + n].rearrange("c p t -> p c t"), out_buf=tb64[gi % 2])
                    K.dma("sp", tb96[gi % 2][:, :, 0:n], tab96[:, :, t0:t0 + n].rearrange("c p t -> p c t"), out_buf=tb96[gi % 2])
                    K.dma("sp", tb32[gi % 2][:, :, 0:n], tab32[:, :, t0:t0 + n].rearrange("c p t -> p c t"), out_buf=tb32[gi % 2])

                load(0)
                for gi, (t0, n, v) in enumerate(grps):
                    if gi + 1 < len(grps):
                        load(gi + 1)
                    xb = xg[gi % 2]
                    T64, T96, T32 = tb64[gi % 2], tb96[gi % 2], tb32[gi % 2]
                    prenorm(xb, xn, rstd, pss, A, Bm, v, n)

                    def proj(c0, nc_, W=Win):
                        p = pb.next()
                        for k in range(8):
                            mm(K, p, p[0:nc_, 0:n], W, W[:, k, c0:c0 + nc_], xn, xn[:, k, 0:n], start=(k == 0), stop=(k == 7))
                        return p

                    def rope_store(p, pr, tb, np_, dst_ap):
                        ob = obf.next()
                        K.op("dve", lambda e: e.tensor_tensor(t1[0:np_, 0:n], p[0:np_, 0:n], tb[0:np_, 0, 0:n], op=ALU.mult),
                             reads=[p, tb], writes=[t1])
                        K.op("dve", lambda e: e.tensor_tensor(t2[0:np_, 0:n], pr[0:np_, 0:n], tb[0:np_, 1, 0:n], op=ALU.mult),
                             reads=[pr, tb], writes=[t2])
                        K.op("pool", lambda e: e.tensor_tensor(ob[0:np_, 0:n], t1[0:np_, 0:n], t2[0:np_, 0:n], op=ALU.add),
                             reads=[t1, t2], writes=[ob])
                        if isinstance(dst_ap, list):
                            K.dma("sp", dst_ap, [ob[0:np_, 0:n]] * len(dst_ap), in_buf=ob)
                        else:
                            K.dma("sp", dst_ap, ob[0:np_, 0:n], in_buf=ob)

                    def plain_store(p, np_, dst_ap, dt=BF16):
                        ob = obf.next() if dt == BF16 else of32.next()
                        evac(ob, ob[0:np_, 0:n], p, p[0:np_, 0:n])
                        K.dma("sp", dst_ap, ob[0:np_, 0:n], in_buf=ob)

                    for ch in range(2):
                        p = proj(ch * 128, 128)
                        pr = proj(ch * 128, 128, Wrot)
                        rope_store(p, pr, T64, 128, swa_qT[ch * 128:(ch + 1) * 128, t0:t0 + n])
                    p = proj(256, 128)
                    pr = proj(256, 128, Wrot)
                    rope_store(p, pr, T64, 128, swa_kT[:, t0:t0 + n])
                    for ch in range(6):
                        p = proj(512 + ch * 128, 128)
                        plain_store(p, 128, dn_qkvT[ch * 128:(ch + 1) * 128, t0:t0 + n], F32)
                    for ch in range(2):
                        p = proj(1968 + ch * 128, 128)
                        plain_store(p, 128, na_qT[ch * 128:(ch + 1) * 128, t0:t0 + n])
                    for ch in range(2):
                        p = proj(2224 + ch * 128, 128)
                        plain_store(p, 128, na_kT[ch * 128:(ch + 1) * 128, t0:t0 + n])
                    p = proj(1936, 32)
                    pr = proj(1936, 32, Wrot)
                    rope_store(p, pr, T32, 32, [mla_kT[h * 96 + 64:h * 96 + 96, t0:t0 + n] for h in range(4)])
                    for c in range(3):
                        p = proj(1552 + c * 128, 128)
                        K.op("act", lambda e: e.copy(cqf[:, c, 0:n], p[:, 0:n]), reads=[p], writes=[cqf])
                    K.op("act", lambda e: e.activation(out=cqs[:, :, 0:n], in_=cqf[:, :, 0:n], func=AF.Square), reads=[cqf], writes=[cqs])
                    pq_ = pb.next()
                    for c in range(2):
                        mm(K, pq_, pq_[:, 0:n], ones_b, ones_b[:], cqs, cqs[:, c, 0:n], start=(c == 0), stop=(c == 1))
                    K.op("act", lambda e: e.activation(out=rq[:, 0:n], in_=pq_[:, 0:n], func=AF.Sqrt, scale=1.0 / 256.0, bias=float(EPS)),
                         reads=[pq_], writes=[rq])
                    K.op("dve", lambda e: e.reciprocal(rq[:, 0:n], rq[:, 0:n]), reads=[rq], writes=[rq])
                    pk_ = pb.next()
                    mm(K, pk_, pk_[:, 0:n], ones_b, ones_b[:], cqs, cqs[:, 2, 0:n])
                    K.op("act", lambda e: e.activation(out=rkv[:, 0:n], in_=pk_[:, 0:n], func=AF.Sqrt, scale=1.0 / 128.0, bias=float(EPS)),
                         reads=[pk_], writes=[rkv])
                    K.op("dve", lambda e: e.reciprocal(rkv[:, 0:n], rkv[:, 0:n]), reads=[rkv], writes=[rkv])
                    for c in range(3):
                        rr_ = rq if c < 2 else rkv
                        K.op("dve", lambda e: e.scalar_tensor_tensor(out=cqn[:, c, 0:n], in0=cqf[:, c, 0:n], scalar=gqk[:, c:c + 1],
                                                                      in1=rr_[:, 0:n], op0=ALU.mult, op1=ALU.mult),
                             reads=[cqf, gqk, rr_], writes=[cqn])
                    for h in range(4):
                        p = pb.next()
                        pr = pb.next()
                        for c in range(2):
                            mm(K, p, p[0:96, 0:n], Wuq, Wuq[:, c, h * 96:(h + 1) * 96], cqn, cqn[:, c, 0:n], start=(c == 0), stop=(c == 1))
                        for c in range(2):
                            mm(K, pr, pr[0:96, 0:n], Wuqr, Wuqr[:, c, h * 96:(h + 1) * 96], cqn, cqn[:, c, 0:n], start=(c == 0), stop=(c == 1))
                        rope_store(p, pr, T96, 96, mla_qT[h * 96:(h + 1) * 96, t0:t0 + n])
                    for h in range(4):
                        p = pb.next()
                        mm(K, p, p[0:64, 0:n], Wukv, Wukv[:, h * 128:h * 128 + 64], cqn, cqn[:, 2, 0:n])
                        plain_store(p, 64, mla_kT[h * 96:h * 96 + 64, t0:t0 + n])
                    wv_ap = Wukv[:].rearrange("p (h c) -> p h c", h=4)[:, :, 64:128]
                    for tt in range(n // 128):
                        r0 = t0 + tt * 128
                        p = pb.next()
                        mm(K, p, p[:, 0:256].rearrange("p (h c) -> p h c", h=4), cqn, cqn[:, 2, tt * 128:(tt + 1) * 128], Wukv, wv_ap)
                        ob = obf.next()
                        evac(ob, ob[:, 0:256], p, p[:, 0:256])
                        K.dma("sp", mla_v[r0:r0 + 128, :], ob[:, 0:256], in_buf=ob)
                        p = pb.next()
                        for k in range(8):
                            mm(K, p, p[:, 0:128], xn, xn[:, k, tt * 128:(tt + 1) * 128], Win, Win[:, k, 384:512], start=(k == 0), stop=(k == 7))
                        ob = obf.next()
                        evac(ob, ob[:, 0:128], p, p[:, 0:128])
                        K.dma("sp", swa_v[r0:r0 + 128, :], ob[:, 0:128], in_buf=ob)
                        p = pb.next()
                        for k in range(8):
                            mm(K, p, p[:, 0:272], xn, xn[:, k, tt * 128:(tt + 1) * 128], Win, Win[:, k, 1280:1552], start=(k == 0), stop=(k == 7))
                        ob = of32.next()
                        evac(ob, ob[:, 0:272], p, p[:, 0:272])
                        K.dma("sp", dn_zab[r0:r0 + 128, :], ob[:, 0:272], in_buf=ob)
                        p = pb.next()
                        for k in range(8):
                            mm(K, p, p[:, 0:256], xn, xn[:, k, tt * 128:(tt + 1) * 128], Win, Win[:, k, 2480:2736], start=(k == 0), stop=(k == 7))
                        ob = obf.next()
                        evac(ob, ob[:, 0:256], p, p[:, 0:256])
                        K.dma("sp", na_v[r0:r0 + 128, :], ob[:, 0:256], in_buf=ob)
                K.barrier()


        def phase_swa(li, need_ctx):
            with ExitStack() as st:
                kT = K.sb(st, [64, 2, T], BF16, "kT", dma=True)
                qT = K.sb(st, [64, 4, T], BF16, "qT", dma=True)
                Va = K.sb(st, [128, 34, 2, 65], BF16, "Va", dma=True)
                mpn = K.sb(st, [128, 2, 128], F32, "mpn", dma=True)
                mpb = K.sb(st, [128, 2, 2, 128], BF16, "mpb")
                snk = K.sb(st, [128, 4], F32, "snk", dma=True)
                es_ = K.sb(st, [128, 4], F32, "es")
                K.dma("sp", kT[:], swa_kT.rearrange("(h d) t -> d h t", d=64), out_buf=kT)
                K.dma("sp", [qT[:, 0:2, :], qT[:, 2:4, :]],
                      [swa_qT[0:128, :].rearrange("(h d) t -> d h t", d=64), swa_qT[128:256, :].rearrange("(h d) t -> d h t", d=64)], out_buf=qT)
                vv = swa_v.rearrange("(j p) (h d) -> p j h d", p=128, d=64)
                K.dma("sp", [Va[:, j0:j0 + 17, h, 0:64] for h in range(2) for j0 in (0, 17)],
                      [vv[:, j0:j0 + 17, h, :] for h in range(2) for j0 in (0, 17)], out_buf=Va)
                K.op("pool", lambda e: e.memset(Va[:, :, :, 64:65], 1.0), writes=[Va])
                K.dma("sp", mpn[:], mask_pn.rearrange("w p q -> p w q"), out_buf=mpn)
                for w in range(2):
                    for g in range(2):
                        K.op("dve", lambda e: e.tensor_copy(mpb[:, w, g, :], mpn[:, w, :]), reads=[mpn], writes=[mpb])
                K.dma("sp", snk[:], swa_sink[li:li + 1, :].partition_broadcast(128), out_buf=snk)
                K.op("act", lambda e: e.activation(out=es_[:], in_=snk[:], func=AF.Exp), reads=[snk], writes=[es_])
                psS = Ring([K.ps(st, [128, 512], F32, "psS") for _ in range(3)])
                po = Ring([K.ps(st, [128, 512], F32, "po") for _ in range(4)])
                Pt = Ring([K.sb(st, [128, 2, 128], BF16, "Pt") for _ in range(4)])
                ysb = Ring([K.sb(st, [128, 256], BF16, "ysb", dma=True) for _ in range(3)])
                dn_ = Ring([K.sb(st, [128, 2], F32, "dn") for _ in range(4)])
                mi = [0]

                jobs = []

                def block(q0, tiles, yrow0):
                    for kh in range(2):
                        for ti, tl in enumerate(tiles):
                            jobs.append((q0, kh, ti, len(tiles), tl, yrow0))

                def run_jobs():
                    def issue_S(i):
                        q0, kh, ti, nt, (k0, vj, mk), yrow0 = jobs[i]
                        ps = psS.next()
                        mm(K, ps, ps[:, 0:256].rearrange("p (g q) -> p g q", g=2), kT, kT[:, kh, k0:k0 + 128], qT, qT[:, 2 * kh:2 * kh + 2, q0:q0 + 128])
                        return ps

                    ps_next = issue_S(0)
                    yb = None
                    pog = None
                    for i, (q0, kh, ti, nt, (k0, vj, mk), yrow0) in enumerate(jobs):
                        ps = ps_next
                        if i + 1 < len(jobs):
                            ps_next = issue_S(i + 1)
                        if kh == 0 and ti == 0:
                            yb = ysb.next()
                        if ti == 0:
                            pog = [po.next(), po.next()]
                        P = Pt.next()
                        K.op("act", lambda e: e.activation(out=P[:], in_=ps[:, 0:256].rearrange("p (g q) -> p g q", g=2), func=AF.Exp, scale=0.125),
                             reads=[ps], writes=[P])
                        if mk is not None:
                            mi[0] += 1
                            e_ = "dve" if mi[0] % 2 else "pool"
                            K.op(e_, lambda e: e.tensor_tensor(P[:], P[:], mpb[:, mk, :, :], op=ALU.mult), reads=[P, mpb], writes=[P])
                        for g in range(2):
                            mm(K, pog[g], pog[g][:, 0:65], P, P[:, g, :], Va, Va[:, vj, kh, :], start=(ti == 0), stop=(ti == nt - 1))
                        if ti == nt - 1:
                            for g in range(2):
                                h = 2 * kh + g
                                d_ = dn_.next()
                                K.op("dve", lambda e: e.tensor_tensor(d_[:, 0:1], pog[g][:, 64:65], es_[:, h:h + 1], op=ALU.add), reads=[pog[g], es_], writes=[d_])
                                K.op("dve", lambda e: e.reciprocal(d_[:, 1:2], d_[:, 0:1]), reads=[d_], writes=[d_])
                                K.op("dve", lambda e: e.tensor_scalar(yb[:, h * 64:(h + 1) * 64], pog[g][:, 0:64], d_[:, 1:2], None, op0=ALU.mult),
                                     reads=[pog[g], d_], writes=[yb])
                            if kh == 1:
                                K.dma("sp", ycat[yrow0:yrow0 + 128, 0:256], yb[:], in_buf=yb)

                if need_ctx:
                    for qt in range(2):
                        block(qt * 128, [(0, 0, None), (128, 1, None)], qt * 128)
                for i in range(S // 128):
                    tiles = [(0, 0, None), (128, 1, None)]
                    if i > 0:
                        tiles.append((L + (i - 1) * 128, 2 + i - 1, 0))
                    tiles.append((L + i * 128, 2 + i, None))
                    if i < S // 128 - 1:
                        tiles.append((L + (i + 1) * 128, 2 + i + 1, 1))
                    block(L + i * 128, tiles, L + i * 128)
                run_jobs()
                K.barrier()

        def phase_mla(li, need_ctx):
            sc = float(96 ** -0.5)
            with ExitStack() as st:
                kT = K.sb(st, [96, 4, T], BF16, "kT", dma=True)
                qT = K.sb(st, [96, 4, T], BF16, "qT", dma=True)
                Va = K.sb(st, [128, 34, 4, 65], BF16, "Va", dma=True)
                K.dma("sp", [kT[:, h, :] for h in range(4)], [mla_kT[h * 96:(h + 1) * 96, :] for h in range(4)], out_buf=kT)
                K.dma("sp", [qT[:, h, :] for h in range(4)], [mla_qT[h * 96:(h + 1) * 96, :] for h in range(4)], out_buf=qT)
                vv = mla_v.rearrange("(j p) (h d) -> p j h d", p=128, d=64)
                K.dma("sp", [Va[:, j0:j0 + 17, h, 0:64] for h in range(4) for j0 in (0, 17)],
                      [vv[:, j0:j0 + 17, h, :] for h in range(4) for j0 in (0, 17)], out_buf=Va)
                K.op("pool", lambda e: e.memset(Va[:, :, :, 64:65], 1.0), writes=[Va])
                psS = Ring([K.ps(st, [128, 512], F32, "psS") for _ in range(3)])
                po = [K.ps(st, [128, 512], F32, "po") for _ in range(4)]
                Pt = Ring([K.sb(st, [128, 512], BF16, "Pt") for _ in range(4)])
                ysb = Ring([K.sb(st, [128, 256], BF16, "ysb", dma=True) for _ in range(8)])
                dn_ = Ring([K.sb(st, [128, 2], F32, "dn") for _ in range(4)])

                def group(q0, n, ktiles, yrow0):
                    nq = n // 128
                    ybs = [ysb.next() for _ in range(nq)]
                    seq = [(h, ti, kt) for h in range(4) for ti, kt in enumerate(ktiles)]

                    def issue_S(i):
                        h, ti, kt = seq[i]
                        ps = psS.next()
                        mm(K, ps, ps[:, 0:n], kT, kT[:, h, kt * 128:(kt + 1) * 128], qT, qT[:, h, q0:q0 + n])
                        return ps

                    ps_next = issue_S(0)
                    for i, (h, ti, kt) in enumerate(seq):
                        ps = ps_next
                        if i + 1 < len(seq):
                            ps_next = issue_S(i + 1)
                        P = Pt.next()
                        K.op("act", lambda e: e.activation(out=P[:, 0:n], in_=ps[:, 0:n], func=AF.Exp, scale=sc), reads=[ps], writes=[P])
                        for qt in range(nq):
                            mm(K, po[qt], po[qt][:, 0:65], P, P[:, qt * 128:(qt + 1) * 128], Va, Va[:, kt, h, :],
                               start=(ti == 0), stop=(ti == len(ktiles) - 1))
                        if ti == len(ktiles) - 1:
                            for qt in range(nq):
                                d_ = dn_.next()
                                K.op("dve", lambda e: e.reciprocal(d_[:, 1:2], po[qt][:, 64:65]), reads=[po[qt]], writes=[d_])
                                K.op("dve", lambda e: e.tensor_scalar(ybs[qt][:, h * 64:(h + 1) * 64], po[qt][:, 0:64], d_[:, 1:2], None, op0=ALU.mult),
                                     reads=[po[qt], d_], writes=[ybs[qt]])
                    for qt in range(nq):
                        K.dma("sp", ycat[yrow0 + qt * 128:yrow0 + (qt + 1) * 128, 512:768], ybs[qt][:], in_buf=ybs[qt])

                if need_ctx:
                    group(0, 256, [0, 1], 0)
                for qg in range(S // 512):
                    group(L + qg * 512, 512, list(range(34)), L + qg * 512)
                K.barrier()


        def phase_na(li, need_ctx):
            with ExitStack() as st:
                kT = K.sb(st, [64, 4, T], BF16, "kT", dma=True)
                qT = K.sb(st, [64, 4, T], BF16, "qT", dma=True)
                Va = K.sb(st, [128, 34, 4, 65], BF16, "Va", dma=True)
                Vs = K.sb(st, [128, 31, 4, 65], BF16, "Vs", dma=True)
                TA = K.sb(st, [128, 4, 15, 64], BF16, "TA")
                for (dst, src) in ((kT, na_kT), (qT, na_qT)):
                    K.dma("sp", [dst[:, 0:2, :], dst[:, 2:4, :]],
                          [src[0:128, :].rearrange("(h d) t -> d h t", d=64), src[128:256, :].rearrange("(h d) t -> d h t", d=64)], out_buf=dst)
                vv = na_v.rearrange("(j p) (h d) -> p j h d", p=128, d=64)
                K.dma("sp", [Va[:, j0:j0 + 17, h, 0:64] for h in range(4) for j0 in (0, 17)],
                      [vv[:, j0:j0 + 17, h, :] for h in range(4) for j0 in (0, 17)], out_buf=Va)
                K.op("pool", lambda e: e.memset(Va[:, :, :, 64:65], 1.0), writes=[Va])
                vs = na_v[L + 64:L + 64 + 31 * 128, :].rearrange("(j p) (h d) -> p j h d", p=128, d=64)
                K.dma("sp", [Vs[:, :, h, 0:64] for h in range(4)], [vs[:, :, h, :] for h in range(4)], out_buf=Vs)
                K.op("pool", lambda e: e.memset(Vs[:, :, :, 64:65], 1.0), writes=[Vs])
                with ExitStack() as st2:
                    TAr = K.sb(st2, [128, 4, 15, 64], F32, "TAr", dma=True)
                    okm = K.sb(st2, [128, 64], F32, "okm", dma=True)
                    K.op("dve", lambda e: e.memset(TAr[:], 0.0), writes=[TAr])
                    K.dma("sp", [TAr[0:64, :, :, :].rearrange("p h r q -> p (h r q)"), TAr[64:128, :, 0:14, :].rearrange("p h r q -> p h (r q)")],
                          [rpbx[li].rearrange("p h r q -> p (h r q)"), rpbx[li][:, :, 1:15, :].rearrange("p h r q -> p h (r q)")], out_buf=TAr)
                    K.dma("sp", okm[:], okm_in[:, :], out_buf=okm)
                    K.op("act", lambda e: e.activation(out=TAr[:], in_=TAr[:], func=AF.Exp), reads=[TAr], writes=[TAr])
                    for h in range(4):
                        for r_ in range(15):
                            K.op("pool", lambda e: e.tensor_tensor(TA[:, h, r_, :], TAr[:, h, r_, :], okm[:], op=ALU.mult), reads=[TAr, okm], writes=[TA])
                    K.barrier()
                psS = Ring([K.ps(st, [128, 512], F32, "psS") for _ in range(3)])
                po = Ring([K.ps(st, [128, 512], F32, "po") for _ in range(3)])
                P6r = Ring([K.sb(st, [128, 6, 4, 64], BF16, "P6") for _ in range(3)])
                Pf = Ring([K.sb(st, [128, 4, 64], F32, "Pf") for _ in range(3)])
                yrow = Ring([K.sb(st, [64, 256], BF16, "yrow", dma=True) for _ in range(3)])
                ysb = Ring([K.sb(st, [128, 256], BF16, "ysb", dma=True) for _ in range(2)])
                Pc = Ring([K.sb(st, [128, 128], BF16, "Pc") for _ in range(3)])
                dn_ = Ring([K.sb(st, [128, 2], F32, "dn") for _ in range(4)])
                mi = [0]
                if need_ctx:
                    for qt in range(2):
                        yb = ysb.next()
                        for h in range(4):
                            pq = po.next()
                            for kt in range(2):
                                ps = psS.next()
                                mm(K, ps, ps[:, 0:128], kT, kT[:, h, kt * 128:(kt + 1) * 128], qT, qT[:, h, qt * 128:(qt + 1) * 128])
                                P = Pc.next()
                                K.op("act", lambda e: e.activation(out=P[:], in_=ps[:, 0:128], func=AF.Exp, scale=0.125), reads=[ps], writes=[P])
                                mm(K, pq, pq[:, 0:65], P, P[:], Va, Va[:, kt, h, :], start=(kt == 0), stop=(kt == 1))
                            d_ = dn_.next()
                            K.op("dve", lambda e: e.reciprocal(d_[:, 1:2], pq[:, 64:65]), reads=[pq], writes=[d_])
                            K.op("dve", lambda e: e.tensor_scalar(yb[:, h * 64:(h + 1) * 64], pq[:, 0:64], d_[:, 1:2], None, op0=ALU.mult),
                                 reads=[pq, d_], writes=[yb])
                        K.dma("sp", ycat[qt * 128:(qt + 1) * 128, 768:1024], yb[:], in_buf=yb)
                def s_stage(r):
                    rs = min(max(r - 4, 0), 56)
                    dlt = r - rs
                    q0 = L + r * 64
                    P6 = P6r.next()
                    tiles = []
                    for kt in range(4):
                        k0 = L + rs * 64 + kt * 128
                        vt = (Va, 2 + rs // 2 + kt) if rs % 2 == 0 else (Vs, (rs - 1) // 2 + kt)
                        tiles.append((k0, vt, 2 * kt - dlt + 7))
                    tiles.append((0, (Va, 0), None))
                    tiles.append((128, (Va, 1), None))
                    for ti, (k0, vt, dr0) in enumerate(tiles):
                        ps = psS.next()
                        for h in range(4):
                            mm(K, ps, ps[:, h * 64:(h + 1) * 64], kT, kT[:, h, k0:k0 + 128], qT, qT[:, h, q0:q0 + 64])
                        psv = ps[:, 0:256].rearrange("p (h q) -> p h q", h=4)
                        if dr0 is None:
                            K.op("act", lambda e: e.activation(out=P6[:, ti, :, :], in_=psv, func=AF.Exp, scale=0.125), reads=[ps], writes=[P6])
                        else:
                            pf = Pf.next()
                            K.op("act", lambda e: e.activation(out=pf[:], in_=psv, func=AF.Exp, scale=0.125), reads=[ps], writes=[pf])
                            mi[0] += 1
                            e_ = "dve" if mi[0] % 2 else "pool"
                            K.op(e_, lambda e: e.tensor_tensor(P6[:, ti, :, :], pf[:], TA[:, :, dr0, :], op=ALU.mult), reads=[pf, TA], writes=[P6])
                    return (P6, tiles, q0)

                def pv_stage(P6, tiles, q0):
                    yb = yrow.next()
                    for h in range(4):
                        pq = po.next()
                        for ti, (k0, vt, dr0) in enumerate(tiles):
                            Vb, vj = vt
                            mm(K, pq, pq[0:64, 0:65], P6, P6[:, ti, h, :], Vb, Vb[:, vj, h, :], start=(ti == 0), stop=(ti == 5))
                        d_ = dn_.next()
                        K.op("dve", lambda e: e.reciprocal(d_[0:64, 1:2], pq[0:64, 64:65]), reads=[pq], writes=[d_])
                        K.op("dve", lambda e: e.tensor_scalar(yb[:, h * 64:(h + 1) * 64], pq[0:64, 0:64], d_[0:64, 1:2], None, op0=ALU.mult),
                             reads=[pq, d_], writes=[yb])
                    K.dma("sp", ycat[q0:q0 + 64, 768:1024], yb[:], in_buf=yb)

                prev = None
                for r in range(64):
                    cur = s_stage(r)
                    if prev is not None:
                        pv_stage(*prev)
                    prev = cur
                pv_stage(*prev)
                K.barrier()

        def phase_outproj(li, with_ctx):
            G = modG[1]
            with ExitStack() as st:
                Wo = K.sb(st, [128, 8, D], BF16, "Wo")
                with ExitStack() as st2:
                    stg = [K.sb(st2, [128, D], F32, "stg", dma=True) for _ in range(5)]
                    for k in range(8):
                        sg_ = stg[k % 5]
                        K.dma("sp", sg_[:], w_out[li, k * 128:(k + 1) * 128, :], out_buf=sg_)
                        if k % 2 == 0:
                            K.op("dve", lambda e: e.tensor_copy(Wo[:, k, :], sg_[:]), reads=[sg_], writes=[Wo])
                        else:
                            K.op("act", lambda e: e.copy(Wo[:, k, :], sg_[:]), reads=[sg_], writes=[Wo])
                    K.barrier()
                xg = [K.sb(st, [128, 8, 512], F32, "xg", dma=True) for _ in range(2)]
                yt = Ring([K.sb(st, [128, D], BF16, "yt", dma=True) for _ in range(3)])
                gN = K.sb(st, [128, 64], F32, "gN", dma=True)
                K.dma("sp", gN[:], dn_norm_g[li:li + 1, :].partition_broadcast(128), out_buf=gN)
                of_ = Ring([K.sb(st, [128, 2, 256], F32, "of", dma=True) for _ in range(3)])
                zr = Ring([K.sb(st, [128, 256], F32, "z", dma=True) for _ in range(3)])
                osum = Ring([K.sb(st, [128, 256], F32, "osum") for _ in range(2)])
                sqd = Ring([K.sb(st, [128, 256], F32, "sqd") for _ in range(2)])
                scd = Ring([K.sb(st, [128, 8], F32, "scd") for _ in range(2)])
                yTs = [K.sb(st, [128, 8, 512], BF16, "yT") for _ in range(2)]
                ptb = Ring([K.ps(st, [128, 4, 128], BF16, "ptb") for _ in range(4)])
                po = Ring([K.ps(st, [128, 512], F32, "po") for _ in range(3)])
                xTv = xT.rearrange("(k p) t -> p k t", p=128)
                grps = groups_all(with_ctx)
                ev = [0]
                def stageA(gi):
                    t0, n, v = grps[gi]
                    xb = xg[gi % 2]
                    yT = yTs[gi % 2]
                    K.dma("sp", [xb[:, 0:4, 0:n], xb[:, 4:8, 0:n]], [xTv[:, 0:4, t0:t0 + n], xTv[:, 4:8, t0:t0 + n]], out_buf=xb)
                    for tt in range(n // 128):
                        y_ = yt.next()
                        r0_ = t0 + tt * 128
                        K.dma("sp", [y_[:, 0:256], y_[:, 512:1024]], [ycat[r0_:r0_ + 128, 0:256], ycat[r0_:r0_ + 128, 512:1024]], out_buf=y_)
                        o_, z_, os_, sq_, sc_ = of_.next(), zr.next(), osum.next(), sqd.next(), scd.next()
                        K.dma("sp", [o_[:, 0, :], o_[:, 1, :]], [dn_o[0, r0_:r0_ + 128, :], dn_o[1, r0_:r0_ + 128, :]], out_buf=o_)
                        K.dma("sp", z_[:], dn_zab[r0_:r0_ + 128, 0:256], out_buf=z_)
                        K.op("pool", lambda e: e.tensor_tensor(os_[:], o_[:, 0, :], o_[:, 1, :], op=ALU.add), reads=[o_], writes=[os_])
                        K.op("pool", lambda e: e.tensor_tensor(sq_[:], os_[:], os_[:], op=ALU.mult), reads=[os_], writes=[sq_])
                        K.op("dve", lambda e: e.tensor_reduce(out=sc_[:, 0:4], in_=sq_[:].rearrange("p (a b) -> p a b", b=64), axis=AX.X, op=ALU.add), reads=[sq_], writes=[sc_])
                        K.op("act", lambda e: e.activation(out=sc_[:, 4:8], in_=sc_[:, 0:4], func=AF.Sqrt, scale=1.0 / 64.0, bias=float(EPS)), reads=[sc_], writes=[sc_])
                        K.op("dve", lambda e: e.reciprocal(sc_[:, 4:8], sc_[:, 4:8]), reads=[sc_], writes=[sc_])
                        K.op("act", lambda e: e.activation(out=z_[:], in_=z_[:], func=AF.Silu), reads=[z_], writes=[z_])
                        for h in range(4):
                            K.op("dve", lambda e: e.scalar_tensor_tensor(out=os_[:, h * 64:(h + 1) * 64], in0=os_[:, h * 64:(h + 1) * 64], scalar=sc_[:, 4 + h:5 + h],
                                                                          in1=gN[:], op0=ALU.mult, op1=ALU.mult), reads=[os_, sc_, gN], writes=[os_])
                        K.op("pool", lambda e: e.tensor_tensor(y_[:, 256:512], os_[:], z_[:], op=ALU.mult), reads=[os_, z_], writes=[y_])
                        for hh in range(2):
                            p = ptb.next()
                            for kk in range(4):
                                k = hh * 4 + kk
                                tr(K, p, p[:, kk, :], y_, y_[:, k * 128:(k + 1) * 128], identb, identb[:])
                            ev[0] += 1
                            if ev[0] % 2:
                                K.op("dve", lambda e: e.tensor_copy(yT[:, hh * 4:(hh + 1) * 4, tt * 128:(tt + 1) * 128], p[:]), reads=[p], writes=[yT])
                            else:
                                K.op("act", lambda e: e.copy(yT[:, hh * 4:(hh + 1) * 4, tt * 128:(tt + 1) * 128], p[:]), reads=[p], writes=[yT])

                def stageB(gi):
                    t0, n, v = grps[gi]
                    xb = xg[gi % 2]
                    yT = yTs[gi % 2]
                    for f in range(8):
                        pf_ = po.next()
                        for k in range(8):
                            mm(K, pf_, pf_[:, 0:n], Wo, Wo[:, k, f * 128:(f + 1) * 128], yT, yT[:, k, 0:n], start=(k == 0), stop=(k == 7))
                            K.pump(2)
                        K.op("dve", lambda e: e.scalar_tensor_tensor(out=xb[:, f, 0:n], in0=pf_[:, 0:n], scalar=G[:, f, v:v + 1],
                                                                      in1=xb[:, f, 0:n], op0=ALU.mult, op1=ALU.add),
                             reads=[pf_, G, xb], writes=[xb])
                    K.dma("sp", [xTv[:, 0:4, t0:t0 + n], xTv[:, 4:8, t0:t0 + n]], [xb[:, 0:4, 0:n], xb[:, 4:8, 0:n]], in_buf=xb)

                stageA(0)
                for gi in range(len(grps)):
                    if gi + 1 < len(grps):
                        K.defer = K.pending
                        stageA(gi + 1)
                        K.defer = None
                    stageB(gi)
                    K.pump(10 ** 6)
                K.barrier()


        def phase_dn1(li):
            with ExitStack() as st:
                cwr = K.sb(st, [1, 3, 768], F32, "cwr", dma=True)
                pcw = K.ps(st, [128, 6, 3], F32, "pcw")
                cw = K.sb(st, [128, 6, 3], F32, "cw")
                K.dma("sp", cwr[0:1, :, :], dn_conv_w[li:li + 1, :, :], out_buf=cwr)
                for c in range(6):
                    for j in range(3):
                        mm(K, pcw, pcw[:, c, j:j + 1], cwr, cwr[0:1, j, c * 128:(c + 1) * 128], ones_f, ones_f[0:1, 0:1])
                K.op("dve", lambda e: e.tensor_copy(cw[:], pcw[:]), reads=[pcw], writes=[cw])
                xin = [K.sb(st, [128, T], F32, "xin", dma=True) for _ in range(2)]
                hc = [K.sb(st, [128, T], F32, "hc", dma=True) for _ in range(2)]
                otm = Ring([K.sb(st, [128, 4, 128], F32, "otm", dma=True) for _ in range(3)])
                pt = Ring([K.ps(st, [128, 512], F32, "pt") for _ in range(4)])
                ev = [0]
                for c in range(6):
                    x_ = xin[c % 2]
                    h_ = hc[c % 2]
                    K.dma("sp", x_[:], dn_qkvT[c * 128:(c + 1) * 128, :], out_buf=x_)
                    K.op("dve", lambda e: e.tensor_scalar(h_[:], x_[:], cw[:, c, 1:2], None, op0=ALU.mult), reads=[x_, cw], writes=[h_])
                    for (a, b) in ((0, L), (L, T)):
                        K.op("dve", lambda e: e.scalar_tensor_tensor(out=h_[:, a + 1:b], in0=x_[:, a:b - 1], scalar=cw[:, c, 0:1], in1=h_[:, a + 1:b],
                                                                      op0=ALU.mult, op1=ALU.add), reads=[x_, cw, h_], writes=[h_])
                        K.op("dve", lambda e: e.scalar_tensor_tensor(out=h_[:, a:b - 1], in0=x_[:, a + 1:b], scalar=cw[:, c, 2:3], in1=h_[:, a:b - 1],
                                                                      op0=ALU.mult, op1=ALU.add), reads=[x_, cw, h_], writes=[h_])
                    K.op("act", lambda e: e.activation(out=h_[:], in_=h_[:], func=AF.Silu), reads=[h_], writes=[h_])
                    K.dma("sp", dn_hT[c * 128:(c + 1) * 128, :], h_[:], in_buf=h_)
                    for j0 in range(0, 34, 4):
                        nj = min(4, 34 - j0)
                        p = pt.next()
                        for jj in range(nj):
                            tr(K, p, p[:, jj * 128:(jj + 1) * 128], h_, h_[:, (j0 + jj) * 128:(j0 + jj + 1) * 128], ident, ident[:])
                        o = otm.next()
                        ev[0] += 1
                        pv = p[:, 0:nj * 128].rearrange("p (j c) -> p j c", c=128)
                        if ev[0] % 2:
                            K.op("act", lambda e: e.copy(o[:, 0:nj, :], pv), reads=[p], writes=[o])
                        else:
                            K.op("pool", lambda e: e.tensor_copy(o[:, 0:nj, :], pv), reads=[p], writes=[o]) if False else \
                                K.op("dve", lambda e: e.tensor_copy(o[:, 0:nj, :], pv), reads=[p], writes=[o])
                        K.dma("sp", dn_h[j0 * 128:(j0 + nj) * 128, c * 128:(c + 1) * 128].rearrange("(j p) c -> p j c", p=128), o[:, 0:nj, :], in_buf=o)
                K.barrier()

        def phase_dn2(li):
            with ExitStack() as st:
                tri = K.sb(st, [128, 4, 128], F32, "tri", dma=True)
                K.dma("sp", tri[:], tri_in.rearrange("w p q -> p w q"), out_buf=tri)
                TINC = [tri[:, 0, :], tri[:, 2, :]]
                MST = [tri[:, 1, :], tri[:, 3, :]]
                onesf = K.sb(st, [128, 128], F32, "onesf")
                K.op("pool", lambda e: e.memset(onesf[:], 1.0), writes=[onesf])
                dtb = K.sb(st, [128, 8], F32, "dtb", dma=True)
                nA = K.sb(st, [128, 8], F32, "nA", dma=True)
                K.dma("sp", dtb[:], dn_dt_bias[li:li + 1, :].partition_broadcast(128), out_buf=dtb)
                K.dma("sp", nA[:], dn_a_log[li:li + 1, :].partition_broadcast(128), out_buf=nA)
                K.op("act", lambda e: e.activation(out=nA[:], in_=nA[:], func=AF.Exp), reads=[nA], writes=[nA])
                K.op("dve", lambda e: e.tensor_scalar(nA[:], nA[:], -1.0, None, op0=ALU.mult), reads=[nA], writes=[nA])
                Sst = [[[K.sb(st, [64, 64], F32, "S") for _ in range(2)] for _ in range(4)] for _ in range(2)]
                for d in range(2):
                    for h in range(4):
                        K.op("pool", lambda e: e.memset(Sst[d][h][0][:], 0.0), writes=[Sst[d][h][0]])
                NSLOT = 2
                hq = Ring([K.sb(st, [64, 4, 128], F32, "hq", dma=True) for _ in range(2 * NSLOT)])
                hk = Ring([K.sb(st, [64, 4, 128], F32, "hk", dma=True) for _ in range(2 * NSLOT)])
                htok = Ring([K.sb(st, [128, 768], F32, "htok", dma=True) for _ in range(2 * NSLOT)])
                abr = Ring([K.sb(st, [128, 16], F32, "ab", dma=True) for _ in range(2 * NSLOT)])
                scr = Ring([K.sb(st, [128, 96], F32, "sc") for _ in range(2 * NSLOT)])
                sqb = Ring([K.sb(st, [128, 512], F32, "sq") for _ in range(2)])
                otile = Ring([K.sb(st, [128, 256], F32, "ot", dma=True) for _ in range(2 * NSLOT)])
                names128 = ["gsm", "Dsm", "DTim", "Pa", "Pb", "Pta", "Ptb", "Tt", "QKm"]
                names64 = ["Xu", "Xw", "u", "Ke", "vn", "tmp"]
                W = {}
                for sl in range(NSLOT):
                    for d in range(2):
                        for h in range(4):
                            w_ = {n_: K.sb(st, [128, 128], F32, n_) for n_ in names128}
                            w_.update({n_: K.sb(st, [128, 64], F32, n_) for n_ in names64})
                            w_["wT"] = K.sb(st, [64, 128], F32, "wT")
                            W[(sl, d, h)] = w_
                pp = Ring([K.ps(st, [128, 512], F32, "ppb") for _ in range(7)])
                pg = Ring([K.ps(st, [128, 512], F32, "pgb")])
                order = [list(range(34)), [1, 0] + list(range(33, 1, -1))]
                hTv = dn_hT.rearrange("(g h d) t -> g d h t", g=3, d=64)
                ei = [0]

                def evac(dst, dst_ap, p, p_ap):
                    ei[0] += 1
                    if ei[0] % 3 == 0:
                        K.op("dve", lambda e: e.tensor_copy(dst_ap, p_ap), reads=[p], writes=[dst])
                    else:
                        K.op("act", lambda e: e.copy(dst_ap, p_ap), reads=[p], writes=[dst])

                import os as _os
                NS_ = int(_os.environ.get('DN_STEPS', '34'))

                def prepA(s_):
                    return [prepA1(s_, d) for d in range(2)]

                def prepA1(s_, d):
                    if True:
                        j = order[d][s_]
                        HQ, HK, HT, AB, sc, sq = hq.next(), hk.next(), htok.next(), abr.next(), scr.next(), sqb.next()
                        K.dma("sp", HQ[:], hTv[0, :, :, j * 128:(j + 1) * 128], out_buf=HQ)
                        K.dma("sp", HK[:], hTv[1, :, :, j * 128:(j + 1) * 128], out_buf=HK)
                        K.dma("sp", HT[:], dn_h[j * 128:(j + 1) * 128, :], out_buf=HT)
                        K.dma("sp", AB[:], dn_zab[j * 128:(j + 1) * 128, 256:272], out_buf=AB)
                        K.op("dve", lambda e: e.tensor_tensor(sq[:], HT[:, 0:512], HT[:, 0:512], op=ALU.mult), reads=[HT], writes=[sq])
                        K.op("dve", lambda e: e.tensor_reduce(out=sc[:, 0:8], in_=sq[:].rearrange("p (a b) -> p a b", b=64), axis=AX.X, op=ALU.add),
                             reads=[sq], writes=[sc])
                        K.op("act", lambda e: e.activation(out=sc[:, 8:16], in_=sc[:, 0:8], func=AF.Ln, bias=float(EPS), scale=1.0), reads=[sc], writes=[sc])
                        K.op("act", lambda e: e.activation(out=sc[:, 16:24], in_=sc[:, 8:16], func=AF.Exp, scale=-0.5), reads=[sc], writes=[sc])
                        K.op("act", lambda e: e.activation(out=sc[:, 24:32], in_=sc[:, 8:16], func=AF.Exp, scale=0.5), reads=[sc], writes=[sc])
                        K.op("dve", lambda e: e.tensor_scalar(sc[:, 16:20], sc[:, 16:20], 0.125, None, op0=ALU.mult), reads=[sc], writes=[sc])
                        K.op("dve", lambda e: e.tensor_tensor(sc[:, 32:36], AB[:, d * 4:d * 4 + 4], dtb[:, d * 4:d * 4 + 4], op=ALU.add), reads=[AB, dtb], writes=[sc])
                        K.op("act", lambda e: e.activation(out=sc[:, 32:36], in_=sc[:, 32:36], func=AF.Exp), reads=[sc], writes=[sc])
                        K.op("act", lambda e: e.activation(out=sc[:, 32:36], in_=sc[:, 32:36], func=AF.Ln, bias=1.0, scale=1.0), reads=[sc], writes=[sc])
                        K.op("dve", lambda e: e.tensor_tensor(sc[:, 36:40], sc[:, 32:36], nA[:, d * 4:d * 4 + 4], op=ALU.mult), reads=[sc, nA], writes=[sc])
                        K.op("act", lambda e: e.activation(out=sc[:, 40:44], in_=AB[:, 8 + d * 4:12 + d * 4], func=AF.Exp, scale=-1.0), reads=[AB], writes=[sc])
                        K.op("dve", lambda e: e.tensor_scalar(sc[:, 40:44], sc[:, 40:44], 1.0, None, op0=ALU.add), reads=[sc], writes=[sc])
                        K.op("dve", lambda e: e.reciprocal(sc[:, 40:44], sc[:, 40:44]), reads=[sc], writes=[sc])
                        return dict(d=d, j=j, HQ=HQ, HK=HK, HT=HT, AB=AB, sc=sc)

                def prepB(ctxs, s_):
                    U = []
                    for c_ in ctxs:
                        prepB1(c_, s_, U)
                    return U

                def mulcols(sc, o0, a0, b0):
                    K.op("dve", lambda e: e.tensor_tensor(sc[:, o0:o0 + 4], sc[:, a0:a0 + 4], sc[:, b0:b0 + 4], op=ALU.mult), reads=[sc], writes=[sc])

                def prepB1(c_, s_, U):
                    sl = s_ % NSLOT
                    if True:
                        d, j, HQ, HK, HT, AB, sc = c_['d'], c_['j'], c_['HQ'], c_['HK'], c_['HT'], c_['AB'], c_['sc']
                        pg_ = pg.next()
                        mm(K, pg_, pg_[:, 0:4], tri, TINC[d], sc, sc[:, 36:40])
                        mm(K, pg_, pg_[:, 4:8], onesf, onesf[:], sc, sc[:, 36:40])
                        K.op("dve", lambda e: e.tensor_copy(sc[:, 44:52], pg_[:, 0:8]), reads=[pg_], writes=[sc])
                        K.op("act", lambda e: e.activation(out=sc[:, 52:56], in_=sc[:, 44:48], func=AF.Exp), reads=[sc], writes=[sc])
                        K.op("dve", lambda e: e.tensor_tensor(sc[:, 56:60], sc[:, 48:52], sc[:, 44:48], op=ALU.subtract), reads=[sc], writes=[sc])
                        K.op("act", lambda e: e.activation(out=sc[:, 56:60], in_=sc[:, 56:60], func=AF.Exp), reads=[sc], writes=[sc])
                        K.op("act", lambda e: e.activation(out=sc[:, 60:64], in_=sc[:, 48:52], func=AF.Exp), reads=[sc], writes=[sc])
                        for (o0, a0, b0) in ((68, 20, 40), (64, 68, 20), (72, 64, 52), (76, 20, 56), (80, 52, 16)):
                            mulcols(sc, o0, a0, b0)
                        K.op("dve", lambda e: e.tensor_scalar(sc[:, 84:88], sc[:, 28:32], -1.0, None, op0=ALU.mult), reads=[sc], writes=[sc])
                        OT = otile.next()
                        for h in range(4):
                            U.append(dict(d=d, h=h, j=j, HQ=HQ, HK=HK, HT=HT, sc=sc, W=W[(sl, d, h)], OT=OT,
                                          S0=Sst[d][h][s_ % 2], S1=Sst[d][h][(s_ + 1) % 2]))

                U_next = prepB(prepA(0), 0)
                _pp_next = pp.next

                def _pp_pump():
                    K.pump(1)
                    return _pp_next()
                pp.next = _pp_pump
                for s_ in range(NS_):
                    sl = s_ % NSLOT
                    U = U_next
                    ctxA = None
                    if s_ + 1 < NS_:
                        K.defer = K.pending
                        U_next = prepB(prepA(s_ + 1), s_ + 1)
                        K.defer = None

                    def col(u, c0):
                        return u["sc"][:, c0 + u["h"]:c0 + u["h"] + 1]

                    _stg = int(_os.environ.get('DN_STAGE', '99'))
                    if _stg >= 1:
                        for u in U:
                            w_ = u["W"]
                            K.op("dve", lambda e: e.tensor_scalar(w_["gsm"][:], MST[u["d"]], col(u, 36), None, op0=ALU.mult), reads=[tri, u["sc"]], writes=[w_["gsm"]])
                    if _stg >= 2:
                        for u in U:
                            w_ = u["W"]
                            pb_ = pp.next()
                            p1, p2 = pb_, pb_
                            mm(K, p1, p1[:, 0:128], tri, TINC[u["d"]], w_["gsm"], w_["gsm"][:])
                            mm(K, p2, p2[:, 128:256], w_["gsm"], w_["gsm"][:], tri, TINC[u["d"]])
                            K.op("act", lambda e: e.activation(out=w_["Dsm"][:], in_=p1[:, 0:128], func=AF.Exp), reads=[p1], writes=[w_["Dsm"]])
                            K.op("act", lambda e: e.activation(out=w_["DTim"][:], in_=p2[:, 128:256], func=AF.Exp), reads=[p2], writes=[w_["DTim"]])
                            K.op("pool", lambda e: e.tensor_tensor(w_["Dsm"][:], w_["Dsm"][:], MST[u["d"]], op=ALU.mult), reads=[w_["Dsm"], tri], writes=[w_["Dsm"]])
                            K.op("pool", lambda e: e.tensor_tensor(w_["DTim"][:], w_["DTim"][:], TINC[u["d"]], op=ALU.mult), reads=[w_["DTim"], tri], writes=[w_["DTim"]])
                    if _stg >= 3:
                        for u in U:
                            w_ = u["W"]
                            h = u["h"]
                            pb_ = pp.next()
                            p1, p2 = pb_, pb_
                            mm(K, p1, p1[:, 0:128], u["HK"], u["HK"][:, h, :], u["HK"], u["HK"][:, h, :])
                            mm(K, p2, p2[:, 128:256], u["HK"], u["HK"][:, h, :], u["HQ"], u["HQ"][:, h, :])
                            K.op("dve", lambda e: e.scalar_tensor_tensor(out=w_["Pa"][:], in0=p1[:, 0:128], scalar=col(u, 64), in1=w_["Dsm"][:], op0=ALU.mult, op1=ALU.mult),
                                 reads=[p1, u["sc"], w_["Dsm"]], writes=[w_["Pa"]])
                            K.op("dve", lambda e: e.scalar_tensor_tensor(out=w_["QKm"][:], in0=p2[:, 128:256], scalar=col(u, 20), in1=w_["DTim"][:], op0=ALU.mult, op1=ALU.mult),
                                 reads=[p2, u["sc"], w_["DTim"]], writes=[w_["QKm"]])
                    if _stg >= 4:
                        for u in U:
                            w_ = u["W"]
                            p1 = pp.next()
                            tr(K, p1, p1[:, 0:128], w_["Pa"], w_["Pa"][:], ident, ident[:])
                            evac(w_["Pta"], w_["Pta"][:], p1, p1[:, 0:128])
                            K.op("pool", lambda e: e.tensor_tensor(w_["Tt"][:], ident[:], w_["Pta"][:], op=ALU.subtract), reads=[ident, w_["Pta"]], writes=[w_["Tt"]])
                            u["P"], u["Pt"], u["Pn"], u["Ptn"] = w_["Pa"], w_["Pta"], w_["Pb"], w_["Ptb"]
                    if _stg >= 5:
                        for lev in range(1, 7):
                            for u in U:
                                pb_ = pp.next()
                                mm(K, pb_, pb_[:, 0:128], u["Pt"], u["Pt"][:], u["P"], u["P"][:])
                                if lev < 6:
                                    mm(K, pb_, pb_[:, 128:256], u["P"], u["P"][:], u["Pt"], u["Pt"][:])
                                K.op("act", lambda e: e.copy(u["Pn"][:], pb_[:, 0:128]), reads=[pb_], writes=[u["Pn"]])
                                if lev < 6:
                                    K.op("act", lambda e: e.copy(u["Ptn"][:], pb_[:, 128:256]), reads=[pb_], writes=[u["Ptn"]])
                            for g0 in range(0, len(U), 4):
                                pb_ = pp.next()
                                for gi_, u in enumerate(U[g0:g0 + 4]):
                                    w_ = u["W"]
                                    mm(K, pb_, pb_[:, gi_ * 128:(gi_ + 1) * 128], u["Pn"], u["Pn"][:], w_["Tt"], w_["Tt"][:])
                                for gi_, u in enumerate(U[g0:g0 + 4]):
                                    w_ = u["W"]
                                    K.op("dve", lambda e: e.tensor_tensor(w_["Tt"][:], w_["Tt"][:], pb_[:, gi_ * 128:(gi_ + 1) * 128], op=ALU.add), reads=[w_["Tt"], pb_], writes=[w_["Tt"]])
                                    u["P"], u["Pn"] = u["Pn"], u["P"]
                                    u["Pt"], u["Ptn"] = u["Ptn"], u["Pt"]
                    if _stg >= 6:
                        for u in U:
                            w_ = u["W"]
                            h = u["h"]
                            HT = u["HT"]
                            _m8 = int(_os.environ.get('DN_S8', '31'))
                            if _m8 & 1:
                                K.op("pool", lambda e: e.tensor_scalar(w_["Xu"][:], HT[:, 512 + h * 64:512 + (h + 1) * 64], col(u, 68), None, op0=ALU.mult), reads=[HT, u["sc"]], writes=[w_["Xu"]])
                                K.op("pool", lambda e: e.tensor_scalar(w_["Xw"][:], HT[:, 256 + h * 64:256 + (h + 1) * 64], col(u, 72), None, op0=ALU.mult), reads=[HT, u["sc"]], writes=[w_["Xw"]])
                                K.op("pool", lambda e: e.tensor_scalar(w_["Ke"][:], HT[:, 256 + h * 64:256 + (h + 1) * 64], col(u, 76), None, op0=ALU.mult), reads=[HT, u["sc"]], writes=[w_["Ke"]])
                            pb_ = pp.next()
                            p1, p2 = pb_, pb_
                            if _m8 & 2:
                                mm(K, p1, p1[:, 0:64], w_["Tt"], w_["Tt"][:], w_["Xu"], w_["Xu"][:])
                            if _m8 & 4:
                                mm(K, p2, p2[0:64, 128:256], w_["Xw"], w_["Xw"][:], w_["Tt"], w_["Tt"][:])
                            if _m8 & 8:
                                K.op("act", lambda e: e.activation(out=w_["u"][:], in_=p1[:, 0:64], func=AF.Identity, scale=col(u, 28)), reads=[p1, u["sc"]], writes=[w_["u"]])
                            if _m8 & 16:
                                K.op("act", lambda e: e.copy(w_["wT"][:], p2[0:64, 128:256]), reads=[p2], writes=[w_["wT"]])
                    if _stg >= 7:
                        for u in U:
                            w_ = u["W"]
                            p1 = pp.next()
                            mm(K, p1, p1[:, 0:64], w_["wT"], w_["wT"][:], u["S0"], u["S0"][:])
                            K.op("dve", lambda e: e.scalar_tensor_tensor(out=w_["vn"][:], in0=p1[:, 0:64], scalar=col(u, 84), in1=w_["u"][:], op0=ALU.mult, op1=ALU.add),
                                 reads=[p1, u["sc"], w_["u"]], writes=[w_["vn"]])
                        for u in U:
                            w_ = u["W"]
                            h = u["h"]
                            pb_ = pp.next()
                            p1, p2, p3 = pp.next(), pb_, pb_
                            mm(K, p1, p1[:, 0:64], u["HQ"], u["HQ"][:, h, :], u["S0"], u["S0"][:])
                            mm(K, p2, p2[:, 128:192], w_["QKm"], w_["QKm"][:], w_["vn"], w_["vn"][:])
                            mm(K, p3, p3[0:64, 256:320], w_["Ke"], w_["Ke"][:], w_["vn"], w_["vn"][:])
                            K.op("act", lambda e: e.activation(out=w_["tmp"][:], in_=p1[:, 0:64], func=AF.Identity, scale=col(u, 80)), reads=[p1, u["sc"]], writes=[w_["tmp"]])
                            K.op("dve", lambda e: e.scalar_tensor_tensor(out=u["OT"][:, h * 64:(h + 1) * 64], in0=p2[:, 128:192], scalar=col(u, 16), in1=w_["tmp"][:], op0=ALU.mult, op1=ALU.add),
                                 reads=[p2, u["sc"], w_["tmp"]], writes=[u["OT"]])
                            K.op("dve", lambda e: e.scalar_tensor_tensor(out=u["S1"][:], in0=u["S0"][:], scalar=u["sc"][0:64, 60 + h:61 + h], in1=p3[0:64, 256:320], op0=ALU.mult, op1=ALU.add),
                                 reads=[u["S0"], u["sc"], p3], writes=[u["S1"]])
                    for u in U:
                        if u["h"] == 3:
                            j = u["j"]
                            K.dma("sp", dn_o[u["d"], j * 128:(j + 1) * 128, :], u["OT"][:], in_buf=u["OT"])
                    K.pump(10 ** 6)
                K.barrier()

        def phase_dn3(li, need_ctx):
            with ExitStack() as st:
                gN = K.sb(st, [128, 64], F32, "gN", dma=True)
                K.dma("sp", gN[:], dn_norm_g[li:li + 1, :].partition_broadcast(128), out_buf=gN)
                of_ = Ring([K.sb(st, [128, 2, 256], F32, "of", dma=True) for _ in range(2)])
                zr = Ring([K.sb(st, [128, 256], F32, "z", dma=True) for _ in range(2)])
                osum = K.sb(st, [128, 256], F32, "osum")
                sq = K.sb(st, [128, 256], F32, "sq")
                sc = K.sb(st, [128, 8], F32, "sc")
                yb = Ring([K.sb(st, [128, 256], BF16, "yb", dma=True) for _ in range(2)])
                for j in range(0 if need_ctx else 2, 34):
                    o_, z_, y_ = of_.next(), zr.next(), yb.next()
                    K.dma("sp", [o_[:, 0, :], o_[:, 1, :]], [dn_o[0, j * 128:(j + 1) * 128, :], dn_o[1, j * 128:(j + 1) * 128, :]], out_buf=o_)
                    K.dma("sp", z_[:], dn_zab[j * 128:(j + 1) * 128, 0:256], out_buf=z_)
                    K.op("dve", lambda e: e.tensor_tensor(osum[:], o_[:, 0, :], o_[:, 1, :], op=ALU.add), reads=[o_], writes=[osum])
                    K.op("pool", lambda e: e.tensor_tensor(sq[:], osum[:], osum[:], op=ALU.mult), reads=[osum], writes=[sq])
                    K.op("dve", lambda e: e.tensor_reduce(out=sc[:, 0:4], in_=sq[:].rearrange("p (a b) -> p a b", b=64), axis=AX.X, op=ALU.add), reads=[sq], writes=[sc])
                    K.op("act", lambda e: e.activation(out=sc[:, 4:8], in_=sc[:, 0:4], func=AF.Sqrt, scale=1.0 / 64.0, bias=float(EPS)), reads=[sc], writes=[sc])
                    K.op("dve", lambda e: e.reciprocal(sc[:, 4:8], sc[:, 4:8]), reads=[sc], writes=[sc])
                    K.op("act", lambda e: e.activation(out=z_[:], in_=z_[:], func=AF.Silu), reads=[z_], writes=[z_])
                    for h in range(4):
                        K.op("dve", lambda e: e.scalar_tensor_tensor(out=osum[:, h * 64:(h + 1) * 64], in0=osum[:, h * 64:(h + 1) * 64], scalar=sc[:, 4 + h:5 + h],
                                                                      in1=gN[:], op0=ALU.mult, op1=ALU.mult), reads=[osum, sc, gN], writes=[osum])
                    K.op("dve", lambda e: e.tensor_tensor(y_[:], osum[:], z_[:], op=ALU.mult), reads=[osum, z_], writes=[y_])
                    K.dma("sp", ycat[j * 128:(j + 1) * 128, 256:512], y_[:], in_buf=y_)
                K.barrier()

        def phase_final():
            with ExitStack() as st:
                xg = [K.sb(st, [128, 8, 512], F32, "xg", dma=True) for _ in range(2)]
                xn = K.sb(st, [128, 8, 512], BF16, "xn")
                rstd = K.sb(st, [128, 512], F32, "rstd")
                fg = K.sb(st, [128, 8], F32, "fg")
                yo = [K.sb(st, [128, D], F32, "yo", dma=True) for _ in range(2)]
                pss = K.ps(st, [128, 512], F32, "pss")
                pt = [K.ps(st, [128, 512], F32, "pt") for _ in range(4)]
                xTv = xT.rearrange("(k p) t -> p k t", p=128)
                K.op("dve", lambda e: e.tensor_scalar(fg[:], fin_g[:], float(np.sqrt(D)), None, op0=ALU.mult), reads=[fin_g], writes=[fg])
                grps = groups_all(False)
                cnt = 0
                def ldf(gi):
                    t0_, n_, v_ = grps[gi]
                    xb_ = xg[gi % 2]
                    K.dma("sp", [xb_[:, 0:4, :], xb_[:, 4:8, :]], [xTv[:, 0:4, t0_:t0_ + n_], xTv[:, 4:8, t0_:t0_ + n_]], out_buf=xb_)

                ldf(0)
                for gi, (t0, n, v) in enumerate(grps):
                    xb = xg[gi % 2]
                    if gi + 1 < len(grps):
                        ldf(gi + 1)
                    K.op("act", lambda e: e.activation(out=xn[:], in_=xb[:], func=AF.Square), reads=[xb], writes=[xn])
                    for k in range(8):
                        mm(K, pss, pss[:], ones_b, ones_b[:], xn, xn[:, k, :], start=(k == 0), stop=(k == 7))
                    K.op("act", lambda e: e.activation(out=rstd[:], in_=pss[:], func=AF.Sqrt, scale=1.0, bias=float(D * EPS)), reads=[pss], writes=[rstd])
                    K.op("dve", lambda e: e.reciprocal(rstd[:], rstd[:]), reads=[rstd], writes=[rstd])
                    for k in range(8):
                        e_ = "dve"
                        K.op(e_, lambda e: e.scalar_tensor_tensor(out=xb[:, k, :], in0=xb[:, k, :], scalar=fg[:, k:k + 1],
                                                                   in1=rstd[:], op0=ALU.mult, op1=ALU.mult),
                             reads=[xb, fg, rstd], writes=[xb])
                    for tt in range(4):
                        o = yo[cnt % 2]
                        for hh in range(2):
                            p = pt[(2 * cnt + hh) % 4]
                            for kk in range(4):
                                k = hh * 4 + kk
                                tr(K, p, p[:, kk * 128:(kk + 1) * 128], xb, xb[:, k, tt * 128:(tt + 1) * 128], ident, ident[:])
                            if hh == 0:
                                K.op("act", lambda e: e.copy(o[:, 0:512], p[:]), reads=[p], writes=[o])
                            else:
                                K.op("dve", lambda e: e.tensor_copy(o[:, 512:1024], p[:]), reads=[p], writes=[o])
                        r0 = t0 - L + tt * 128
                        K.dma("sp", y_out[r0:r0 + 128, :], o[:], in_buf=o)
                        cnt += 1
                K.barrier()

        if only is not None:
            name, li, need_ctx = only
            if name == "dn":
                K.dma("sp", ident[:], ident_in[:, :], out_buf=ident)
                import os
                sub = os.environ.get("DN_SUB", "123")
                if "1" in sub:
                    phase_dn1(li)
                if "2" in sub:
                    phase_dn2(li)
                if "3" in sub:
                    phase_dn3(li, need_ctx)
            else:
                {"swa": phase_swa, "mla": phase_mla, "na": phase_na}[name](li, need_ctx)
            K.finish()
            return nc
        phase_init()
        for li in range(DEPTH):
            need_ctx = li < DEPTH - 1
            phase_mod(li)
            phase_ffn(ffn1_wg[li], ffn1_wu[li], ffn1_wd[li], 0, with_ctx=True)
            if stop_after == "ffn1":
                break
            phase_inproj(li)
            if stop_after == "inproj":
                break
            phase_swa(li, need_ctx)
            phase_mla(li, need_ctx)
            phase_na(li, need_ctx)
            phase_dn1(li)
            phase_dn2(li)
            phase_outproj(li, need_ctx)
            phase_ffn(ffn2_wg[li], ffn2_wu[li], ffn2_wd[li], 2, with_ctx=need_ctx)
        phase_final()
        K.finish()
    return nc


INPUT_NAMES = ["ada_w", "ada_b", "norm1_g", "ffn1_wg", "ffn1_wu", "ffn1_wd", "norm2_g", "norm3_g",
               "ffn2_wg", "ffn2_wu", "ffn2_wd", "w_in", "w_out", "swa_sink", "mla_q_norm_g", "mla_w_uq",
               "mla_kv_norm_g", "mla_w_ukv", "dn_conv_w", "dn_norm_g"]


def host_consts():
    theta = 10000.0
    tpos = np.arange(S)
    row = (tpos // 64).astype(np.float64)
    col = (tpos % 64).astype(np.float64)

    def tab(nrows, qw, nfreq_total):
        c = np.ones((nrows, T), np.float64)
        s_ = np.zeros((nrows, T), np.float64)
        for d in range(nrows):
            dd = d % (4 * qw)
            q, j = dd // qw, dd % qw
            inv = theta ** (-(2.0 * j) / (2 * qw))
            pos = row if q < 2 else col
            c[d, L:] = np.cos(pos * inv)
            s_[d, L:] = np.sin(pos * inv)
        return np.stack([c, s_]).astype(np.float32)

    t64 = tab(128, 16, 16)
    t32 = tab(32, 8, 8)
    t96 = np.concatenate([np.stack([np.ones((64, T)), np.zeros((64, T))]).astype(np.float32), t32], axis=1)
    kp = np.arange(128)[:, None]
    qq = np.arange(128)[None, :]
    mask_pn = np.stack([(qq <= kp), (kp <= qq)]).astype(np.float32)
    qc = np.arange(64)[None, :]
    kc = np.arange(64)[:, None]
    cs = np.clip(qc - 8, 0, 48)
    ok = ((kc >= cs) & (kc < cs + 16)).astype(np.float32)
    okm = np.concatenate([ok, ok], 0)
    a_ = np.arange(128)[:, None]
    b_ = np.arange(128)[None, :]
    tri = np.stack([a_ <= b_, a_ > b_, a_ >= b_, a_ < b_]).astype(np.float32)
    return {"tri": tri, "okm": okm, "ident": np.eye(128, dtype=np.float32), "tab64": t64, "tab96": np.ascontiguousarray(t96), "tab32": t32,
            "mask_pn": mask_pn}


def make_in_maps(inputs):
    consts = host_consts()
    idx = np.clip(np.arange(64)[:, None] - np.arange(64)[None, :] + 15, 0, 30)
    rpbx = np.ascontiguousarray(np.transpose(inputs["na_rpb"][:, :, :, idx], (0, 3, 1, 2, 4)))
    maps = []
    for b in range(8):
        m = {
            "x": np.ascontiguousarray(inputs["x"][b]),
            "c": np.ascontiguousarray(inputs["c"][b:b + 1]),
            "ctx": np.ascontiguousarray(inputs["ctx"][b]),
            "c_ctx": np.ascontiguousarray(inputs["c_ctx"][None, :]),
            "final_norm_g": np.ascontiguousarray(inputs["final_norm_g"][None, :]),
        }
        m.update(consts)
        m["rpbx"] = rpbx
        m["dn_a_log"] = np.ascontiguousarray(inputs["dn_a_log"].reshape(DEPTH, 8))
        m["dn_dt_bias"] = np.ascontiguousarray(inputs["dn_dt_bias"].reshape(DEPTH, 8))
        for n in INPUT_NAMES:
            m[n] = np.ascontiguousarray(inputs[n])
        maps.append(m)
    return maps


def kernel(**inputs):
    inputs = {k: np.asarray(v) for k, v in inputs.items()}
    nc = build_program()
    res = run_bass_kernel_spmd(nc, make_in_maps(inputs), core_ids=list(range(8)))
    return np.stack([r["y"] for r in res.results], axis=0).astype(np.float32)
```
